# Optimizing a Trainium2 kernel written in Bass

```python
import math
import jax, jax.numpy as jnp
from jax import lax
import numpy as np

D_MODEL = 1024
BATCH = 32
SEQ = 2048
DEPTH = 2

N_EVEN = (DEPTH + 1) // 2
N_ODD = DEPTH // 2
NORM_EPS = 1e-6
ROPE_THETA = 10000.0
NEG_INF = -1e30

DN_HEADS = 4
DN_HEAD_DIM = 128
DN_WIDTH = DN_HEADS * DN_HEAD_DIM
DN_CONV = 3
DN_CHUNK = 64
DN_COLS = 4 * DN_WIDTH + 4 * DN_HEADS
DIL_HEADS = 8
DIL_HEAD_DIM = 64
DIL_WIDTH = DIL_HEADS * DIL_HEAD_DIM
DIL_PATTERNS = ((128, 1), (512, 4), (2048, 16))
AB_IN = DN_COLS + 3 * DIL_WIDTH
AB_OUT = DN_WIDTH + DIL_WIDTH
RET_HEADS = 4
RET_KEY_DIM = 64
RET_VAL_DIM = 128
RET_QK = RET_HEADS * RET_KEY_DIM
RET_V = RET_HEADS * RET_VAL_DIM
RET_CHUNK = 64
RET_COLS = 2 * RET_QK + 2 * RET_V
HY_WIDTH = 512
HY_SHORT = 3
HY_EMB = 33
HY_ORDER = 64
HY_TARGET = 1e-2
HY_FAST = 0.3
HY_SLOW = 1.5
CD_IN = RET_COLS + 3 * HY_WIDTH
CD_OUT = RET_V + HY_WIDTH
D_FF = 2816
FFN_CONV = 3

kernel_name = 'bidir_hybrid_deltanet_dilated_retention_hyena'

f32 = jnp.float32


def rmsnorm(x, w=None):
    xf = x.astype(f32)
    y = xf * lax.rsqrt(jnp.mean(xf * xf, axis=-1, keepdims=True) + NORM_EPS)
    if w is not None:
        y = y * w.astype(f32)
    return y.astype(x.dtype)


def l2norm(t):
    return t * lax.rsqrt(jnp.sum(t * t, axis=-1, keepdims=True) + 1e-6)


def dwconv(x, w):
    K, C = w.shape
    return lax.conv_general_dilated(x, w[:, None, :].astype(x.dtype), window_strides=(1,),
                                    padding=[(K // 2, K // 2)],
                                    dimension_numbers=('NWC', 'WIO', 'NWC'),
                                    feature_group_count=C)


def rope(t):
    S, E = t.shape[1], t.shape[-1]
    inv = ROPE_THETA ** (-jnp.arange(0, E, 2, dtype=f32) / E)
    ang = jnp.arange(S, dtype=f32)[:, None] * inv[None, :]
    cos, sin = jnp.cos(ang)[None, :, None, :], jnp.sin(ang)[None, :, None, :]
    tf = t.astype(f32)
    t1, t2 = tf[..., : E // 2], tf[..., E // 2:]
    return jnp.concatenate([t1 * cos - t2 * sin, t1 * sin + t2 * cos], axis=-1)


def gated_delta_chunked(q, k, v, g, beta):
    Bsz, H, S, dk = q.shape
    dv = v.shape[-1]
    C = DN_CHUNK
    N = S // C
    chunk = lambda t: t.reshape((Bsz, H, N, C) + t.shape[3:])
    q = chunk(q * dk ** -0.5)
    k = chunk(k)
    v = chunk(v)
    beta = chunk(beta)
    g = jnp.cumsum(chunk(g), axis=-1)
    k_beta = k * beta[..., None]
    lower = jnp.tril(jnp.ones((C, C), bool))
    strict = jnp.tril(jnp.ones((C, C), bool), -1)
    diff = g[..., :, None] - g[..., None, :]
    decay = jnp.where(lower, jnp.exp(jnp.where(lower, diff, 0.0)), 0.0)
    a = jnp.where(strict, jnp.einsum('bhnid,bhnjd->bhnij', k_beta, k) * decay, 0.0)
    m = a + jnp.eye(C, dtype=a.dtype)
    rhs = jnp.concatenate([v * beta[..., None], k_beta * jnp.exp(g)[..., None]], axis=-1)
    sol = lax.linalg.triangular_solve(m, rhs, left_side=True, lower=True, unit_diagonal=True)
    u, w = sol[..., :dv], sol[..., dv:]
    intra = jnp.einsum('bhnid,bhnjd->bhnij', q, k) * decay
    q_dec = q * jnp.exp(g)[..., None]
    g_last = g[..., -1]
    k_dec = k * jnp.exp(g_last[..., None] - g)[..., None]
    lead = lambda t: jnp.moveaxis(t, 2, 0)

    def step(state, inp):
        q_c, k_c, u_c, w_c, a_c, gl = inp
        v_new = u_c - jnp.einsum('bhik,bhkv->bhiv', w_c, state)
        o_c = jnp.einsum('bhik,bhkv->bhiv', q_c, state) + jnp.einsum('bhij,bhjv->bhiv', a_c, v_new)
        state = state * jnp.exp(gl)[..., None, None] + jnp.einsum('bhjk,bhjv->bhkv', k_c, v_new)
        return state, o_c

    state0 = jnp.zeros((Bsz, H, dk, dv), f32)
    _, o = lax.scan(step, state0, (lead(q_dec), lead(k_dec), lead(u), lead(w), lead(intra), lead(g_last)))
    return jnp.moveaxis(o, 0, 2).reshape(Bsz, H, S, dv)


def gated_deltanet(proj, conv_w, a_log, dt_bias, norm_w):
    Bsz, S, _ = proj.shape
    W, H, E = DN_WIDTH, DN_HEADS, DN_HEAD_DIM
    qkv = jax.nn.silu(dwconv(proj[..., : 3 * W], conv_w))
    q, k, v = [t.reshape(Bsz, S, H, E).astype(f32) for t in jnp.split(qkv, 3, axis=-1)]
    q, k = l2norm(q), l2norm(k)
    z = proj[..., 3 * W: 4 * W].reshape(Bsz, S, H, E).astype(f32)
    b = proj[..., 4 * W: 4 * W + 2 * H].reshape(Bsz, S, 2, H).astype(f32)
    a = proj[..., 4 * W + 2 * H:].reshape(Bsz, S, 2, H).astype(f32)
    beta = jax.nn.sigmoid(b)
    g = -jnp.exp(a_log.astype(f32)) * jax.nn.softplus(a + dt_bias.astype(f32))
    bhs = lambda t: jnp.swapaxes(t, 1, 2)
    rev = lambda t: jnp.flip(t, axis=2)
    q, k, v = bhs(q), bhs(k), bhs(v)
    o_f = gated_delta_chunked(q, k, v, bhs(g[:, :, 0]), bhs(beta[:, :, 0]))
    o_b = rev(gated_delta_chunked(rev(q), rev(k), rev(v), rev(bhs(g[:, :, 1])), rev(bhs(beta[:, :, 1]))))
    o = bhs(o_f + o_b)
    o = rmsnorm(o, norm_w) * jax.nn.silu(z)
    return o.reshape(Bsz, S, W)


def dilated_branch(q, k, v, window, dilation):
    Bsz, S, H, E = q.shape
    half = window // (2 * dilation)
    L = S // dilation
    blk = half
    nb = -(-L // blk)
    Lp = nb * blk
    strided = lambda t: t.reshape(Bsz, L, dilation, H, E).transpose(0, 2, 3, 1, 4)
    qs, ks, vs = strided(q), strided(k), strided(v)
    qb = jnp.pad(qs, [(0, 0)] * 3 + [(0, Lp - L), (0, 0)]).reshape(Bsz, dilation, H, nb, blk, E)

    def band(t):
        tb = jnp.pad(t, [(0, 0)] * 3 + [(blk, Lp - L + blk), (0, 0)]).reshape(Bsz, dilation, H, nb + 2, blk, E)
        return jnp.concatenate([tb[:, :, :, :-2], tb[:, :, :, 1:-1], tb[:, :, :, 2:]], axis=-2)

    kb, vb = band(ks), band(vs)
    qi = jnp.arange(nb)[:, None] * blk + jnp.arange(blk)[None, :]
    ki = jnp.arange(nb)[:, None] * blk - blk + jnp.arange(3 * blk)[None, :]
    rel = ki[:, None, :] - qi[:, :, None]
    valid = (jnp.abs(rel) <= half) & (ki[:, None, :] >= 0) & (ki[:, None, :] < L)
    s = jnp.einsum('bdhnqe,bdhnke->bdhnqk', qb, kb) * E ** -0.5
    s = jnp.where(valid, s, NEG_INF)
    mx = jnp.max(s, axis=-1, keepdims=True)
    ex = jnp.exp(s - mx)
    den = jnp.sum(ex, axis=-1, keepdims=True)
    o = jnp.einsum('bdhnqk,bdhnke->bdhnqe', ex / den, vb)
    lse = (mx + jnp.log(den))[..., 0]
    o = o.reshape(Bsz, dilation, H, Lp, E)[:, :, :, :L].transpose(0, 3, 1, 2, 4).reshape(Bsz, S, H, E)
    lse = lse.reshape(Bsz, dilation, H, Lp)[..., :L].transpose(0, 3, 1, 2).reshape(Bsz, S, H)
    return o, lse


def dilated_attention(proj):
    Bsz, S, _ = proj.shape
    shp = (Bsz, S, DIL_HEADS, DIL_HEAD_DIM)
    q, k, v = jnp.split(proj, 3, axis=-1)
    q, k = rope(q.reshape(shp)), rope(k.reshape(shp))
    v = v.reshape(shp).astype(f32)
    outs, lses = [], []
    for window, dilation in DIL_PATTERNS:
        o, l = dilated_branch(q, k, v, window, dilation)
        outs.append(o)
        lses.append(l)
    wts = jax.nn.softmax(jnp.stack(lses), axis=0)
    o = jnp.einsum('pbsh,pbshe->bshe', wts, jnp.stack(outs))
    return o.reshape(Bsz, S, DIL_WIDTH)


def mixer_ab(h, w_in, conv_w, a_log, dt_bias, norm_w, w_out):
    proj = h @ w_in
    y_a = gated_deltanet(proj[..., :DN_COLS], conv_w, a_log, dt_bias, norm_w)
    y_b = dilated_attention(proj[..., DN_COLS:])
    return (jnp.concatenate([y_a, y_b], axis=-1) @ w_out).astype(h.dtype)


def retention_chunked(q, k, v, log_gamma, include_diag):
    Bsz, H, S, dk = q.shape
    dv = v.shape[-1]
    C = RET_CHUNK
    N = S // C
    chunk = lambda t: t.reshape(Bsz, H, N, C, t.shape[-1])
    q, k, v = chunk(q), chunk(k), chunk(v)
    log_gamma = log_gamma.astype(f32)
    idx = jnp.arange(C, dtype=f32)
    rel = idx[:, None] - idx[None, :]
    mask = rel >= 0 if include_diag else rel > 0
    dmat = jnp.where(mask, jnp.exp(log_gamma[:, None, None] * jnp.where(mask, rel, 0.0)), 0.0)
    scores = jnp.einsum('bhnid,bhnjd->bhnij', q, k) * dmat[:, None]
    o = jnp.einsum('bhnij,bhnje->bhnie', scores, v)
    lgv = log_gamma[:, None]
    k_dec = k * jnp.exp(lgv * (C - 1 - idx))[:, None, :, None]
    kv = jnp.einsum('bhnjd,bhnje->nbhde', k_dec, v)
    chunk_decay = jnp.exp(log_gamma * C)[:, None, None]

    def step(state, kv_c):
        return state * chunk_decay + kv_c, state

    _, prev = lax.scan(step, jnp.zeros((Bsz, H, dk, dv), f32), kv)
    q_dec = q * jnp.exp(lgv * (idx + 1))[:, None, :, None]
    o = o + jnp.einsum('bhnid,nbhde->bhnie', q_dec, prev)
    return o.reshape(Bsz, H, S, dv)


def retention(proj, log_decay):
    Bsz, S, _ = proj.shape
    H = RET_HEADS
    q = proj[..., :RET_QK].reshape(Bsz, S, H, RET_KEY_DIM)
    k = proj[..., RET_QK: 2 * RET_QK].reshape(Bsz, S, H, RET_KEY_DIM)
    v = proj[..., 2 * RET_QK: 2 * RET_QK + RET_V].reshape(Bsz, S, H, RET_VAL_DIM).astype(f32)
    gate = proj[..., 2 * RET_QK + RET_V:].reshape(Bsz, S, H, RET_VAL_DIM).astype(f32)
    q = rope(q) * RET_KEY_DIM ** -0.5
    k = rope(k)
    bhs = lambda t: jnp.swapaxes(t, 1, 2)
    rev = lambda t: jnp.flip(t, axis=2)
    q, k, v = bhs(q), bhs(k), bhs(v)
    o_f = retention_chunked(q, k, v, log_decay[0], True)
    o_b = rev(retention_chunked(rev(q), rev(k), rev(v), log_decay[1], False))
    o = rmsnorm(bhs(o_f + o_b)) * jax.nn.silu(gate)
    return o.reshape(Bsz, S, RET_V)


def hyena_filters(L, w1, b1, f1, w2, b2, f2, w3):
    t = jnp.linspace(0.0, 1.0, L, dtype=f32)[:, None]
    bands = (HY_EMB - 1) // 2
    w = 2.0 * math.pi * jnp.arange(L, dtype=f32) / L
    f = jnp.linspace(1e-4, bands - 1, bands, dtype=f32)
    ang = w[:, None] * f[None, :]
    z = jnp.concatenate([t, jnp.cos(ang), -jnp.sin(ang)], axis=-1)
    hid = jnp.sin(f1 * (z @ w1 + b1))
    hid = jnp.sin(f2 * (hid @ w2 + b2))
    filt = (hid @ w3).astype(f32).reshape(L, 2, HY_WIDTH)
    deltas = jnp.abs(jnp.linspace(math.log(HY_TARGET) / HY_SLOW, math.log(HY_TARGET) / HY_FAST, HY_WIDTH, dtype=f32))
    filt = filt * jnp.exp(-t * deltas)[:, None, :]
    return filt[:, 0], filt[:, 1]


def long_conv_bidir(u, h_fwd, h_bwd):
    L = u.shape[1]
    n = 2 * L
    spec = jnp.fft.rfft(h_fwd, n=n, axis=0) + jnp.conj(jnp.fft.rfft(h_bwd, n=n, axis=0))
    y = jnp.fft.irfft(jnp.fft.rfft(u, n=n, axis=1) * spec[None], n=n, axis=1)
    return y[:, :L]


def hyena(proj, conv_w, conv_b, w1, b1, f1, w2, b2, f2, w3, bias):
    S = proj.shape[1]
    uc = (dwconv(proj, conv_w) + conv_b).astype(f32)
    x0, x1, v = jnp.split(uc, 3, axis=-1)
    h_fwd, h_bwd = hyena_filters(S, w1, b1, f1, w2, b2, f2, w3)
    v = v * x1
    v = long_conv_bidir(v, h_fwd, h_bwd) + v * bias.astype(f32)
    return v * x0


def mixer_cd(h, w_in, log_decay, conv_w, conv_b, w1, b1, f1, w2, b2, f2, w3, bias, w_out):
    proj = h @ w_in
    y_c = retention(proj[..., :RET_COLS], log_decay)
    y_d = hyena(proj[..., RET_COLS:], conv_w, conv_b, w1, b1, f1, w2, b2, f2, w3, bias)
    return (jnp.concatenate([y_c, y_d], axis=-1) @ w_out).astype(h.dtype)


def conv_ffn(h, w_in, conv_w, conv_b, w_out):
    gate, up = jnp.split(h @ w_in, 2, axis=-1)
    gate = dwconv(gate, conv_w) + conv_b
    return (jax.nn.silu(gate) * up) @ w_out


def setup_inputs(seed: int = 0) -> dict:
    key = jax.random.key(seed)
    ks = iter(jax.random.split(key, 32))
    nrm = lambda shape, std: jax.random.normal(next(ks), shape, f32) * std
    dense = lambda shape: nrm(shape, shape[-2] ** -0.5)
    x = nrm((BATCH, SEQ, D_MODEL), 1.0)
    norm_mix = 1.0 + nrm((DEPTH, D_MODEL), 0.02)
    norm_ffn = 1.0 + nrm((DEPTH, D_MODEL), 0.02)
    final_norm = 1.0 + nrm((D_MODEL,), 0.02)
    ab_w_in = dense((N_EVEN, D_MODEL, AB_IN))
    dn_conv_w = nrm((N_EVEN, DN_CONV, 3 * DN_WIDTH), DN_CONV ** -0.5)
    dn_a_log = jnp.log(jax.random.uniform(next(ks), (N_EVEN, 2, DN_HEADS), f32, 1.0, 16.0))
    dt = jnp.exp(jax.random.uniform(next(ks), (N_EVEN, 2, DN_HEADS), f32, math.log(1e-3), math.log(1e-1)))
    dn_dt_bias = dt + jnp.log(-jnp.expm1(-dt))
    dn_norm_w = 1.0 + nrm((N_EVEN, DN_HEAD_DIM), 0.02)
    ab_w_out = dense((N_EVEN, AB_OUT, D_MODEL))
    cd_w_in = dense((N_ODD, D_MODEL, CD_IN))
    base = jnp.log1p(-(2.0 ** (-5.0 - jnp.arange(RET_HEADS, dtype=f32))))
    ret_log_decay = base * (1.0 + nrm((N_ODD, 2, RET_HEADS), 0.01))
    hy_conv_w = nrm((N_ODD, HY_SHORT, 3 * HY_WIDTH), HY_SHORT ** -0.5)
    hy_conv_b = nrm((N_ODD, 3 * HY_WIDTH), 0.02)
    hy_w1 = dense((N_ODD, HY_EMB, HY_ORDER))
    hy_b1 = nrm((N_ODD, HY_ORDER), 0.1)
    hy_f1 = 1.0 + nrm((N_ODD, HY_ORDER), 0.01)
    hy_w2 = dense((N_ODD, HY_ORDER, HY_ORDER))
    hy_b2 = nrm((N_ODD, HY_ORDER), 0.1)
    hy_f2 = 1.0 + nrm((N_ODD, HY_ORDER), 0.01)
    hy_w3 = nrm((N_ODD, HY_ORDER, 2 * HY_WIDTH), 0.05 * HY_ORDER ** -0.5)
    hy_bias = nrm((N_ODD, HY_WIDTH), 1.0)
    cd_w_out = dense((N_ODD, CD_OUT, D_MODEL))
    ffn_w_in = dense((DEPTH, D_MODEL, 2 * D_FF))
    ffn_conv_w = nrm((DEPTH, FFN_CONV, D_FF), FFN_CONV ** -0.5)
    ffn_conv_b = nrm((DEPTH, D_FF), 0.02)
    ffn_w_out = dense((DEPTH, D_FF, D_MODEL))
    return {'x': x, 'norm_mix': norm_mix, 'norm_ffn': norm_ffn, 'final_norm': final_norm,
            'ab_w_in': ab_w_in, 'dn_conv_w': dn_conv_w, 'dn_a_log': dn_a_log, 'dn_dt_bias': dn_dt_bias,
            'dn_norm_w': dn_norm_w, 'ab_w_out': ab_w_out,
            'cd_w_in': cd_w_in, 'ret_log_decay': ret_log_decay, 'hy_conv_w': hy_conv_w, 'hy_conv_b': hy_conv_b,
            'hy_w1': hy_w1, 'hy_b1': hy_b1, 'hy_f1': hy_f1, 'hy_w2': hy_w2, 'hy_b2': hy_b2, 'hy_f2': hy_f2,
            'hy_w3': hy_w3, 'hy_bias': hy_bias, 'cd_w_out': cd_w_out,
            'ffn_w_in': ffn_w_in, 'ffn_conv_w': ffn_conv_w, 'ffn_conv_b': ffn_conv_b, 'ffn_w_out': ffn_w_out}


def reference(x, norm_mix, norm_ffn, final_norm,
              ab_w_in, dn_conv_w, dn_a_log, dn_dt_bias, dn_norm_w, ab_w_out,
              cd_w_in, ret_log_decay, hy_conv_w, hy_conv_b, hy_w1, hy_b1, hy_f1,
              hy_w2, hy_b2, hy_f2, hy_w3, hy_bias, cd_w_out,
              ffn_w_in, ffn_conv_w, ffn_conv_b, ffn_w_out):
    for layer in range(DEPTH):
        i = layer // 2
        h = rmsnorm(x, norm_mix[layer])
        if layer % 2 == 0:
            mixed = mixer_ab(h, ab_w_in[i], dn_conv_w[i], dn_a_log[i], dn_dt_bias[i], dn_norm_w[i], ab_w_out[i])
        else:
            mixed = mixer_cd(h, cd_w_in[i], ret_log_decay[i], hy_conv_w[i], hy_conv_b[i],
                             hy_w1[i], hy_b1[i], hy_f1[i], hy_w2[i], hy_b2[i], hy_f2[i], hy_w3[i],
                             hy_bias[i], cd_w_out[i])
        x = x + mixed
        x = x + conv_ffn(rmsnorm(x, norm_ffn[layer]), ffn_w_in[layer], ffn_conv_w[layer],
                         ffn_conv_b[layer], ffn_w_out[layer])
    return rmsnorm(x, final_norm)
```

```python
import math
import numpy as np
import ml_dtypes
import concourse.bass as bass
import concourse.mybir as mybir
from concourse.bass_utils import run_bass_kernel_spmd

F32 = mybir.dt.float32
BF16 = mybir.dt.bfloat16
I32 = mybir.dt.int32
AF = mybir.ActivationFunctionType
ALU = mybir.AluOpType
AX = mybir.AxisListType

S = 2048
D = 1024
NT = 4
DFF = 2816
NCORES = 8


class Buf:
    __slots__ = ("t", "writer", "readers", "dsem", "dval", "name", "depth", "pver", "bdepth")

    def __init__(self, t, name=""):
        self.t = t
        self.writer = None
        self.readers = {}
        self.dsem = None
        self.dval = 0
        self.name = name
        self.pver = 0
        self.bdepth = 0

    def __getitem__(self, idx):
        return self.t[idx]


class Eng:
    def __init__(self, h, sem, name):
        self.applied = 0
        self.h = h
        self.sem = sem
        self.cnt = 0
        self.seen = {}
        self.name = name


class FW:
    def __init__(self, nc):
        self.nc = nc
        self._ctx = []
        self._semctx = []
        self.dsems = []
        self.pe = Eng(nc.tensor, self.sem("pe"), "pe")
        self.act = Eng(nc.scalar, self.sem("act"), "act")
        self.dve = Eng(nc.vector, self.sem("dve"), "dve")
        self.pool = Eng(nc.gpsimd, self.sem("pool"), "pool")
        self.sp = Eng(nc.sync, self.sem("sp"), "sp")
        self.engs = (self.pe, self.act, self.dve, self.pool, self.sp)
        self.out_stamps = []
        self.dma_bufs = []
        self.free_dsems = []
        self.live = []
        self.gp = {}
        self.gpv = 0
        self.gp_snap = {0: []}
        self.nds = 0
        self.nps = 0

    def sem(self, name):
        cm = self.nc.semaphore(name)
        s = cm.__enter__()
        self._semctx.append(cm)
        return s

    def sb(self, shape, dt, name):
        self.nps += 1
        name = "%s_u%d" % (name, self.nps)
        cm = self.nc.sbuf_tensor(name, list(shape), dt)
        t = cm.__enter__()
        self._ctx.append(cm)
        return t

    def ps(self, shape, dt, name):
        cm = self.nc.psum_tensor(name, list(shape), dt)
        t = cm.__enter__()
        self._ctx.append(cm)
        return t

    def buf(self, shape, dt, name):
        b = Buf(self.sb(shape, dt, name), name)
        b.pver = self.gpv
        b.bdepth = len(self._ctx)
        self.live.append(b)
        return b

    def view(self, t, name="", alias=()):
        b = Buf(t, name)
        if alias:
            for a in alias:
                self._merge(a)
            self._snap()
        b.pver = self.gpv
        b.bdepth = len(self._ctx) + 1
        self.live.append(b)
        return b

    def _merge(self, b):
        st = list(b.readers.values())
        if b.writer is not None:
            st.append(b.writer)
        for (sm, v) in st:
            k = id(sm)
            if k not in self.gp or self.gp[k][1] < v:
                self.gp[k] = (sm, v)

    def _snap(self):
        self.gpv += 1
        self.gp_snap[self.gpv] = list(self.gp.values())

    def mark(self):
        return len(self._ctx)

    def release(self, mark, hard=False):
        if hard:
            self.barrier()
        kl = []
        for b in self.live:
            if b.bdepth > mark:
                self._merge(b)
            else:
                kl.append(b)
        self.live = kl
        self._snap()
        keep = []
        for tb in self.dsems:
            if tb.depth > mark:
                self.free_dsems.append((tb.dsem, tb.dval))
            else:
                keep.append(tb)
        self.dsems = keep
        while len(self._ctx) > mark:
            self._ctx.pop().__exit__(None, None, None)

    def close(self):
        while self._ctx:
            self._ctx.pop().__exit__(None, None, None)
        while self._semctx:
            self._semctx.pop().__exit__(None, None, None)

    def _wait(self, E, sem, val):
        k = id(sem)
        if E.seen.get(k, 0) >= val:
            return
        E.h.wait_ge(sem, val)
        E.seen[k] = val

    def _deps(self, E, reads, writes, own_dsem=None):
        pv = 0
        for b in reads:
            if b.pver > pv:
                pv = b.pver
        for b in writes:
            if b.pver > pv:
                pv = b.pver
        if pv > E.applied:
            for (sm, v) in self.gp_snap[pv]:
                self._wait(E, sm, v)
            E.applied = pv
        for b in reads:
            if b.writer is not None:
                s, v = b.writer
                if s is E.sem and E is self.pe:
                    continue
                self._wait(E, s, v)
        for b in writes:
            if b.writer is not None:
                s, v = b.writer
                if not (s is E.sem and E is self.pe) and s is not own_dsem:
                    self._wait(E, s, v)
            for k, (s, v) in b.readers.items():
                if s is E.sem:
                    continue
                self._wait(E, s, v)

    def op(self, E, issue, reads=(), writes=(), inc=True):
        self._deps(E, reads, writes)
        ins = issue(E.h)
        if inc:
            E.cnt += 1
            ins.then_inc(E.sem, 1)
            stamp = (E.sem, E.cnt)
        else:
            stamp = (E.sem, E.cnt + 1)
        for b in reads:
            b.readers[id(E.sem)] = stamp
        for b in writes:
            b.writer = stamp
            b.readers = {}
        return ins

    def dma(self, Q, out_ap, in_ap, reads=(), writes=(), is_output=False, **kw):
        tb = writes[0] if writes else reads[0]
        if tb.dsem is None:
            if self.free_dsems:
                tb.dsem, tb.dval = self.free_dsems.pop()
            else:
                self.nds += 1
                tb.dsem = self.sem("d%d" % self.nds)
            tb.depth = len(self._ctx)
            self.dsems.append(tb)
        self._deps(Q, reads, writes, own_dsem=tb.dsem)
        tb.dval += 16
        Q.h.dma_start(out=out_ap, in_=in_ap, **kw).then_inc(tb.dsem, 16)
        stamp = (tb.dsem, tb.dval)
        for b in reads:
            b.readers[id(tb.dsem)] = stamp
        for b in writes:
            b.writer = stamp
            b.readers = {}
        if is_output:
            self.out_stamps.append(stamp)

    def barrier(self):
        for E in self.engs:
            for X in self.engs:
                if X is not E and X.cnt:
                    self._wait(E, X.sem, X.cnt)
            for tb in self.dsems:
                if tb.dval:
                    self._wait(E, tb.dsem, tb.dval)

    def finish(self):
        for s, v in self.out_stamps:
            self._wait(self.sp, s, v)
        for E in (self.pe, self.act, self.dve, self.pool):
            if E.cnt:
                self._wait(self.sp, E.sem, E.cnt)


class Prog:
    def __init__(self, nseq, stages):
        self.nseq = nseq
        self.stages = stages
        self.nc = bass.Bass("TRN2", target_bir_lowering=False)
        self.fw = FW(self.nc)
        self.dram = {}

    def din(self, name, shape, dt=F32):
        t = self.nc.dram_tensor(name, list(shape), dt, kind="ExternalInput").ap()
        self.dram[name] = t
        return t

    def nextp(self):
        p = self.P[self.prot[self.pi % len(self.prot)]]
        self.pi += 1
        return p

    def build(self):
        nc, fw = self.nc, self.fw
        nseq = self.nseq
        x_d = self.din("x", [nseq, S, D])
        out_d = nc.dram_tensor("out", [nseq, S, D], F32, kind="ExternalOutput").ap()
        norm_mix = self.din("norm_mix", [2, D])
        norm_ffn = self.din("norm_ffn", [2, D])
        final_norm = self.din("final_norm", [D])
        self.ffn_w_in = self.din("ffn_w_in_t", [2, 22, 128, 2048])
        self.ffn_conv_w = self.din("ffn_conv_w", [2, 3, DFF])
        self.ffn_conv_b = self.din("ffn_conv_b", [2, DFF])
        self.ffn_w_out = self.din("ffn_w_out", [2, DFF, D])
        ident_d = self.din("ident", [128, 128])
        rperm_d = self.din("rperm", [128, 128], BF16)
        L1 = "cd" in self.stages
        L0 = "ab" in self.stages
        if L0:
            self.ab_w_in = self.din("ab_w_in", [D, 3600])
            self.ab_w_rot = self.din("ab_w_rot", [D, 1024])
            self.ab_w_out = self.din("ab_w_out", [D, D])
            self.dilmask = self.din("dilmask", [128, 2944], BF16)
            self.dnp_d = self.din("dnp", [128, 25])
            self.dcw_d = self.din("dcw", [128, 12, 3])
            self.tri_d = self.din("dn_tri", [128, 2, 128])
            self.msk_d = self.din("dn_msk", [128, 3, 512], BF16)
        if L0 and not L1:
            ropec_d = self.din("ropec", [128, S], BF16)
            ropes_d = self.din("ropes", [128, S], BF16)
        if L1:
            self.cd_w_in = self.din("cd_w_in", [D, 3072])
            self.cd_w_rot = self.din("cd_w_rot", [D, 512])
            self.cd_w_out = self.din("cd_w_out", [D, D])
            ret_ld = self.din("ret_log_decay", [8])
            ropec_d = self.din("ropec", [128, S], BF16)
            ropes_d = self.din("ropes", [128, S], BF16)
            dlt_d = self.din("dlt", [128, 512])
            iota16_d = self.din("iota16", [128, 16])
            self.hyz = self.din("hyz", [33, S])
            self.hy_w1 = self.din("hy_w1", [33, 64])
            self.hy_w2 = self.din("hy_w2", [64, 64])
            self.hy_w3 = self.din("hy_w3", [64, 1024])
            hyp_d = self.din("hyp", [64, 4])
            self.hydec = self.din("hydec", [S, 512])
            self.dftc = self.din("dftc", [S, S], BF16)
            self.dfts = self.din("dfts", [S, S], BF16)
            self.dftcf = self.din("dftcf", [16, 128, 2048], BF16)
            self.dftsf = self.din("dftsf", [16, 128, 2048], BF16)
            cnycol_d = self.din("cnycol", [128, 16], BF16)
            cnyrow_d = self.din("cnyrow", [1, S], BF16)
            wf_d = self.din("wf", [128, 17])
            hcw_d = self.din("hcw", [128, 12, 4])
            hybias_d = self.din("hybias", [128, 4])
            self.spec_d = nc.dram_tensor("spec_scratch", [2, 2176, 512], F32, kind="Internal").ap()

        self.xT_t = fw.sb([128, 8, S], F32, "xT")
        self.xT = [[fw.view(self.xT_t, "xT%d_%d" % (c, n)) for n in range(NT)] for c in range(8)]
        self.hT_t = fw.sb([128, 8, S], BF16, "hT")
        self.hT = [[fw.view(self.hT_t, "hT%d_%d" % (c, n)) for n in range(NT)] for c in range(8)]
        self.yT_t = fw.sb([128, 8, S], BF16, "yT")
        self.yT = [fw.view(self.yT_t, "yT%d" % c) for c in range(8)]
        self.P = [fw.view(fw.ps([128, 512], F32, "ps%d" % i), "ps%d" % i) for i in range(8)]
        self.pi = 0
        self.prot = list(range(8))
        self.ident = fw.buf([128, 128], F32, "ident")
        fw.dma(fw.sp, self.ident[:], ident_d[:, :], writes=[self.ident])
        self.identb = fw.buf([128, 128], BF16, "identb")
        fw.op(fw.dve, lambda e: e.tensor_copy(out=self.identb[:], in_=self.ident[:]), reads=[self.ident], writes=[self.identb])
        self.onesb = fw.buf([128, 128], BF16, "onesb")
        fw.op(fw.dve, lambda e: e.memset(self.onesb[:], 1.0), writes=[self.onesb])
        self.eps = fw.buf([128, 1], F32, "eps")
        fw.op(fw.dve, lambda e: e.memset(self.eps[:], 1e-6), writes=[self.eps])
        if L1 or L0:
            self.rperm = fw.buf([128, 128], BF16, "rperm")
            fw.dma(fw.sp, self.rperm[:], rperm_d[:, :], writes=[self.rperm])
            self.ropec = fw.buf([128, S], BF16, "ropec")
            self.ropes = fw.buf([128, S], BF16, "ropes")
            fw.dma(fw.sp, self.ropec[:], ropec_d[:, :], writes=[self.ropec])
            fw.dma(fw.sp, self.ropes[:], ropes_d[:, :], writes=[self.ropes])
        if L1:
            self.dlt_d = dlt_d
            self.iota16 = fw.buf([128, 16], F32, "iota16")
            fw.dma(fw.sp, self.iota16[:], iota16_d[:, :], writes=[self.iota16])
            self.ld = fw.buf([128, 8], F32, "ld")
            fw.dma(fw.sp, self.ld[:], ret_ld.partition_broadcast(128), writes=[self.ld])
            self.ld128 = fw.buf([128, 8], F32, "ld128")
            self.ld512 = fw.buf([128, 8], F32, "ld512")
            self.ldn = fw.buf([128, 8], F32, "ldn")
            fw.op(fw.dve, lambda e: e.tensor_scalar(out=self.ld128[:], in0=self.ld[:], scalar1=128.0, scalar2=None, op0=ALU.mult), reads=[self.ld], writes=[self.ld128])
            fw.op(fw.dve, lambda e: e.tensor_scalar(out=self.ld512[:], in0=self.ld[:], scalar1=512.0, scalar2=None, op0=ALU.mult), reads=[self.ld], writes=[self.ld512])
            fw.op(fw.dve, lambda e: e.tensor_scalar(out=self.ldn[:], in0=self.ld[:], scalar1=-1.0, scalar2=None, op0=ALU.mult), reads=[self.ld], writes=[self.ldn])
            self.hyp = fw.buf([64, 4], F32, "hyp")
            fw.dma(fw.sp, self.hyp[:], hyp_d[:, :], writes=[self.hyp])
            self.cnycol = fw.buf([128, 16], BF16, "cnycol")
            fw.dma(fw.sp, self.cnycol[:], cnycol_d[:, :], writes=[self.cnycol])
            self.cnyrow_d = cnyrow_d
            self.wf = fw.buf([128, 17], F32, "wf")
            fw.dma(fw.sp, self.wf[:], wf_d[:, :], writes=[self.wf])
            self.hcw = fw.buf([128, 12, 4], F32, "hcw")
            fw.dma(fw.sp, self.hcw[:], hcw_d[:, :, :], writes=[self.hcw])
            self.hybias = fw.buf([128, 4], F32, "hybias")
            fw.dma(fw.sp, self.hybias[:], hybias_d[:, :], writes=[self.hybias])
            self.ln8 = fw.buf([128, 1], F32, "ln8")
            fw.op(fw.dve, lambda e: e.memset(self.ln8[:], math.log(0.125)), writes=[self.ln8])
        self.nw = fw.buf([128, 5, 8], F32, "nw")
        srcs = [norm_mix[0], norm_ffn[0], norm_mix[1], norm_ffn[1], final_norm]
        for i, sap in enumerate(srcs):
            fw.dma(fw.sp, self.nw[:, i, :], sap.rearrange("(c p) -> p c", p=128), writes=[self.nw],
                   allow_slow_non_contiguous=True)
        self.fcw = fw.buf([128, 2, 22, 4], F32, "fcw")
        for l in range(2):
            for k in range(3):
                fw.dma(fw.sp, self.fcw[:, l, :, k], self.ffn_conv_w[l, k].rearrange("(j p) -> p j", p=128),
                       writes=[self.fcw], allow_slow_non_contiguous=True)
            fw.dma(fw.sp, self.fcw[:, l, :, 3], self.ffn_conv_b[l].rearrange("(j p) -> p j", p=128),
                   writes=[self.fcw], allow_slow_non_contiguous=True)

        if L1 and "hy" in self.stages:
            self.hyena_setup()
        for s in range(nseq):
            if s == 0:
                self.load_x(x_d[s])
            for layer in range(2):
                if ("ab" if layer == 0 else "cd") in self.stages:
                    self.rmsnorm(2 * layer)
                    (self.mixer_ab if layer == 0 else self.mixer_cd)()
                if ("ffn%d" % layer) in self.stages:
                    self.rmsnorm(2 * layer + 1)
                    self.ffn(layer)
            self.store_out(out_d[s], x_d[s + 1] if s + 1 < nseq else None)
        fw.finish()
        fw.close()
        return nc

    def l0c(self):
        fw = self.fw
        self.dnp = fw.buf([128, 25], F32, "dnp")
        fw.dma(fw.sp, self.dnp[:], self.dnp_d[:, :], writes=[self.dnp])
        fw.op(fw.act, lambda e: e.activation(out=self.dnp[:, 16:24], in_=self.dnp[:, 0:8], func=AF.Exp), reads=[self.dnp], writes=[self.dnp])
        fw.op(fw.dve, lambda e: e.tensor_scalar(out=self.dnp[:, 16:24], in0=self.dnp[:, 16:24], scalar1=-1.0, scalar2=None, op0=ALU.mult), reads=[self.dnp], writes=[self.dnp])
        self.dcw = fw.buf([128, 12, 3], F32, "dcw")
        fw.dma(fw.sp, self.dcw[:], self.dcw_d[:, :, :], writes=[self.dcw])
        tri = fw.buf([128, 2, 128], F32, "dn_tri")
        fw.dma(fw.sp, tri[:], self.tri_d[:, :, :], writes=[tri])
        self.triF = fw.view(tri.t[:, 0, :], "triF")
        self.triB = fw.view(tri.t[:, 1, :], "triB")
        msk = fw.buf([128, 3, 512], BF16, "dn_msk")
        fw.dma(fw.sp, msk[:], self.msk_d[:, :, :], writes=[msk])
        self.mL4 = fw.view(msk.t[:, 0, :], "mL4")
        self.mU4 = fw.view(msk.t[:, 1, :], "mU4")
        self.noti4 = fw.view(msk.t[:, 2, :], "noti4")
        for v_ in (self.triF, self.triB, self.mL4, self.mU4, self.noti4):
            v_.writer = (tri.writer if v_ in (self.triF, self.triB) else msk.writer)
        self.identf4 = fw.buf([128, 512], BF16, "identf4")
        for i4 in range(4):
            fw.op(fw.dve, lambda e: e.tensor_copy(out=self.identf4[:, i4 * 128:(i4 + 1) * 128], in_=self.ident[:]), reads=[self.ident], writes=[self.identf4])
        self.onesf = fw.buf([128, 128], F32, "onesf")
        fw.op(fw.dve, lambda e: e.memset(self.onesf[:], 1.0), writes=[self.onesf])
        self.one1 = fw.buf([128, 1], F32, "one1")
        fw.op(fw.dve, lambda e: e.memset(self.one1[:], 1.0), writes=[self.one1])

    def load_tile(self, xs, tt, xb):
        fw = self.fw
        fw.dma(fw.sp, xb[:], xs[tt * 128:(tt + 1) * 128, :], writes=[xb])
        nt = tt // 4
        for half in range(2):
            p = self.nextp()
            for cc in range(4):
                c = half * 4 + cc
                fw.op(fw.pe, lambda e: e.transpose(p[:, cc * 128:(cc + 1) * 128], xb[:, c * 128:(c + 1) * 128], self.ident[:]),
                      reads=[xb, self.ident], writes=[p], inc=(cc == 3))
            dst = self.xT_t[:, half * 4:half * 4 + 4, tt * 128:(tt + 1) * 128]
            wr = [self.xT[half * 4 + cc][nt] for cc in range(4)]
            if half == 0:
                fw.op(fw.act, lambda e: e.activation(out=dst, in_=p[:].rearrange("p (c t) -> p c t", c=4), func=AF.Copy), reads=[p], writes=wr)
            else:
                fw.op(fw.dve, lambda e: e.tensor_copy(out=dst, in_=p[:].rearrange("p (c t) -> p c t", c=4)), reads=[p], writes=wr)

    def load_x(self, xs):
        fw = self.fw
        m = fw.mark()
        xin = [fw.buf([128, D], F32, "xin%d" % i) for i in range(2)]
        for tt in range(16):
            self.load_tile(xs, tt, xin[tt % 2])
        fw.release(m)

    def store_out(self, outs, next_xs=None):
        fw = self.fw
        self.rmsnorm(4, inplace=True)
        m = fw.mark()
        ob = [fw.buf([128, D], F32, "ob%d" % i) for i in range(2)]
        xin = [fw.buf([128, D], F32, "xinb%d" % i) for i in range(2)] if next_xs is not None else None
        for tt in range(16):
            o = ob[tt % 2]
            nt = tt // 4
            for half in range(2):
                p = self.nextp()
                for cc in range(4):
                    c = half * 4 + cc
                    fw.op(fw.pe, lambda e, c=c, cc=cc, p=p: e.transpose(p[:, cc * 128:(cc + 1) * 128], self.xT_t[:, c, tt * 128:(tt + 1) * 128], self.ident[:]),
                          reads=[self.xT[c][nt], self.ident], writes=[p], inc=(cc == 3))
                if half == 0:
                    fw.op(fw.act, lambda e, p=p, o=o: e.activation(out=o[:, 0:512], in_=p[:], func=AF.Copy), reads=[p], writes=[o])
                else:
                    fw.op(fw.dve, lambda e, p=p, o=o: e.tensor_copy(out=o[:, 512:1024], in_=p[:]), reads=[p], writes=[o])
            fw.dma(fw.sp, outs[tt * 128:(tt + 1) * 128, :], o[:], reads=[o], is_output=True)
            if next_xs is not None and tt % 4 == 3:
                for t2 in range(tt - 3, tt + 1):
                    self.load_tile(next_xs, t2, xin[t2 % 2])
        fw.release(m)

    def rmsnorm(self, widx, inplace=False):
        fw = self.fw
        m = fw.mark()
        sq = [fw.buf([128, 512], BF16, "sq%d" % i) for i in range(3)]
        rstd = [fw.buf([128, 512], F32, "rstd%d" % i) for i in range(2)]
        k = 0
        for nt in range(NT):
            sl = slice(nt * 512, (nt + 1) * 512)
            p = self.nextp()
            for c in range(8):
                q = sq[k % 3]
                k += 1
                fw.op(fw.act, lambda e, q=q, c=c: e.activation(out=q[:], in_=self.xT_t[:, c, sl], func=AF.Square),
                      reads=[self.xT[c][nt]], writes=[q])
                fw.op(fw.pe, lambda e, q=q, c=c, p=p: e.matmul(p[:], lhsT=self.onesb[:], rhs=q[:], start=(c == 0), stop=(c == 7)),
                      reads=[q, self.onesb], writes=[p], inc=True)
            r = rstd[nt % 2]
            fw.op(fw.act, lambda e, r=r, p=p: e.activation(out=r[:], in_=p[:], func=AF.Ln, bias=self.eps[:], scale=1.0 / D),
                  reads=[p, self.eps], writes=[r])
            fw.op(fw.act, lambda e, r=r: e.activation(out=r[:], in_=r[:], func=AF.Exp, scale=-0.5), reads=[r], writes=[r])
            for c in range(8):
                if inplace:
                    dst, wr = self.xT_t[:, c, sl], [self.xT[c][nt]]
                else:
                    dst, wr = self.hT_t[:, c, sl], [self.hT[c][nt]]
                fw.op(fw.dve, lambda e, c=c, r=r, dst=dst: e.scalar_tensor_tensor(out=dst, in0=self.xT_t[:, c, sl], scalar=self.nw[:, widx, c:c + 1],
                                                                                 in1=r[:], op0=ALU.mult, op1=ALU.mult),
                      reads=[self.xT[c][nt], r, self.nw], writes=wr)
        fw.release(m)

    def ffn(self, layer):
        fw = self.fw
        m = fw.mark()
        win_d = self.ffn_w_in[layer]
        wout_d = self.ffn_w_out[layer].rearrange("(j p) n -> p j n", p=128)
        win = [fw.buf([128, 8, 256], BF16, "win%d" % i) for i in range(2)]
        wout = fw.buf([128, 8, D], BF16, "wout")
        gbuf = [fw.buf([128, S + 2], F32, "gbuf%d" % i) for i in range(2)]
        ubuf = [fw.buf([128, S], BF16, "ubuf%d" % i) for i in range(2)]
        acc = [fw.buf([128, 512], F32, "acc%d" % i) for i in range(2)]
        sg = [fw.buf([128, 512], BF16, "sg%d" % i) for i in range(2)]
        for g in gbuf:
            fw.op(fw.dve, lambda e, g=g: e.memset(g[:, 0:1], 0.0), writes=[g])
            fw.op(fw.dve, lambda e, g=g: e.memset(g[:, S + 1:S + 2], 0.0), writes=[g])
        groups = [list(range(0, 8)), list(range(8, 16)), list(range(16, 22))]
        k = 0
        for grp in groups:
            for si, j in enumerate(grp):
                fw.dma(fw.pool, wout[:, si, :], wout_d[:, j, :], writes=[wout])
            for si, j in enumerate(grp):
                w = win[j % 2]
                gb = gbuf[j % 2]
                ub = ubuf[j % 2]
                wj = win_d[j].rearrange("p (kc n) -> p kc n", kc=8)
                fw.dma(fw.pool, w[:, 0:4, :], wj[:, 0:4, :], writes=[w])
                fw.dma(fw.pool, w[:, 4:8, :], wj[:, 4:8, :], writes=[w])
                for nt in range(NT):
                    sl = slice(nt * 512, (nt + 1) * 512)
                    pg = self.nextp()
                    for kc in range(8):
                        fw.op(fw.pe, lambda e, kc=kc, pg=pg, w=w: e.matmul(pg[:], lhsT=w[:, kc, 0:128], rhs=self.hT_t[:, kc, sl], start=(kc == 0), stop=(kc == 7)),
                              reads=[w, self.hT[kc][nt]], writes=[pg], inc=(kc == 7))
                    fw.op(fw.act, lambda e, pg=pg, gb=gb, nt=nt: e.activation(out=gb[:, 1 + nt * 512:1 + (nt + 1) * 512], in_=pg[:], func=AF.Copy),
                          reads=[pg], writes=[gb])
                    pu = self.nextp()
                    for kc in range(8):
                        fw.op(fw.pe, lambda e, kc=kc, pu=pu, w=w: e.matmul(pu[:], lhsT=w[:, kc, 128:256], rhs=self.hT_t[:, kc, sl], start=(kc == 0), stop=(kc == 7)),
                              reads=[w, self.hT[kc][nt]], writes=[pu], inc=(kc == 7))
                    fw.op(fw.act, lambda e, pu=pu, ub=ub, sl=sl: e.activation(out=ub[:, sl], in_=pu[:], func=AF.Copy),
                          reads=[pu], writes=[ub])
                cw = self.fcw
                for nt in range(NT):
                    a = acc[k % 2]
                    sgb = sg[k % 2]
                    k += 1
                    o = nt * 512
                    fw.op(fw.dve, lambda e, a=a, gb=gb, o=o, j=j: e.tensor_scalar(out=a[:], in0=gb[:, o:o + 512], scalar1=cw[:, layer, j, 0:1], scalar2=None, op0=ALU.mult),
                          reads=[gb, cw], writes=[a])
                    fw.op(fw.dve, lambda e, a=a, gb=gb, o=o, j=j: e.scalar_tensor_tensor(out=a[:], in0=gb[:, o + 1:o + 513], scalar=cw[:, layer, j, 1:2], in1=a[:], op0=ALU.mult, op1=ALU.add),
                          reads=[gb, cw, a], writes=[a])
                    fw.op(fw.dve, lambda e, a=a, gb=gb, o=o, j=j: e.scalar_tensor_tensor(out=a[:], in0=gb[:, o + 2:o + 514], scalar=cw[:, layer, j, 2:3], in1=a[:], op0=ALU.mult, op1=ALU.add),
                          reads=[gb, cw, a], writes=[a])
                    fw.op(fw.act, lambda e, a=a, sgb=sgb, j=j: e.activation(out=sgb[:], in_=a[:], func=AF.Silu, bias=cw[:, layer, j, 3:4], scale=1.0),
                          reads=[a, cw], writes=[sgb])
                    fw.op(fw.dve, lambda e, sgb=sgb, ub=ub, si=si, o=o: e.tensor_tensor(out=self.yT_t[:, si, o:o + 512], in0=sgb[:], in1=ub[:, o:o + 512], op=ALU.mult),
                          reads=[sgb, ub], writes=[self.yT[si]])
            for c in range(8):
                for nt in range(NT):
                    sl = slice(nt * 512, (nt + 1) * 512)
                    po = self.nextp()
                    for si in range(len(grp)):
                        fw.op(fw.pe, lambda e, si=si, po=po, c=c: e.matmul(po[:], lhsT=wout[:, si, c * 128:(c + 1) * 128], rhs=self.yT_t[:, si, sl], start=(si == 0), stop=(si == len(grp) - 1)),
                              reads=[wout, self.yT[si]], writes=[po], inc=(si == len(grp) - 1))
                    fw.op(fw.dve, lambda e, po=po, c=c: e.tensor_tensor(out=self.xT_t[:, c, sl], in0=self.xT_t[:, c, sl], in1=po[:], op=ALU.add),
                          reads=[po, self.xT[c][nt]], writes=[self.xT[c][nt]])
        fw.release(m)


    def mixer_ab(self):
        fw = self.fw
        m0 = fw.mark()
        self.l0c()
        if "dn" in self.stages:
            self.deltanet()
        else:
            for c in range(4):
                fw.op(fw.dve, lambda e: e.memset(self.yT_t[:, c, :], 0.0), writes=[self.yT[c]])
        if "dil" in self.stages:
            self.dilated()
        else:
            for c in range(4, 8):
                fw.op(fw.dve, lambda e: e.memset(self.yT_t[:, c, :], 0.0), writes=[self.yT[c]])
        fw.release(m0)
        self.out_proj(self.ab_w_out)

    def dilated(self):
        fw = self.fw
        W = self.ab_w_in
        wsrc = W.rearrange("(kc p) n -> p kc n", p=128)
        rsrc = self.ab_w_rot.rearrange("(kc p) n -> p kc n", p=128)
        m = fw.mark()
        Q0, K0, V0 = 2064, 2064 + 512, 2064 + 1024
        mask = fw.buf([128, 2944], BF16, "dmask")
        fw.dma(fw.sp, mask[:], self.dilmask[:, :], writes=[mask])
        qe = fw.buf([65, S], BF16, "qe")
        ke = fw.buf([65, S], BF16, "ke")
        fw.op(fw.dve, lambda e: e.memset(ke[64:65, :], 1.0), writes=[ke])
        vx = fw.buf([128, 16, 128], BF16, "vx")
        wq = [fw.buf([128, 8, 64], BF16, "dwq%d" % i) for i in range(4)]
        wv = fw.buf([128, 8, 64], BF16, "dwv")
        t1 = [fw.buf([64, 512], F32, "dt1_%d" % i) for i in range(2)]
        t2 = [fw.buf([64, 512], F32, "dt2_%d" % i) for i in range(2)]
        sqq = fw.buf([64, S], BF16, "dsqq")
        sqk = fw.buf([64, S], BF16, "dsqk")
        kmx = fw.buf([128, 8], F32, "dkmx")
        qn = [fw.buf([128, 512], F32, "dqn%d" % i) for i in range(2)]
        ex = [fw.buf([128, 512], BF16, "dex%d" % i) for i in range(6)]
        pt = [fw.buf([128, 512], BF16, "dpt%d" % i) for i in range(7)]
        rd = [fw.buf([128, 512], F32, "drd%d" % i) for i in range(2)]
        kx = 0
        kp = 0
        for h in range(8):
            hc, odd = h // 2, h % 2
            self.prot = list(range(2, 8))
            fw.dma(fw.pool, wq[0][:], wsrc[:, :, Q0 + h * 64:Q0 + (h + 1) * 64], writes=[wq[0]])
            fw.dma(fw.pool, wq[2][:], wsrc[:, :, K0 + h * 64:K0 + (h + 1) * 64], writes=[wq[2]])
            fw.dma(fw.pool, wv[:], wsrc[:, :, V0 + h * 64:V0 + (h + 1) * 64], writes=[wv])
            vcol, ocol = (64, 0) if odd else (0, 64)
            fw.op(fw.dve, lambda e: e.memset(vx[:, :, ocol:ocol + 64], 1.0), writes=[vx])
            for tq in range(4):
                p = self.nextp()
                for t4 in range(4):
                    tt = tq * 4 + t4
                    for kc in range(8):
                        fw.op(fw.pe, lambda e: e.matmul(p[:, t4 * 64:(t4 + 1) * 64], lhsT=self.hT_t[:, kc, tt * 128:(tt + 1) * 128], rhs=wv[:, kc, :], start=(kc == 0), stop=(kc == 7)),
                              reads=[wv, self.hT[kc][tq]], writes=[p], inc=(kc == 7 and t4 == 3))
                fw.op(fw.act, lambda e: e.activation(out=vx[:, tq * 4:tq * 4 + 4, vcol:vcol + 64], in_=p[:, 0:256].rearrange("p (a b) -> p a b", a=4), func=AF.Copy), reads=[p], writes=[vx])
            for which, dst, sq in ((0, qe, sqq), (1, ke, sqk)):
                for nt in range(NT):
                    sl = slice(nt * 512, (nt + 1) * 512)
                    p1 = self.proj(wq[2 * which], 0, 64, nt)
                    qs_ = wq[1] if kx % 2 == 0 else wq[3]
                    qsv = qs_[:].rearrange("p a b -> p (a b)")
                    fw.op(fw.dve, lambda e: e.tensor_copy(out=qsv[0:64, :], in_=p1[0:64, :]), reads=[p1], writes=[qs_])
                    p2 = self.nextp()
                    fw.op(fw.pe, lambda e: e.matmul(p2[0:64, :], lhsT=self.rperm[0:64, 0:64], rhs=qsv[0:64, :], start=True, stop=True), reads=[self.rperm, qs_], writes=[p2])
                    a, b = t1[kx % 2], t2[kx % 2]
                    kx += 1
                    sc_ = 0.125 if which == 0 else 1.0
                    fw.op(fw.dve, lambda e: e.scalar_tensor_tensor(out=a[:], in0=p1[0:64, :], scalar=sc_, in1=self.ropec[0:64, sl], op0=ALU.mult, op1=ALU.mult), reads=[p1, self.ropec], writes=[a])
                    fw.op(fw.dve, lambda e: e.scalar_tensor_tensor(out=b[:], in0=p2[0:64, :], scalar=sc_, in1=self.ropes[0:64, sl], op0=ALU.mult, op1=ALU.mult), reads=[p2, self.ropes], writes=[b])
                    fw.op(fw.pool, lambda e: e.tensor_tensor(out=dst[0:64, sl], in0=a[:], in1=b[:], op=ALU.add), reads=[a, b], writes=[dst])
                    fw.op(fw.act, lambda e: e.activation(out=sq[:, sl], in_=dst[0:64, sl], func=AF.Square), reads=[dst], writes=[sq])
            for nt in range(NT):
                sl = slice(nt * 512, (nt + 1) * 512)
                pk = self.nextp()
                fw.op(fw.pe, lambda e: e.matmul(pk[:], lhsT=self.onesb[0:64, :], rhs=sqk[:, sl], start=True, stop=True), reads=[self.onesb, sqk], writes=[pk])
                fw.op(fw.dve, lambda e: e.tensor_reduce(out=kmx[:, nt:nt + 1], in_=pk[:], axis=AX.X, op=ALU.max), reads=[pk], writes=[kmx])
            fw.op(fw.dve, lambda e: e.tensor_reduce(out=kmx[:, 4:5], in_=kmx[:, 0:4], axis=AX.X, op=ALU.max), reads=[kmx], writes=[kmx])
            fw.op(fw.act, lambda e: e.activation(out=kmx[:, 5:6], in_=kmx[:, 4:5], func=AF.Sqrt), reads=[kmx], writes=[kmx])
            fw.op(fw.dve, lambda e: e.tensor_scalar(out=kmx[:, 6:7], in0=kmx[:, 5:6], scalar1=-1.0, scalar2=None, op0=ALU.mult), reads=[kmx], writes=[kmx])
            for nt in range(NT):
                sl = slice(nt * 512, (nt + 1) * 512)
                pq = self.nextp()
                fw.op(fw.pe, lambda e: e.matmul(pq[:], lhsT=self.onesb[0:64, :], rhs=sqq[:, sl], start=True, stop=True), reads=[self.onesb, sqq], writes=[pq])
                q_ = qn[nt % 2]
                fw.op(fw.act, lambda e: e.activation(out=q_[:], in_=pq[:], func=AF.Sqrt), reads=[pq], writes=[q_])
                fw.op(fw.dve, lambda e: e.tensor_scalar(out=qe[64:65, sl], in0=q_[64:65, :], scalar1=kmx[64:65, 6:7], scalar2=None, op0=ALU.mult), reads=[q_, kmx], writes=[qe])
            for nj in range(NT):
                pacc = self.P[nj % 2]
                pend = []
                mis = [mi for mi in range(16) if -8 <= mi - 4 * nj <= 11]

                def pv(sc, mi):
                    fw.op(fw.pe, lambda e: e.matmul(pacc[:], lhsT=vx[:, mi, :], rhs=sc[:], start=(mi == mis[0]), stop=(mi == mis[-1])),
                          reads=[vx, sc], writes=[pacc], inc=True)
                for mi in mis:
                    r = mi - 4 * nj
                    ps = self.nextp()
                    fw.op(fw.pe, lambda e: e.matmul(ps[:], lhsT=ke[0:65, mi * 128:(mi + 1) * 128], rhs=qe[0:65, nj * 512:(nj + 1) * 512], start=True, stop=True),
                          reads=[ke, qe], writes=[ps], inc=True)
                    e_ = ex[kp % 6]
                    sc = pt[kp % 7]
                    kp += 1
                    fw.op(fw.act, lambda e: e.activation(out=e_[:], in_=ps[:], func=AF.Exp), reads=[ps], writes=[e_])
                    fw.op(fw.dve, lambda e: e.tensor_tensor(out=sc[:], in0=e_[:], in1=mask[:, (11 - r) * 128:(11 - r) * 128 + 512], op=ALU.mult), reads=[e_, mask], writes=[sc])
                    pend.append((sc, mi))
                    if len(pend) > 4:
                        pv(*pend.pop(0))
                while pend:
                    pv(*pend.pop(0))
                r_ = rd[nj % 2]
                nlo, dlo = (64, 0) if odd else (0, 64)
                fw.op(fw.act, lambda e: e.activation(out=r_[nlo:nlo + 64, :], in_=pacc[dlo:dlo + 64, :], func=AF.Ln), reads=[pacc], writes=[r_])
                fw.op(fw.act, lambda e: e.activation(out=r_[nlo:nlo + 64, :], in_=r_[nlo:nlo + 64, :], func=AF.Exp, scale=-1.0), reads=[r_], writes=[r_])
                fw.op(fw.dve, lambda e: e.tensor_tensor(out=self.yT_t[nlo:nlo + 64, 4 + hc, nj * 512:(nj + 1) * 512], in0=pacc[nlo:nlo + 64, :], in1=r_[nlo:nlo + 64, :], op=ALU.mult),
                      reads=[pacc, r_], writes=[self.yT[4 + hc]])
        self.prot = list(range(8))
        fw.release(m)


    def deltanet(self):
        fw = self.fw
        W = self.ab_w_in
        wsrc = W.rearrange("(kc p) n -> p kc n", p=128)
        P = self.P
        m = fw.mark()
        beta = fw.buf([128, 8, 16], F32, "dn_beta")
        nbeta = fw.buf([128, 8, 16], F32, "dn_nbeta")
        gl = fw.buf([128, 8, 16], F32, "dn_g")
        mba = fw.mark()
        ba = fw.buf([128, 16, 16], F32, "dn_ba")
        wba = fw.buf([128, 8, 16], BF16, "dn_wba")
        fw.dma(fw.pool, wba[:], wsrc[:, :, 2048:2064], writes=[wba])
        p = P[0]
        for tt in range(16):
            for kc in range(8):
                fw.op(fw.pe, lambda e: e.matmul(p[:, tt * 16:(tt + 1) * 16], lhsT=self.hT_t[:, kc, tt * 128:(tt + 1) * 128], rhs=wba[:, kc, :], start=(kc == 0), stop=(kc == 7)),
                      reads=[wba, self.hT[kc][tt // 4]], writes=[p], inc=(kc == 7 and tt == 15))
        fw.op(fw.dve, lambda e: e.tensor_copy(out=ba[:], in_=p[:, 0:256].rearrange("p (t c) -> p c t", c=16)), reads=[p], writes=[ba])
        fw.op(fw.act, lambda e: e.activation(out=beta[:], in_=ba[:, 0:8, :], func=AF.Sigmoid), reads=[ba], writes=[beta])
        fw.op(fw.dve, lambda e: e.tensor_scalar(out=nbeta[:], in0=beta[:], scalar1=-1.0, scalar2=None, op0=ALU.mult), reads=[beta], writes=[nbeta])
        tx = fw.buf([128, 16], F32, "dn_tx")
        ta = fw.buf([128, 16], F32, "dn_ta")
        for c in range(8):
            fw.op(fw.dve, lambda e: e.tensor_scalar(out=tx[:], in0=ba[:, 8 + c, :], scalar1=self.dnp[:, 8 + c:9 + c], scalar2=None, op0=ALU.add), reads=[ba, self.dnp], writes=[tx])
            fw.op(fw.dve, lambda e: e.tensor_scalar(out=ta[:], in0=tx[:], scalar1=-1.0, scalar2=None, op0=ALU.mult), reads=[tx], writes=[ta])
            fw.op(fw.dve, lambda e: e.tensor_tensor(out=ta[:], in0=ta[:], in1=tx[:], op=ALU.max), reads=[tx, ta], writes=[ta])
            fw.op(fw.act, lambda e: e.activation(out=ta[:], in_=ta[:], func=AF.Exp, scale=-1.0), reads=[ta], writes=[ta])
            fw.op(fw.act, lambda e: e.activation(out=ta[:], in_=ta[:], func=AF.Ln, bias=self.one1[:], scale=1.0), reads=[ta, self.one1], writes=[ta])
            fw.op(fw.dve, lambda e: e.scalar_tensor_tensor(out=tx[:], in0=tx[:], scalar=0.0, in1=ta[:], op0=ALU.max, op1=ALU.add), reads=[tx, ta], writes=[tx])
            fw.op(fw.dve, lambda e: e.tensor_scalar(out=gl[:, c, :], in0=tx[:], scalar1=self.dnp[:, 16 + c:17 + c], scalar2=None, op0=ALU.mult), reads=[tx, self.dnp], writes=[gl])
        fw.release(mba)
        al = [self.yT[4], self.yT[5], self.yT[6], self.yT[7]]
        qT = fw.view(self.yT_t[:, 6, :], "dn_qT", alias=al)
        kT = fw.view(self.yT_t[:, 7, :], "dn_kT", alias=al)
        oT = fw.view(self.yT_t[:, 4:6, :].rearrange("p a s -> p (a s)").bitcast(F32), "dn_oT", alias=al)
        for h in range(4):
            mh = fw.mark()
            ktok = fw.buf([128, 16, 128], BF16, "dn_ktok")
            vtok = fw.buf([128, 16, 128], BF16, "dn_vtok")
            mp = fw.mark()
            self.prot = list(range(8))
            gb = fw.buf([128, S + 2], F32, "dn_gb")
            fw.op(fw.dve, lambda e: e.memset(gb[:, 0:1], 0.0), writes=[gb])
            fw.op(fw.dve, lambda e: e.memset(gb[:, S + 1:S + 2], 0.0), writes=[gb])
            a1 = fw.buf([128, S], F32, "dn_a1")
            vT = fw.buf([128, S], BF16, "dn_vT")
            wps = [fw.buf([128, 8, 128], BF16, "dn_wp%d" % i) for i in range(3)]
            for part in range(3):
                ch_ = part * 4 + h
                fw.dma(fw.pool, wps[part][:], wsrc[:, :, ch_ * 128:(ch_ + 1) * 128], writes=[wps[part]])
            sq = [fw.buf([128, 512], BF16, "dn_sq%d" % i) for i in range(2)]
            rs = [fw.buf([128, 512], F32, "dn_rs%d" % i) for i in range(2)]
            dcw = self.dcw
            for part in range(3):
                ch = part * 4 + h
                wp = wps[part]
                for nt in range(NT):
                    pp = self.proj(wp, 0, 128, nt)
                    fw.op(fw.act, lambda e: e.activation(out=gb[:, 1 + nt * 512:1 + (nt + 1) * 512], in_=pp[:], func=AF.Copy), reads=[pp], writes=[gb])
                fw.op(fw.dve, lambda e: e.tensor_scalar(out=a1[:], in0=gb[:, 0:S], scalar1=dcw[:, ch, 0:1], scalar2=None, op0=ALU.mult), reads=[gb, dcw], writes=[a1])
                fw.op(fw.dve, lambda e: e.scalar_tensor_tensor(out=a1[:], in0=gb[:, 1:S + 1], scalar=dcw[:, ch, 1:2], in1=a1[:], op0=ALU.mult, op1=ALU.add), reads=[gb, dcw, a1], writes=[a1])
                fw.op(fw.dve, lambda e: e.scalar_tensor_tensor(out=a1[:], in0=gb[:, 2:S + 2], scalar=dcw[:, ch, 2:3], in1=a1[:], op0=ALU.mult, op1=ALU.add), reads=[gb, dcw, a1], writes=[a1])
                if part == 2:
                    fw.op(fw.act, lambda e: e.activation(out=vT[:], in_=a1[:], func=AF.Silu), reads=[a1], writes=[vT])
                    srcT, dtok = vT, vtok
                else:
                    fw.op(fw.act, lambda e: e.activation(out=a1[:], in_=a1[:], func=AF.Silu), reads=[a1], writes=[a1])
                    dstT = qT if part == 0 else kT
                    for nt in range(NT):
                        sl = slice(nt * 512, (nt + 1) * 512)
                        q_, r_ = sq[nt % 2], rs[nt % 2]
                        fw.op(fw.act, lambda e: e.activation(out=q_[:], in_=a1[:, sl], func=AF.Square), reads=[a1], writes=[q_])
                        pn = self.nextp()
                        fw.op(fw.pe, lambda e: e.matmul(pn[:], lhsT=self.onesb[:], rhs=q_[:], start=True, stop=True), reads=[q_, self.onesb], writes=[pn])
                        fw.op(fw.act, lambda e: e.activation(out=r_[:], in_=pn[:], func=AF.Ln, bias=self.eps[:], scale=1.0), reads=[pn, self.eps], writes=[r_])
                        fw.op(fw.act, lambda e: e.activation(out=r_[:], in_=r_[:], func=AF.Exp, scale=-0.5), reads=[r_], writes=[r_])
                        scl = 128.0 ** -0.5 if part == 0 else 1.0
                        fw.op(fw.dve, lambda e: e.scalar_tensor_tensor(out=dstT[:, sl], in0=a1[:, sl], scalar=scl, in1=r_[:], op0=ALU.mult, op1=ALU.mult), reads=[a1, r_], writes=[dstT])
                    srcT, dtok = kT, ktok
                if part >= 1:
                    for tq in range(4):
                        pp = self.nextp()
                        pb = pp[:].bitcast(BF16)
                        for t4 in range(4):
                            tt = tq * 4 + t4
                            fw.op(fw.pe, lambda e: e.transpose(pb[:, t4 * 128:(t4 + 1) * 128], srcT[:, tt * 128:(tt + 1) * 128], self.identb[:]),
                                  reads=[srcT, self.identb], writes=[pp], inc=(t4 == 3))
                        fw.op(fw.act, lambda e: e.activation(out=dtok[:, tq * 4:tq * 4 + 4, :], in_=pb[:, 0:512].rearrange("p (a b) -> p a b", a=4), func=AF.Copy), reads=[pp], writes=[dtok])
            fw.release(mp)
            import os
            def dir_tables(d):
                    col = d * 4 + h
                    qdT = fw.buf([128, S], BF16, "dn_qdT")
                    TT = fw.buf([128, 16, 128], BF16, "dn_TT")
                    inT = fw.buf([128, S], BF16, "dn_inT")
                    sm = fw.buf([128, 6, 16], F32, "dn_sm")
                    TRI = self.triF if d == 0 else self.triB
                    MI = self.mL4 if d == 0 else self.mU4
                    MT = self.mU4 if d == 0 else self.mL4
                    gcol = gl[:, col, :]
                    pc = P[0]
                    fw.op(fw.pe, lambda e: e.matmul(pc[:, 0:16], lhsT=TRI[:], rhs=gcol, start=True, stop=True), reads=[TRI, gl], writes=[pc], inc=False)
                    fw.op(fw.pe, lambda e: e.matmul(pc[:, 16:32], lhsT=self.onesf[:], rhs=gcol, start=True, stop=True), reads=[self.onesf, gl], writes=[pc])
                    fw.op(fw.dve, lambda e: e.tensor_copy(out=sm[:, 0, :], in_=pc[:, 0:16]), reads=[pc], writes=[sm])
                    fw.op(fw.dve, lambda e: e.tensor_scalar(out=sm[:, 1, :], in0=pc[:, 0:16], scalar1=-1.0, scalar2=None, op0=ALU.mult), reads=[pc], writes=[sm])
                    fw.op(fw.dve, lambda e: e.tensor_copy(out=sm[:, 2, :], in_=pc[:, 16:32]), reads=[pc], writes=[sm])
                    fw.op(fw.dve, lambda e: e.tensor_tensor(out=sm[:, 3, :], in0=sm[:, 2, :], in1=sm[:, 0, :], op=ALU.subtract), reads=[sm], writes=[sm])
                    fw.op(fw.act, lambda e: e.activation(out=sm[:, 3, :], in_=sm[:, 3, :], func=AF.Exp), reads=[sm], writes=[sm])
                    fw.op(fw.act, lambda e: e.activation(out=sm[:, 4, :], in_=sm[:, 2, :], func=AF.Exp), reads=[sm], writes=[sm])
                    fw.op(fw.act, lambda e: e.activation(out=sm[:, 5, :], in_=sm[:, 0, :], func=AF.Exp), reads=[sm], writes=[sm])
                    fw.op(fw.dve, lambda e: e.tensor_scalar(out=sm[:, 5, :], in0=sm[:, 5, :], scalar1=-1.0, scalar2=None, op0=ALU.mult), reads=[sm], writes=[sm])
                    mt = fw.mark()
                    def mk_tmp(tag):
                        return (fw.buf([128, 512], F32, 'dn_tI' + tag), fw.buf([128, 512], F32, 'dn_tT' + tag),
                                [fw.buf([128, 512], F32, 'dn_Pb%d%s' % (i, tag)) for i in range(2)],
                                [fw.buf([128, 512], F32, 'dn_Qb%d%s' % (i, tag)) for i in range(2)],
                                fw.buf([128, 512], F32, 'dn_Rb' + tag))
                    def tab_gen(gq, tmp, banks):
                        tI, tT, Pb, Qb, Rb1 = tmp
                        dg, eg = tT, tI
                        Rb = [Rb1, Rb1]
                        cs = slice(gq * 512, (gq + 1) * 512)
                        chs = [gq * 4 + i for i in range(4)]
                        C4 = lambda i: slice(i * 128, (i + 1) * 128)
                        pcb, pG, pKQ = banks; pP, pQ, pR = banks
                        for i, ch in enumerate(chs):
                            fw.op(fw.dve, lambda e: e.tensor_scalar(out=dg[:, C4(i)], in0=self.ident[:], scalar1=sm[:, 0, ch:ch + 1], scalar2=None, op0=ALU.mult), reads=[self.ident, sm], writes=[dg])
                            yield
                        for i, ch in enumerate(chs):
                            fw.op(fw.pe, lambda e: e.matmul(pcb[:, C4(i)], lhsT=self.onesf[:], rhs=dg[:, C4(i)], start=True, stop=True), reads=[self.onesf, dg], writes=[pcb], inc=(i == 3))
                            yield
                        fw.op(fw.act, lambda e: e.activation(out=eg[:], in_=pcb[:], func=AF.Exp), reads=[pcb], writes=[eg])
                        yield
                        fw.op(fw.dve, lambda e: e.tensor_tensor(out=qdT[:, cs], in0=qT[:, cs], in1=eg[:], op=ALU.mult), reads=[qT, eg], writes=[qdT])
                        yield
                        fw.op(fw.dve, lambda e: e.scalar_tensor_tensor(out=tI[:], in0=pcb[:], scalar=-1.0, in1=MI[:], op0=ALU.mult, op1=ALU.add), reads=[pcb, MI], writes=[tI])
                        yield
                        for i, ch in enumerate(chs):
                            fw.op(fw.act, lambda e: e.activation(out=tI[:, C4(i)], in_=tI[:, C4(i)], func=AF.Exp, bias=sm[:, 0, ch:ch + 1], scale=1.0), reads=[tI, sm], writes=[tI])
                            yield
                            fw.op(fw.dve, lambda e: e.scalar_tensor_tensor(out=tT[:, C4(i)], in0=pcb[:, C4(i)], scalar=sm[:, 1, ch:ch + 1], in1=MT[:, C4(i)], op0=ALU.add, op1=ALU.add), reads=[pcb, sm, MT], writes=[tT])
                            yield
                        fw.op(fw.act, lambda e: e.activation(out=tT[:], in_=tT[:], func=AF.Exp), reads=[tT], writes=[tT])
                        yield
                        for i, ch in enumerate(chs):
                            c128 = slice(ch * 128, (ch + 1) * 128)
                            fw.op(fw.pe, lambda e: e.matmul(pG[:, C4(i)], lhsT=kT[:, c128], rhs=kT[:, c128], start=True, stop=True), reads=[kT], writes=[pG], inc=(i == 3))
                            yield
                        for i, ch in enumerate(chs):
                            c128 = slice(ch * 128, (ch + 1) * 128)
                            fw.op(fw.pe, lambda e: e.matmul(pKQ[:, C4(i)], lhsT=kT[:, c128], rhs=qT[:, c128], start=True, stop=True), reads=[kT, qT], writes=[pKQ], inc=(i == 3))
                            yield
                        fw.op(fw.dve, lambda e: e.tensor_tensor(out=inT[:, cs], in0=pKQ[:], in1=tT[:], op=ALU.mult), reads=[pKQ, tT], writes=[inT])
                        yield
                        fw.op(fw.pool, lambda e: e.tensor_tensor(out=tI[:], in0=tI[:], in1=self.noti4[:], op=ALU.mult), reads=[tI, self.noti4], writes=[tI])
                        yield
                        Pc, Qc, Rc = Pb[0], Qb[0], Rb[0]
                        for i, ch in enumerate(chs):
                            fw.op(fw.dve, lambda e: e.scalar_tensor_tensor(out=Pc[:, C4(i)], in0=pG[:, C4(i)], scalar=nbeta[:, col, ch:ch + 1], in1=tI[:, C4(i)], op0=ALU.mult, op1=ALU.mult), reads=[pG, nbeta, tI], writes=[Pc])
                            yield
                        for i in range(4):
                            fw.op(fw.pe, lambda e: e.transpose(pQ[:, C4(i)], Pc[:, C4(i)], self.ident[:]), reads=[Pc, self.ident], writes=[pQ], inc=(i == 3))
                            yield
                        fw.op(fw.act, lambda e: e.activation(out=Qc[:], in_=pQ[:], func=AF.Copy), reads=[pQ], writes=[Qc])
                        yield
                        fw.op(fw.dve, lambda e: e.tensor_tensor(out=Rc[:], in0=Qc[:], in1=self.identf4[:], op=ALU.add), reads=[Qc, self.identf4], writes=[Rc])
                        yield
                        for k in (range(6, 7) if os.environ.get('DN_SKIPT') else range(1, 7)):
                            Pn, Qn, Rn = Pb[k % 2], Qb[k % 2], Rb[k % 2]
                            for i in range(4):
                                fw.op(fw.pe, lambda e: e.matmul(pP[:, C4(i)], lhsT=Qc[:, C4(i)], rhs=Pc[:, C4(i)], start=True, stop=True), reads=[Qc, Pc], writes=[pP], inc=(i == 3))
                                yield
                            fw.op(fw.dve, lambda e: e.tensor_copy(out=Pn[:], in_=pP[:]), reads=[pP], writes=[Pn])
                            yield
                            if k < 6:
                                for i in range(4):
                                    fw.op(fw.pe, lambda e: e.matmul(pQ[:, C4(i)], lhsT=Pc[:, C4(i)], rhs=Qc[:, C4(i)], start=True, stop=True), reads=[Qc, Pc], writes=[pQ], inc=(i == 3))
                                    yield
                                fw.op(fw.act, lambda e: e.activation(out=Qn[:], in_=pQ[:], func=AF.Copy), reads=[pQ], writes=[Qn])
                                yield
                            for i in range(4):
                                fw.op(fw.pe, lambda e: e.matmul(pR[:, C4(i)], lhsT=Pn[:, C4(i)], rhs=Rc[:, C4(i)], start=True, stop=True), reads=[Pn, Rc], writes=[pR], inc=(i == 3))
                                yield
                            fw.op(fw.dve, lambda e: e.tensor_tensor(out=Rn[:], in0=Rc[:], in1=pR[:], op=ALU.add), reads=[pR, Rc], writes=[Rn])
                            yield
                            if k == 6:
                                for i, ch in enumerate(chs):
                                    if os.environ.get("DN_NOT"):
                                        fw.op(fw.dve, lambda e: e.tensor_scalar(out=TT[:, ch, :], in0=self.ident[:], scalar1=beta[:, col, ch:ch + 1], scalar2=None, op0=ALU.mult), reads=[pR, beta], writes=[TT])
                                        yield
                                    else:
                                        fw.op(fw.pool, lambda e: e.tensor_scalar(out=TT[:, ch, :], in0=Rn[:, C4(i)], scalar1=beta[:, col, ch:ch + 1], scalar2=None, op0=ALU.mult), reads=[Rn, beta], writes=[TT])
                                        yield
                            Pc, Qc, Rc = Pn, Qn, Rn
                    tmpA, tmpB = mk_tmp('a'), mk_tmp('b')
                    if not os.environ.get('DN_SKIPTAB'):
                        for ga, gb_ in ((0, 1), (2, 3)):
                            self.interleave([tab_gen(ga, tmpA, [P[1], P[2], P[3]]), tab_gen(gb_, tmpB, [P[4], P[5], P[6]])])
                    fw.release(mt)
                    return dict(col=col, qdT=qdT, TT=TT, inT=inT, sm=sm)
            def scan_gen(d, B):
                    col, qdT, TT, inT, sm = B["col"], B["qdT"], B["TT"], B["inT"], B["sm"]
                    Sf = fw.buf([128, 128], F32, "dn_Sf")
                    v2b = [fw.buf([128, 128], BF16, "dn_v2%d" % i) for i in range(2)]
                    Sb = fw.buf([128, 128], BF16, "dn_Sb")
                    rb = [fw.buf([128, 128], BF16, "dn_r%d" % i) for i in range(2)]
                    vn = [fw.buf([128, 128], BF16, "dn_vn%d" % i) for i in range(2)]
                    fw.op(fw.dve, lambda e: e.memset(Sf[:], 0.0), writes=[Sf])
                    yield
                    fw.op(fw.dve, lambda e: e.memset(Sb[:], 0.0), writes=[Sb])
                    yield
                    order = list(range(16)) if d == 0 else list(range(15, -1, -1))
                    for si, ch in enumerate(order[:1] if os.environ.get('DN_SKIPS') else order):
                        c128 = slice(ch * 128, (ch + 1) * 128)
                        pa, po = P[4 * d + (si % 2)], P[4 * d + 2 + (si % 2)]
                        v2_ = v2b[si % 2]
                        r_, v_ = rb[si % 2], vn[si % 2]
                        fw.op(fw.pe, lambda e: e.matmul(pa[:, 0:128], lhsT=kT[:, c128], rhs=Sb[:], start=True, stop=True), reads=[kT, Sb], writes=[pa])
                        yield
                        fw.op(fw.dve, lambda e: e.scalar_tensor_tensor(out=r_[:], in0=pa[:, 0:128], scalar=sm[:, 5, ch:ch + 1], in1=vtok[:, ch, :], op0=ALU.mult, op1=ALU.add), reads=[vtok, pa, sm], writes=[r_])
                        yield
                        fw.op(fw.pe, lambda e: e.matmul(pa[:, 128:256], lhsT=TT[:, ch, :], rhs=r_[:], start=True, stop=True), reads=[TT, r_], writes=[pa])
                        yield
                        fw.op(fw.act, lambda e: e.activation(out=v_[:], in_=pa[:, 128:256], func=AF.Copy), reads=[pa], writes=[v_])
                        yield
                        fw.op(fw.act, lambda e: e.activation(out=v2_[:], in_=pa[:, 128:256], func=AF.Copy, scale=sm[:, 3, ch:ch + 1]), reads=[pa, sm], writes=[v2_])
                        yield
                        fw.op(fw.pe, lambda e: e.matmul(po[:, 0:128], lhsT=Sb[:], rhs=qdT[:, c128], start=True, stop=False), reads=[Sb, qdT], writes=[po], inc=False)
                        yield
                        fw.op(fw.pe, lambda e: e.matmul(po[:, 0:128], lhsT=v_[:], rhs=inT[:, c128], start=False, stop=True), reads=[v_, inT], writes=[po])
                        yield
                        fw.op(fw.dve, lambda e: e.tensor_tensor(out=oT[:, c128], in0=oT[:, c128], in1=po[:, 0:128], op=ALU.add), reads=[po, oT], writes=[oT])
                        yield
                        fw.op(fw.pe, lambda e: e.matmul(pa[:, 256:384], lhsT=ktok[:, ch, :], rhs=v2_[:], start=True, stop=True), reads=[ktok, v2_], writes=[pa])
                        yield
                        fw.op(fw.dve, lambda e: e.scalar_tensor_tensor(out=Sb[:], in0=Sf[:], scalar=sm[:, 4, ch:ch + 1], in1=pa[:, 256:384], op0=ALU.mult, op1=ALU.add), reads=[Sf, sm, pa], writes=[Sb])
                        yield
                        fw.op(fw.dve, lambda e: e.scalar_tensor_tensor(out=Sf[:], in0=Sf[:], scalar=sm[:, 4, ch:ch + 1], in1=pa[:, 256:384], op0=ALU.mult, op1=ALU.add), reads=[Sf, sm, pa], writes=[Sf])
                        yield
            dirs_ = [int(v) for v in os.environ.get('DN_DIRS', '0,1').split(',')]
            fw.op(fw.dve, lambda e: e.memset(oT[:], 0.0), writes=[oT])
            BB = [dir_tables(d) for d in dirs_]
            self.interleave([scan_gen(d, B) for d, B in zip(dirs_, BB)])
            self.prot = list(range(8))
            wz = fw.buf([128, 8, 128], BF16, "dn_wz")
            fw.dma(fw.pool, wz[:], wsrc[:, :, 1536 + h * 128:1536 + (h + 1) * 128], writes=[wz])
            sq = [fw.buf([128, 512], BF16, "dn_fsq%d" % i) for i in range(2)]
            rs = [fw.buf([128, 512], F32, "dn_frs%d" % i) for i in range(2)]
            sg = [fw.buf([128, 512], BF16, "dn_fsg%d" % i) for i in range(2)]
            for nt in range(NT):
                sl = slice(nt * 512, (nt + 1) * 512)
                q_, r_, g_ = sq[nt % 2], rs[nt % 2], sg[nt % 2]
                fw.op(fw.act, lambda e: e.activation(out=q_[:], in_=oT[:, sl], func=AF.Square), reads=[oT], writes=[q_])
                pn = self.nextp()
                fw.op(fw.pe, lambda e: e.matmul(pn[:], lhsT=self.onesb[:], rhs=q_[:], start=True, stop=True), reads=[q_, self.onesb], writes=[pn])
                fw.op(fw.act, lambda e: e.activation(out=r_[:], in_=pn[:], func=AF.Ln, bias=self.eps[:], scale=1.0 / 128), reads=[pn, self.eps], writes=[r_])
                fw.op(fw.act, lambda e: e.activation(out=r_[:], in_=r_[:], func=AF.Exp, scale=-0.5), reads=[r_], writes=[r_])
                pz = self.proj(wz, 0, 128, nt)
                fw.op(fw.act, lambda e: e.activation(out=g_[:], in_=pz[:], func=AF.Silu), reads=[pz], writes=[g_])
                fw.op(fw.dve, lambda e: e.scalar_tensor_tensor(out=r_[:], in0=oT[:, sl], scalar=self.dnp[:, 24:25], in1=r_[:], op0=ALU.mult, op1=ALU.mult), reads=[oT, self.dnp, r_], writes=[r_])
                fw.op(fw.dve, lambda e: e.tensor_tensor(out=self.yT_t[:, h, sl], in0=r_[:], in1=g_[:], op=ALU.mult), reads=[r_, g_], writes=[self.yT[h]])
            fw.release(mh)
        fw.release(m, hard=True)

    def interleave(self, gens):
        gens = list(gens)
        while gens:
            for g in list(gens):
                try:
                    next(g)
                except StopIteration:
                    gens.remove(g)

    def load_w(self, wap, c0, ncols, tag):
        fw = self.fw
        w = fw.buf([128, 8, ncols], BF16, tag)
        src = wap.rearrange("(kc p) n -> p kc n", p=128)
        half = ncols // 2 if ncols >= 256 else ncols
        for a in range(0, ncols, half):
            fw.dma(fw.pool, w[:, :, a:a + half], src[:, :, c0 + a:c0 + a + half], writes=[w])
        return w

    def proj(self, w, col, ncols, nt, p=None):
        fw = self.fw
        if p is None:
            p = self.nextp()
        sl = slice(nt * 512, (nt + 1) * 512)
        for kc in range(8):
            fw.op(fw.pe, lambda e: e.matmul(p[0:ncols, :], lhsT=w[:, kc, col:col + ncols], rhs=self.hT_t[:, kc, sl], start=(kc == 0), stop=(kc == 7)),
                  reads=[w, self.hT[kc][nt]], writes=[p], inc=(kc == 7))
        return p

    def out_proj(self, wap):
        fw = self.fw
        m = fw.mark()
        self.prot = list(range(8))
        wo = self.load_w(wap, 0, D, "wo")
        for c in range(8):
            for nt in range(NT):
                sl = slice(nt * 512, (nt + 1) * 512)
                po = self.nextp()
                for kc in range(8):
                    fw.op(fw.pe, lambda e: e.matmul(po[:], lhsT=wo[:, kc, c * 128:(c + 1) * 128], rhs=self.yT_t[:, kc, sl], start=(kc == 0), stop=(kc == 7)),
                          reads=[wo, self.yT[kc]], writes=[po], inc=(kc == 7))
                fw.op(fw.dve, lambda e: e.tensor_tensor(out=self.xT_t[:, c, sl], in0=self.xT_t[:, c, sl], in1=po[:], op=ALU.add),
                      reads=[po, self.xT[c][nt]], writes=[self.xT[c][nt]])
        fw.release(m)

    def mixer_cd(self):
        fw = self.fw
        m0 = fw.mark()
        self.dlt = fw.buf([128, 512], F32, "dlt")
        fw.dma(fw.sp, self.dlt[:], self.dlt_d[:, :], writes=[self.dlt])
        self.cnyrow = fw.buf([1, S], BF16, "cnyrow")
        fw.dma(fw.sp, self.cnyrow[:], self.cnyrow_d[:, :], writes=[self.cnyrow])
        if "ret" in self.stages:
            self.retention()
        else:
            for c in range(4):
                fw.op(fw.dve, lambda e: e.memset(self.yT_t[:, c, :], 0.0), writes=[self.yT[c]])
        if "hy" in self.stages:
            self.hyena()
        else:
            for c in range(4, 8):
                fw.op(fw.dve, lambda e: e.memset(self.yT_t[:, c, :], 0.0), writes=[self.yT[c]])
        fw.release(m0)
        self.out_proj(self.cd_w_out)

    def retention(self):
        fw = self.fw
        W = self.cd_w_in
        m = fw.mark()
        self.prot = list(range(8))
        vtok = fw.buf([128, 16, 512], BF16, "vtok")
        qr = [fw.buf([128, S], BF16, "qr%d" % i) for i in range(2)]
        kr = [fw.buf([128, S], BF16, "kr%d" % i) for i in range(2)]
        m2 = fw.mark()
        wv = self.load_w(W, 512, 512, "wv")
        for tt in range(16):
            p = self.nextp()
            for kc in range(8):
                fw.op(fw.pe, lambda e: e.matmul(p[:], lhsT=self.hT_t[:, kc, tt * 128:(tt + 1) * 128], rhs=wv[:, kc, :], start=(kc == 0), stop=(kc == 7)),
                      reads=[wv, self.hT[kc][tt // 4]], writes=[p], inc=(kc == 7))
            fw.op(fw.act, lambda e: e.activation(out=vtok[:, tt, :], in_=p[:], func=AF.Copy), reads=[p], writes=[vtok])
        fw.release(m2)
        m2 = fw.mark()
        wqk = self.load_w(W, 0, 512, "wqk")
        qsb = [fw.buf([128, 512], BF16, "rqs%d" % i) for i in range(2)]
        t1 = [fw.buf([128, 512], F32, "rt1_%d" % i) for i in range(2)]
        t2 = [fw.buf([128, 512], F32, "rt2_%d" % i) for i in range(2)]
        k = 0
        for which, dst in ((0, qr), (1, kr)):
            for qc in range(2):
                col = which * 256 + qc * 128
                for nt in range(NT):
                    sl = slice(nt * 512, (nt + 1) * 512)
                    p1 = self.proj(wqk, col, 128, nt)
                    qs_ = qsb[k % 2]
                    fw.op(fw.dve, lambda e: e.tensor_copy(out=qs_[:], in_=p1[:]), reads=[p1], writes=[qs_])
                    p2 = self.nextp()
                    fw.op(fw.pe, lambda e: e.matmul(p2[:], lhsT=self.rperm[:], rhs=qs_[:], start=True, stop=True), reads=[self.rperm, qs_], writes=[p2])
                    a, b = t1[k % 2], t2[k % 2]
                    k += 1
                    fw.op(fw.dve, lambda e: e.tensor_tensor(out=a[:], in0=p1[:], in1=self.ropec[:, sl], op=ALU.mult), reads=[p1, self.ropec], writes=[a])
                    fw.op(fw.dve, lambda e: e.tensor_tensor(out=b[:], in0=p2[:], in1=self.ropes[:, sl], op=ALU.mult), reads=[p2, self.ropes], writes=[b])
                    fw.op(fw.pool, lambda e: e.tensor_tensor(out=dst[qc][:, sl], in0=a[:], in1=b[:], op=ALU.add), reads=[a, b], writes=[dst[qc]])
        fw.release(m2)
        ld = self.ld
        Lf = fw.buf([128, 512], BF16, "Lf")
        Lb = fw.buf([128, 512], BF16, "Lb")
        DC = [fw.buf([128, 512], BF16, "DC%d" % r) for r in range(4)]
        fac = fw.buf([128, 32], F32, "fac")
        scb = [fw.buf([128, 512], BF16, "scb%d" % i) for i in range(6)]
        oT = fw.buf([128, S], F32, "oT")
        sqb = [fw.buf([128, 512], BF16, "rsq%d" % i) for i in range(2)]
        rsb = [fw.buf([128, 512], F32, "rrs%d" % i) for i in range(2)]
        ea, eb = rsb[0], rsb[1]
        sgt = [fw.buf([128, 512], BF16, "rsg%d" % i) for i in range(2)]
        wg = fw.buf([128, 8, 128], BF16, "wg")
        wsrc = W.rearrange("(kc p) n -> p kc n", p=128)
        ks = 0
        for h in range(4):
            qc, po = h // 2, (h % 2) * 64
            lgf, lgb = ld[:, h:h + 1], ld[:, 4 + h:5 + h]
            fw.op(fw.act, lambda e: e.activation(out=Lf[:], in_=self.dlt[:], func=AF.Exp, bias=self.ld128[:, h:h + 1], scale=lgf), reads=[self.dlt, self.ld128, ld], writes=[Lf])
            fw.op(fw.act, lambda e: e.activation(out=Lb[:], in_=self.dlt[:], func=AF.Exp, bias=self.ld512[:, 4 + h:5 + h], scale=self.ldn[:, 4 + h:5 + h]), reads=[self.dlt, self.ld512, self.ldn], writes=[Lb])
            fw.op(fw.act, lambda e: e.activation(out=fac[:, 0:16], in_=self.iota16[:], func=AF.Exp, bias=self.ln8[:], scale=lgf), reads=[self.iota16, self.ln8, ld], writes=[fac])
            fw.op(fw.act, lambda e: e.activation(out=fac[:, 16:32], in_=self.iota16[:], func=AF.Exp, bias=self.ln8[:], scale=lgb), reads=[self.iota16, self.ln8, ld], writes=[fac])
            for r in range(4):
                fw.op(fw.dve, lambda e: e.tensor_scalar(out=ea[:], in0=self.dlt[:], scalar1=float(-128 * r), scalar2=0.0, op0=ALU.add, op1=ALU.max), reads=[self.dlt], writes=[ea])
                fw.op(fw.dve, lambda e: e.tensor_scalar(out=ea[:], in0=ea[:], scalar1=lgf, scalar2=None, op0=ALU.mult), reads=[ea, ld], writes=[ea])
                fw.op(fw.dve, lambda e: e.tensor_scalar(out=eb[:], in0=self.dlt[:], scalar1=float(-128 * r), scalar2=0.0, op0=ALU.add, op1=ALU.min), reads=[self.dlt], writes=[eb])
                fw.op(fw.dve, lambda e: e.scalar_tensor_tensor(out=eb[:], in0=eb[:], scalar=self.ldn[:, 4 + h:5 + h], in1=ea[:], op0=ALU.mult, op1=ALU.add), reads=[eb, ea, self.ldn], writes=[eb])
                fw.op(fw.act, lambda e: e.activation(out=DC[r][:], in_=eb[:], func=AF.Exp, bias=self.ln8[:], scale=1.0), reads=[eb, self.ln8], writes=[DC[r]])
            for nj in range(NT):
                pacc = self.P[nj % 2]
                self.prot = list(range(2, 8))
                pend = []

                def pv(sc, mi):
                    fw.op(fw.pe, lambda e: e.matmul(pacc[:], lhsT=vtok[:, mi, h * 128:(h + 1) * 128], rhs=sc[:], start=(mi == 0), stop=(mi == 15)),
                          reads=[vtok, sc], writes=[pacc], inc=True)
                for mi in range(16):
                    ps = self.nextp()
                    fw.op(fw.pe, lambda e: e.matmul(ps[:], lhsT=kr[qc][po:po + 64, mi * 128:(mi + 1) * 128], rhs=qr[qc][po:po + 64, nj * 512:(nj + 1) * 512], start=True, stop=True),
                          reads=[kr[qc], qr[qc]], writes=[ps], inc=True)
                    sc = scb[ks % 6]
                    ks += 1
                    r = mi - 4 * nj
                    if 0 <= r <= 3:
                        fw.op(fw.dve, lambda e: e.tensor_tensor(out=sc[:], in0=ps[:], in1=DC[r][:], op=ALU.mult), reads=[ps, DC[r]], writes=[sc])
                    elif r < 0:
                        kk = -r - 1
                        fw.op(fw.dve, lambda e: e.scalar_tensor_tensor(out=sc[:], in0=ps[:], scalar=fac[:, kk:kk + 1], in1=Lf[:], op0=ALU.mult, op1=ALU.mult), reads=[ps, fac, Lf], writes=[sc])
                    else:
                        kk = r - 4
                        fw.op(fw.dve, lambda e: e.scalar_tensor_tensor(out=sc[:], in0=ps[:], scalar=fac[:, 16 + kk:17 + kk], in1=Lb[:], op0=ALU.mult, op1=ALU.mult), reads=[ps, fac, Lb], writes=[sc])
                    pend.append((sc, mi))
                    if len(pend) > 4:
                        pv(*pend.pop(0))
                while pend:
                    pv(*pend.pop(0))
                fw.op(fw.act, lambda e: e.activation(out=oT[:, nj * 512:(nj + 1) * 512], in_=pacc[:], func=AF.Copy), reads=[pacc], writes=[oT])
            self.prot = list(range(2, 8))
            fw.dma(fw.pool, wg[:], wsrc[:, :, 1024 + h * 128:1024 + (h + 1) * 128], writes=[wg])
            for nt in range(NT):
                sl = slice(nt * 512, (nt + 1) * 512)
                sq, rs, sg = sqb[nt % 2], rsb[nt % 2], sgt[nt % 2]
                tb = rs
                fw.op(fw.act, lambda e: e.activation(out=sq[:], in_=oT[:, sl], func=AF.Square), reads=[oT], writes=[sq])
                pn = self.nextp()
                fw.op(fw.pe, lambda e: e.matmul(pn[:], lhsT=self.onesb[:], rhs=sq[:], start=True, stop=True), reads=[sq, self.onesb], writes=[pn])
                fw.op(fw.act, lambda e: e.activation(out=rs[:], in_=pn[:], func=AF.Ln, bias=self.eps[:], scale=1.0 / 128), reads=[pn, self.eps], writes=[rs])
                fw.op(fw.act, lambda e: e.activation(out=rs[:], in_=rs[:], func=AF.Exp, scale=-0.5), reads=[rs], writes=[rs])
                pg = self.proj(wg, 0, 128, nt)
                fw.op(fw.act, lambda e: e.activation(out=sg[:], in_=pg[:], func=AF.Silu), reads=[pg], writes=[sg])
                fw.op(fw.dve, lambda e: e.tensor_tensor(out=tb[:], in0=oT[:, sl], in1=rs[:], op=ALU.mult), reads=[oT, rs], writes=[tb])
                fw.op(fw.dve, lambda e: e.tensor_tensor(out=self.yT_t[:, h, sl], in0=tb[:], in1=sg[:], op=ALU.mult), reads=[tb, sg], writes=[self.yT[h]])
        self.prot = list(range(8))
        fw.release(m)


    def sin_rr(self, dst, src_ps, b_ap, f_ap, rows, tmpf, tmpi):
        fw = self.fw
        R = slice(0, rows)
        fw.op(fw.dve, lambda e: e.tensor_scalar(out=tmpf[0][R, :], in0=src_ps[R, :], scalar1=b_ap, scalar2=f_ap, op0=ALU.add, op1=ALU.mult),
              reads=[src_ps, self.hyp], writes=[tmpf[0]])
        fw.op(fw.dve, lambda e: e.tensor_scalar(out=tmpf[1][R, :], in0=tmpf[0][R, :], scalar1=1.0 / (2 * math.pi), scalar2=None, op0=ALU.mult),
              reads=[tmpf[0]], writes=[tmpf[1]])
        fw.op(fw.dve, lambda e: e.tensor_copy(out=tmpi[R, :], in_=tmpf[1][R, :]), reads=[tmpf[1]], writes=[tmpi])
        fw.op(fw.dve, lambda e: e.tensor_copy(out=tmpf[1][R, :], in_=tmpi[R, :]), reads=[tmpi], writes=[tmpf[1]])
        fw.op(fw.dve, lambda e: e.scalar_tensor_tensor(out=tmpf[0][R, :], in0=tmpf[1][R, :], scalar=-2 * math.pi, in1=tmpf[0][R, :], op0=ALU.mult, op1=ALU.add),
              reads=[tmpf[0], tmpf[1]], writes=[tmpf[0]])
        fw.op(fw.dve, lambda e: e.tensor_scalar(out=tmpf[0][R, :], in0=tmpf[0][R, :], scalar1=3.14159, scalar2=-3.14159, op0=ALU.min, op1=ALU.max),
              reads=[tmpf[0]], writes=[tmpf[0]])
        fw.op(fw.act, lambda e: e.activation(out=dst, in_=tmpf[0][R, :], func=AF.Sin), reads=[tmpf[0]], writes=[self.hidb])

    def fwd_dft(self, inC, inS, inCb, inSb, consume):
        fw = self.fw
        tabC = [fw.buf([128, 8, 128], BF16, "ftabC%d" % i) for i in range(2)]
        tabS = [fw.buf([128, 8, 128], BF16, "ftabS%d" % i) for i in range(2)]
        self.prot = [4, 5, 6, 7]
        for ft in range(16):
            csrc = self.dftcf[ft].rearrange("p (tt f) -> p tt f", tt=16)
            ssrc = self.dftsf[ft].rearrange("p (tt f) -> p tt f", tt=16)
            for hf in range(2):
                fw.dma(fw.sp, tabC[hf][:], csrc[:, hf * 8:hf * 8 + 8, :], writes=[tabC[hf]])
            pr = self.nextp()
            for tt in range(16):
                tb_ = tabC[tt // 8]
                fw.op(fw.pe, lambda e: e.matmul(pr[:], lhsT=tb_[:, tt % 8, :], rhs=inC[:, tt, :], start=(tt == 0), stop=(tt == 15)),
                      reads=[tb_, inCb], writes=[pr], inc=(tt % 8 == 7))
            for hf in range(2):
                fw.dma(fw.act, tabS[hf][:], ssrc[:, hf * 8:hf * 8 + 8, :], writes=[tabS[hf]])
            pi = self.nextp()
            for tt in range(16):
                tb_ = tabS[tt // 8]
                fw.op(fw.pe, lambda e: e.matmul(pi[:], lhsT=tb_[:, tt % 8, :], rhs=inS[:, tt, :], start=(tt == 0), stop=(tt == 15)),
                      reads=[tb_, inSb], writes=[pi], inc=(tt % 8 == 7))
            consume(ft, pr, pi)
        pr = self.nextp()
        for tt in range(16):
            fw.op(fw.pe, lambda e: e.matmul(pr[0:1, :], lhsT=self.cnycol[:, tt:tt + 1], rhs=inC[:, tt, :], start=(tt == 0), stop=(tt == 15)),
                  reads=[self.cnycol, inCb], writes=[pr], inc=(tt == 15))
        consume(16, pr, None)

    def hyena_setup(self):
        fw = self.fw
        m = fw.mark()
        self.prot = list(range(4))
        self.hidb = fw.buf([64, 2, S], F32, "hid")
        w3 = fw.buf([64, 1024], F32, "hw3")
        fw.dma(fw.sp, w3[:], self.hy_w3[:, :], writes=[w3])
        ma = fw.mark()
        zT = fw.buf([33, S], F32, "zT")
        fw.dma(fw.sp, zT[:], self.hyz[:, :], writes=[zT])
        w1 = fw.buf([33, 64], F32, "hw1")
        fw.dma(fw.sp, w1[:], self.hy_w1[:, :], writes=[w1])
        w2 = fw.buf([64, 64], F32, "hw2")
        fw.dma(fw.sp, w2[:], self.hy_w2[:, :], writes=[w2])
        hid = self.hidb
        tmpf = [fw.buf([64, 512], F32, "stf%d" % i) for i in range(2)]
        tmpi = fw.buf([64, 512], I32, "sti")
        hp = self.hyp
        for lyr in range(2):
            for nt in range(NT):
                sl = slice(nt * 512, (nt + 1) * 512)
                p = self.nextp()
                if lyr == 0:
                    fw.op(fw.pe, lambda e: e.matmul(p[0:64, :], lhsT=w1[0:33, :], rhs=zT[0:33, sl], start=True, stop=True), reads=[w1, zT], writes=[p])
                else:
                    fw.op(fw.pe, lambda e: e.matmul(p[0:64, :], lhsT=w2[0:64, :], rhs=hid[0:64, 0, sl], start=True, stop=True), reads=[w2, hid], writes=[p])
                self.sin_rr(hid[0:64, lyr, sl], p, hp[0:64, 2 * lyr:2 * lyr + 1], hp[0:64, 2 * lyr + 1:2 * lyr + 2], 64, tmpf, tmpi)
        fw.release(ma)
        hs = self.hT_t[:, 0:4, :].rearrange("p c (a b) -> p (c a) b", b=512)
        hd = self.hT_t[:, 4:8, :].rearrange("p c (a b) -> p (c a) b", b=512)
        hsb = fw.view(self.hT_t, "hsb", alias=[self.hT[c][n] for c in range(8) for n in range(NT)])
        dec = [fw.buf([128, 512], F32, "dec%d" % i) for i in range(2)]
        hfb = [fw.buf([128, 512], F32, "hfb%d" % i) for i in range(2)]
        hbb = [fw.buf([128, 512], F32, "hbb%d" % i) for i in range(2)]
        for tt in range(16):
            d_, hf, hb = dec[tt % 2], hfb[tt % 2], hbb[tt % 2]
            fw.dma(fw.sp, d_[:], self.hydec[tt * 128:(tt + 1) * 128, :], writes=[d_])
            pf = self.nextp()
            fw.op(fw.pe, lambda e: e.matmul(pf[:], lhsT=hid[0:64, 1, tt * 128:(tt + 1) * 128], rhs=w3[0:64, 0:512], start=True, stop=True), reads=[hid, w3], writes=[pf])
            pb = self.nextp()
            fw.op(fw.pe, lambda e: e.matmul(pb[:], lhsT=hid[0:64, 1, tt * 128:(tt + 1) * 128], rhs=w3[0:64, 512:1024], start=True, stop=True), reads=[hid, w3], writes=[pb])
            fw.op(fw.dve, lambda e: e.tensor_tensor(out=hf[:], in0=pf[:], in1=d_[:], op=ALU.mult), reads=[pf, d_], writes=[hf])
            fw.op(fw.dve, lambda e: e.tensor_tensor(out=hb[:], in0=pb[:], in1=d_[:], op=ALU.mult), reads=[pb, d_], writes=[hb])
            fw.op(fw.dve, lambda e: e.tensor_tensor(out=hs[:, tt, :], in0=hf[:], in1=hb[:], op=ALU.add), reads=[hf, hb], writes=[hsb])
            fw.op(fw.dve, lambda e: e.tensor_tensor(out=hd[:, tt, :], in0=hf[:], in1=hb[:], op=ALU.subtract), reads=[hf, hb], writes=[hsb])
        fw.release(m)
        m = fw.mark()
        so = [fw.buf([128, 2, 512], F32, "so%d" % i) for i in range(2)]

        def consume(ft, pr, pi):
            o = so[ft % 2]
            if ft < 16:
                fw.op(fw.dve, lambda e: e.tensor_scalar(out=o[:, 0, :], in0=pr[:], scalar1=self.wf[:, ft:ft + 1], scalar2=None, op0=ALU.mult), reads=[pr, self.wf], writes=[o])
                fw.op(fw.dve, lambda e: e.tensor_scalar(out=o[:, 1, :], in0=pi[:], scalar1=self.wf[:, ft:ft + 1], scalar2=None, op0=ALU.mult), reads=[pi, self.wf], writes=[o])
                fw.dma(fw.sp, self.spec_d[:, ft * 128:(ft + 1) * 128, :].rearrange("a p c -> p a c"), o[:], reads=[o])
            else:
                fw.op(fw.dve, lambda e: e.tensor_scalar(out=o[0:1, 0, :], in0=pr[0:1, :], scalar1=self.wf[0:1, 16:17], scalar2=None, op0=ALU.mult), reads=[pr, self.wf], writes=[o])
                fw.dma(fw.sp, self.spec_d[0:1, 2048:2049, :].rearrange("a p c -> p a c"), o[0:1, 0:1, :], reads=[o])
        self.fwd_dft(hs, hd, hsb, hsb, consume)
        self.prot = list(range(8))
        fw.release(m, hard=True)

    def hyena(self):
        fw = self.fw
        W = self.cd_w_in
        wsrc = W.rearrange("(kc p) n -> p kc n", p=128)
        m = fw.mark()
        x0T = fw.buf([128, 4, S], BF16, "x0T")
        utok = fw.buf([128, 16, 512], BF16, "utok")
        m2 = fw.mark()
        self.prot = list(range(8))
        gb = [fw.buf([128, S + 2], F32, "hgb%d" % i) for i in range(1)]
        for g in gb:
            fw.op(fw.dve, lambda e: e.memset(g[:, 0:1], 0.0), writes=[g])
            fw.op(fw.dve, lambda e: e.memset(g[:, S + 1:S + 2], 0.0), writes=[g])
        a1 = fw.buf([128, S], F32, "ha1")
        a2 = fw.buf([128, S], F32, "ha2")
        wp = [fw.buf([128, 8, 128], BF16, "hwp%d" % i) for i in range(2)]
        hcw = self.hcw
        k = 0
        for cc in range(4):
            for part in (1, 2, 0):
                ch = part * 4 + cc
                w = wp[k % 2]
                g = gb[0]
                k += 1
                fw.dma(fw.pool, w[:], wsrc[:, :, 1536 + ch * 128:1536 + (ch + 1) * 128], writes=[w])
                for nt in range(NT):
                    p = self.proj(w, 0, 128, nt)
                    fw.op(fw.act, lambda e: e.activation(out=g[:, 1 + nt * 512:1 + (nt + 1) * 512], in_=p[:], func=AF.Copy), reads=[p], writes=[g])
                dst = a1 if part == 1 else a2
                fw.op(fw.dve, lambda e: e.tensor_scalar(out=dst[:], in0=g[:, 0:S], scalar1=hcw[:, ch, 0:1], scalar2=hcw[:, ch, 3:4], op0=ALU.mult, op1=ALU.add), reads=[g, hcw], writes=[dst])
                fw.op(fw.dve, lambda e: e.scalar_tensor_tensor(out=dst[:], in0=g[:, 1:S + 1], scalar=hcw[:, ch, 1:2], in1=dst[:], op0=ALU.mult, op1=ALU.add), reads=[g, hcw, dst], writes=[dst])
                if part == 1:
                    fw.op(fw.dve, lambda e: e.scalar_tensor_tensor(out=dst[:], in0=g[:, 2:S + 2], scalar=hcw[:, ch, 2:3], in1=dst[:], op0=ALU.mult, op1=ALU.add), reads=[g, hcw, dst], writes=[dst])
                elif part == 2:
                    fw.op(fw.dve, lambda e: e.scalar_tensor_tensor(out=dst[:], in0=g[:, 2:S + 2], scalar=hcw[:, ch, 2:3], in1=dst[:], op0=ALU.mult, op1=ALU.add), reads=[g, hcw, dst], writes=[dst])
                    fw.op(fw.dve, lambda e: e.tensor_tensor(out=self.yT_t[:, 4 + cc, :], in0=a1[:], in1=a2[:], op=ALU.mult), reads=[a1, a2], writes=[self.yT[4 + cc]])
                    for tq in range(4):
                        p = self.nextp()
                        pb = p[:].bitcast(BF16)
                        for t4 in range(4):
                            tt = tq * 4 + t4
                            fw.op(fw.pe, lambda e: e.transpose(pb[:, t4 * 128:(t4 + 1) * 128], self.yT_t[:, 4 + cc, tt * 128:(tt + 1) * 128], self.identb[:]),
                                  reads=[self.yT[4 + cc], self.identb], writes=[p], inc=(t4 == 3))
                        fw.op(fw.act, lambda e: e.activation(out=utok[:, tq * 4:tq * 4 + 4, cc * 128:(cc + 1) * 128], in_=pb[:, 0:512].rearrange("p (a b) -> p a b", a=4), func=AF.Copy),
                              reads=[p], writes=[utok])
                else:
                    fw.op(fw.dve, lambda e: e.scalar_tensor_tensor(out=x0T[:, cc, :], in0=g[:, 2:S + 2], scalar=hcw[:, ch, 2:3], in1=dst[:], op0=ALU.mult, op1=ALU.add), reads=[g, hcw, dst], writes=[x0T])
        fw.release(m2)
        Yr_t = self.hT_t[:, 0:4, :].rearrange("p c (a b) -> p (c a) b", b=512)
        Yi_t = self.hT_t[:, 4:8, :].rearrange("p c (a b) -> p (c a) b", b=512)
        Y = fw.view(self.hT_t, "Yspec", alias=[self.hT[c][n] for c in range(8) for n in range(NT)])
        Yny = fw.buf([1, 512], BF16, "Yny")
        m3 = fw.mark()
        spt = [fw.buf([128, 2, 512], F32, "spt%d" % i) for i in range(2)]
        tA = [fw.buf([128, 512], F32, "tA%d" % i) for i in range(2)]
        tB = [fw.buf([128, 512], F32, "tB%d" % i) for i in range(2)]

        def consume(ft, pr, pi):
            sp_ = spt[ft % 2]
            A, B = tA[ft % 2], tB[ft % 2]
            if ft < 16:
                fw.dma(fw.sp, sp_[:], self.spec_d[:, ft * 128:(ft + 1) * 128, :].rearrange("a p c -> p a c"), writes=[sp_])
                fw.op(fw.dve, lambda e: e.tensor_tensor(out=A[:], in0=pr[:], in1=sp_[:, 0, :], op=ALU.mult), reads=[pr, sp_], writes=[A])
                fw.op(fw.dve, lambda e: e.tensor_tensor(out=B[:], in0=pi[:], in1=sp_[:, 1, :], op=ALU.mult), reads=[pi, sp_], writes=[B])
                fw.op(fw.pool, lambda e: e.tensor_tensor(out=Yr_t[:, ft, :], in0=A[:], in1=B[:], op=ALU.subtract), reads=[A, B], writes=[Y])
                A2, B2 = tA[(ft + 1) % 2], tB[(ft + 1) % 2]
                fw.op(fw.dve, lambda e: e.tensor_tensor(out=A2[:], in0=pr[:], in1=sp_[:, 1, :], op=ALU.mult), reads=[pr, sp_], writes=[A2])
                fw.op(fw.dve, lambda e: e.tensor_tensor(out=B2[:], in0=pi[:], in1=sp_[:, 0, :], op=ALU.mult), reads=[pi, sp_], writes=[B2])
                fw.op(fw.pool, lambda e: e.tensor_tensor(out=Yi_t[:, ft, :], in0=A2[:], in1=B2[:], op=ALU.add), reads=[A2, B2], writes=[Y])
            else:
                fw.dma(fw.sp, sp_[0:1, 0:1, :], self.spec_d[0:1, 2048:2049, :].rearrange("a p c -> p a c"), writes=[sp_])
                fw.op(fw.dve, lambda e: e.tensor_tensor(out=Yny[0:1, :], in0=pr[0:1, :], in1=sp_[0:1, 0, :], op=ALU.mult), reads=[pr, sp_], writes=[Yny])
        self.fwd_dft(utok[:, :, :], utok[:, :, :], utok, utok, consume)
        fw.release(m3)
        itC = [fw.buf([128, 4, 512], BF16, "itC%d" % i) for i in range(2)]
        itS = [fw.buf([128, 4, 512], BF16, "itS%d" % i) for i in range(2)]
        csrc = self.dftc.rearrange("(ft p) t -> p ft t", p=128)
        ssrc = self.dfts.rearrange("(ft p) t -> p ft t", p=128)
        ep = [fw.buf([128, 512], F32, "hep%d" % i) for i in range(2)]
        kk = 0
        ke = 0
        for nt in range(NT):
            sl = slice(nt * 512, (nt + 1) * 512)
            banks = [self.P[(nt % 2) * 4 + cc] for cc in range(4)]
            for fg in range(4):
                tc_, ts_ = itC[kk % 2], itS[kk % 2]
                kk += 1
                fw.dma(fw.sp, tc_[:], csrc[:, fg * 4:fg * 4 + 4, sl], writes=[tc_])
                fw.dma(fw.act, ts_[:], ssrc[:, fg * 4:fg * 4 + 4, sl], writes=[ts_])
                for f4 in range(4):
                    ft = fg * 4 + f4
                    for cc in range(4):
                        fw.op(fw.pe, lambda e: e.matmul(banks[cc][:], lhsT=Yr_t[:, ft, cc * 128:(cc + 1) * 128], rhs=tc_[:, f4, :], start=(ft == 0), stop=False),
                              reads=[Y, tc_], writes=[banks[cc]], inc=False)
                        fw.op(fw.pe, lambda e: e.matmul(banks[cc][:], lhsT=Yi_t[:, ft, cc * 128:(cc + 1) * 128], rhs=ts_[:, f4, :], start=False, stop=False),
                              reads=[Y, ts_], writes=[banks[cc]], inc=(cc == 3 and f4 == 3))
            for cc in range(4):
                fw.op(fw.pe, lambda e: e.matmul(banks[cc][:], lhsT=Yny[0:1, cc * 128:(cc + 1) * 128], rhs=self.cnyrow[0:1, sl], start=False, stop=True),
                      reads=[Yny, self.cnyrow], writes=[banks[cc]], inc=True)
                t_ = ep[ke % 2]
                ke += 1
                fw.op(fw.dve, lambda e: e.scalar_tensor_tensor(out=t_[:], in0=self.yT_t[:, 4 + cc, sl], scalar=self.hybias[:, cc:cc + 1], in1=banks[cc][:], op0=ALU.mult, op1=ALU.add),
                      reads=[self.yT[4 + cc], self.hybias, banks[cc]], writes=[t_])
                fw.op(fw.dve, lambda e: e.tensor_tensor(out=self.yT_t[:, 4 + cc, sl], in0=t_[:], in1=x0T[:, cc, sl], op=ALU.mult), reads=[t_, x0T], writes=[self.yT[4 + cc]])
        self.prot = list(range(8))
        fw.release(m, hard=True)


def host_consts():
    c = {"ident": np.eye(128, dtype=np.float32)}
    inv = 10000.0 ** (-np.arange(0, 64, 2, dtype=np.float64) / 64)
    ang = np.arange(S, dtype=np.float64)[None, :] * inv[:, None]
    cos64 = np.concatenate([np.cos(ang), np.cos(ang)], 0)
    sin64 = np.concatenate([-np.sin(ang), np.sin(ang)], 0)
    c["ropec"] = np.concatenate([cos64, cos64], 0).astype(ml_dtypes.bfloat16)
    c["ropes"] = np.concatenate([sin64, sin64], 0).astype(ml_dtypes.bfloat16)
    pm = np.zeros((128, 128), np.float32)
    pm[rot_perm(128, 64), np.arange(128)] = 1.0
    c["rperm"] = pm.astype(ml_dtypes.bfloat16)
    c["dlt"] = (np.arange(512, dtype=np.float32)[None, :] - np.arange(128, dtype=np.float32)[:, None]).astype(np.float32)
    c["iota16"] = np.tile(128.0 * np.arange(16, dtype=np.float32)[None, :], (128, 1)).astype(np.float32)
    dl = np.arange(2944)[None, :] - np.arange(128)[:, None] - 1408
    ad = np.abs(dl)
    mult = (ad <= 64).astype(np.float32) + ((dl % 4 == 0) & (ad <= 256)) + ((dl % 16 == 0) & (ad <= 1024))
    c["dilmask"] = mult.astype(ml_dtypes.bfloat16)
    ii = np.arange(128)
    triF = (ii[:, None] <= ii[None, :]).astype(np.float32)
    triB = (ii[:, None] >= ii[None, :]).astype(np.float32)
    c["dn_tri"] = np.ascontiguousarray(np.stack([triF, triB], 1))
    mL = np.where(ii[None, :] <= ii[:, None], 0.0, -30000.0).astype(np.float32)
    mU = np.where(ii[None, :] >= ii[:, None], 0.0, -30000.0).astype(np.float32)
    noti = (1.0 - np.eye(128)).astype(np.float32)
    c["dn_msk"] = np.ascontiguousarray(np.stack([np.tile(mL, (1, 4)), np.tile(mU, (1, 4)), np.tile(noti, (1, 4))], 1)).astype(ml_dtypes.bfloat16)
    L = S
    t = np.linspace(0.0, 1.0, L, dtype=np.float32)[:, None]
    w = (2.0 * math.pi * np.arange(L, dtype=np.float32) / L).astype(np.float32)
    fb = np.linspace(1e-4, 15, 16, dtype=np.float32)
    angz = w[:, None] * fb[None, :]
    z = np.concatenate([t, np.cos(angz), -np.sin(angz)], -1).astype(np.float32)
    c["hyz"] = np.ascontiguousarray(z.T)
    deltas = np.abs(np.linspace(math.log(1e-2) / 1.5, math.log(1e-2) / 0.3, 512, dtype=np.float32))
    c["hydec"] = np.exp(-t * deltas[None, :]).astype(np.float32)
    ab = (np.arange(S, dtype=np.int64)[:, None] * np.arange(S, dtype=np.int64)[None, :]) % (2 * S)
    th = ab.astype(np.float64) * (2.0 * math.pi / (2 * S))
    c["dftc"] = np.cos(th).astype(ml_dtypes.bfloat16)
    c["dfts"] = np.sin(th).astype(ml_dtypes.bfloat16)
    c["dftcf"] = np.ascontiguousarray(c["dftc"].reshape(16, 128, 16, 128).transpose(2, 1, 0, 3).reshape(16, 128, 2048))
    c["dftsf"] = np.ascontiguousarray(c["dfts"].reshape(16, 128, 16, 128).transpose(2, 1, 0, 3).reshape(16, 128, 2048))
    sign = (1.0 - 2.0 * (np.arange(S) % 2)).astype(np.float32)
    c["cnycol"] = np.ascontiguousarray(sign.reshape(16, 128).T).astype(ml_dtypes.bfloat16)
    c["cnyrow"] = sign.reshape(1, S).astype(ml_dtypes.bfloat16)
    wf = np.full((128, 17), 2.0 / (2 * S), dtype=np.float32)
    wf[0, 0] = 1.0 / (2 * S)
    wf[:, 16] = 1.0 / (2 * S)
    c["wf"] = wf
    return c


def rot_perm(ncols, hd):
    idx = np.arange(ncols)
    h, i = idx // hd, idx % hd
    return h * hd + (i + hd // 2) % hd


def host_layout(inputs):
    f = lambda a: np.ascontiguousarray(a, dtype=np.float32)
    o = {}
    for k in ("norm_mix", "norm_ffn", "final_norm", "ffn_conv_w", "ffn_conv_b", "ffn_w_out"):
        o[k] = f(inputs[k])
    wi = np.asarray(inputs["ffn_w_in"], dtype=np.float32).reshape(2, 8, 128, 2, 22, 128)
    o["ffn_w_in_t"] = f(wi.transpose(0, 4, 2, 1, 3, 5).reshape(2, 22, 128, 2048))
    ab = f(inputs["ab_w_in"][0])
    o["ab_w_in"] = ab
    o["ab_w_rot"] = f(ab[:, 2064:2064 + 1024][:, rot_perm(1024, 64)])
    o["ab_w_out"] = f(inputs["ab_w_out"][0])
    dnp = np.zeros((128, 25), np.float32)
    dnp[:, 0:8] = inputs["dn_a_log"][0].reshape(1, 8)
    dnp[:, 8:16] = inputs["dn_dt_bias"][0].reshape(1, 8)
    dnp[:, 24] = inputs["dn_norm_w"][0]
    o["dnp"] = dnp
    o["dcw"] = f(inputs["dn_conv_w"][0].reshape(3, 12, 128).transpose(2, 1, 0))
    cd = f(inputs["cd_w_in"][0])
    o["cd_w_in"] = cd
    o["cd_w_rot"] = f(cd[:, 0:512][:, rot_perm(512, 64)])
    o["cd_w_out"] = f(inputs["cd_w_out"][0])
    o["ret_log_decay"] = f(inputs["ret_log_decay"][0].reshape(8))
    o["hy_w1"] = f(inputs["hy_w1"][0])
    o["hy_w2"] = f(inputs["hy_w2"][0])
    o["hy_w3"] = f(inputs["hy_w3"][0])
    o["hyp"] = f(np.stack([inputs["hy_b1"][0], inputs["hy_f1"][0], inputs["hy_b2"][0], inputs["hy_f2"][0]], 1))
    hc = np.concatenate([inputs["hy_conv_w"][0], inputs["hy_conv_b"][0][None, :]], 0)
    o["hcw"] = f(hc.reshape(4, 12, 128).transpose(2, 1, 0))
    o["hybias"] = f(inputs["hy_bias"][0].reshape(4, 128).T)
    return o


_CACHE = {}


def run(inputs, nseq_per_core=4, ncores=NCORES, stages=("ab", "dn", "dil", "ffn0", "cd", "ret", "hy", "ffn1")):
    key = (nseq_per_core, tuple(stages))
    prog = Prog(nseq_per_core, stages)
    nc = prog.build()
    consts = host_consts()
    lay = host_layout(inputs)
    x = np.ascontiguousarray(inputs["x"], dtype=np.float32)
    in_maps = []
    for c in range(ncores):
        m = {}
        for name in prog.dram:
            if name == "x":
                m[name] = x[c * nseq_per_core:(c + 1) * nseq_per_core]
            elif name in consts:
                m[name] = consts[name]
            else:
                m[name] = lay[name]
        in_maps.append(m)
    res = run_bass_kernel_spmd(nc, in_maps, core_ids=list(range(ncores)))
    return np.concatenate([res.results[c]["out"] for c in range(ncores)], axis=0)


def kernel(**inputs):
    return run(inputs).astype(np.float32)
```

```python
import math
import numpy as np
import ml_dtypes
import concourse.bass as bass
import concourse.mybir as mybir
from concourse.bass_utils import run_bass_kernel_spmd

F32 = mybir.dt.float32
BF16 = mybir.dt.bfloat16
I32 = mybir.dt.int32
AF = mybir.ActivationFunctionType
ALU = mybir.AluOpType
AX = mybir.AxisListType

S = 2048
D = 1024
NT = 4
DFF = 2816
NCORES = 8


class Buf:
    __slots__ = ("t", "writer", "readers", "dsem", "dval", "name", "depth", "pver", "bdepth")

    def __init__(self, t, name=""):
        self.t = t
        self.writer = None
        self.readers = {}
        self.dsem = None
        self.dval = 0
        self.name = name
        self.pver = 0
        self.bdepth = 0

    def __getitem__(self, idx):
        return self.t[idx]


class Eng:
    def __init__(self, h, sem, name):
        self.applied = 0
        self.h = h
        self.sem = sem
        self.cnt = 0
        self.seen = {}
        self.name = name


class FW:
    def __init__(self, nc):
        self.nc = nc
        self._ctx = []
        self._semctx = []
        self.dsems = []
        self.pe = Eng(nc.tensor, self.sem("pe"), "pe")
        self.act = Eng(nc.scalar, self.sem("act"), "act")
        self.dve = Eng(nc.vector, self.sem("dve"), "dve")
        self.pool = Eng(nc.gpsimd, self.sem("pool"), "pool")
        self.sp = Eng(nc.sync, self.sem("sp"), "sp")
        self.engs = (self.pe, self.act, self.dve, self.pool, self.sp)
        self.out_stamps = []
        self.dma_bufs = []
        self.free_dsems = []
        self.live = []
        self.gp = {}
        self.gpv = 0
        self.gp_snap = {0: []}
        self.nds = 0
        self.nps = 0

    def sem(self, name):
        cm = self.nc.semaphore(name)
        s = cm.__enter__()
        self._semctx.append(cm)
        return s

    def sb(self, shape, dt, name):
        self.nps += 1
        name = "%s_u%d" % (name, self.nps)
        cm = self.nc.sbuf_tensor(name, list(shape), dt)
        t = cm.__enter__()
        self._ctx.append(cm)
        return t

    def ps(self, shape, dt, name):
        cm = self.nc.psum_tensor(name, list(shape), dt)
        t = cm.__enter__()
        self._ctx.append(cm)
        return t

    def buf(self, shape, dt, name):
        b = Buf(self.sb(shape, dt, name), name)
        b.pver = self.gpv
        b.bdepth = len(self._ctx)
        self.live.append(b)
        return b

    def view(self, t, name="", alias=()):
        b = Buf(t, name)
        if alias:
            for a in alias:
                self._merge(a)
            self._snap()
        b.pver = self.gpv
        b.bdepth = len(self._ctx) + 1
        self.live.append(b)
        return b

    def _merge(self, b):
        st = list(b.readers.values())
        if b.writer is not None:
            st.append(b.writer)
        for (sm, v) in st:
            k = id(sm)
            if k not in self.gp or self.gp[k][1] < v:
                self.gp[k] = (sm, v)

    def _snap(self):
        self.gpv += 1
        self.gp_snap[self.gpv] = list(self.gp.values())

    def mark(self):
        return len(self._ctx)

    def release(self, mark, hard=False):
        if hard:
            self.barrier()
        kl = []
        for b in self.live:
            if b.bdepth > mark:
                self._merge(b)
            else:
                kl.append(b)
        self.live = kl
        self._snap()
        keep = []
        for tb in self.dsems:
            if tb.depth > mark:
                self.free_dsems.append((tb.dsem, tb.dval))
            else:
                keep.append(tb)
        self.dsems = keep
        while len(self._ctx) > mark:
            self._ctx.pop().__exit__(None, None, None)

    def close(self):
        while self._ctx:
            self._ctx.pop().__exit__(None, None, None)
        while self._semctx:
            self._semctx.pop().__exit__(None, None, None)

    def _wait(self, E, sem, val):
        k = id(sem)
        if E.seen.get(k, 0) >= val:
            return
        E.h.wait_ge(sem, val)
        E.seen[k] = val

    def _deps(self, E, reads, writes, own_dsem=None):
        pv = 0
        for b in reads:
            if b.pver > pv:
                pv = b.pver
        for b in writes:
            if b.pver > pv:
                pv = b.pver
        if pv > E.applied:
            for (sm, v) in self.gp_snap[pv]:
                self._wait(E, sm, v)
            E.applied = pv
        for b in reads:
            if b.writer is not None:
                s, v = b.writer
                if s is E.sem and E is self.pe:
                    continue
                self._wait(E, s, v)
        for b in writes:
            if b.writer is not None:
                s, v = b.writer
                if not (s is E.sem and E is self.pe) and s is not own_dsem:
                    self._wait(E, s, v)
            for k, (s, v) in b.readers.items():
                if s is E.sem:
                    continue
                self._wait(E, s, v)

    def op(self, E, issue, reads=(), writes=(), inc=True):
        self._deps(E, reads, writes)
        ins = issue(E.h)
        if inc:
            E.cnt += 1
            ins.then_inc(E.sem, 1)
            stamp = (E.sem, E.cnt)
        else:
            stamp = (E.sem, E.cnt + 1)
        for b in reads:
            b.readers[id(E.sem)] = stamp
        for b in writes:
            b.writer = stamp
            b.readers = {}
        return ins

    def dma(self, Q, out_ap, in_ap, reads=(), writes=(), is_output=False, **kw):
        tb = writes[0] if writes else reads[0]
        if tb.dsem is None:
            if self.free_dsems:
                tb.dsem, tb.dval = self.free_dsems.pop()
            else:
                self.nds += 1
                tb.dsem = self.sem("d%d" % self.nds)
            tb.depth = len(self._ctx)
            self.dsems.append(tb)
        self._deps(Q, reads, writes, own_dsem=tb.dsem)
        tb.dval += 16
        Q.h.dma_start(out=out_ap, in_=in_ap, **kw).then_inc(tb.dsem, 16)
        stamp = (tb.dsem, tb.dval)
        for b in reads:
            b.readers[id(tb.dsem)] = stamp
        for b in writes:
            b.writer = stamp
            b.readers = {}
        if is_output:
            self.out_stamps.append(stamp)

    def barrier(self):
        for E in self.engs:
            for X in self.engs:
                if X is not E and X.cnt:
                    self._wait(E, X.sem, X.cnt)
            for tb in self.dsems:
                if tb.dval:
                    self._wait(E, tb.dsem, tb.dval)

    def finish(self):
        for s, v in self.out_stamps:
            self._wait(self.sp, s, v)
        for E in (self.pe, self.act, self.dve, self.pool):
            if E.cnt:
                self._wait(self.sp, E.sem, E.cnt)


class Prog:
    def __init__(self, nseq, stages):
        self.nseq = nseq
        self.stages = stages
        self.nc = bass.Bass("TRN2", target_bir_lowering=False)
        self.fw = FW(self.nc)
        self.dram = {}

    def din(self, name, shape, dt=F32):
        t = self.nc.dram_tensor(name, list(shape), dt, kind="ExternalInput").ap()
        self.dram[name] = t
        return t

    def nextp(self):
        p = self.P[self.prot[self.pi % len(self.prot)]]
        self.pi += 1
        return p

    def build(self):
        nc, fw = self.nc, self.fw
        nseq = self.nseq
        x_d = self.din("x", [nseq, S, D])
        out_d = nc.dram_tensor("out", [nseq, S, D], F32, kind="ExternalOutput").ap()
        norm_mix = self.din("norm_mix", [2, D])
        norm_ffn = self.din("norm_ffn", [2, D])
        final_norm = self.din("final_norm", [D])
        self.ffn_w_in = self.din("ffn_w_in_t", [2, 22, 128, 2048])
        self.ffn_conv_w = self.din("ffn_conv_w", [2, 3, DFF])
        self.ffn_conv_b = self.din("ffn_conv_b", [2, DFF])
        self.ffn_w_out = self.din("ffn_w_out", [2, DFF, D])
        ident_d = self.din("ident", [128, 128])
        rperm_d = self.din("rperm", [128, 128], BF16)
        L1 = "cd" in self.stages
        L0 = "ab" in self.stages
        if L0:
            self.ab_w_in = self.din("ab_w_in", [D, 3600])
            self.ab_w_rot = self.din("ab_w_rot", [D, 1024])
            self.ab_w_out = self.din("ab_w_out", [D, D])
            self.dilmask = self.din("dilmask", [128, 2944], BF16)
            self.dnp_d = self.din("dnp", [128, 25])
            self.dcw_d = self.din("dcw", [128, 12, 3])
            self.tri_d = self.din("dn_tri", [128, 2, 128])
            self.msk_d = self.din("dn_msk", [128, 3, 512], BF16)
        if L0 and not L1:
            ropec_d = self.din("ropec", [128, S], BF16)
            ropes_d = self.din("ropes", [128, S], BF16)
        if L1:
            self.cd_w_in = self.din("cd_w_in", [D, 3072])
            self.cd_w_rot = self.din("cd_w_rot", [D, 512])
            self.cd_w_out = self.din("cd_w_out", [D, D])
            ret_ld = self.din("ret_log_decay", [8])
            ropec_d = self.din("ropec", [128, S], BF16)
            ropes_d = self.din("ropes", [128, S], BF16)
            dlt_d = self.din("dlt", [128, 512])
            iota16_d = self.din("iota16", [128, 16])
            self.hyz = self.din("hyz", [33, S])
            self.hy_w1 = self.din("hy_w1", [33, 64])
            self.hy_w2 = self.din("hy_w2", [64, 64])
            self.hy_w3 = self.din("hy_w3", [64, 1024])
            hyp_d = self.din("hyp", [64, 4])
            self.hydec = self.din("hydec", [S, 512])
            self.dftc = self.din("dftc", [S, S], BF16)
            self.dfts = self.din("dfts", [S, S], BF16)
            self.dftcf = self.din("dftcf", [16, 128, 2048], BF16)
            self.dftsf = self.din("dftsf", [16, 128, 2048], BF16)
            cnycol_d = self.din("cnycol", [128, 16], BF16)
            cnyrow_d = self.din("cnyrow", [1, S], BF16)
            wf_d = self.din("wf", [128, 17])
            hcw_d = self.din("hcw", [128, 12, 4])
            hybias_d = self.din("hybias", [128, 4])
            self.spec_d = nc.dram_tensor("spec_scratch", [2, 2176, 512], F32, kind="Internal").ap()

        self.xT_t = fw.sb([128, 8, S], F32, "xT")
        self.xT = [[fw.view(self.xT_t, "xT%d_%d" % (c, n)) for n in range(NT)] for c in range(8)]
        self.hT_t = fw.sb([128, 8, S], BF16, "hT")
        self.hT = [[fw.view(self.hT_t, "hT%d_%d" % (c, n)) for n in range(NT)] for c in range(8)]
        self.yT_t = fw.sb([128, 8, S], BF16, "yT")
        self.yT = [fw.view(self.yT_t, "yT%d" % c) for c in range(8)]
        self.P = [fw.view(fw.ps([128, 512], F32, "ps%d" % i), "ps%d" % i) for i in range(8)]
        self.pi = 0
        self.prot = list(range(8))
        self.ident = fw.buf([128, 128], F32, "ident")
        fw.dma(fw.sp, self.ident[:], ident_d[:, :], writes=[self.ident])
        self.identb = fw.buf([128, 128], BF16, "identb")
        fw.op(fw.dve, lambda e: e.tensor_copy(out=self.identb[:], in_=self.ident[:]), reads=[self.ident], writes=[self.identb])
        self.onesb = fw.buf([128, 128], BF16, "onesb")
        fw.op(fw.dve, lambda e: e.memset(self.onesb[:], 1.0), writes=[self.onesb])
        self.eps = fw.buf([128, 1], F32, "eps")
        fw.op(fw.dve, lambda e: e.memset(self.eps[:], 1e-6), writes=[self.eps])
        if L1 or L0:
            self.rperm = fw.buf([128, 128], BF16, "rperm")
            fw.dma(fw.sp, self.rperm[:], rperm_d[:, :], writes=[self.rperm])
            self.ropec = fw.buf([128, S], BF16, "ropec")
            self.ropes = fw.buf([128, S], BF16, "ropes")
            fw.dma(fw.sp, self.ropec[:], ropec_d[:, :], writes=[self.ropec])
            fw.dma(fw.sp, self.ropes[:], ropes_d[:, :], writes=[self.ropes])
        if L1:
            self.dlt_d = dlt_d
            self.iota16 = fw.buf([128, 16], F32, "iota16")
            fw.dma(fw.sp, self.iota16[:], iota16_d[:, :], writes=[self.iota16])
            self.ld = fw.buf([128, 8], F32, "ld")
            fw.dma(fw.sp, self.ld[:], ret_ld.partition_broadcast(128), writes=[self.ld])
            self.ld128 = fw.buf([128, 8], F32, "ld128")
            self.ld512 = fw.buf([128, 8], F32, "ld512")
            self.ldn = fw.buf([128, 8], F32, "ldn")
            fw.op(fw.dve, lambda e: e.tensor_scalar(out=self.ld128[:], in0=self.ld[:], scalar1=128.0, scalar2=None, op0=ALU.mult), reads=[self.ld], writes=[self.ld128])
            fw.op(fw.dve, lambda e: e.tensor_scalar(out=self.ld512[:], in0=self.ld[:], scalar1=512.0, scalar2=None, op0=ALU.mult), reads=[self.ld], writes=[self.ld512])
            fw.op(fw.dve, lambda e: e.tensor_scalar(out=self.ldn[:], in0=self.ld[:], scalar1=-1.0, scalar2=None, op0=ALU.mult), reads=[self.ld], writes=[self.ldn])
            self.hyp = fw.buf([64, 4], F32, "hyp")
            fw.dma(fw.sp, self.hyp[:], hyp_d[:, :], writes=[self.hyp])
            self.cnycol = fw.buf([128, 16], BF16, "cnycol")
            fw.dma(fw.sp, self.cnycol[:], cnycol_d[:, :], writes=[self.cnycol])
            self.cnyrow_d = cnyrow_d
            self.wf = fw.buf([128, 17], F32, "wf")
            fw.dma(fw.sp, self.wf[:], wf_d[:, :], writes=[self.wf])
            self.hcw = fw.buf([128, 12, 4], F32, "hcw")
            fw.dma(fw.sp, self.hcw[:], hcw_d[:, :, :], writes=[self.hcw])
            self.hybias = fw.buf([128, 4], F32, "hybias")
            fw.dma(fw.sp, self.hybias[:], hybias_d[:, :], writes=[self.hybias])
            self.ln8 = fw.buf([128, 1], F32, "ln8")
            fw.op(fw.dve, lambda e: e.memset(self.ln8[:], math.log(0.125)), writes=[self.ln8])
        self.nw = fw.buf([128, 5, 8], F32, "nw")
        srcs = [norm_mix[0], norm_ffn[0], norm_mix[1], norm_ffn[1], final_norm]
        for i, sap in enumerate(srcs):
            fw.dma(fw.sp, self.nw[:, i, :], sap.rearrange("(c p) -> p c", p=128), writes=[self.nw],
                   allow_slow_non_contiguous=True)
        self.fcw = fw.buf([128, 2, 22, 4], F32, "fcw")
        for l in range(2):
            for k in range(3):
                fw.dma(fw.sp, self.fcw[:, l, :, k], self.ffn_conv_w[l, k].rearrange("(j p) -> p j", p=128),
                       writes=[self.fcw], allow_slow_non_contiguous=True)
            fw.dma(fw.sp, self.fcw[:, l, :, 3], self.ffn_conv_b[l].rearrange("(j p) -> p j", p=128),
                   writes=[self.fcw], allow_slow_non_contiguous=True)

        if L1 and "hy" in self.stages:
            self.hyena_setup()
        for s in range(nseq):
            if s == 0:
                self.load_x(x_d[s])
            for layer in range(2):
                if ("ab" if layer == 0 else "cd") in self.stages:
                    self.rmsnorm(2 * layer)
                    (self.mixer_ab if layer == 0 else self.mixer_cd)()
                if ("ffn%d" % layer) in self.stages:
                    self.rmsnorm(2 * layer + 1)
                    self.ffn(layer)
            self.store_out(out_d[s], x_d[s + 1] if s + 1 < nseq else None)
        fw.finish()
        fw.close()
        return nc

    def l0c(self):
        fw = self.fw
        self.dnp = fw.buf([128, 25], F32, "dnp")
        fw.dma(fw.sp, self.dnp[:], self.dnp_d[:, :], writes=[self.dnp])
        fw.op(fw.act, lambda e: e.activation(out=self.dnp[:, 16:24], in_=self.dnp[:, 0:8], func=AF.Exp), reads=[self.dnp], writes=[self.dnp])
        fw.op(fw.dve, lambda e: e.tensor_scalar(out=self.dnp[:, 16:24], in0=self.dnp[:, 16:24], scalar1=-1.0, scalar2=None, op0=ALU.mult), reads=[self.dnp], writes=[self.dnp])
        self.dcw = fw.buf([128, 12, 3], F32, "dcw")
        fw.dma(fw.sp, self.dcw[:], self.dcw_d[:, :, :], writes=[self.dcw])
        tri = fw.buf([128, 2, 128], F32, "dn_tri")
        fw.dma(fw.sp, tri[:], self.tri_d[:, :, :], writes=[tri])
        self.triF = fw.view(tri.t[:, 0, :], "triF")
        self.triB = fw.view(tri.t[:, 1, :], "triB")
        msk = fw.buf([128, 3, 512], BF16, "dn_msk")
        fw.dma(fw.sp, msk[:], self.msk_d[:, :, :], writes=[msk])
        self.mL4 = fw.view(msk.t[:, 0, :], "mL4")
        self.mU4 = fw.view(msk.t[:, 1, :], "mU4")
        self.noti4 = fw.view(msk.t[:, 2, :], "noti4")
        for v_ in (self.triF, self.triB, self.mL4, self.mU4, self.noti4):
            v_.writer = (tri.writer if v_ in (self.triF, self.triB) else msk.writer)
        self.identf4 = fw.buf([128, 512], BF16, "identf4")
        for i4 in range(4):
            fw.op(fw.dve, lambda e: e.tensor_copy(out=self.identf4[:, i4 * 128:(i4 + 1) * 128], in_=self.ident[:]), reads=[self.ident], writes=[self.identf4])
        self.onesf = fw.buf([128, 128], F32, "onesf")
        fw.op(fw.dve, lambda e: e.memset(self.onesf[:], 1.0), writes=[self.onesf])
        self.one1 = fw.buf([128, 1], F32, "one1")
        fw.op(fw.dve, lambda e: e.memset(self.one1[:], 1.0), writes=[self.one1])

    def load_tile(self, xs, tt, xb):
        fw = self.fw
        fw.dma(fw.sp, xb[:], xs[tt * 128:(tt + 1) * 128, :], writes=[xb])
        nt = tt // 4
        for half in range(2):
            p = self.nextp()
            for cc in range(4):
                c = half * 4 + cc
                fw.op(fw.pe, lambda e: e.transpose(p[:, cc * 128:(cc + 1) * 128], xb[:, c * 128:(c + 1) * 128], self.ident[:]),
                      reads=[xb, self.ident], writes=[p], inc=(cc == 3))
            dst = self.xT_t[:, half * 4:half * 4 + 4, tt * 128:(tt + 1) * 128]
            wr = [self.xT[half * 4 + cc][nt] for cc in range(4)]
            if half == 0:
                fw.op(fw.act, lambda e: e.activation(out=dst, in_=p[:].rearrange("p (c t) -> p c t", c=4), func=AF.Copy), reads=[p], writes=wr)
            else:
                fw.op(fw.dve, lambda e: e.tensor_copy(out=dst, in_=p[:].rearrange("p (c t) -> p c t", c=4)), reads=[p], writes=wr)

    def load_x(self, xs):
        fw = self.fw
        m = fw.mark()
        xin = [fw.buf([128, D], F32, "xin%d" % i) for i in range(2)]
        for tt in range(16):
            self.load_tile(xs, tt, xin[tt % 2])
        fw.release(m)

    def store_out(self, outs, next_xs=None):
        fw = self.fw
        self.rmsnorm(4, inplace=True)
        m = fw.mark()
        ob = [fw.buf([128, D], F32, "ob%d" % i) for i in range(2)]
        xin = [fw.buf([128, D], F32, "xinb%d" % i) for i in range(2)] if next_xs is not None else None
        for tt in range(16):
            o = ob[tt % 2]
            nt = tt // 4
            for half in range(2):
                p = self.nextp()
                for cc in range(4):
                    c = half * 4 + cc
                    fw.op(fw.pe, lambda e, c=c, cc=cc, p=p: e.transpose(p[:, cc * 128:(cc + 1) * 128], self.xT_t[:, c, tt * 128:(tt + 1) * 128], self.ident[:]),
                          reads=[self.xT[c][nt], self.ident], writes=[p], inc=(cc == 3))
                if half == 0:
                    fw.op(fw.act, lambda e, p=p, o=o: e.activation(out=o[:, 0:512], in_=p[:], func=AF.Copy), reads=[p], writes=[o])
                else:
                    fw.op(fw.dve, lambda e, p=p, o=o: e.tensor_copy(out=o[:, 512:1024], in_=p[:]), reads=[p], writes=[o])
            fw.dma(fw.sp, outs[tt * 128:(tt + 1) * 128, :], o[:], reads=[o], is_output=True)
            if next_xs is not None and tt % 4 == 3:
                for t2 in range(tt - 3, tt + 1):
                    self.load_tile(next_xs, t2, xin[t2 % 2])
        fw.release(m)

    def rmsnorm(self, widx, inplace=False):
        fw = self.fw
        m = fw.mark()
        sq = [fw.buf([128, 512], BF16, "sq%d" % i) for i in range(3)]
        rstd = [fw.buf([128, 512], F32, "rstd%d" % i) for i in range(2)]
        k = 0
        for nt in range(NT):
            sl = slice(nt * 512, (nt + 1) * 512)
            p = self.nextp()
            for c in range(8):
                q = sq[k % 3]
                k += 1
                fw.op(fw.act, lambda e, q=q, c=c: e.activation(out=q[:], in_=self.xT_t[:, c, sl], func=AF.Square),
                      reads=[self.xT[c][nt]], writes=[q])
                fw.op(fw.pe, lambda e, q=q, c=c, p=p: e.matmul(p[:], lhsT=self.onesb[:], rhs=q[:], start=(c == 0), stop=(c == 7)),
                      reads=[q, self.onesb], writes=[p], inc=True)
            r = rstd[nt % 2]
            fw.op(fw.act, lambda e, r=r, p=p: e.activation(out=r[:], in_=p[:], func=AF.Ln, bias=self.eps[:], scale=1.0 / D),
                  reads=[p, self.eps], writes=[r])
            fw.op(fw.act, lambda e, r=r: e.activation(out=r[:], in_=r[:], func=AF.Exp, scale=-0.5), reads=[r], writes=[r])
            for c in range(8):
                if inplace:
                    dst, wr = self.xT_t[:, c, sl], [self.xT[c][nt]]
                else:
                    dst, wr = self.hT_t[:, c, sl], [self.hT[c][nt]]
                fw.op(fw.dve, lambda e, c=c, r=r, dst=dst: e.scalar_tensor_tensor(out=dst, in0=self.xT_t[:, c, sl], scalar=self.nw[:, widx, c:c + 1],
                                                                                 in1=r[:], op0=ALU.mult, op1=ALU.mult),
                      reads=[self.xT[c][nt], r, self.nw], writes=wr)
        fw.release(m)

    def ffn(self, layer):
        fw = self.fw
        m = fw.mark()
        win_d = self.ffn_w_in[layer]
        wout_d = self.ffn_w_out[layer].rearrange("(j p) n -> p j n", p=128)
        win = [fw.buf([128, 8, 256], BF16, "win%d" % i) for i in range(2)]
        wout = fw.buf([128, 8, D], BF16, "wout")
        gbuf = [fw.buf([128, S + 2], F32, "gbuf%d" % i) for i in range(2)]
        ubuf = [fw.buf([128, S], BF16, "ubuf%d" % i) for i in range(2)]
        acc = [fw.buf([128, 512], F32, "acc%d" % i) for i in range(2)]
        sg = [fw.buf([128, 512], BF16, "sg%d" % i) for i in range(2)]
        for g in gbuf:
            fw.op(fw.dve, lambda e, g=g: e.memset(g[:, 0:1], 0.0), writes=[g])
            fw.op(fw.dve, lambda e, g=g: e.memset(g[:, S + 1:S + 2], 0.0), writes=[g])
        groups = [list(range(0, 8)), list(range(8, 16)), list(range(16, 22))]
        k = 0
        for grp in groups:
            for si, j in enumerate(grp):
                fw.dma(fw.pool, wout[:, si, :], wout_d[:, j, :], writes=[wout])
            for si, j in enumerate(grp):
                w = win[j % 2]
                gb = gbuf[j % 2]
                ub = ubuf[j % 2]
                wj = win_d[j].rearrange("p (kc n) -> p kc n", kc=8)
                fw.dma(fw.pool, w[:, 0:4, :], wj[:, 0:4, :], writes=[w])
                fw.dma(fw.pool, w[:, 4:8, :], wj[:, 4:8, :], writes=[w])
                for nt in range(NT):
                    sl = slice(nt * 512, (nt + 1) * 512)
                    pg = self.nextp()
                    for kc in range(8):
                        fw.op(fw.pe, lambda e, kc=kc, pg=pg, w=w: e.matmul(pg[:], lhsT=w[:, kc, 0:128], rhs=self.hT_t[:, kc, sl], start=(kc == 0), stop=(kc == 7)),
                              reads=[w, self.hT[kc][nt]], writes=[pg], inc=(kc == 7))
                    fw.op(fw.act, lambda e, pg=pg, gb=gb, nt=nt: e.activation(out=gb[:, 1 + nt * 512:1 + (nt + 1) * 512], in_=pg[:], func=AF.Copy),
                          reads=[pg], writes=[gb])
                    pu = self.nextp()
                    for kc in range(8):
                        fw.op(fw.pe, lambda e, kc=kc, pu=pu, w=w: e.matmul(pu[:], lhsT=w[:, kc, 128:256], rhs=self.hT_t[:, kc, sl], start=(kc == 0), stop=(kc == 7)),
                              reads=[w, self.hT[kc][nt]], writes=[pu], inc=(kc == 7))
                    fw.op(fw.act, lambda e, pu=pu, ub=ub, sl=sl: e.activation(out=ub[:, sl], in_=pu[:], func=AF.Copy),
                          reads=[pu], writes=[ub])
                cw = self.fcw
                for nt in range(NT):
                    a = acc[k % 2]
                    sgb = sg[k % 2]
                    k += 1
                    o = nt * 512
                    fw.op(fw.dve, lambda e, a=a, gb=gb, o=o, j=j: e.tensor_scalar(out=a[:], in0=gb[:, o:o + 512], scalar1=cw[:, layer, j, 0:1], scalar2=None, op0=ALU.mult),
                          reads=[gb, cw], writes=[a])
                    fw.op(fw.dve, lambda e, a=a, gb=gb, o=o, j=j: e.scalar_tensor_tensor(out=a[:], in0=gb[:, o + 1:o + 513], scalar=cw[:, layer, j, 1:2], in1=a[:], op0=ALU.mult, op1=ALU.add),
                          reads=[gb, cw, a], writes=[a])
                    fw.op(fw.dve, lambda e, a=a, gb=gb, o=o, j=j: e.scalar_tensor_tensor(out=a[:], in0=gb[:, o + 2:o + 514], scalar=cw[:, layer, j, 2:3], in1=a[:], op0=ALU.mult, op1=ALU.add),
                          reads=[gb, cw, a], writes=[a])
                    fw.op(fw.act, lambda e, a=a, sgb=sgb, j=j: e.activation(out=sgb[:], in_=a[:], func=AF.Silu, bias=cw[:, layer, j, 3:4], scale=1.0),
                          reads=[a, cw], writes=[sgb])
                    fw.op(fw.dve, lambda e, sgb=sgb, ub=ub, si=si, o=o: e.tensor_tensor(out=self.yT_t[:, si, o:o + 512], in0=sgb[:], in1=ub[:, o:o + 512], op=ALU.mult),
                          reads=[sgb, ub], writes=[self.yT[si]])
            for nt in range(NT):
                for c in range(8):
                    sl = slice(nt * 512, (nt + 1) * 512)
                    po = self.nextp()
                    for si in range(len(grp)):
                        fw.op(fw.pe, lambda e, si=si, po=po, c=c: e.matmul(po[:], lhsT=wout[:, si, c * 128:(c + 1) * 128], rhs=self.yT_t[:, si, sl], start=(si == 0), stop=(si == len(grp) - 1)),
                              reads=[wout, self.yT[si]], writes=[po], inc=(si == len(grp) - 1))
                    fw.op(fw.dve, lambda e, po=po, c=c: e.tensor_tensor(out=self.xT_t[:, c, sl], in0=self.xT_t[:, c, sl], in1=po[:], op=ALU.add),
                          reads=[po, self.xT[c][nt]], writes=[self.xT[c][nt]])
        fw.release(m)


    def mixer_ab(self):
        fw = self.fw
        m0 = fw.mark()
        self.l0c()
        if "dn" in self.stages:
            self.deltanet()
        else:
            for c in range(4):
                fw.op(fw.dve, lambda e: e.memset(self.yT_t[:, c, :], 0.0), writes=[self.yT[c]])
        if "dil" in self.stages:
            self.dilated()
        else:
            for c in range(4, 8):
                fw.op(fw.dve, lambda e: e.memset(self.yT_t[:, c, :], 0.0), writes=[self.yT[c]])
        fw.release(m0)
        self.out_proj(self.ab_w_out)

    def dilated(self):
        fw = self.fw
        W = self.ab_w_in
        wsrc = W.rearrange("(kc p) n -> p kc n", p=128)
        rsrc = self.ab_w_rot.rearrange("(kc p) n -> p kc n", p=128)
        m = fw.mark()
        Q0, K0, V0 = 2064, 2064 + 512, 2064 + 1024
        mask = fw.buf([128, 2944], BF16, "dmask")
        fw.dma(fw.sp, mask[:], self.dilmask[:, :], writes=[mask])
        qe = fw.buf([65, S], BF16, "qe")
        ke = fw.buf([65, S], BF16, "ke")
        fw.op(fw.dve, lambda e: e.memset(ke[64:65, :], 1.0), writes=[ke])
        vx = fw.buf([128, 16, 128], BF16, "vx")
        wq = [fw.buf([128, 8, 64], BF16, "dwq%d" % i) for i in range(4)]
        wv = fw.buf([128, 8, 64], BF16, "dwv")
        t1 = [fw.buf([64, 512], F32, "dt1_%d" % i) for i in range(2)]
        t2 = [fw.buf([64, 512], F32, "dt2_%d" % i) for i in range(2)]
        sqq = fw.buf([64, S], BF16, "dsqq")
        sqk = fw.buf([64, S], BF16, "dsqk")
        kmx = fw.buf([128, 8], F32, "dkmx")
        qn = [fw.buf([128, 512], F32, "dqn%d" % i) for i in range(2)]
        ex = [fw.buf([128, 512], BF16, "dex%d" % i) for i in range(6)]
        pt = [fw.buf([128, 512], BF16, "dpt%d" % i) for i in range(7)]
        rd = [fw.buf([128, 512], F32, "drd%d" % i) for i in range(2)]
        kx = 0
        kp = 0
        for h in range(8):
            hc, odd = h // 2, h % 2
            self.prot = list(range(2, 8))
            fw.dma(fw.pool, wq[0][:], wsrc[:, :, Q0 + h * 64:Q0 + (h + 1) * 64], writes=[wq[0]])
            fw.dma(fw.pool, wq[2][:], wsrc[:, :, K0 + h * 64:K0 + (h + 1) * 64], writes=[wq[2]])
            fw.dma(fw.pool, wv[:], wsrc[:, :, V0 + h * 64:V0 + (h + 1) * 64], writes=[wv])
            vcol, ocol = (64, 0) if odd else (0, 64)
            fw.op(fw.dve, lambda e: e.memset(vx[:, :, ocol:ocol + 64], 1.0), writes=[vx])
            for tq in range(4):
                p = self.nextp()
                for t4 in range(4):
                    tt = tq * 4 + t4
                    for kc in range(8):
                        fw.op(fw.pe, lambda e: e.matmul(p[:, t4 * 64:(t4 + 1) * 64], lhsT=self.hT_t[:, kc, tt * 128:(tt + 1) * 128], rhs=wv[:, kc, :], start=(kc == 0), stop=(kc == 7)),
                              reads=[wv, self.hT[kc][tq]], writes=[p], inc=(kc == 7 and t4 == 3))
                fw.op(fw.act, lambda e: e.activation(out=vx[:, tq * 4:tq * 4 + 4, vcol:vcol + 64], in_=p[:, 0:256].rearrange("p (a b) -> p a b", a=4), func=AF.Copy), reads=[p], writes=[vx])
            for which, dst, sq in ((0, qe, sqq), (1, ke, sqk)):
                for nt in range(NT):
                    sl = slice(nt * 512, (nt + 1) * 512)
                    p1 = self.proj(wq[2 * which], 0, 64, nt)
                    qs_ = wq[1] if kx % 2 == 0 else wq[3]
                    qsv = qs_[:].rearrange("p a b -> p (a b)")
                    fw.op(fw.dve, lambda e: e.tensor_copy(out=qsv[0:64, :], in_=p1[0:64, :]), reads=[p1], writes=[qs_])
                    p2 = self.nextp()
                    fw.op(fw.pe, lambda e: e.matmul(p2[0:64, :], lhsT=self.rperm[0:64, 0:64], rhs=qsv[0:64, :], start=True, stop=True), reads=[self.rperm, qs_], writes=[p2])
                    a, b = t1[kx % 2], t2[kx % 2]
                    kx += 1
                    sc_ = 0.125 if which == 0 else 1.0
                    fw.op(fw.dve, lambda e: e.scalar_tensor_tensor(out=a[:], in0=p1[0:64, :], scalar=sc_, in1=self.ropec[0:64, sl], op0=ALU.mult, op1=ALU.mult), reads=[p1, self.ropec], writes=[a])
                    fw.op(fw.dve, lambda e: e.scalar_tensor_tensor(out=b[:], in0=p2[0:64, :], scalar=sc_, in1=self.ropes[0:64, sl], op0=ALU.mult, op1=ALU.mult), reads=[p2, self.ropes], writes=[b])
                    fw.op(fw.pool, lambda e: e.tensor_tensor(out=dst[0:64, sl], in0=a[:], in1=b[:], op=ALU.add), reads=[a, b], writes=[dst])
                    fw.op(fw.act, lambda e: e.activation(out=sq[:, sl], in_=dst[0:64, sl], func=AF.Square), reads=[dst], writes=[sq])
            for nt in range(NT):
                sl = slice(nt * 512, (nt + 1) * 512)
                pk = self.nextp()
                fw.op(fw.pe, lambda e: e.matmul(pk[:], lhsT=self.onesb[0:64, :], rhs=sqk[:, sl], start=True, stop=True), reads=[self.onesb, sqk], writes=[pk])
                fw.op(fw.dve, lambda e: e.tensor_reduce(out=kmx[:, nt:nt + 1], in_=pk[:], axis=AX.X, op=ALU.max), reads=[pk], writes=[kmx])
            fw.op(fw.dve, lambda e: e.tensor_reduce(out=kmx[:, 4:5], in_=kmx[:, 0:4], axis=AX.X, op=ALU.max), reads=[kmx], writes=[kmx])
            fw.op(fw.act, lambda e: e.activation(out=kmx[:, 5:6], in_=kmx[:, 4:5], func=AF.Sqrt), reads=[kmx], writes=[kmx])
            fw.op(fw.dve, lambda e: e.tensor_scalar(out=kmx[:, 6:7], in0=kmx[:, 5:6], scalar1=-1.0, scalar2=None, op0=ALU.mult), reads=[kmx], writes=[kmx])
            for nt in range(NT):
                sl = slice(nt * 512, (nt + 1) * 512)
                pq = self.nextp()
                fw.op(fw.pe, lambda e: e.matmul(pq[:], lhsT=self.onesb[0:64, :], rhs=sqq[:, sl], start=True, stop=True), reads=[self.onesb, sqq], writes=[pq])
                q_ = qn[nt % 2]
                fw.op(fw.act, lambda e: e.activation(out=q_[:], in_=pq[:], func=AF.Sqrt), reads=[pq], writes=[q_])
                fw.op(fw.dve, lambda e: e.tensor_scalar(out=qe[64:65, sl], in0=q_[64:65, :], scalar1=kmx[64:65, 6:7], scalar2=None, op0=ALU.mult), reads=[q_, kmx], writes=[qe])
            for nj in range(NT):
                pacc = self.P[nj % 2]
                pend = []
                mis = [mi for mi in range(16) if -8 <= mi - 4 * nj <= 11]

                def pv(sc, mi):
                    fw.op(fw.pe, lambda e: e.matmul(pacc[:], lhsT=vx[:, mi, :], rhs=sc[:], start=(mi == mis[0]), stop=(mi == mis[-1])),
                          reads=[vx, sc], writes=[pacc], inc=True)
                for mi in mis:
                    r = mi - 4 * nj
                    ps = self.nextp()
                    fw.op(fw.pe, lambda e: e.matmul(ps[:], lhsT=ke[0:65, mi * 128:(mi + 1) * 128], rhs=qe[0:65, nj * 512:(nj + 1) * 512], start=True, stop=True),
                          reads=[ke, qe], writes=[ps], inc=True)
                    e_ = ex[kp % 6]
                    sc = pt[kp % 7]
                    kp += 1
                    fw.op(fw.act, lambda e: e.activation(out=e_[:], in_=ps[:], func=AF.Exp), reads=[ps], writes=[e_])
                    fw.op(fw.dve, lambda e: e.tensor_tensor(out=sc[:], in0=e_[:], in1=mask[:, (11 - r) * 128:(11 - r) * 128 + 512], op=ALU.mult), reads=[e_, mask], writes=[sc])
                    pend.append((sc, mi))
                    if len(pend) > 4:
                        pv(*pend.pop(0))
                while pend:
                    pv(*pend.pop(0))
                r_ = rd[nj % 2]
                nlo, dlo = (64, 0) if odd else (0, 64)
                fw.op(fw.act, lambda e: e.activation(out=r_[nlo:nlo + 64, :], in_=pacc[dlo:dlo + 64, :], func=AF.Ln), reads=[pacc], writes=[r_])
                fw.op(fw.act, lambda e: e.activation(out=r_[nlo:nlo + 64, :], in_=r_[nlo:nlo + 64, :], func=AF.Exp, scale=-1.0), reads=[r_], writes=[r_])
                fw.op(fw.dve, lambda e: e.tensor_tensor(out=self.yT_t[nlo:nlo + 64, 4 + hc, nj * 512:(nj + 1) * 512], in0=pacc[nlo:nlo + 64, :], in1=r_[nlo:nlo + 64, :], op=ALU.mult),
                      reads=[pacc, r_], writes=[self.yT[4 + hc]])
        self.prot = list(range(8))
        fw.release(m)


    def deltanet(self):
        fw = self.fw
        W = self.ab_w_in
        wsrc = W.rearrange("(kc p) n -> p kc n", p=128)
        P = self.P
        m = fw.mark()
        beta = fw.buf([128, 8, 16], F32, "dn_beta")
        nbeta = fw.buf([128, 8, 16], F32, "dn_nbeta")
        gl = fw.buf([128, 8, 16], F32, "dn_g")
        mba = fw.mark()
        ba = fw.buf([128, 16, 16], F32, "dn_ba")
        wba = fw.buf([128, 8, 16], BF16, "dn_wba")
        fw.dma(fw.pool, wba[:], wsrc[:, :, 2048:2064], writes=[wba])
        p = P[0]
        for tt in range(16):
            for kc in range(8):
                fw.op(fw.pe, lambda e: e.matmul(p[:, tt * 16:(tt + 1) * 16], lhsT=self.hT_t[:, kc, tt * 128:(tt + 1) * 128], rhs=wba[:, kc, :], start=(kc == 0), stop=(kc == 7)),
                      reads=[wba, self.hT[kc][tt // 4]], writes=[p], inc=(kc == 7 and tt == 15))
        fw.op(fw.dve, lambda e: e.tensor_copy(out=ba[:], in_=p[:, 0:256].rearrange("p (t c) -> p c t", c=16)), reads=[p], writes=[ba])
        fw.op(fw.act, lambda e: e.activation(out=beta[:], in_=ba[:, 0:8, :], func=AF.Sigmoid), reads=[ba], writes=[beta])
        fw.op(fw.dve, lambda e: e.tensor_scalar(out=nbeta[:], in0=beta[:], scalar1=-1.0, scalar2=None, op0=ALU.mult), reads=[beta], writes=[nbeta])
        tx = fw.buf([128, 16], F32, "dn_tx")
        ta = fw.buf([128, 16], F32, "dn_ta")
        for c in range(8):
            fw.op(fw.dve, lambda e: e.tensor_scalar(out=tx[:], in0=ba[:, 8 + c, :], scalar1=self.dnp[:, 8 + c:9 + c], scalar2=None, op0=ALU.add), reads=[ba, self.dnp], writes=[tx])
            fw.op(fw.dve, lambda e: e.tensor_scalar(out=ta[:], in0=tx[:], scalar1=-1.0, scalar2=None, op0=ALU.mult), reads=[tx], writes=[ta])
            fw.op(fw.dve, lambda e: e.tensor_tensor(out=ta[:], in0=ta[:], in1=tx[:], op=ALU.max), reads=[tx, ta], writes=[ta])
            fw.op(fw.act, lambda e: e.activation(out=ta[:], in_=ta[:], func=AF.Exp, scale=-1.0), reads=[ta], writes=[ta])
            fw.op(fw.act, lambda e: e.activation(out=ta[:], in_=ta[:], func=AF.Ln, bias=self.one1[:], scale=1.0), reads=[ta, self.one1], writes=[ta])
            fw.op(fw.dve, lambda e: e.scalar_tensor_tensor(out=tx[:], in0=tx[:], scalar=0.0, in1=ta[:], op0=ALU.max, op1=ALU.add), reads=[tx, ta], writes=[tx])
            fw.op(fw.dve, lambda e: e.tensor_scalar(out=gl[:, c, :], in0=tx[:], scalar1=self.dnp[:, 16 + c:17 + c], scalar2=None, op0=ALU.mult), reads=[tx, self.dnp], writes=[gl])
        fw.release(mba)
        for h in range(4):
            mh = fw.mark()
            al = [self.yT[4], self.yT[5], self.yT[6], self.yT[7]]
            qT = fw.view(self.yT_t[:, 6, :], "dn_qT", alias=al)
            kT = fw.view(self.yT_t[:, 7, :], "dn_kT", alias=al)
            oT = fw.view(self.yT_t[:, 4:6, :].rearrange("p a s -> p (a s)").bitcast(F32), "dn_oT", alias=al)
            ktok = fw.buf([128, 16, 128], BF16, "dn_ktok")
            vtok = fw.buf([128, 16, 128], BF16, "dn_vtok")
            mp = fw.mark()
            self.prot = list(range(8))
            gb = fw.buf([128, S + 2], F32, "dn_gb")
            fw.op(fw.dve, lambda e: e.memset(gb[:, 0:1], 0.0), writes=[gb])
            fw.op(fw.dve, lambda e: e.memset(gb[:, S + 1:S + 2], 0.0), writes=[gb])
            a1 = fw.buf([128, S], F32, "dn_a1")
            vT = fw.buf([128, S], BF16, "dn_vT")
            wps = [fw.buf([128, 8, 128], BF16, "dn_wp%d" % i) for i in range(3)]
            for part in range(3):
                ch_ = part * 4 + h
                fw.dma(fw.pool, wps[part][:], wsrc[:, :, ch_ * 128:(ch_ + 1) * 128], writes=[wps[part]])
            sq = [fw.buf([128, 512], BF16, "dn_sq%d" % i) for i in range(2)]
            rs = [fw.buf([128, 512], F32, "dn_rs%d" % i) for i in range(2)]
            dcw = self.dcw
            for part in range(3):
                ch = part * 4 + h
                wp = wps[part]
                for nt in range(NT):
                    pp = self.proj(wp, 0, 128, nt)
                    fw.op(fw.act, lambda e: e.activation(out=gb[:, 1 + nt * 512:1 + (nt + 1) * 512], in_=pp[:], func=AF.Copy), reads=[pp], writes=[gb])
                fw.op(fw.dve, lambda e: e.tensor_scalar(out=a1[:], in0=gb[:, 0:S], scalar1=dcw[:, ch, 0:1], scalar2=None, op0=ALU.mult), reads=[gb, dcw], writes=[a1])
                fw.op(fw.dve, lambda e: e.scalar_tensor_tensor(out=a1[:], in0=gb[:, 1:S + 1], scalar=dcw[:, ch, 1:2], in1=a1[:], op0=ALU.mult, op1=ALU.add), reads=[gb, dcw, a1], writes=[a1])
                fw.op(fw.dve, lambda e: e.scalar_tensor_tensor(out=a1[:], in0=gb[:, 2:S + 2], scalar=dcw[:, ch, 2:3], in1=a1[:], op0=ALU.mult, op1=ALU.add), reads=[gb, dcw, a1], writes=[a1])
                if part == 2:
                    fw.op(fw.act, lambda e: e.activation(out=vT[:], in_=a1[:], func=AF.Silu), reads=[a1], writes=[vT])
                    srcT, dtok = vT, vtok
                else:
                    fw.op(fw.act, lambda e: e.activation(out=a1[:], in_=a1[:], func=AF.Silu), reads=[a1], writes=[a1])
                    dstT = qT if part == 0 else kT
                    for nt in range(NT):
                        sl = slice(nt * 512, (nt + 1) * 512)
                        q_, r_ = sq[nt % 2], rs[nt % 2]
                        fw.op(fw.act, lambda e: e.activation(out=q_[:], in_=a1[:, sl], func=AF.Square), reads=[a1], writes=[q_])
                        pn = self.nextp()
                        fw.op(fw.pe, lambda e: e.matmul(pn[:], lhsT=self.onesb[:], rhs=q_[:], start=True, stop=True), reads=[q_, self.onesb], writes=[pn])
                        fw.op(fw.act, lambda e: e.activation(out=r_[:], in_=pn[:], func=AF.Ln, bias=self.eps[:], scale=1.0), reads=[pn, self.eps], writes=[r_])
                        fw.op(fw.act, lambda e: e.activation(out=r_[:], in_=r_[:], func=AF.Exp, scale=-0.5), reads=[r_], writes=[r_])
                        scl = 128.0 ** -0.5 if part == 0 else 1.0
                        fw.op(fw.dve, lambda e: e.scalar_tensor_tensor(out=dstT[:, sl], in0=a1[:, sl], scalar=scl, in1=r_[:], op0=ALU.mult, op1=ALU.mult), reads=[a1, r_], writes=[dstT])
                    srcT, dtok = kT, ktok
                if part >= 1:
                    for tq in range(4):
                        pp = self.nextp()
                        pb = pp[:].bitcast(BF16)
                        for t4 in range(4):
                            tt = tq * 4 + t4
                            fw.op(fw.pe, lambda e: e.transpose(pb[:, t4 * 128:(t4 + 1) * 128], srcT[:, tt * 128:(tt + 1) * 128], self.identb[:]),
                                  reads=[srcT, self.identb], writes=[pp], inc=(t4 == 3))
                        fw.op(fw.act, lambda e: e.activation(out=dtok[:, tq * 4:tq * 4 + 4, :], in_=pb[:, 0:512].rearrange("p (a b) -> p a b", a=4), func=AF.Copy), reads=[pp], writes=[dtok])
            fw.release(mp)
            import os
            def dir_tables(d):
                    col = d * 4 + h
                    qdT = fw.buf([128, S], BF16, "dn_qdT")
                    TT = fw.buf([128, 16, 128], BF16, "dn_TT")
                    inT = fw.buf([128, S], BF16, "dn_inT")
                    sm = fw.buf([128, 6, 16], F32, "dn_sm")
                    TRI = self.triF if d == 0 else self.triB
                    MI = self.mL4 if d == 0 else self.mU4
                    MT = self.mU4 if d == 0 else self.mL4
                    gcol = gl[:, col, :]
                    pc = P[0]
                    fw.op(fw.pe, lambda e: e.matmul(pc[:, 0:16], lhsT=TRI[:], rhs=gcol, start=True, stop=True), reads=[TRI, gl], writes=[pc], inc=False)
                    fw.op(fw.pe, lambda e: e.matmul(pc[:, 16:32], lhsT=self.onesf[:], rhs=gcol, start=True, stop=True), reads=[self.onesf, gl], writes=[pc])
                    fw.op(fw.dve, lambda e: e.tensor_copy(out=sm[:, 0, :], in_=pc[:, 0:16]), reads=[pc], writes=[sm])
                    fw.op(fw.dve, lambda e: e.tensor_scalar(out=sm[:, 1, :], in0=pc[:, 0:16], scalar1=-1.0, scalar2=None, op0=ALU.mult), reads=[pc], writes=[sm])
                    fw.op(fw.dve, lambda e: e.tensor_copy(out=sm[:, 2, :], in_=pc[:, 16:32]), reads=[pc], writes=[sm])
                    fw.op(fw.dve, lambda e: e.tensor_tensor(out=sm[:, 3, :], in0=sm[:, 2, :], in1=sm[:, 0, :], op=ALU.subtract), reads=[sm], writes=[sm])
                    fw.op(fw.act, lambda e: e.activation(out=sm[:, 3, :], in_=sm[:, 3, :], func=AF.Exp), reads=[sm], writes=[sm])
                    fw.op(fw.act, lambda e: e.activation(out=sm[:, 4, :], in_=sm[:, 2, :], func=AF.Exp), reads=[sm], writes=[sm])
                    fw.op(fw.act, lambda e: e.activation(out=sm[:, 5, :], in_=sm[:, 0, :], func=AF.Exp), reads=[sm], writes=[sm])
                    fw.op(fw.dve, lambda e: e.tensor_scalar(out=sm[:, 5, :], in0=sm[:, 5, :], scalar1=-1.0, scalar2=None, op0=ALU.mult), reads=[sm], writes=[sm])
                    mt = fw.mark()
                    def mk_tmp(tag):
                        return (fw.buf([128, 512], F32, 'dn_tI' + tag), fw.buf([128, 512], F32, 'dn_tT' + tag),
                                [fw.buf([128, 512], F32, 'dn_Pb%d%s' % (i, tag)) for i in range(2)],
                                [fw.buf([128, 512], F32, 'dn_Qb%d%s' % (i, tag)) for i in range(2)],
                                fw.buf([128, 512], F32, 'dn_Rb' + tag))
                    def tab_gen(gq, tmp, banks):
                        tI, tT, Pb, Qb, Rb1 = tmp
                        dg, eg = tT, tI
                        Rb = [Rb1, Rb1]
                        cs = slice(gq * 512, (gq + 1) * 512)
                        chs = [gq * 4 + i for i in range(4)]
                        C4 = lambda i: slice(i * 128, (i + 1) * 128)
                        pcb, pG, pKQ = banks; pP, pQ, pR = banks
                        for i, ch in enumerate(chs):
                            fw.op(fw.dve, lambda e: e.tensor_scalar(out=dg[:, C4(i)], in0=self.ident[:], scalar1=sm[:, 0, ch:ch + 1], scalar2=None, op0=ALU.mult), reads=[self.ident, sm], writes=[dg])
                            yield
                        for i, ch in enumerate(chs):
                            fw.op(fw.pe, lambda e: e.matmul(pcb[:, C4(i)], lhsT=self.onesf[:], rhs=dg[:, C4(i)], start=True, stop=True), reads=[self.onesf, dg], writes=[pcb], inc=(i == 3))
                            yield
                        fw.op(fw.act, lambda e: e.activation(out=eg[:], in_=pcb[:], func=AF.Exp), reads=[pcb], writes=[eg])
                        yield
                        fw.op(fw.dve, lambda e: e.tensor_tensor(out=qdT[:, cs], in0=qT[:, cs], in1=eg[:], op=ALU.mult), reads=[qT, eg], writes=[qdT])
                        yield
                        fw.op(fw.dve, lambda e: e.scalar_tensor_tensor(out=tI[:], in0=pcb[:], scalar=-1.0, in1=MI[:], op0=ALU.mult, op1=ALU.add), reads=[pcb, MI], writes=[tI])
                        yield
                        for i, ch in enumerate(chs):
                            fw.op(fw.act, lambda e: e.activation(out=tI[:, C4(i)], in_=tI[:, C4(i)], func=AF.Exp, bias=sm[:, 0, ch:ch + 1], scale=1.0), reads=[tI, sm], writes=[tI])
                            yield
                            fw.op(fw.dve, lambda e: e.scalar_tensor_tensor(out=tT[:, C4(i)], in0=pcb[:, C4(i)], scalar=sm[:, 1, ch:ch + 1], in1=MT[:, C4(i)], op0=ALU.add, op1=ALU.add), reads=[pcb, sm, MT], writes=[tT])
                            yield
                        fw.op(fw.act, lambda e: e.activation(out=tT[:], in_=tT[:], func=AF.Exp), reads=[tT], writes=[tT])
                        yield
                        for i, ch in enumerate(chs):
                            c128 = slice(ch * 128, (ch + 1) * 128)
                            fw.op(fw.pe, lambda e: e.matmul(pG[:, C4(i)], lhsT=kT[:, c128], rhs=kT[:, c128], start=True, stop=True), reads=[kT], writes=[pG], inc=(i == 3))
                            yield
                        for i, ch in enumerate(chs):
                            c128 = slice(ch * 128, (ch + 1) * 128)
                            fw.op(fw.pe, lambda e: e.matmul(pKQ[:, C4(i)], lhsT=kT[:, c128], rhs=qT[:, c128], start=True, stop=True), reads=[kT, qT], writes=[pKQ], inc=(i == 3))
                            yield
                        fw.op(fw.dve, lambda e: e.tensor_tensor(out=inT[:, cs], in0=pKQ[:], in1=tT[:], op=ALU.mult), reads=[pKQ, tT], writes=[inT])
                        yield
                        fw.op(fw.pool, lambda e: e.tensor_tensor(out=tI[:], in0=tI[:], in1=self.noti4[:], op=ALU.mult), reads=[tI, self.noti4], writes=[tI])
                        yield
                        Pc, Qc, Rc = Pb[0], Qb[0], Rb[0]
                        for i, ch in enumerate(chs):
                            fw.op(fw.dve, lambda e: e.scalar_tensor_tensor(out=Pc[:, C4(i)], in0=pG[:, C4(i)], scalar=nbeta[:, col, ch:ch + 1], in1=tI[:, C4(i)], op0=ALU.mult, op1=ALU.mult), reads=[pG, nbeta, tI], writes=[Pc])
                            yield
                        for i in range(4):
                            fw.op(fw.pe, lambda e: e.transpose(pQ[:, C4(i)], Pc[:, C4(i)], self.ident[:]), reads=[Pc, self.ident], writes=[pQ], inc=(i == 3))
                            yield
                        fw.op(fw.act, lambda e: e.activation(out=Qc[:], in_=pQ[:], func=AF.Copy), reads=[pQ], writes=[Qc])
                        yield
                        fw.op(fw.dve, lambda e: e.tensor_tensor(out=Rc[:], in0=Qc[:], in1=self.identf4[:], op=ALU.add), reads=[Qc, self.identf4], writes=[Rc])
                        yield
                        for k in (range(6, 7) if os.environ.get('DN_SKIPT') else range(1, 7)):
                            Pn, Qn, Rn = Pb[k % 2], Qb[k % 2], Rb[k % 2]
                            for i in range(4):
                                fw.op(fw.pe, lambda e: e.matmul(pP[:, C4(i)], lhsT=Qc[:, C4(i)], rhs=Pc[:, C4(i)], start=True, stop=True), reads=[Qc, Pc], writes=[pP], inc=(i == 3))
                                yield
                            fw.op(fw.dve, lambda e: e.tensor_copy(out=Pn[:], in_=pP[:]), reads=[pP], writes=[Pn])
                            yield
                            if k < 6:
                                for i in range(4):
                                    fw.op(fw.pe, lambda e: e.matmul(pQ[:, C4(i)], lhsT=Pc[:, C4(i)], rhs=Qc[:, C4(i)], start=True, stop=True), reads=[Qc, Pc], writes=[pQ], inc=(i == 3))
                                    yield
                                fw.op(fw.act, lambda e: e.activation(out=Qn[:], in_=pQ[:], func=AF.Copy), reads=[pQ], writes=[Qn])
                                yield
                            for i in range(4):
                                fw.op(fw.pe, lambda e: e.matmul(pR[:, C4(i)], lhsT=Pn[:, C4(i)], rhs=Rc[:, C4(i)], start=True, stop=True), reads=[Pn, Rc], writes=[pR], inc=(i == 3))
                                yield
                            fw.op(fw.dve, lambda e: e.tensor_tensor(out=Rn[:], in0=Rc[:], in1=pR[:], op=ALU.add), reads=[pR, Rc], writes=[Rn])
                            yield
                            if k == 6:
                                for i, ch in enumerate(chs):
                                    if os.environ.get("DN_NOT"):
                                        fw.op(fw.dve, lambda e: e.tensor_scalar(out=TT[:, ch, :], in0=self.ident[:], scalar1=beta[:, col, ch:ch + 1], scalar2=None, op0=ALU.mult), reads=[pR, beta], writes=[TT])
                                        yield
                                    else:
                                        fw.op(fw.pool, lambda e: e.tensor_scalar(out=TT[:, ch, :], in0=Rn[:, C4(i)], scalar1=beta[:, col, ch:ch + 1], scalar2=None, op0=ALU.mult), reads=[Rn, beta], writes=[TT])
                                        yield
                            Pc, Qc, Rc = Pn, Qn, Rn
                    tmpA, tmpB = mk_tmp('a'), mk_tmp('b')
                    if not os.environ.get('DN_SKIPTAB'):
                        for ga, gb_ in ((0, 1), (2, 3)):
                            self.interleave([tab_gen(ga, tmpA, [P[1], P[2], P[3]]), tab_gen(gb_, tmpB, [P[4], P[5], P[6]])])
                    fw.release(mt)
                    return dict(col=col, qdT=qdT, TT=TT, inT=inT, sm=sm)
            def scan_gen(d, B):
                    col, qdT, TT, inT, sm = B["col"], B["qdT"], B["TT"], B["inT"], B["sm"]
                    Sf = fw.buf([128, 128], F32, "dn_Sf")
                    v2b = [fw.buf([128, 128], BF16, "dn_v2%d" % i) for i in range(2)]
                    Sb = fw.buf([128, 128], BF16, "dn_Sb")
                    rb = [fw.buf([128, 128], BF16, "dn_r%d" % i) for i in range(2)]
                    vn = [fw.buf([128, 128], BF16, "dn_vn%d" % i) for i in range(2)]
                    fw.op(fw.dve, lambda e: e.memset(Sf[:], 0.0), writes=[Sf])
                    yield
                    fw.op(fw.dve, lambda e: e.memset(Sb[:], 0.0), writes=[Sb])
                    yield
                    order = list(range(16)) if d == 0 else list(range(15, -1, -1))
                    for si, ch in enumerate(order[:1] if os.environ.get('DN_SKIPS') else order):
                        c128 = slice(ch * 128, (ch + 1) * 128)
                        pa, po = P[4 * d + (si % 2)], P[4 * d + 2 + (si % 2)]
                        v2_ = v2b[si % 2]
                        r_, v_ = rb[si % 2], vn[si % 2]
                        fw.op(fw.pe, lambda e: e.matmul(pa[:, 0:128], lhsT=kT[:, c128], rhs=Sb[:], start=True, stop=True), reads=[kT, Sb], writes=[pa])
                        yield
                        fw.op(fw.dve, lambda e: e.scalar_tensor_tensor(out=r_[:], in0=pa[:, 0:128], scalar=sm[:, 5, ch:ch + 1], in1=vtok[:, ch, :], op0=ALU.mult, op1=ALU.add), reads=[vtok, pa, sm], writes=[r_])
                        yield
                        fw.op(fw.pe, lambda e: e.matmul(pa[:, 128:256], lhsT=TT[:, ch, :], rhs=r_[:], start=True, stop=True), reads=[TT, r_], writes=[pa])
                        yield
                        fw.op(fw.act, lambda e: e.activation(out=v_[:], in_=pa[:, 128:256], func=AF.Copy), reads=[pa], writes=[v_])
                        yield
                        fw.op(fw.act, lambda e: e.activation(out=v2_[:], in_=pa[:, 128:256], func=AF.Copy, scale=sm[:, 3, ch:ch + 1]), reads=[pa, sm], writes=[v2_])
                        yield
                        fw.op(fw.pe, lambda e: e.matmul(po[:, 0:128], lhsT=Sb[:], rhs=qdT[:, c128], start=True, stop=False), reads=[Sb, qdT], writes=[po], inc=False)
                        yield
                        fw.op(fw.pe, lambda e: e.matmul(po[:, 0:128], lhsT=v_[:], rhs=inT[:, c128], start=False, stop=True), reads=[v_, inT], writes=[po])
                        yield
                        fw.op(fw.dve, lambda e: e.tensor_tensor(out=oT[:, c128], in0=oT[:, c128], in1=po[:, 0:128], op=ALU.add), reads=[po, oT], writes=[oT])
                        yield
                        fw.op(fw.pe, lambda e: e.matmul(pa[:, 256:384], lhsT=ktok[:, ch, :], rhs=v2_[:], start=True, stop=True), reads=[ktok, v2_], writes=[pa])
                        yield
                        fw.op(fw.dve, lambda e: e.scalar_tensor_tensor(out=Sb[:], in0=Sf[:], scalar=sm[:, 4, ch:ch + 1], in1=pa[:, 256:384], op0=ALU.mult, op1=ALU.add), reads=[Sf, sm, pa], writes=[Sb])
                        yield
                        fw.op(fw.dve, lambda e: e.scalar_tensor_tensor(out=Sf[:], in0=Sf[:], scalar=sm[:, 4, ch:ch + 1], in1=pa[:, 256:384], op0=ALU.mult, op1=ALU.add), reads=[Sf, sm, pa], writes=[Sf])
                        yield
            dirs_ = [int(v) for v in os.environ.get('DN_DIRS', '0,1').split(',')]
            fw.op(fw.dve, lambda e: e.memset(oT[:], 0.0), writes=[oT])
            BB = [dir_tables(d) for d in dirs_]
            self.interleave([scan_gen(d, B) for d, B in zip(dirs_, BB)])
            self.prot = list(range(8))
            wz = fw.buf([128, 8, 128], BF16, "dn_wz")
            fw.dma(fw.pool, wz[:], wsrc[:, :, 1536 + h * 128:1536 + (h + 1) * 128], writes=[wz])
            sq = [fw.buf([128, 512], BF16, "dn_fsq%d" % i) for i in range(2)]
            rs = [fw.buf([128, 512], F32, "dn_frs%d" % i) for i in range(2)]
            sg = [fw.buf([128, 512], BF16, "dn_fsg%d" % i) for i in range(2)]
            for nt in range(NT):
                sl = slice(nt * 512, (nt + 1) * 512)
                q_, r_, g_ = sq[nt % 2], rs[nt % 2], sg[nt % 2]
                fw.op(fw.act, lambda e: e.activation(out=q_[:], in_=oT[:, sl], func=AF.Square), reads=[oT], writes=[q_])
                pn = self.nextp()
                fw.op(fw.pe, lambda e: e.matmul(pn[:], lhsT=self.onesb[:], rhs=q_[:], start=True, stop=True), reads=[q_, self.onesb], writes=[pn])
                fw.op(fw.act, lambda e: e.activation(out=r_[:], in_=pn[:], func=AF.Ln, bias=self.eps[:], scale=1.0 / 128), reads=[pn, self.eps], writes=[r_])
                fw.op(fw.act, lambda e: e.activation(out=r_[:], in_=r_[:], func=AF.Exp, scale=-0.5), reads=[r_], writes=[r_])
                pz = self.proj(wz, 0, 128, nt)
                fw.op(fw.act, lambda e: e.activation(out=g_[:], in_=pz[:], func=AF.Silu), reads=[pz], writes=[g_])
                fw.op(fw.dve, lambda e: e.scalar_tensor_tensor(out=r_[:], in0=oT[:, sl], scalar=self.dnp[:, 24:25], in1=r_[:], op0=ALU.mult, op1=ALU.mult), reads=[oT, self.dnp, r_], writes=[r_])
                fw.op(fw.dve, lambda e: e.tensor_tensor(out=self.yT_t[:, h, sl], in0=r_[:], in1=g_[:], op=ALU.mult), reads=[r_, g_], writes=[self.yT[h]])
            fw.release(mh, hard=True)
        fw.release(m)

    def interleave(self, gens):
        gens = list(gens)
        while gens:
            for g in list(gens):
                try:
                    next(g)
                except StopIteration:
                    gens.remove(g)

    def load_w(self, wap, c0, ncols, tag):
        fw = self.fw
        w = fw.buf([128, 8, ncols], BF16, tag)
        src = wap.rearrange("(kc p) n -> p kc n", p=128)
        half = ncols // 2 if ncols >= 256 else ncols
        for a in range(0, ncols, half):
            fw.dma(fw.pool, w[:, :, a:a + half], src[:, :, c0 + a:c0 + a + half], writes=[w])
        return w

    def proj(self, w, col, ncols, nt, p=None):
        fw = self.fw
        if p is None:
            p = self.nextp()
        sl = slice(nt * 512, (nt + 1) * 512)
        for kc in range(8):
            fw.op(fw.pe, lambda e: e.matmul(p[0:ncols, :], lhsT=w[:, kc, col:col + ncols], rhs=self.hT_t[:, kc, sl], start=(kc == 0), stop=(kc == 7)),
                  reads=[w, self.hT[kc][nt]], writes=[p], inc=(kc == 7))
        return p

    def out_proj(self, wap):
        fw = self.fw
        m = fw.mark()
        self.prot = list(range(8))
        wo = self.load_w(wap, 0, D, "wo")
        for nt in range(NT):
            for c in range(8):
                sl = slice(nt * 512, (nt + 1) * 512)
                po = self.nextp()
                for kc in range(8):
                    fw.op(fw.pe, lambda e: e.matmul(po[:], lhsT=wo[:, kc, c * 128:(c + 1) * 128], rhs=self.yT_t[:, kc, sl], start=(kc == 0), stop=(kc == 7)),
                          reads=[wo, self.yT[kc]], writes=[po], inc=(kc == 7))
                fw.op(fw.dve, lambda e: e.tensor_tensor(out=self.xT_t[:, c, sl], in0=self.xT_t[:, c, sl], in1=po[:], op=ALU.add),
                      reads=[po, self.xT[c][nt]], writes=[self.xT[c][nt]])
        fw.release(m)

    def mixer_cd(self):
        fw = self.fw
        m0 = fw.mark()
        self.dlt = fw.buf([128, 512], F32, "dlt")
        fw.dma(fw.sp, self.dlt[:], self.dlt_d[:, :], writes=[self.dlt])
        self.cnyrow = fw.buf([1, S], BF16, "cnyrow")
        fw.dma(fw.sp, self.cnyrow[:], self.cnyrow_d[:, :], writes=[self.cnyrow])
        if "ret" in self.stages:
            self.retention()
        else:
            for c in range(4):
                fw.op(fw.dve, lambda e: e.memset(self.yT_t[:, c, :], 0.0), writes=[self.yT[c]])
        if "hy" in self.stages:
            self.hyena()
        else:
            for c in range(4, 8):
                fw.op(fw.dve, lambda e: e.memset(self.yT_t[:, c, :], 0.0), writes=[self.yT[c]])
        fw.release(m0)
        self.out_proj(self.cd_w_out)

    def retention(self):
        fw = self.fw
        W = self.cd_w_in
        m = fw.mark()
        self.prot = list(range(8))
        vtok = fw.buf([128, 16, 512], BF16, "vtok")
        qr = [fw.buf([128, S], BF16, "qr%d" % i) for i in range(2)]
        kr = [fw.buf([128, S], BF16, "kr%d" % i) for i in range(2)]
        m2 = fw.mark()
        wv = self.load_w(W, 512, 512, "wv")
        for tt in range(16):
            p = self.nextp()
            for kc in range(8):
                fw.op(fw.pe, lambda e: e.matmul(p[:], lhsT=self.hT_t[:, kc, tt * 128:(tt + 1) * 128], rhs=wv[:, kc, :], start=(kc == 0), stop=(kc == 7)),
                      reads=[wv, self.hT[kc][tt // 4]], writes=[p], inc=(kc == 7))
            fw.op(fw.act, lambda e: e.activation(out=vtok[:, tt, :], in_=p[:], func=AF.Copy), reads=[p], writes=[vtok])
        fw.release(m2)
        m2 = fw.mark()
        wqk = self.load_w(W, 0, 512, "wqk")
        qsb = [fw.buf([128, 512], BF16, "rqs%d" % i) for i in range(2)]
        t1 = [fw.buf([128, 512], F32, "rt1_%d" % i) for i in range(2)]
        t2 = [fw.buf([128, 512], F32, "rt2_%d" % i) for i in range(2)]
        k = 0
        for which, dst in ((0, qr), (1, kr)):
            for qc in range(2):
                col = which * 256 + qc * 128
                for nt in range(NT):
                    sl = slice(nt * 512, (nt + 1) * 512)
                    p1 = self.proj(wqk, col, 128, nt)
                    qs_ = qsb[k % 2]
                    fw.op(fw.dve, lambda e: e.tensor_copy(out=qs_[:], in_=p1[:]), reads=[p1], writes=[qs_])
                    p2 = self.nextp()
                    fw.op(fw.pe, lambda e: e.matmul(p2[:], lhsT=self.rperm[:], rhs=qs_[:], start=True, stop=True), reads=[self.rperm, qs_], writes=[p2])
                    a, b = t1[k % 2], t2[k % 2]
                    k += 1
                    fw.op(fw.dve, lambda e: e.tensor_tensor(out=a[:], in0=p1[:], in1=self.ropec[:, sl], op=ALU.mult), reads=[p1, self.ropec], writes=[a])
                    fw.op(fw.dve, lambda e: e.tensor_tensor(out=b[:], in0=p2[:], in1=self.ropes[:, sl], op=ALU.mult), reads=[p2, self.ropes], writes=[b])
                    fw.op(fw.pool, lambda e: e.tensor_tensor(out=dst[qc][:, sl], in0=a[:], in1=b[:], op=ALU.add), reads=[a, b], writes=[dst[qc]])
        fw.release(m2)
        ld = self.ld
        Lf = fw.buf([128, 512], BF16, "Lf")
        Lb = fw.buf([128, 512], BF16, "Lb")
        DC = [fw.buf([128, 512], BF16, "DC%d" % r) for r in range(4)]
        fac = fw.buf([128, 32], F32, "fac")
        scb = [fw.buf([128, 512], BF16, "scb%d" % i) for i in range(6)]
        oT = fw.buf([128, S], F32, "oT")
        sqb = [fw.buf([128, 512], BF16, "rsq%d" % i) for i in range(2)]
        rsb = [fw.buf([128, 512], F32, "rrs%d" % i) for i in range(2)]
        ea, eb = rsb[0], rsb[1]
        sgt = [fw.buf([128, 512], BF16, "rsg%d" % i) for i in range(2)]
        wg = fw.buf([128, 8, 128], BF16, "wg")
        wsrc = W.rearrange("(kc p) n -> p kc n", p=128)
        ks = 0
        for h in range(4):
            qc, po = h // 2, (h % 2) * 64
            lgf, lgb = ld[:, h:h + 1], ld[:, 4 + h:5 + h]
            fw.op(fw.act, lambda e: e.activation(out=Lf[:], in_=self.dlt[:], func=AF.Exp, bias=self.ld128[:, h:h + 1], scale=lgf), reads=[self.dlt, self.ld128, ld], writes=[Lf])
            fw.op(fw.act, lambda e: e.activation(out=Lb[:], in_=self.dlt[:], func=AF.Exp, bias=self.ld512[:, 4 + h:5 + h], scale=self.ldn[:, 4 + h:5 + h]), reads=[self.dlt, self.ld512, self.ldn], writes=[Lb])
            fw.op(fw.act, lambda e: e.activation(out=fac[:, 0:16], in_=self.iota16[:], func=AF.Exp, bias=self.ln8[:], scale=lgf), reads=[self.iota16, self.ln8, ld], writes=[fac])
            fw.op(fw.act, lambda e: e.activation(out=fac[:, 16:32], in_=self.iota16[:], func=AF.Exp, bias=self.ln8[:], scale=lgb), reads=[self.iota16, self.ln8, ld], writes=[fac])
            for r in range(4):
                fw.op(fw.dve, lambda e: e.tensor_scalar(out=ea[:], in0=self.dlt[:], scalar1=float(-128 * r), scalar2=0.0, op0=ALU.add, op1=ALU.max), reads=[self.dlt], writes=[ea])
                fw.op(fw.dve, lambda e: e.tensor_scalar(out=ea[:], in0=ea[:], scalar1=lgf, scalar2=None, op0=ALU.mult), reads=[ea, ld], writes=[ea])
                fw.op(fw.dve, lambda e: e.tensor_scalar(out=eb[:], in0=self.dlt[:], scalar1=float(-128 * r), scalar2=0.0, op0=ALU.add, op1=ALU.min), reads=[self.dlt], writes=[eb])
                fw.op(fw.dve, lambda e: e.scalar_tensor_tensor(out=eb[:], in0=eb[:], scalar=self.ldn[:, 4 + h:5 + h], in1=ea[:], op0=ALU.mult, op1=ALU.add), reads=[eb, ea, self.ldn], writes=[eb])
                fw.op(fw.act, lambda e: e.activation(out=DC[r][:], in_=eb[:], func=AF.Exp, bias=self.ln8[:], scale=1.0), reads=[eb, self.ln8], writes=[DC[r]])
            for nj in range(NT):
                pacc = self.P[nj % 2]
                self.prot = list(range(2, 8))
                pend = []

                def pv(sc, mi):
                    fw.op(fw.pe, lambda e: e.matmul(pacc[:], lhsT=vtok[:, mi, h * 128:(h + 1) * 128], rhs=sc[:], start=(mi == 0), stop=(mi == 15)),
                          reads=[vtok, sc], writes=[pacc], inc=True)
                for mi in range(16):
                    ps = self.nextp()
                    fw.op(fw.pe, lambda e: e.matmul(ps[:], lhsT=kr[qc][po:po + 64, mi * 128:(mi + 1) * 128], rhs=qr[qc][po:po + 64, nj * 512:(nj + 1) * 512], start=True, stop=True),
                          reads=[kr[qc], qr[qc]], writes=[ps], inc=True)
                    sc = scb[ks % 6]
                    ks += 1
                    r = mi - 4 * nj
                    if 0 <= r <= 3:
                        fw.op(fw.dve, lambda e: e.tensor_tensor(out=sc[:], in0=ps[:], in1=DC[r][:], op=ALU.mult), reads=[ps, DC[r]], writes=[sc])
                    elif r < 0:
                        kk = -r - 1
                        fw.op(fw.dve, lambda e: e.scalar_tensor_tensor(out=sc[:], in0=ps[:], scalar=fac[:, kk:kk + 1], in1=Lf[:], op0=ALU.mult, op1=ALU.mult), reads=[ps, fac, Lf], writes=[sc])
                    else:
                        kk = r - 4
                        fw.op(fw.dve, lambda e: e.scalar_tensor_tensor(out=sc[:], in0=ps[:], scalar=fac[:, 16 + kk:17 + kk], in1=Lb[:], op0=ALU.mult, op1=ALU.mult), reads=[ps, fac, Lb], writes=[sc])
                    pend.append((sc, mi))
                    if len(pend) > 4:
                        pv(*pend.pop(0))
                while pend:
                    pv(*pend.pop(0))
                fw.op(fw.act, lambda e: e.activation(out=oT[:, nj * 512:(nj + 1) * 512], in_=pacc[:], func=AF.Copy), reads=[pacc], writes=[oT])
            self.prot = list(range(2, 8))
            fw.dma(fw.pool, wg[:], wsrc[:, :, 1024 + h * 128:1024 + (h + 1) * 128], writes=[wg])
            for nt in range(NT):
                sl = slice(nt * 512, (nt + 1) * 512)
                sq, rs, sg = sqb[nt % 2], rsb[nt % 2], sgt[nt % 2]
                tb = rs
                fw.op(fw.act, lambda e: e.activation(out=sq[:], in_=oT[:, sl], func=AF.Square), reads=[oT], writes=[sq])
                pn = self.nextp()
                fw.op(fw.pe, lambda e: e.matmul(pn[:], lhsT=self.onesb[:], rhs=sq[:], start=True, stop=True), reads=[sq, self.onesb], writes=[pn])
                fw.op(fw.act, lambda e: e.activation(out=rs[:], in_=pn[:], func=AF.Ln, bias=self.eps[:], scale=1.0 / 128), reads=[pn, self.eps], writes=[rs])
                fw.op(fw.act, lambda e: e.activation(out=rs[:], in_=rs[:], func=AF.Exp, scale=-0.5), reads=[rs], writes=[rs])
                pg = self.proj(wg, 0, 128, nt)
                fw.op(fw.act, lambda e: e.activation(out=sg[:], in_=pg[:], func=AF.Silu), reads=[pg], writes=[sg])
                fw.op(fw.dve, lambda e: e.tensor_tensor(out=tb[:], in0=oT[:, sl], in1=rs[:], op=ALU.mult), reads=[oT, rs], writes=[tb])
                fw.op(fw.dve, lambda e: e.tensor_tensor(out=self.yT_t[:, h, sl], in0=tb[:], in1=sg[:], op=ALU.mult), reads=[tb, sg], writes=[self.yT[h]])
        self.prot = list(range(8))
        fw.release(m)


    def sin_rr(self, dst, src_ps, b_ap, f_ap, rows, tmpf, tmpi):
        fw = self.fw
        R = slice(0, rows)
        fw.op(fw.dve, lambda e: e.tensor_scalar(out=tmpf[0][R, :], in0=src_ps[R, :], scalar1=b_ap, scalar2=f_ap, op0=ALU.add, op1=ALU.mult),
              reads=[src_ps, self.hyp], writes=[tmpf[0]])
        fw.op(fw.dve, lambda e: e.tensor_scalar(out=tmpf[1][R, :], in0=tmpf[0][R, :], scalar1=1.0 / (2 * math.pi), scalar2=None, op0=ALU.mult),
              reads=[tmpf[0]], writes=[tmpf[1]])
        fw.op(fw.dve, lambda e: e.tensor_copy(out=tmpi[R, :], in_=tmpf[1][R, :]), reads=[tmpf[1]], writes=[tmpi])
        fw.op(fw.dve, lambda e: e.tensor_copy(out=tmpf[1][R, :], in_=tmpi[R, :]), reads=[tmpi], writes=[tmpf[1]])
        fw.op(fw.dve, lambda e: e.scalar_tensor_tensor(out=tmpf[0][R, :], in0=tmpf[1][R, :], scalar=-2 * math.pi, in1=tmpf[0][R, :], op0=ALU.mult, op1=ALU.add),
              reads=[tmpf[0], tmpf[1]], writes=[tmpf[0]])
        fw.op(fw.dve, lambda e: e.tensor_scalar(out=tmpf[0][R, :], in0=tmpf[0][R, :], scalar1=3.14159, scalar2=-3.14159, op0=ALU.min, op1=ALU.max),
              reads=[tmpf[0]], writes=[tmpf[0]])
        fw.op(fw.act, lambda e: e.activation(out=dst, in_=tmpf[0][R, :], func=AF.Sin), reads=[tmpf[0]], writes=[self.hidb])

    def fwd_dft(self, inC, inS, inCb, inSb, consume):
        fw = self.fw
        tabC = [fw.buf([128, 8, 128], BF16, "ftabC%d" % i) for i in range(2)]
        tabS = [fw.buf([128, 8, 128], BF16, "ftabS%d" % i) for i in range(2)]
        self.prot = [4, 5, 6, 7]
        for ft in range(16):
            csrc = self.dftcf[ft].rearrange("p (tt f) -> p tt f", tt=16)
            ssrc = self.dftsf[ft].rearrange("p (tt f) -> p tt f", tt=16)
            for hf in range(2):
                fw.dma(fw.sp, tabC[hf][:], csrc[:, hf * 8:hf * 8 + 8, :], writes=[tabC[hf]])
            pr = self.nextp()
            for tt in range(16):
                tb_ = tabC[tt // 8]
                fw.op(fw.pe, lambda e: e.matmul(pr[:], lhsT=tb_[:, tt % 8, :], rhs=inC[:, tt, :], start=(tt == 0), stop=(tt == 15)),
                      reads=[tb_, inCb], writes=[pr], inc=(tt % 8 == 7))
            for hf in range(2):
                fw.dma(fw.act, tabS[hf][:], ssrc[:, hf * 8:hf * 8 + 8, :], writes=[tabS[hf]])
            pi = self.nextp()
            for tt in range(16):
                tb_ = tabS[tt // 8]
                fw.op(fw.pe, lambda e: e.matmul(pi[:], lhsT=tb_[:, tt % 8, :], rhs=inS[:, tt, :], start=(tt == 0), stop=(tt == 15)),
                      reads=[tb_, inSb], writes=[pi], inc=(tt % 8 == 7))
            consume(ft, pr, pi)
        pr = self.nextp()
        for tt in range(16):
            fw.op(fw.pe, lambda e: e.matmul(pr[0:1, :], lhsT=self.cnycol[:, tt:tt + 1], rhs=inC[:, tt, :], start=(tt == 0), stop=(tt == 15)),
                  reads=[self.cnycol, inCb], writes=[pr], inc=(tt == 15))
        consume(16, pr, None)

    def hyena_setup(self):
        fw = self.fw
        m = fw.mark()
        self.prot = list(range(4))
        self.hidb = fw.buf([64, 2, S], F32, "hid")
        w3 = fw.buf([64, 1024], F32, "hw3")
        fw.dma(fw.sp, w3[:], self.hy_w3[:, :], writes=[w3])
        ma = fw.mark()
        zT = fw.buf([33, S], F32, "zT")
        fw.dma(fw.sp, zT[:], self.hyz[:, :], writes=[zT])
        w1 = fw.buf([33, 64], F32, "hw1")
        fw.dma(fw.sp, w1[:], self.hy_w1[:, :], writes=[w1])
        w2 = fw.buf([64, 64], F32, "hw2")
        fw.dma(fw.sp, w2[:], self.hy_w2[:, :], writes=[w2])
        hid = self.hidb
        tmpf = [fw.buf([64, 512], F32, "stf%d" % i) for i in range(2)]
        tmpi = fw.buf([64, 512], I32, "sti")
        hp = self.hyp
        for lyr in range(2):
            for nt in range(NT):
                sl = slice(nt * 512, (nt + 1) * 512)
                p = self.nextp()
                if lyr == 0:
                    fw.op(fw.pe, lambda e: e.matmul(p[0:64, :], lhsT=w1[0:33, :], rhs=zT[0:33, sl], start=True, stop=True), reads=[w1, zT], writes=[p])
                else:
                    fw.op(fw.pe, lambda e: e.matmul(p[0:64, :], lhsT=w2[0:64, :], rhs=hid[0:64, 0, sl], start=True, stop=True), reads=[w2, hid], writes=[p])
                self.sin_rr(hid[0:64, lyr, sl], p, hp[0:64, 2 * lyr:2 * lyr + 1], hp[0:64, 2 * lyr + 1:2 * lyr + 2], 64, tmpf, tmpi)
        fw.release(ma)
        hs = self.hT_t[:, 0:4, :].rearrange("p c (a b) -> p (c a) b", b=512)
        hd = self.hT_t[:, 4:8, :].rearrange("p c (a b) -> p (c a) b", b=512)
        hsb = fw.view(self.hT_t, "hsb", alias=[self.hT[c][n] for c in range(8) for n in range(NT)])
        dec = [fw.buf([128, 512], F32, "dec%d" % i) for i in range(2)]
        hfb = [fw.buf([128, 512], F32, "hfb%d" % i) for i in range(2)]
        hbb = [fw.buf([128, 512], F32, "hbb%d" % i) for i in range(2)]
        for tt in range(16):
            d_, hf, hb = dec[tt % 2], hfb[tt % 2], hbb[tt % 2]
            fw.dma(fw.sp, d_[:], self.hydec[tt * 128:(tt + 1) * 128, :], writes=[d_])
            pf = self.nextp()
            fw.op(fw.pe, lambda e: e.matmul(pf[:], lhsT=hid[0:64, 1, tt * 128:(tt + 1) * 128], rhs=w3[0:64, 0:512], start=True, stop=True), reads=[hid, w3], writes=[pf])
            pb = self.nextp()
            fw.op(fw.pe, lambda e: e.matmul(pb[:], lhsT=hid[0:64, 1, tt * 128:(tt + 1) * 128], rhs=w3[0:64, 512:1024], start=True, stop=True), reads=[hid, w3], writes=[pb])
            fw.op(fw.dve, lambda e: e.tensor_tensor(out=hf[:], in0=pf[:], in1=d_[:], op=ALU.mult), reads=[pf, d_], writes=[hf])
            fw.op(fw.dve, lambda e: e.tensor_tensor(out=hb[:], in0=pb[:], in1=d_[:], op=ALU.mult), reads=[pb, d_], writes=[hb])
            fw.op(fw.dve, lambda e: e.tensor_tensor(out=hs[:, tt, :], in0=hf[:], in1=hb[:], op=ALU.add), reads=[hf, hb], writes=[hsb])
            fw.op(fw.dve, lambda e: e.tensor_tensor(out=hd[:, tt, :], in0=hf[:], in1=hb[:], op=ALU.subtract), reads=[hf, hb], writes=[hsb])
        fw.release(m)
        m = fw.mark()
        so = [fw.buf([128, 2, 512], F32, "so%d" % i) for i in range(2)]

        def consume(ft, pr, pi):
            o = so[ft % 2]
            if ft < 16:
                fw.op(fw.dve, lambda e: e.tensor_scalar(out=o[:, 0, :], in0=pr[:], scalar1=self.wf[:, ft:ft + 1], scalar2=None, op0=ALU.mult), reads=[pr, self.wf], writes=[o])
                fw.op(fw.dve, lambda e: e.tensor_scalar(out=o[:, 1, :], in0=pi[:], scalar1=self.wf[:, ft:ft + 1], scalar2=None, op0=ALU.mult), reads=[pi, self.wf], writes=[o])
                fw.dma(fw.sp, self.spec_d[:, ft * 128:(ft + 1) * 128, :].rearrange("a p c -> p a c"), o[:], reads=[o])
            else:
                fw.op(fw.dve, lambda e: e.tensor_scalar(out=o[0:1, 0, :], in0=pr[0:1, :], scalar1=self.wf[0:1, 16:17], scalar2=None, op0=ALU.mult), reads=[pr, self.wf], writes=[o])
                fw.dma(fw.sp, self.spec_d[0:1, 2048:2049, :].rearrange("a p c -> p a c"), o[0:1, 0:1, :], reads=[o])
        self.fwd_dft(hs, hd, hsb, hsb, consume)
        self.prot = list(range(8))
        fw.release(m, hard=True)

    def hyena(self):
        fw = self.fw
        W = self.cd_w_in
        wsrc = W.rearrange("(kc p) n -> p kc n", p=128)
        m = fw.mark()
        x0T = fw.buf([128, 4, S], BF16, "x0T")
        utok = fw.buf([128, 16, 512], BF16, "utok")
        m2 = fw.mark()
        self.prot = list(range(8))
        gb = [fw.buf([128, S + 2], F32, "hgb%d" % i) for i in range(1)]
        for g in gb:
            fw.op(fw.dve, lambda e: e.memset(g[:, 0:1], 0.0), writes=[g])
            fw.op(fw.dve, lambda e: e.memset(g[:, S + 1:S + 2], 0.0), writes=[g])
        a1 = fw.buf([128, S], F32, "ha1")
        a2 = fw.buf([128, S], F32, "ha2")
        wp = [fw.buf([128, 8, 128], BF16, "hwp%d" % i) for i in range(2)]
        hcw = self.hcw
        k = 0
        for cc in range(4):
            for part in (1, 2, 0):
                ch = part * 4 + cc
                w = wp[k % 2]
                g = gb[0]
                k += 1
                fw.dma(fw.pool, w[:], wsrc[:, :, 1536 + ch * 128:1536 + (ch + 1) * 128], writes=[w])
                for nt in range(NT):
                    p = self.proj(w, 0, 128, nt)
                    fw.op(fw.act, lambda e: e.activation(out=g[:, 1 + nt * 512:1 + (nt + 1) * 512], in_=p[:], func=AF.Copy), reads=[p], writes=[g])
                dst = a1 if part == 1 else a2
                fw.op(fw.dve, lambda e: e.tensor_scalar(out=dst[:], in0=g[:, 0:S], scalar1=hcw[:, ch, 0:1], scalar2=hcw[:, ch, 3:4], op0=ALU.mult, op1=ALU.add), reads=[g, hcw], writes=[dst])
                fw.op(fw.dve, lambda e: e.scalar_tensor_tensor(out=dst[:], in0=g[:, 1:S + 1], scalar=hcw[:, ch, 1:2], in1=dst[:], op0=ALU.mult, op1=ALU.add), reads=[g, hcw, dst], writes=[dst])
                if part == 1:
                    fw.op(fw.dve, lambda e: e.scalar_tensor_tensor(out=dst[:], in0=g[:, 2:S + 2], scalar=hcw[:, ch, 2:3], in1=dst[:], op0=ALU.mult, op1=ALU.add), reads=[g, hcw, dst], writes=[dst])
                elif part == 2:
                    fw.op(fw.dve, lambda e: e.scalar_tensor_tensor(out=dst[:], in0=g[:, 2:S + 2], scalar=hcw[:, ch, 2:3], in1=dst[:], op0=ALU.mult, op1=ALU.add), reads=[g, hcw, dst], writes=[dst])
                    fw.op(fw.dve, lambda e: e.tensor_tensor(out=self.yT_t[:, 4 + cc, :], in0=a1[:], in1=a2[:], op=ALU.mult), reads=[a1, a2], writes=[self.yT[4 + cc]])
                    for tq in range(4):
                        p = self.nextp()
                        pb = p[:].bitcast(BF16)
                        for t4 in range(4):
                            tt = tq * 4 + t4
                            fw.op(fw.pe, lambda e: e.transpose(pb[:, t4 * 128:(t4 + 1) * 128], self.yT_t[:, 4 + cc, tt * 128:(tt + 1) * 128], self.identb[:]),
                                  reads=[self.yT[4 + cc], self.identb], writes=[p], inc=(t4 == 3))
                        fw.op(fw.act, lambda e: e.activation(out=utok[:, tq * 4:tq * 4 + 4, cc * 128:(cc + 1) * 128], in_=pb[:, 0:512].rearrange("p (a b) -> p a b", a=4), func=AF.Copy),
                              reads=[p], writes=[utok])
                else:
                    fw.op(fw.dve, lambda e: e.scalar_tensor_tensor(out=x0T[:, cc, :], in0=g[:, 2:S + 2], scalar=hcw[:, ch, 2:3], in1=dst[:], op0=ALU.mult, op1=ALU.add), reads=[g, hcw, dst], writes=[x0T])
        fw.release(m2)
        Yr_t = self.hT_t[:, 0:4, :].rearrange("p c (a b) -> p (c a) b", b=512)
        Yi_t = self.hT_t[:, 4:8, :].rearrange("p c (a b) -> p (c a) b", b=512)
        Y = fw.view(self.hT_t, "Yspec", alias=[self.hT[c][n] for c in range(8) for n in range(NT)])
        Yny = fw.buf([1, 512], BF16, "Yny")
        m3 = fw.mark()
        spt = [fw.buf([128, 2, 512], F32, "spt%d" % i) for i in range(2)]
        tA = [fw.buf([128, 512], F32, "tA%d" % i) for i in range(2)]
        tB = [fw.buf([128, 512], F32, "tB%d" % i) for i in range(2)]

        def consume(ft, pr, pi):
            sp_ = spt[ft % 2]
            A, B = tA[ft % 2], tB[ft % 2]
            if ft < 16:
                fw.dma(fw.sp, sp_[:], self.spec_d[:, ft * 128:(ft + 1) * 128, :].rearrange("a p c -> p a c"), writes=[sp_])
                fw.op(fw.dve, lambda e: e.tensor_tensor(out=A[:], in0=pr[:], in1=sp_[:, 0, :], op=ALU.mult), reads=[pr, sp_], writes=[A])
                fw.op(fw.dve, lambda e: e.tensor_tensor(out=B[:], in0=pi[:], in1=sp_[:, 1, :], op=ALU.mult), reads=[pi, sp_], writes=[B])
                fw.op(fw.pool, lambda e: e.tensor_tensor(out=Yr_t[:, ft, :], in0=A[:], in1=B[:], op=ALU.subtract), reads=[A, B], writes=[Y])
                A2, B2 = tA[(ft + 1) % 2], tB[(ft + 1) % 2]
                fw.op(fw.dve, lambda e: e.tensor_tensor(out=A2[:], in0=pr[:], in1=sp_[:, 1, :], op=ALU.mult), reads=[pr, sp_], writes=[A2])
                fw.op(fw.dve, lambda e: e.tensor_tensor(out=B2[:], in0=pi[:], in1=sp_[:, 0, :], op=ALU.mult), reads=[pi, sp_], writes=[B2])
                fw.op(fw.pool, lambda e: e.tensor_tensor(out=Yi_t[:, ft, :], in0=A2[:], in1=B2[:], op=ALU.add), reads=[A2, B2], writes=[Y])
            else:
                fw.dma(fw.sp, sp_[0:1, 0:1, :], self.spec_d[0:1, 2048:2049, :].rearrange("a p c -> p a c"), writes=[sp_])
                fw.op(fw.dve, lambda e: e.tensor_tensor(out=Yny[0:1, :], in0=pr[0:1, :], in1=sp_[0:1, 0, :], op=ALU.mult), reads=[pr, sp_], writes=[Yny])
        self.fwd_dft(utok[:, :, :], utok[:, :, :], utok, utok, consume)
        fw.release(m3)
        itC = [fw.buf([128, 4, 512], BF16, "itC%d" % i) for i in range(2)]
        itS = [fw.buf([128, 4, 512], BF16, "itS%d" % i) for i in range(2)]
        csrc = self.dftc.rearrange("(ft p) t -> p ft t", p=128)
        ssrc = self.dfts.rearrange("(ft p) t -> p ft t", p=128)
        ep = [fw.buf([128, 512], F32, "hep%d" % i) for i in range(2)]
        kk = 0
        ke = 0
        for nt in range(NT):
            sl = slice(nt * 512, (nt + 1) * 512)
            banks = [self.P[(nt % 2) * 4 + cc] for cc in range(4)]
            for fg in range(4):
                tc_, ts_ = itC[kk % 2], itS[kk % 2]
                kk += 1
                fw.dma(fw.sp, tc_[:], csrc[:, fg * 4:fg * 4 + 4, sl], writes=[tc_])
                fw.dma(fw.act, ts_[:], ssrc[:, fg * 4:fg * 4 + 4, sl], writes=[ts_])
                for f4 in range(4):
                    ft = fg * 4 + f4
                    for cc in range(4):
                        fw.op(fw.pe, lambda e: e.matmul(banks[cc][:], lhsT=Yr_t[:, ft, cc * 128:(cc + 1) * 128], rhs=tc_[:, f4, :], start=(ft == 0), stop=False),
                              reads=[Y, tc_], writes=[banks[cc]], inc=False)
                        fw.op(fw.pe, lambda e: e.matmul(banks[cc][:], lhsT=Yi_t[:, ft, cc * 128:(cc + 1) * 128], rhs=ts_[:, f4, :], start=False, stop=False),
                              reads=[Y, ts_], writes=[banks[cc]], inc=(cc == 3 and f4 == 3))
            for cc in range(4):
                fw.op(fw.pe, lambda e: e.matmul(banks[cc][:], lhsT=Yny[0:1, cc * 128:(cc + 1) * 128], rhs=self.cnyrow[0:1, sl], start=False, stop=True),
                      reads=[Yny, self.cnyrow], writes=[banks[cc]], inc=True)
                t_ = ep[ke % 2]
                ke += 1
                fw.op(fw.dve, lambda e: e.scalar_tensor_tensor(out=t_[:], in0=self.yT_t[:, 4 + cc, sl], scalar=self.hybias[:, cc:cc + 1], in1=banks[cc][:], op0=ALU.mult, op1=ALU.add),
                      reads=[self.yT[4 + cc], self.hybias, banks[cc]], writes=[t_])
                fw.op(fw.dve, lambda e: e.tensor_tensor(out=self.yT_t[:, 4 + cc, sl], in0=t_[:], in1=x0T[:, cc, sl], op=ALU.mult), reads=[t_, x0T], writes=[self.yT[4 + cc]])
        self.prot = list(range(8))
        fw.release(m, hard=True)


def host_consts():
    c = {"ident": np.eye(128, dtype=np.float32)}
    inv = 10000.0 ** (-np.arange(0, 64, 2, dtype=np.float64) / 64)
    ang = np.arange(S, dtype=np.float64)[None, :] * inv[:, None]
    cos64 = np.concatenate([np.cos(ang), np.cos(ang)], 0)
    sin64 = np.concatenate([-np.sin(ang), np.sin(ang)], 0)
    c["ropec"] = np.concatenate([cos64, cos64], 0).astype(ml_dtypes.bfloat16)
    c["ropes"] = np.concatenate([sin64, sin64], 0).astype(ml_dtypes.bfloat16)
    pm = np.zeros((128, 128), np.float32)
    pm[rot_perm(128, 64), np.arange(128)] = 1.0
    c["rperm"] = pm.astype(ml_dtypes.bfloat16)
    c["dlt"] = (np.arange(512, dtype=np.float32)[None, :] - np.arange(128, dtype=np.float32)[:, None]).astype(np.float32)
    c["iota16"] = np.tile(128.0 * np.arange(16, dtype=np.float32)[None, :], (128, 1)).astype(np.float32)
    dl = np.arange(2944)[None, :] - np.arange(128)[:, None] - 1408
    ad = np.abs(dl)
    mult = (ad <= 64).astype(np.float32) + ((dl % 4 == 0) & (ad <= 256)) + ((dl % 16 == 0) & (ad <= 1024))
    c["dilmask"] = mult.astype(ml_dtypes.bfloat16)
    ii = np.arange(128)
    triF = (ii[:, None] <= ii[None, :]).astype(np.float32)
    triB = (ii[:, None] >= ii[None, :]).astype(np.float32)
    c["dn_tri"] = np.ascontiguousarray(np.stack([triF, triB], 1))
    mL = np.where(ii[None, :] <= ii[:, None], 0.0, -30000.0).astype(np.float32)
    mU = np.where(ii[None, :] >= ii[:, None], 0.0, -30000.0).astype(np.float32)
    noti = (1.0 - np.eye(128)).astype(np.float32)
    c["dn_msk"] = np.ascontiguousarray(np.stack([np.tile(mL, (1, 4)), np.tile(mU, (1, 4)), np.tile(noti, (1, 4))], 1)).astype(ml_dtypes.bfloat16)
    L = S
    t = np.linspace(0.0, 1.0, L, dtype=np.float32)[:, None]
    w = (2.0 * math.pi * np.arange(L, dtype=np.float32) / L).astype(np.float32)
    fb = np.linspace(1e-4, 15, 16, dtype=np.float32)
    angz = w[:, None] * fb[None, :]
    z = np.concatenate([t, np.cos(angz), -np.sin(angz)], -1).astype(np.float32)
    c["hyz"] = np.ascontiguousarray(z.T)
    deltas = np.abs(np.linspace(math.log(1e-2) / 1.5, math.log(1e-2) / 0.3, 512, dtype=np.float32))
    c["hydec"] = np.exp(-t * deltas[None, :]).astype(np.float32)
    ab = (np.arange(S, dtype=np.int64)[:, None] * np.arange(S, dtype=np.int64)[None, :]) % (2 * S)
    th = ab.astype(np.float64) * (2.0 * math.pi / (2 * S))
    c["dftc"] = np.cos(th).astype(ml_dtypes.bfloat16)
    c["dfts"] = np.sin(th).astype(ml_dtypes.bfloat16)
    c["dftcf"] = np.ascontiguousarray(c["dftc"].reshape(16, 128, 16, 128).transpose(2, 1, 0, 3).reshape(16, 128, 2048))
    c["dftsf"] = np.ascontiguousarray(c["dfts"].reshape(16, 128, 16, 128).transpose(2, 1, 0, 3).reshape(16, 128, 2048))
    sign = (1.0 - 2.0 * (np.arange(S) % 2)).astype(np.float32)
    c["cnycol"] = np.ascontiguousarray(sign.reshape(16, 128).T).astype(ml_dtypes.bfloat16)
    c["cnyrow"] = sign.reshape(1, S).astype(ml_dtypes.bfloat16)
    wf = np.full((128, 17), 2.0 / (2 * S), dtype=np.float32)
    wf[0, 0] = 1.0 / (2 * S)
    wf[:, 16] = 1.0 / (2 * S)
    c["wf"] = wf
    return c


def rot_perm(ncols, hd):
    idx = np.arange(ncols)
    h, i = idx // hd, idx % hd
    return h * hd + (i + hd // 2) % hd


def host_layout(inputs):
    f = lambda a: np.ascontiguousarray(a, dtype=np.float32)
    o = {}
    for k in ("norm_mix", "norm_ffn", "final_norm", "ffn_conv_w", "ffn_conv_b", "ffn_w_out"):
        o[k] = f(inputs[k])
    wi = np.asarray(inputs["ffn_w_in"], dtype=np.float32).reshape(2, 8, 128, 2, 22, 128)
    o["ffn_w_in_t"] = f(wi.transpose(0, 4, 2, 1, 3, 5).reshape(2, 22, 128, 2048))
    ab = f(inputs["ab_w_in"][0])
    o["ab_w_in"] = ab
    o["ab_w_rot"] = f(ab[:, 2064:2064 + 1024][:, rot_perm(1024, 64)])
    o["ab_w_out"] = f(inputs["ab_w_out"][0])
    dnp = np.zeros((128, 25), np.float32)
    dnp[:, 0:8] = inputs["dn_a_log"][0].reshape(1, 8)
    dnp[:, 8:16] = inputs["dn_dt_bias"][0].reshape(1, 8)
    dnp[:, 24] = inputs["dn_norm_w"][0]
    o["dnp"] = dnp
    o["dcw"] = f(inputs["dn_conv_w"][0].reshape(3, 12, 128).transpose(2, 1, 0))
    cd = f(inputs["cd_w_in"][0])
    o["cd_w_in"] = cd
    o["cd_w_rot"] = f(cd[:, 0:512][:, rot_perm(512, 64)])
    o["cd_w_out"] = f(inputs["cd_w_out"][0])
    o["ret_log_decay"] = f(inputs["ret_log_decay"][0].reshape(8))
    o["hy_w1"] = f(inputs["hy_w1"][0])
    o["hy_w2"] = f(inputs["hy_w2"][0])
    o["hy_w3"] = f(inputs["hy_w3"][0])
    o["hyp"] = f(np.stack([inputs["hy_b1"][0], inputs["hy_f1"][0], inputs["hy_b2"][0], inputs["hy_f2"][0]], 1))
    hc = np.concatenate([inputs["hy_conv_w"][0], inputs["hy_conv_b"][0][None, :]], 0)
    o["hcw"] = f(hc.reshape(4, 12, 128).transpose(2, 1, 0))
    o["hybias"] = f(inputs["hy_bias"][0].reshape(4, 128).T)
    return o


_CACHE = {}


def run(inputs, nseq_per_core=4, ncores=NCORES, stages=("ab", "dn", "dil", "ffn0", "cd", "ret", "hy", "ffn1")):
    key = (nseq_per_core, tuple(stages))
    prog = Prog(nseq_per_core, stages)
    nc = prog.build()
    consts = host_consts()
    lay = host_layout(inputs)
    x = np.ascontiguousarray(inputs["x"], dtype=np.float32)
    in_maps = []
    for c in range(ncores):
        m = {}
        for name in prog.dram:
            if name == "x":
                m[name] = x[c * nseq_per_core:(c + 1) * nseq_per_core]
            elif name in consts:
                m[name] = consts[name]
            else:
                m[name] = lay[name]
        in_maps.append(m)
    res = run_bass_kernel_spmd(nc, in_maps, core_ids=list(range(ncores)))
    return np.concatenate([res.results[c]["out"] for c in range(ncores)], axis=0)


def kernel(**inputs):
    return run(inputs).astype(np.float32)
```

```python
import math
import numpy as np
import ml_dtypes
import concourse.bass as bass
import concourse.mybir as mybir
from concourse.bass_utils import run_bass_kernel_spmd

F32 = mybir.dt.float32
BF16 = mybir.dt.bfloat16
I32 = mybir.dt.int32
AF = mybir.ActivationFunctionType
ALU = mybir.AluOpType
AX = mybir.AxisListType

S = 2048
D = 1024
NT = 4
DFF = 2816
NCORES = 8


class Buf:
    __slots__ = ("t", "writer", "readers", "dsem", "dval", "name", "depth", "pver", "bdepth")

    def __init__(self, t, name=""):
        self.t = t
        self.writer = None
        self.readers = {}
        self.dsem = None
        self.dval = 0
        self.name = name
        self.pver = 0
        self.bdepth = 0

    def __getitem__(self, idx):
        return self.t[idx]


class Eng:
    def __init__(self, h, sem, name):
        self.applied = 0
        self.h = h
        self.sem = sem
        self.cnt = 0
        self.seen = {}
        self.name = name


class FW:
    def __init__(self, nc):
        self.nc = nc
        self._ctx = []
        self._semctx = []
        self.dsems = []
        self.pe = Eng(nc.tensor, self.sem("pe"), "pe")
        self.act = Eng(nc.scalar, self.sem("act"), "act")
        self.dve = Eng(nc.vector, self.sem("dve"), "dve")
        self.pool = Eng(nc.gpsimd, self.sem("pool"), "pool")
        self.sp = Eng(nc.sync, self.sem("sp"), "sp")
        self.engs = (self.pe, self.act, self.dve, self.pool, self.sp)
        self.out_stamps = []
        self.dma_bufs = []
        self.free_dsems = []
        self.live = []
        self.gp = {}
        self.gpv = 0
        self.gp_snap = {0: []}
        self.nds = 0
        self.nps = 0

    def sem(self, name):
        cm = self.nc.semaphore(name)
        s = cm.__enter__()
        self._semctx.append(cm)
        return s

    def sb(self, shape, dt, name):
        self.nps += 1
        name = "%s_u%d" % (name, self.nps)
        cm = self.nc.sbuf_tensor(name, list(shape), dt)
        t = cm.__enter__()
        self._ctx.append(cm)
        return t

    def ps(self, shape, dt, name):
        cm = self.nc.psum_tensor(name, list(shape), dt)
        t = cm.__enter__()
        self._ctx.append(cm)
        return t

    def buf(self, shape, dt, name):
        b = Buf(self.sb(shape, dt, name), name)
        b.pver = self.gpv
        b.bdepth = len(self._ctx)
        self.live.append(b)
        return b

    def view(self, t, name="", alias=()):
        b = Buf(t, name)
        if alias:
            for a in alias:
                self._merge(a)
            self._snap()
        b.pver = self.gpv
        b.bdepth = len(self._ctx) + 1
        self.live.append(b)
        return b

    def _merge(self, b):
        st = list(b.readers.values())
        if b.writer is not None:
            st.append(b.writer)
        for (sm, v) in st:
            k = id(sm)
            if k not in self.gp or self.gp[k][1] < v:
                self.gp[k] = (sm, v)

    def _snap(self):
        self.gpv += 1
        self.gp_snap[self.gpv] = list(self.gp.values())

    def mark(self):
        return len(self._ctx)

    def release(self, mark, hard=False):
        if hard:
            self.barrier()
        kl = []
        for b in self.live:
            if b.bdepth > mark:
                self._merge(b)
            else:
                kl.append(b)
        self.live = kl
        self._snap()
        keep = []
        for tb in self.dsems:
            if tb.depth > mark:
                self.free_dsems.append((tb.dsem, tb.dval))
            else:
                keep.append(tb)
        self.dsems = keep
        while len(self._ctx) > mark:
            self._ctx.pop().__exit__(None, None, None)

    def close(self):
        while self._ctx:
            self._ctx.pop().__exit__(None, None, None)
        while self._semctx:
            self._semctx.pop().__exit__(None, None, None)

    def _wait(self, E, sem, val):
        k = id(sem)
        if E.seen.get(k, 0) >= val:
            return
        E.h.wait_ge(sem, val)
        E.seen[k] = val

    def _deps(self, E, reads, writes, own_dsem=None):
        pv = 0
        for b in reads:
            if b.pver > pv:
                pv = b.pver
        for b in writes:
            if b.pver > pv:
                pv = b.pver
        if pv > E.applied:
            for (sm, v) in self.gp_snap[pv]:
                self._wait(E, sm, v)
            E.applied = pv
        for b in reads:
            if b.writer is not None:
                s, v = b.writer
                if s is E.sem and E is self.pe:
                    continue
                self._wait(E, s, v)
        for b in writes:
            if b.writer is not None:
                s, v = b.writer
                if not (s is E.sem and E is self.pe) and s is not own_dsem:
                    self._wait(E, s, v)
            for k, (s, v) in b.readers.items():
                if s is E.sem:
                    continue
                self._wait(E, s, v)

    def op(self, E, issue, reads=(), writes=(), inc=True):
        self._deps(E, reads, writes)
        ins = issue(E.h)
        if inc:
            E.cnt += 1
            ins.then_inc(E.sem, 1)
            stamp = (E.sem, E.cnt)
        else:
            stamp = (E.sem, E.cnt + 1)
        for b in reads:
            b.readers[id(E.sem)] = stamp
        for b in writes:
            b.writer = stamp
            b.readers = {}
        return ins

    def dma(self, Q, out_ap, in_ap, reads=(), writes=(), is_output=False, **kw):
        tb = writes[0] if writes else reads[0]
        if tb.dsem is None:
            if self.free_dsems:
                tb.dsem, tb.dval = self.free_dsems.pop()
            else:
                self.nds += 1
                tb.dsem = self.sem("d%d" % self.nds)
            tb.depth = len(self._ctx)
            self.dsems.append(tb)
        self._deps(Q, reads, writes, own_dsem=tb.dsem)
        tb.dval += 16
        Q.h.dma_start(out=out_ap, in_=in_ap, **kw).then_inc(tb.dsem, 16)
        stamp = (tb.dsem, tb.dval)
        for b in reads:
            b.readers[id(tb.dsem)] = stamp
        for b in writes:
            b.writer = stamp
            b.readers = {}
        if is_output:
            self.out_stamps.append(stamp)

    def barrier(self):
        for E in self.engs:
            for X in self.engs:
                if X is not E and X.cnt:
                    self._wait(E, X.sem, X.cnt)
            for tb in self.dsems:
                if tb.dval:
                    self._wait(E, tb.dsem, tb.dval)

    def finish(self):
        for s, v in self.out_stamps:
            self._wait(self.sp, s, v)
        for E in (self.pe, self.act, self.dve, self.pool):
            if E.cnt:
                self._wait(self.sp, E.sem, E.cnt)


class Prog:
    def __init__(self, nseq, stages):
        self.nseq = nseq
        self.stages = stages
        self.nc = bass.Bass("TRN2", target_bir_lowering=False)
        self.fw = FW(self.nc)
        self.dram = {}

    def din(self, name, shape, dt=F32):
        t = self.nc.dram_tensor(name, list(shape), dt, kind="ExternalInput").ap()
        self.dram[name] = t
        return t

    def nextp(self):
        p = self.P[self.prot[self.pi % len(self.prot)]]
        self.pi += 1
        return p

    def build(self):
        nc, fw = self.nc, self.fw
        nseq = self.nseq
        x_d = self.din("x", [nseq, S, D])
        out_d = nc.dram_tensor("out", [nseq, S, D], F32, kind="ExternalOutput").ap()
        norm_mix = self.din("norm_mix", [2, D])
        norm_ffn = self.din("norm_ffn", [2, D])
        final_norm = self.din("final_norm", [D])
        self.ffn_w_in = self.din("ffn_w_in_t", [2, 22, 128, 2048])
        self.ffn_conv_w = self.din("ffn_conv_w", [2, 3, DFF])
        self.ffn_conv_b = self.din("ffn_conv_b", [2, DFF])
        self.ffn_w_out = self.din("ffn_w_out", [2, DFF, D])
        ident_d = self.din("ident", [128, 128])
        rperm_d = self.din("rperm", [128, 128], BF16)
        L1 = "cd" in self.stages
        L0 = "ab" in self.stages
        if L0:
            self.ab_w_in = self.din("ab_w_in", [D, 3600])
            self.ab_w_rot = self.din("ab_w_rot", [D, 1024])
            self.ab_w_out = self.din("ab_w_out", [D, D])
            self.dilmask = self.din("dilmask", [128, 2944], BF16)
            self.dnp_d = self.din("dnp", [128, 25])
            self.dcw_d = self.din("dcw", [128, 12, 3])
            self.tri_d = self.din("dn_tri", [128, 2, 128])
            self.msk_d = self.din("dn_msk", [128, 3, 512], BF16)
        if L0 and not L1:
            ropec_d = self.din("ropec", [128, S], BF16)
            ropes_d = self.din("ropes", [128, S], BF16)
        if L1:
            self.cd_w_in = self.din("cd_w_in", [D, 3072])
            self.cd_w_rot = self.din("cd_w_rot", [D, 512])
            self.cd_w_out = self.din("cd_w_out", [D, D])
            ret_ld = self.din("ret_log_decay", [8])
            ropec_d = self.din("ropec", [128, S], BF16)
            ropes_d = self.din("ropes", [128, S], BF16)
            dlt_d = self.din("dlt", [128, 512])
            iota16_d = self.din("iota16", [128, 16])
            self.hyz = self.din("hyz", [33, S])
            self.hy_w1 = self.din("hy_w1", [33, 64])
            self.hy_w2 = self.din("hy_w2", [64, 64])
            self.hy_w3 = self.din("hy_w3", [64, 1024])
            hyp_d = self.din("hyp", [64, 4])
            self.hydec = self.din("hydec", [S, 512])
            self.dftc = self.din("dftc", [S, S], BF16)
            self.dfts = self.din("dfts", [S, S], BF16)
            self.dftcf = self.din("dftcf", [16, 128, 2048], BF16)
            self.dftsf = self.din("dftsf", [16, 128, 2048], BF16)
            cnycol_d = self.din("cnycol", [128, 16], BF16)
            cnyrow_d = self.din("cnyrow", [1, S], BF16)
            wf_d = self.din("wf", [128, 17])
            hcw_d = self.din("hcw", [128, 12, 4])
            hybias_d = self.din("hybias", [128, 4])
            self.spec_d = nc.dram_tensor("spec_scratch", [2, 2176, 512], F32, kind="Internal").ap()

        self.xT_t = fw.sb([128, 8, S], F32, "xT")
        self.xT = [[fw.view(self.xT_t, "xT%d_%d" % (c, n)) for n in range(NT)] for c in range(8)]
        self.hT_t = fw.sb([128, 8, S], BF16, "hT")
        self.hT = [[fw.view(self.hT_t, "hT%d_%d" % (c, n)) for n in range(NT)] for c in range(8)]
        self.yT_t = fw.sb([128, 8, S], BF16, "yT")
        self.yT = [fw.view(self.yT_t, "yT%d" % c) for c in range(8)]
        self.P = [fw.view(fw.ps([128, 512], F32, "ps%d" % i), "ps%d" % i) for i in range(8)]
        self.pi = 0
        self.prot = list(range(8))
        self.ident = fw.buf([128, 128], F32, "ident")
        fw.dma(fw.sp, self.ident[:], ident_d[:, :], writes=[self.ident])
        self.identb = fw.buf([128, 128], BF16, "identb")
        fw.op(fw.dve, lambda e: e.tensor_copy(out=self.identb[:], in_=self.ident[:]), reads=[self.ident], writes=[self.identb])
        self.onesb = fw.buf([128, 128], BF16, "onesb")
        fw.op(fw.dve, lambda e: e.memset(self.onesb[:], 1.0), writes=[self.onesb])
        self.eps = fw.buf([128, 1], F32, "eps")
        fw.op(fw.dve, lambda e: e.memset(self.eps[:], 1e-6), writes=[self.eps])
        if L1 or L0:
            self.rperm = fw.buf([128, 128], BF16, "rperm")
            fw.dma(fw.sp, self.rperm[:], rperm_d[:, :], writes=[self.rperm])
            self.ropec = fw.buf([128, S], BF16, "ropec")
            self.ropes = fw.buf([128, S], BF16, "ropes")
            fw.dma(fw.sp, self.ropec[:], ropec_d[:, :], writes=[self.ropec])
            fw.dma(fw.sp, self.ropes[:], ropes_d[:, :], writes=[self.ropes])
        if L1:
            self.dlt_d = dlt_d
            self.iota16 = fw.buf([128, 16], F32, "iota16")
            fw.dma(fw.sp, self.iota16[:], iota16_d[:, :], writes=[self.iota16])
            self.ld = fw.buf([128, 8], F32, "ld")
            fw.dma(fw.sp, self.ld[:], ret_ld.partition_broadcast(128), writes=[self.ld])
            self.ld128 = fw.buf([128, 8], F32, "ld128")
            self.ld512 = fw.buf([128, 8], F32, "ld512")
            self.ldn = fw.buf([128, 8], F32, "ldn")
            fw.op(fw.dve, lambda e: e.tensor_scalar(out=self.ld128[:], in0=self.ld[:], scalar1=128.0, scalar2=None, op0=ALU.mult), reads=[self.ld], writes=[self.ld128])
            fw.op(fw.dve, lambda e: e.tensor_scalar(out=self.ld512[:], in0=self.ld[:], scalar1=512.0, scalar2=None, op0=ALU.mult), reads=[self.ld], writes=[self.ld512])
            fw.op(fw.dve, lambda e: e.tensor_scalar(out=self.ldn[:], in0=self.ld[:], scalar1=-1.0, scalar2=None, op0=ALU.mult), reads=[self.ld], writes=[self.ldn])
            self.hyp = fw.buf([64, 4], F32, "hyp")
            fw.dma(fw.sp, self.hyp[:], hyp_d[:, :], writes=[self.hyp])
            self.cnycol = fw.buf([128, 16], BF16, "cnycol")
            fw.dma(fw.sp, self.cnycol[:], cnycol_d[:, :], writes=[self.cnycol])
            self.cnyrow_d = cnyrow_d
            self.wf = fw.buf([128, 17], F32, "wf")
            fw.dma(fw.sp, self.wf[:], wf_d[:, :], writes=[self.wf])
            self.hcw = fw.buf([128, 12, 4], F32, "hcw")
            fw.dma(fw.sp, self.hcw[:], hcw_d[:, :, :], writes=[self.hcw])
            self.hybias = fw.buf([128, 4], F32, "hybias")
            fw.dma(fw.sp, self.hybias[:], hybias_d[:, :], writes=[self.hybias])
            self.ln8 = fw.buf([128, 1], F32, "ln8")
            fw.op(fw.dve, lambda e: e.memset(self.ln8[:], math.log(0.125)), writes=[self.ln8])
        self.nw = fw.buf([128, 5, 8], F32, "nw")
        srcs = [norm_mix[0], norm_ffn[0], norm_mix[1], norm_ffn[1], final_norm]
        for i, sap in enumerate(srcs):
            fw.dma(fw.sp, self.nw[:, i, :], sap.rearrange("(c p) -> p c", p=128), writes=[self.nw],
                   allow_slow_non_contiguous=True)
        self.fcw = fw.buf([128, 2, 22, 4], F32, "fcw")
        for l in range(2):
            for k in range(3):
                fw.dma(fw.sp, self.fcw[:, l, :, k], self.ffn_conv_w[l, k].rearrange("(j p) -> p j", p=128),
                       writes=[self.fcw], allow_slow_non_contiguous=True)
            fw.dma(fw.sp, self.fcw[:, l, :, 3], self.ffn_conv_b[l].rearrange("(j p) -> p j", p=128),
                   writes=[self.fcw], allow_slow_non_contiguous=True)

        if L1 and "hy" in self.stages:
            self.hyena_setup()
        for s in range(nseq):
            if s == 0:
                self.load_x(x_d[s])
            for layer in range(2):
                if ("ab" if layer == 0 else "cd") in self.stages:
                    self.rmsnorm(2 * layer)
                    (self.mixer_ab if layer == 0 else self.mixer_cd)()
                if ("ffn%d" % layer) in self.stages:
                    self.rmsnorm(2 * layer + 1)
                    self.ffn(layer)
            self.store_out(out_d[s], x_d[s + 1] if s + 1 < nseq else None)
        fw.finish()
        fw.close()
        return nc

    def l0c(self):
        fw = self.fw
        self.dnp = fw.buf([128, 25], F32, "dnp")
        fw.dma(fw.sp, self.dnp[:], self.dnp_d[:, :], writes=[self.dnp])
        fw.op(fw.act, lambda e: e.activation(out=self.dnp[:, 16:24], in_=self.dnp[:, 0:8], func=AF.Exp), reads=[self.dnp], writes=[self.dnp])
        fw.op(fw.dve, lambda e: e.tensor_scalar(out=self.dnp[:, 16:24], in0=self.dnp[:, 16:24], scalar1=-1.0, scalar2=None, op0=ALU.mult), reads=[self.dnp], writes=[self.dnp])
        self.dcw = fw.buf([128, 12, 3], F32, "dcw")
        fw.dma(fw.sp, self.dcw[:], self.dcw_d[:, :, :], writes=[self.dcw])
        tri = fw.buf([128, 2, 128], F32, "dn_tri")
        fw.dma(fw.sp, tri[:], self.tri_d[:, :, :], writes=[tri])
        self.triF = fw.view(tri.t[:, 0, :], "triF")
        self.triB = fw.view(tri.t[:, 1, :], "triB")
        msk = fw.buf([128, 3, 512], BF16, "dn_msk")
        fw.dma(fw.sp, msk[:], self.msk_d[:, :, :], writes=[msk])
        self.mL4 = fw.view(msk.t[:, 0, :], "mL4")
        self.mU4 = fw.view(msk.t[:, 1, :], "mU4")
        self.noti4 = fw.view(msk.t[:, 2, :], "noti4")
        for v_ in (self.triF, self.triB, self.mL4, self.mU4, self.noti4):
            v_.writer = (tri.writer if v_ in (self.triF, self.triB) else msk.writer)
        self.identf4 = fw.buf([128, 512], BF16, "identf4")
        for i4 in range(4):
            fw.op(fw.dve, lambda e: e.tensor_copy(out=self.identf4[:, i4 * 128:(i4 + 1) * 128], in_=self.ident[:]), reads=[self.ident], writes=[self.identf4])
        self.onesf = fw.buf([128, 128], F32, "onesf")
        fw.op(fw.dve, lambda e: e.memset(self.onesf[:], 1.0), writes=[self.onesf])
        self.one1 = fw.buf([128, 1], F32, "one1")
        fw.op(fw.dve, lambda e: e.memset(self.one1[:], 1.0), writes=[self.one1])

    def load_tile(self, xs, tt, xb):
        fw = self.fw
        fw.dma(fw.sp, xb[:], xs[tt * 128:(tt + 1) * 128, :], writes=[xb])
        nt = tt // 4
        for half in range(2):
            p = self.nextp()
            for cc in range(4):
                c = half * 4 + cc
                fw.op(fw.pe, lambda e: e.transpose(p[:, cc * 128:(cc + 1) * 128], xb[:, c * 128:(c + 1) * 128], self.ident[:]),
                      reads=[xb, self.ident], writes=[p], inc=(cc == 3))
            dst = self.xT_t[:, half * 4:half * 4 + 4, tt * 128:(tt + 1) * 128]
            wr = [self.xT[half * 4 + cc][nt] for cc in range(4)]
            if half == 0:
                fw.op(fw.act, lambda e: e.activation(out=dst, in_=p[:].rearrange("p (c t) -> p c t", c=4), func=AF.Copy), reads=[p], writes=wr)
            else:
                fw.op(fw.dve, lambda e: e.tensor_copy(out=dst, in_=p[:].rearrange("p (c t) -> p c t", c=4)), reads=[p], writes=wr)

    def load_x(self, xs):
        fw = self.fw
        m = fw.mark()
        xin = [fw.buf([128, D], F32, "xin%d" % i) for i in range(2)]
        for tt in range(16):
            self.load_tile(xs, tt, xin[tt % 2])
        fw.release(m)

    def store_out(self, outs, next_xs=None):
        fw = self.fw
        self.rmsnorm(4, inplace=True)
        m = fw.mark()
        ob = [fw.buf([128, D], F32, "ob%d" % i) for i in range(2)]
        xin = [fw.buf([128, D], F32, "xinb%d" % i) for i in range(2)] if next_xs is not None else None
        for tt in range(16):
            o = ob[tt % 2]
            nt = tt // 4
            for half in range(2):
                p = self.nextp()
                for cc in range(4):
                    c = half * 4 + cc
                    fw.op(fw.pe, lambda e, c=c, cc=cc, p=p: e.transpose(p[:, cc * 128:(cc + 1) * 128], self.xT_t[:, c, tt * 128:(tt + 1) * 128], self.ident[:]),
                          reads=[self.xT[c][nt], self.ident], writes=[p], inc=(cc == 3))
                if half == 0:
                    fw.op(fw.act, lambda e, p=p, o=o: e.activation(out=o[:, 0:512], in_=p[:], func=AF.Copy), reads=[p], writes=[o])
                else:
                    fw.op(fw.dve, lambda e, p=p, o=o: e.tensor_copy(out=o[:, 512:1024], in_=p[:]), reads=[p], writes=[o])
            fw.dma(fw.sp, outs[tt * 128:(tt + 1) * 128, :], o[:], reads=[o], is_output=True)
            if next_xs is not None and tt % 4 == 3:
                for t2 in range(tt - 3, tt + 1):
                    self.load_tile(next_xs, t2, xin[t2 % 2])
        fw.release(m)

    def rmsnorm(self, widx, inplace=False):
        fw = self.fw
        m = fw.mark()
        sq = [fw.buf([128, 512], BF16, "sq%d" % i) for i in range(3)]
        rstd = [fw.buf([128, 512], F32, "rstd%d" % i) for i in range(2)]
        k = 0
        for nt in range(NT):
            sl = slice(nt * 512, (nt + 1) * 512)
            p = self.nextp()
            for c in range(8):
                q = sq[k % 3]
                k += 1
                fw.op(fw.act, lambda e, q=q, c=c: e.activation(out=q[:], in_=self.xT_t[:, c, sl], func=AF.Square),
                      reads=[self.xT[c][nt]], writes=[q])
                fw.op(fw.pe, lambda e, q=q, c=c, p=p: e.matmul(p[:], lhsT=self.onesb[:], rhs=q[:], start=(c == 0), stop=(c == 7)),
                      reads=[q, self.onesb], writes=[p], inc=True)
            r = rstd[nt % 2]
            fw.op(fw.act, lambda e, r=r, p=p: e.activation(out=r[:], in_=p[:], func=AF.Ln, bias=self.eps[:], scale=1.0 / D),
                  reads=[p, self.eps], writes=[r])
            fw.op(fw.act, lambda e, r=r: e.activation(out=r[:], in_=r[:], func=AF.Exp, scale=-0.5), reads=[r], writes=[r])
            for c in range(8):
                if inplace:
                    dst, wr = self.xT_t[:, c, sl], [self.xT[c][nt]]
                else:
                    dst, wr = self.hT_t[:, c, sl], [self.hT[c][nt]]
                fw.op(fw.dve, lambda e, c=c, r=r, dst=dst: e.scalar_tensor_tensor(out=dst, in0=self.xT_t[:, c, sl], scalar=self.nw[:, widx, c:c + 1],
                                                                                 in1=r[:], op0=ALU.mult, op1=ALU.mult),
                      reads=[self.xT[c][nt], r, self.nw], writes=wr)
        fw.release(m)

    def ffn(self, layer):
        fw = self.fw
        m = fw.mark()
        win_d = self.ffn_w_in[layer]
        wout_d = self.ffn_w_out[layer].rearrange("(j p) n -> p j n", p=128)
        win = [fw.buf([128, 8, 256], BF16, "win%d" % i) for i in range(2)]
        wout = fw.buf([128, 8, D], BF16, "wout")
        gbuf = [fw.buf([128, S + 2], F32, "gbuf%d" % i) for i in range(2)]
        ubuf = [fw.buf([128, S], BF16, "ubuf%d" % i) for i in range(2)]
        acc = [fw.buf([128, 512], F32, "acc%d" % i) for i in range(2)]
        sg = [fw.buf([128, 512], BF16, "sg%d" % i) for i in range(2)]
        for g in gbuf:
            fw.op(fw.dve, lambda e, g=g: e.memset(g[:, 0:1], 0.0), writes=[g])
            fw.op(fw.dve, lambda e, g=g: e.memset(g[:, S + 1:S + 2], 0.0), writes=[g])
        groups = [list(range(0, 8)), list(range(8, 16)), list(range(16, 22))]
        k = 0
        def load_win(j):
            w = win[j % 2]
            wj = win_d[j].rearrange("p (kc n) -> p kc n", kc=8)
            fw.dma(fw.pool, w[:, 0:4, :], wj[:, 0:4, :], writes=[w])
            fw.dma(fw.pool, w[:, 4:8, :], wj[:, 4:8, :], writes=[w])
        for grp in groups:
            load_win(grp[0])
            for si, j in enumerate(grp):
                fw.dma(fw.pool, wout[:, si, :], wout_d[:, j, :], writes=[wout])
            for si, j in enumerate(grp):
                w = win[j % 2]
                gb = gbuf[j % 2]
                ub = ubuf[j % 2]
                if si > 0:
                    load_win(j)
                for nt in range(NT):
                    sl = slice(nt * 512, (nt + 1) * 512)
                    pg = self.nextp()
                    for kc in range(8):
                        fw.op(fw.pe, lambda e, kc=kc, pg=pg, w=w: e.matmul(pg[:], lhsT=w[:, kc, 0:128], rhs=self.hT_t[:, kc, sl], start=(kc == 0), stop=(kc == 7)),
                              reads=[w, self.hT[kc][nt]], writes=[pg], inc=(kc == 7))
                    fw.op(fw.act, lambda e, pg=pg, gb=gb, nt=nt: e.activation(out=gb[:, 1 + nt * 512:1 + (nt + 1) * 512], in_=pg[:], func=AF.Copy),
                          reads=[pg], writes=[gb])
                    pu = self.nextp()
                    for kc in range(8):
                        fw.op(fw.pe, lambda e, kc=kc, pu=pu, w=w: e.matmul(pu[:], lhsT=w[:, kc, 128:256], rhs=self.hT_t[:, kc, sl], start=(kc == 0), stop=(kc == 7)),
                              reads=[w, self.hT[kc][nt]], writes=[pu], inc=(kc == 7))
                    fw.op(fw.act, lambda e, pu=pu, ub=ub, sl=sl: e.activation(out=ub[:, sl], in_=pu[:], func=AF.Copy),
                          reads=[pu], writes=[ub])
                cw = self.fcw
                for nt in range(NT):
                    a = acc[k % 2]
                    sgb = sg[k % 2]
                    k += 1
                    o = nt * 512
                    fw.op(fw.dve, lambda e, a=a, gb=gb, o=o, j=j: e.tensor_scalar(out=a[:], in0=gb[:, o:o + 512], scalar1=cw[:, layer, j, 0:1], scalar2=None, op0=ALU.mult),
                          reads=[gb, cw], writes=[a])
                    fw.op(fw.dve, lambda e, a=a, gb=gb, o=o, j=j: e.scalar_tensor_tensor(out=a[:], in0=gb[:, o + 1:o + 513], scalar=cw[:, layer, j, 1:2], in1=a[:], op0=ALU.mult, op1=ALU.add),
                          reads=[gb, cw, a], writes=[a])
                    fw.op(fw.dve, lambda e, a=a, gb=gb, o=o, j=j: e.scalar_tensor_tensor(out=a[:], in0=gb[:, o + 2:o + 514], scalar=cw[:, layer, j, 2:3], in1=a[:], op0=ALU.mult, op1=ALU.add),
                          reads=[gb, cw, a], writes=[a])
                    fw.op(fw.act, lambda e, a=a, sgb=sgb, j=j: e.activation(out=sgb[:], in_=a[:], func=AF.Silu, bias=cw[:, layer, j, 3:4], scale=1.0),
                          reads=[a, cw], writes=[sgb])
                    fw.op(fw.dve, lambda e, sgb=sgb, ub=ub, si=si, o=o: e.tensor_tensor(out=self.yT_t[:, si, o:o + 512], in0=sgb[:], in1=ub[:, o:o + 512], op=ALU.mult),
                          reads=[sgb, ub], writes=[self.yT[si]])
            for nt in range(NT):
                for c in range(8):
                    sl = slice(nt * 512, (nt + 1) * 512)
                    po = self.nextp()
                    for si in range(len(grp)):
                        fw.op(fw.pe, lambda e, si=si, po=po, c=c: e.matmul(po[:], lhsT=wout[:, si, c * 128:(c + 1) * 128], rhs=self.yT_t[:, si, sl], start=(si == 0), stop=(si == len(grp) - 1)),
                              reads=[wout, self.yT[si]], writes=[po], inc=(si == len(grp) - 1))
                    fw.op(fw.dve, lambda e, po=po, c=c: e.tensor_tensor(out=self.xT_t[:, c, sl], in0=self.xT_t[:, c, sl], in1=po[:], op=ALU.add),
                          reads=[po, self.xT[c][nt]], writes=[self.xT[c][nt]])
        fw.release(m)


    def mixer_ab(self):
        fw = self.fw
        m0 = fw.mark()
        self.l0c()
        if "dn" in self.stages:
            self.deltanet()
        else:
            for c in range(4):
                fw.op(fw.dve, lambda e: e.memset(self.yT_t[:, c, :], 0.0), writes=[self.yT[c]])
        if "dil" in self.stages:
            self.dilated()
        else:
            for c in range(4, 8):
                fw.op(fw.dve, lambda e: e.memset(self.yT_t[:, c, :], 0.0), writes=[self.yT[c]])
        fw.release(m0)
        self.out_proj(self.ab_w_out)

    def dilated(self):
        fw = self.fw
        W = self.ab_w_in
        wsrc = W.rearrange("(kc p) n -> p kc n", p=128)
        rsrc = self.ab_w_rot.rearrange("(kc p) n -> p kc n", p=128)
        m = fw.mark()
        Q0, K0, V0 = 2064, 2064 + 512, 2064 + 1024
        mask = fw.buf([128, 2944], BF16, "dmask")
        fw.dma(fw.sp, mask[:], self.dilmask[:, :], writes=[mask])
        qe = fw.buf([65, S], BF16, "qe")
        ke = fw.buf([65, S], BF16, "ke")
        fw.op(fw.dve, lambda e: e.memset(ke[64:65, :], 1.0), writes=[ke])
        vx = fw.buf([128, 16, 128], BF16, "vx")
        wq = [fw.buf([128, 8, 64], BF16, "dwq%d" % i) for i in range(4)]
        wv = fw.buf([128, 8, 64], BF16, "dwv")
        t1 = [fw.buf([64, 512], F32, "dt1_%d" % i) for i in range(2)]
        t2 = [fw.buf([64, 512], F32, "dt2_%d" % i) for i in range(2)]
        sqq = fw.buf([64, S], BF16, "dsqq")
        sqk = fw.buf([64, S], BF16, "dsqk")
        kmx = fw.buf([128, 8], F32, "dkmx")
        qn = [fw.buf([128, 512], F32, "dqn%d" % i) for i in range(2)]
        ex = [fw.buf([128, 512], BF16, "dex%d" % i) for i in range(6)]
        pt = [fw.buf([128, 512], BF16, "dpt%d" % i) for i in range(7)]
        rd = [fw.buf([128, 512], F32, "drd%d" % i) for i in range(2)]
        kx = 0
        kp = 0
        for h in range(8):
            hc, odd = h // 2, h % 2
            self.prot = list(range(2, 8))
            fw.dma(fw.pool, wq[0][:], wsrc[:, :, Q0 + h * 64:Q0 + (h + 1) * 64], writes=[wq[0]])
            fw.dma(fw.pool, wq[2][:], wsrc[:, :, K0 + h * 64:K0 + (h + 1) * 64], writes=[wq[2]])
            fw.dma(fw.pool, wv[:], wsrc[:, :, V0 + h * 64:V0 + (h + 1) * 64], writes=[wv])
            vcol, ocol = (64, 0) if odd else (0, 64)
            fw.op(fw.dve, lambda e: e.memset(vx[:, :, ocol:ocol + 64], 1.0), writes=[vx])
            for tq in range(4):
                p = self.nextp()
                for t4 in range(4):
                    tt = tq * 4 + t4
                    for kc in range(8):
                        fw.op(fw.pe, lambda e: e.matmul(p[:, t4 * 64:(t4 + 1) * 64], lhsT=self.hT_t[:, kc, tt * 128:(tt + 1) * 128], rhs=wv[:, kc, :], start=(kc == 0), stop=(kc == 7)),
                              reads=[wv, self.hT[kc][tq]], writes=[p], inc=(kc == 7 and t4 == 3))
                fw.op(fw.act, lambda e: e.activation(out=vx[:, tq * 4:tq * 4 + 4, vcol:vcol + 64], in_=p[:, 0:256].rearrange("p (a b) -> p a b", a=4), func=AF.Copy), reads=[p], writes=[vx])
            for which, dst, sq in ((0, qe, sqq), (1, ke, sqk)):
                for nt in range(NT):
                    sl = slice(nt * 512, (nt + 1) * 512)
                    p1 = self.proj(wq[2 * which], 0, 64, nt)
                    qs_ = wq[1] if kx % 2 == 0 else wq[3]
                    qsv = qs_[:].rearrange("p a b -> p (a b)")
                    fw.op(fw.dve, lambda e: e.tensor_copy(out=qsv[0:64, :], in_=p1[0:64, :]), reads=[p1], writes=[qs_])
                    p2 = self.nextp()
                    fw.op(fw.pe, lambda e: e.matmul(p2[0:64, :], lhsT=self.rperm[0:64, 0:64], rhs=qsv[0:64, :], start=True, stop=True), reads=[self.rperm, qs_], writes=[p2])
                    a, b = t1[kx % 2], t2[kx % 2]
                    kx += 1
                    sc_ = 0.125 if which == 0 else 1.0
                    fw.op(fw.dve, lambda e: e.scalar_tensor_tensor(out=a[:], in0=p1[0:64, :], scalar=sc_, in1=self.ropec[0:64, sl], op0=ALU.mult, op1=ALU.mult), reads=[p1, self.ropec], writes=[a])
                    fw.op(fw.dve, lambda e: e.scalar_tensor_tensor(out=b[:], in0=p2[0:64, :], scalar=sc_, in1=self.ropes[0:64, sl], op0=ALU.mult, op1=ALU.mult), reads=[p2, self.ropes], writes=[b])
                    fw.op(fw.pool, lambda e: e.tensor_tensor(out=dst[0:64, sl], in0=a[:], in1=b[:], op=ALU.add), reads=[a, b], writes=[dst])
                    fw.op(fw.act, lambda e: e.activation(out=sq[:, sl], in_=dst[0:64, sl], func=AF.Square), reads=[dst], writes=[sq])
            for nt in range(NT):
                sl = slice(nt * 512, (nt + 1) * 512)
                pk = self.nextp()
                fw.op(fw.pe, lambda e: e.matmul(pk[:], lhsT=self.onesb[0:64, :], rhs=sqk[:, sl], start=True, stop=True), reads=[self.onesb, sqk], writes=[pk])
                fw.op(fw.dve, lambda e: e.tensor_reduce(out=kmx[:, nt:nt + 1], in_=pk[:], axis=AX.X, op=ALU.max), reads=[pk], writes=[kmx])
            fw.op(fw.dve, lambda e: e.tensor_reduce(out=kmx[:, 4:5], in_=kmx[:, 0:4], axis=AX.X, op=ALU.max), reads=[kmx], writes=[kmx])
            fw.op(fw.act, lambda e: e.activation(out=kmx[:, 5:6], in_=kmx[:, 4:5], func=AF.Sqrt), reads=[kmx], writes=[kmx])
            fw.op(fw.dve, lambda e: e.tensor_scalar(out=kmx[:, 6:7], in0=kmx[:, 5:6], scalar1=-1.0, scalar2=None, op0=ALU.mult), reads=[kmx], writes=[kmx])
            for nt in range(NT):
                sl = slice(nt * 512, (nt + 1) * 512)
                pq = self.nextp()
                fw.op(fw.pe, lambda e: e.matmul(pq[:], lhsT=self.onesb[0:64, :], rhs=sqq[:, sl], start=True, stop=True), reads=[self.onesb, sqq], writes=[pq])
                q_ = qn[nt % 2]
                fw.op(fw.act, lambda e: e.activation(out=q_[:], in_=pq[:], func=AF.Sqrt), reads=[pq], writes=[q_])
                fw.op(fw.dve, lambda e: e.tensor_scalar(out=qe[64:65, sl], in0=q_[64:65, :], scalar1=kmx[64:65, 6:7], scalar2=None, op0=ALU.mult), reads=[q_, kmx], writes=[qe])
            for nj in range(NT):
                pacc = self.P[nj % 2]
                pend = []
                mis = [mi for mi in range(16) if -8 <= mi - 4 * nj <= 11]

                def pv(sc, mi):
                    fw.op(fw.pe, lambda e: e.matmul(pacc[:], lhsT=vx[:, mi, :], rhs=sc[:], start=(mi == mis[0]), stop=(mi == mis[-1])),
                          reads=[vx, sc], writes=[pacc], inc=True)
                for mi in mis:
                    r = mi - 4 * nj
                    ps = self.nextp()
                    fw.op(fw.pe, lambda e: e.matmul(ps[:], lhsT=ke[0:65, mi * 128:(mi + 1) * 128], rhs=qe[0:65, nj * 512:(nj + 1) * 512], start=True, stop=True),
                          reads=[ke, qe], writes=[ps], inc=True)
                    e_ = ex[kp % 6]
                    sc = pt[kp % 7]
                    kp += 1
                    fw.op(fw.act, lambda e: e.activation(out=e_[:], in_=ps[:], func=AF.Exp), reads=[ps], writes=[e_])
                    fw.op(fw.dve, lambda e: e.tensor_tensor(out=sc[:], in0=e_[:], in1=mask[:, (11 - r) * 128:(11 - r) * 128 + 512], op=ALU.mult), reads=[e_, mask], writes=[sc])
                    pend.append((sc, mi))
                    if len(pend) > 4:
                        pv(*pend.pop(0))
                while pend:
                    pv(*pend.pop(0))
                r_ = rd[nj % 2]
                nlo, dlo = (64, 0) if odd else (0, 64)
                fw.op(fw.act, lambda e: e.activation(out=r_[nlo:nlo + 64, :], in_=pacc[dlo:dlo + 64, :], func=AF.Ln), reads=[pacc], writes=[r_])
                fw.op(fw.act, lambda e: e.activation(out=r_[nlo:nlo + 64, :], in_=r_[nlo:nlo + 64, :], func=AF.Exp, scale=-1.0), reads=[r_], writes=[r_])
                fw.op(fw.dve, lambda e: e.tensor_tensor(out=self.yT_t[nlo:nlo + 64, 4 + hc, nj * 512:(nj + 1) * 512], in0=pacc[nlo:nlo + 64, :], in1=r_[nlo:nlo + 64, :], op=ALU.mult),
                      reads=[pacc, r_], writes=[self.yT[4 + hc]])
        self.prot = list(range(8))
        fw.release(m)


    def deltanet(self):
        fw = self.fw
        W = self.ab_w_in
        wsrc = W.rearrange("(kc p) n -> p kc n", p=128)
        P = self.P
        m = fw.mark()
        beta = fw.buf([128, 8, 16], F32, "dn_beta")
        nbeta = fw.buf([128, 8, 16], F32, "dn_nbeta")
        gl = fw.buf([128, 8, 16], F32, "dn_g")
        mba = fw.mark()
        ba = fw.buf([128, 16, 16], F32, "dn_ba")
        wba = fw.buf([128, 8, 16], BF16, "dn_wba")
        fw.dma(fw.pool, wba[:], wsrc[:, :, 2048:2064], writes=[wba])
        p = P[0]
        for tt in range(16):
            for kc in range(8):
                fw.op(fw.pe, lambda e: e.matmul(p[:, tt * 16:(tt + 1) * 16], lhsT=self.hT_t[:, kc, tt * 128:(tt + 1) * 128], rhs=wba[:, kc, :], start=(kc == 0), stop=(kc == 7)),
                      reads=[wba, self.hT[kc][tt // 4]], writes=[p], inc=(kc == 7 and tt == 15))
        fw.op(fw.dve, lambda e: e.tensor_copy(out=ba[:], in_=p[:, 0:256].rearrange("p (t c) -> p c t", c=16)), reads=[p], writes=[ba])
        fw.op(fw.act, lambda e: e.activation(out=beta[:], in_=ba[:, 0:8, :], func=AF.Sigmoid), reads=[ba], writes=[beta])
        fw.op(fw.dve, lambda e: e.tensor_scalar(out=nbeta[:], in0=beta[:], scalar1=-1.0, scalar2=None, op0=ALU.mult), reads=[beta], writes=[nbeta])
        tx = fw.buf([128, 16], F32, "dn_tx")
        ta = fw.buf([128, 16], F32, "dn_ta")
        for c in range(8):
            fw.op(fw.dve, lambda e: e.tensor_scalar(out=tx[:], in0=ba[:, 8 + c, :], scalar1=self.dnp[:, 8 + c:9 + c], scalar2=None, op0=ALU.add), reads=[ba, self.dnp], writes=[tx])
            fw.op(fw.dve, lambda e: e.tensor_scalar(out=ta[:], in0=tx[:], scalar1=-1.0, scalar2=None, op0=ALU.mult), reads=[tx], writes=[ta])
            fw.op(fw.dve, lambda e: e.tensor_tensor(out=ta[:], in0=ta[:], in1=tx[:], op=ALU.max), reads=[tx, ta], writes=[ta])
            fw.op(fw.act, lambda e: e.activation(out=ta[:], in_=ta[:], func=AF.Exp, scale=-1.0), reads=[ta], writes=[ta])
            fw.op(fw.act, lambda e: e.activation(out=ta[:], in_=ta[:], func=AF.Ln, bias=self.one1[:], scale=1.0), reads=[ta, self.one1], writes=[ta])
            fw.op(fw.dve, lambda e: e.scalar_tensor_tensor(out=tx[:], in0=tx[:], scalar=0.0, in1=ta[:], op0=ALU.max, op1=ALU.add), reads=[tx, ta], writes=[tx])
            fw.op(fw.dve, lambda e: e.tensor_scalar(out=gl[:, c, :], in0=tx[:], scalar1=self.dnp[:, 16 + c:17 + c], scalar2=None, op0=ALU.mult), reads=[tx, self.dnp], writes=[gl])
        fw.release(mba)
        for h in range(4):
            mh = fw.mark()
            al = [self.yT[4], self.yT[5], self.yT[6], self.yT[7]]
            qT = fw.view(self.yT_t[:, 6, :], "dn_qT", alias=al)
            kT = fw.view(self.yT_t[:, 7, :], "dn_kT", alias=al)
            oT = fw.view(self.yT_t[:, 4:6, :].rearrange("p a s -> p (a s)").bitcast(F32), "dn_oT", alias=al)
            ktok = fw.buf([128, 16, 128], BF16, "dn_ktok")
            vtok = fw.buf([128, 16, 128], BF16, "dn_vtok")
            mp = fw.mark()
            self.prot = list(range(8))
            gb = fw.buf([128, S + 2], F32, "dn_gb")
            fw.op(fw.dve, lambda e: e.memset(gb[:, 0:1], 0.0), writes=[gb])
            fw.op(fw.dve, lambda e: e.memset(gb[:, S + 1:S + 2], 0.0), writes=[gb])
            a1 = fw.buf([128, S], F32, "dn_a1")
            vT = fw.buf([128, S], BF16, "dn_vT")
            wps = [fw.buf([128, 8, 128], BF16, "dn_wp%d" % i) for i in range(3)]
            for part in range(3):
                ch_ = part * 4 + h
                fw.dma(fw.pool, wps[part][:], wsrc[:, :, ch_ * 128:(ch_ + 1) * 128], writes=[wps[part]])
            sq = [fw.buf([128, 512], BF16, "dn_sq%d" % i) for i in range(2)]
            rs = [fw.buf([128, 512], F32, "dn_rs%d" % i) for i in range(2)]
            dcw = self.dcw
            for part in range(3):
                ch = part * 4 + h
                wp = wps[part]
                for nt in range(NT):
                    pp = self.proj(wp, 0, 128, nt)
                    fw.op(fw.act, lambda e: e.activation(out=gb[:, 1 + nt * 512:1 + (nt + 1) * 512], in_=pp[:], func=AF.Copy), reads=[pp], writes=[gb])
                fw.op(fw.dve, lambda e: e.tensor_scalar(out=a1[:], in0=gb[:, 0:S], scalar1=dcw[:, ch, 0:1], scalar2=None, op0=ALU.mult), reads=[gb, dcw], writes=[a1])
                fw.op(fw.dve, lambda e: e.scalar_tensor_tensor(out=a1[:], in0=gb[:, 1:S + 1], scalar=dcw[:, ch, 1:2], in1=a1[:], op0=ALU.mult, op1=ALU.add), reads=[gb, dcw, a1], writes=[a1])
                fw.op(fw.dve, lambda e: e.scalar_tensor_tensor(out=a1[:], in0=gb[:, 2:S + 2], scalar=dcw[:, ch, 2:3], in1=a1[:], op0=ALU.mult, op1=ALU.add), reads=[gb, dcw, a1], writes=[a1])
                if part == 2:
                    fw.op(fw.act, lambda e: e.activation(out=vT[:], in_=a1[:], func=AF.Silu), reads=[a1], writes=[vT])
                    srcT, dtok = vT, vtok
                else:
                    fw.op(fw.act, lambda e: e.activation(out=a1[:], in_=a1[:], func=AF.Silu), reads=[a1], writes=[a1])
                    dstT = qT if part == 0 else kT
                    for nt in range(NT):
                        sl = slice(nt * 512, (nt + 1) * 512)
                        q_, r_ = sq[nt % 2], rs[nt % 2]
                        fw.op(fw.act, lambda e: e.activation(out=q_[:], in_=a1[:, sl], func=AF.Square), reads=[a1], writes=[q_])
                        pn = self.nextp()
                        fw.op(fw.pe, lambda e: e.matmul(pn[:], lhsT=self.onesb[:], rhs=q_[:], start=True, stop=True), reads=[q_, self.onesb], writes=[pn])
                        fw.op(fw.act, lambda e: e.activation(out=r_[:], in_=pn[:], func=AF.Ln, bias=self.eps[:], scale=1.0), reads=[pn, self.eps], writes=[r_])
                        fw.op(fw.act, lambda e: e.activation(out=r_[:], in_=r_[:], func=AF.Exp, scale=-0.5), reads=[r_], writes=[r_])
                        scl = 128.0 ** -0.5 if part == 0 else 1.0
                        fw.op(fw.dve, lambda e: e.scalar_tensor_tensor(out=dstT[:, sl], in0=a1[:, sl], scalar=scl, in1=r_[:], op0=ALU.mult, op1=ALU.mult), reads=[a1, r_], writes=[dstT])
                    srcT, dtok = kT, ktok
                if part >= 1:
                    for tq in range(4):
                        pp = self.nextp()
                        pb = pp[:].bitcast(BF16)
                        for t4 in range(4):
                            tt = tq * 4 + t4
                            fw.op(fw.pe, lambda e: e.transpose(pb[:, t4 * 128:(t4 + 1) * 128], srcT[:, tt * 128:(tt + 1) * 128], self.identb[:]),
                                  reads=[srcT, self.identb], writes=[pp], inc=(t4 == 3))
                        fw.op(fw.act, lambda e: e.activation(out=dtok[:, tq * 4:tq * 4 + 4, :], in_=pb[:, 0:512].rearrange("p (a b) -> p a b", a=4), func=AF.Copy), reads=[pp], writes=[dtok])
            fw.release(mp)
            import os
            def dir_tables(d):
                    col = d * 4 + h
                    qdT = fw.buf([128, S], BF16, "dn_qdT")
                    TT = fw.buf([128, 16, 128], BF16, "dn_TT")
                    inT = fw.buf([128, S], BF16, "dn_inT")
                    sm = fw.buf([128, 6, 16], F32, "dn_sm")
                    TRI = self.triF if d == 0 else self.triB
                    MI = self.mL4 if d == 0 else self.mU4
                    MT = self.mU4 if d == 0 else self.mL4
                    gcol = gl[:, col, :]
                    pc = P[0]
                    fw.op(fw.pe, lambda e: e.matmul(pc[:, 0:16], lhsT=TRI[:], rhs=gcol, start=True, stop=True), reads=[TRI, gl], writes=[pc], inc=False)
                    fw.op(fw.pe, lambda e: e.matmul(pc[:, 16:32], lhsT=self.onesf[:], rhs=gcol, start=True, stop=True), reads=[self.onesf, gl], writes=[pc])
                    fw.op(fw.dve, lambda e: e.tensor_copy(out=sm[:, 0, :], in_=pc[:, 0:16]), reads=[pc], writes=[sm])
                    fw.op(fw.dve, lambda e: e.tensor_scalar(out=sm[:, 1, :], in0=pc[:, 0:16], scalar1=-1.0, scalar2=None, op0=ALU.mult), reads=[pc], writes=[sm])
                    fw.op(fw.dve, lambda e: e.tensor_copy(out=sm[:, 2, :], in_=pc[:, 16:32]), reads=[pc], writes=[sm])
                    fw.op(fw.dve, lambda e: e.tensor_tensor(out=sm[:, 3, :], in0=sm[:, 2, :], in1=sm[:, 0, :], op=ALU.subtract), reads=[sm], writes=[sm])
                    fw.op(fw.act, lambda e: e.activation(out=sm[:, 3, :], in_=sm[:, 3, :], func=AF.Exp), reads=[sm], writes=[sm])
                    fw.op(fw.act, lambda e: e.activation(out=sm[:, 4, :], in_=sm[:, 2, :], func=AF.Exp), reads=[sm], writes=[sm])
                    fw.op(fw.act, lambda e: e.activation(out=sm[:, 5, :], in_=sm[:, 0, :], func=AF.Exp), reads=[sm], writes=[sm])
                    fw.op(fw.dve, lambda e: e.tensor_scalar(out=sm[:, 5, :], in0=sm[:, 5, :], scalar1=-1.0, scalar2=None, op0=ALU.mult), reads=[sm], writes=[sm])
                    mt = fw.mark()
                    def mk_tmp(tag):
                        return (fw.buf([128, 512], F32, 'dn_tI' + tag), fw.buf([128, 512], F32, 'dn_tT' + tag),
                                [fw.buf([128, 512], F32, 'dn_Pb%d%s' % (i, tag)) for i in range(2)],
                                [fw.buf([128, 512], F32, 'dn_Qb%d%s' % (i, tag)) for i in range(2)],
                                fw.buf([128, 512], F32, 'dn_Rb' + tag))
                    def tab_gen(gq, tmp, banks):
                        tI, tT, Pb, Qb, Rb1 = tmp
                        dg, eg = tT, tI
                        Rb = [Rb1, Rb1]
                        cs = slice(gq * 512, (gq + 1) * 512)
                        chs = [gq * 4 + i for i in range(4)]
                        C4 = lambda i: slice(i * 128, (i + 1) * 128)
                        pcb, pG, pKQ = banks; pP, pQ, pR = banks
                        for i, ch in enumerate(chs):
                            fw.op(fw.dve, lambda e: e.tensor_scalar(out=dg[:, C4(i)], in0=self.ident[:], scalar1=sm[:, 0, ch:ch + 1], scalar2=None, op0=ALU.mult), reads=[self.ident, sm], writes=[dg])
                            yield
                        for i, ch in enumerate(chs):
                            fw.op(fw.pe, lambda e: e.matmul(pcb[:, C4(i)], lhsT=self.onesf[:], rhs=dg[:, C4(i)], start=True, stop=True), reads=[self.onesf, dg], writes=[pcb], inc=(i == 3))
                            yield
                        fw.op(fw.act, lambda e: e.activation(out=eg[:], in_=pcb[:], func=AF.Exp), reads=[pcb], writes=[eg])
                        yield
                        fw.op(fw.dve, lambda e: e.tensor_tensor(out=qdT[:, cs], in0=qT[:, cs], in1=eg[:], op=ALU.mult), reads=[qT, eg], writes=[qdT])
                        yield
                        fw.op(fw.dve, lambda e: e.scalar_tensor_tensor(out=tI[:], in0=pcb[:], scalar=-1.0, in1=MI[:], op0=ALU.mult, op1=ALU.add), reads=[pcb, MI], writes=[tI])
                        yield
                        for i, ch in enumerate(chs):
                            fw.op(fw.act, lambda e: e.activation(out=tI[:, C4(i)], in_=tI[:, C4(i)], func=AF.Exp, bias=sm[:, 0, ch:ch + 1], scale=1.0), reads=[tI, sm], writes=[tI])
                            yield
                            fw.op(fw.dve, lambda e: e.scalar_tensor_tensor(out=tT[:, C4(i)], in0=pcb[:, C4(i)], scalar=sm[:, 1, ch:ch + 1], in1=MT[:, C4(i)], op0=ALU.add, op1=ALU.add), reads=[pcb, sm, MT], writes=[tT])
                            yield
                        fw.op(fw.act, lambda e: e.activation(out=tT[:], in_=tT[:], func=AF.Exp), reads=[tT], writes=[tT])
                        yield
                        for i, ch in enumerate(chs):
                            c128 = slice(ch * 128, (ch + 1) * 128)
                            fw.op(fw.pe, lambda e: e.matmul(pG[:, C4(i)], lhsT=kT[:, c128], rhs=kT[:, c128], start=True, stop=True), reads=[kT], writes=[pG], inc=(i == 3))
                            yield
                        for i, ch in enumerate(chs):
                            c128 = slice(ch * 128, (ch + 1) * 128)
                            fw.op(fw.pe, lambda e: e.matmul(pKQ[:, C4(i)], lhsT=kT[:, c128], rhs=qT[:, c128], start=True, stop=True), reads=[kT, qT], writes=[pKQ], inc=(i == 3))
                            yield
                        fw.op(fw.dve, lambda e: e.tensor_tensor(out=inT[:, cs], in0=pKQ[:], in1=tT[:], op=ALU.mult), reads=[pKQ, tT], writes=[inT])
                        yield
                        fw.op(fw.pool, lambda e: e.tensor_tensor(out=tI[:], in0=tI[:], in1=self.noti4[:], op=ALU.mult), reads=[tI, self.noti4], writes=[tI])
                        yield
                        Pc, Qc, Rc = Pb[0], Qb[0], Rb[0]
                        for i, ch in enumerate(chs):
                            fw.op(fw.dve, lambda e: e.scalar_tensor_tensor(out=Pc[:, C4(i)], in0=pG[:, C4(i)], scalar=nbeta[:, col, ch:ch + 1], in1=tI[:, C4(i)], op0=ALU.mult, op1=ALU.mult), reads=[pG, nbeta, tI], writes=[Pc])
                            yield
                        for i in range(4):
                            fw.op(fw.pe, lambda e: e.transpose(pQ[:, C4(i)], Pc[:, C4(i)], self.ident[:]), reads=[Pc, self.ident], writes=[pQ], inc=(i == 3))
                            yield
                        fw.op(fw.act, lambda e: e.activation(out=Qc[:], in_=pQ[:], func=AF.Copy), reads=[pQ], writes=[Qc])
                        yield
                        fw.op(fw.dve, lambda e: e.tensor_tensor(out=Rc[:], in0=Qc[:], in1=self.identf4[:], op=ALU.add), reads=[Qc, self.identf4], writes=[Rc])
                        yield
                        for k in (range(6, 7) if os.environ.get('DN_SKIPT') else range(1, 7)):
                            Pn, Qn, Rn = Pb[k % 2], Qb[k % 2], Rb[k % 2]
                            for i in range(4):
                                fw.op(fw.pe, lambda e: e.matmul(pP[:, C4(i)], lhsT=Qc[:, C4(i)], rhs=Pc[:, C4(i)], start=True, stop=True), reads=[Qc, Pc], writes=[pP], inc=(i == 3))
                                yield
                            fw.op(fw.dve, lambda e: e.tensor_copy(out=Pn[:], in_=pP[:]), reads=[pP], writes=[Pn])
                            yield
                            if k < 6:
                                for i in range(4):
                                    fw.op(fw.pe, lambda e: e.matmul(pQ[:, C4(i)], lhsT=Pc[:, C4(i)], rhs=Qc[:, C4(i)], start=True, stop=True), reads=[Qc, Pc], writes=[pQ], inc=(i == 3))
                                    yield
                                fw.op(fw.act, lambda e: e.activation(out=Qn[:], in_=pQ[:], func=AF.Copy), reads=[pQ], writes=[Qn])
                                yield
                            for i in range(4):
                                fw.op(fw.pe, lambda e: e.matmul(pR[:, C4(i)], lhsT=Pn[:, C4(i)], rhs=Rc[:, C4(i)], start=True, stop=True), reads=[Pn, Rc], writes=[pR], inc=(i == 3))
                                yield
                            fw.op(fw.dve, lambda e: e.tensor_tensor(out=Rn[:], in0=Rc[:], in1=pR[:], op=ALU.add), reads=[pR, Rc], writes=[Rn])
                            yield
                            if k == 6:
                                for i, ch in enumerate(chs):
                                    if os.environ.get("DN_NOT"):
                                        fw.op(fw.dve, lambda e: e.tensor_scalar(out=TT[:, ch, :], in0=self.ident[:], scalar1=beta[:, col, ch:ch + 1], scalar2=None, op0=ALU.mult), reads=[pR, beta], writes=[TT])
                                        yield
                                    else:
                                        fw.op(fw.pool, lambda e: e.tensor_scalar(out=TT[:, ch, :], in0=Rn[:, C4(i)], scalar1=beta[:, col, ch:ch + 1], scalar2=None, op0=ALU.mult), reads=[Rn, beta], writes=[TT])
                                        yield
                            Pc, Qc, Rc = Pn, Qn, Rn
                    tmpA, tmpB = mk_tmp('a'), mk_tmp('b')
                    if not os.environ.get('DN_SKIPTAB'):
                        for ga, gb_ in ((0, 1), (2, 3)):
                            self.interleave([tab_gen(ga, tmpA, [P[1], P[2], P[3]]), tab_gen(gb_, tmpB, [P[4], P[5], P[6]])])
                    fw.release(mt)
                    return dict(col=col, qdT=qdT, TT=TT, inT=inT, sm=sm)
            def scan_gen(d, B):
                    col, qdT, TT, inT, sm = B["col"], B["qdT"], B["TT"], B["inT"], B["sm"]
                    Sf = fw.buf([128, 128], F32, "dn_Sf")
                    v2b = [fw.buf([128, 128], BF16, "dn_v2%d" % i) for i in range(2)]
                    Sb = fw.buf([128, 128], BF16, "dn_Sb")
                    rb = [fw.buf([128, 128], BF16, "dn_r%d" % i) for i in range(2)]
                    vn = [fw.buf([128, 128], BF16, "dn_vn%d" % i) for i in range(2)]
                    fw.op(fw.dve, lambda e: e.memset(Sf[:], 0.0), writes=[Sf])
                    yield
                    fw.op(fw.dve, lambda e: e.memset(Sb[:], 0.0), writes=[Sb])
                    yield
                    order = list(range(16)) if d == 0 else list(range(15, -1, -1))
                    for si, ch in enumerate(order[:1] if os.environ.get('DN_SKIPS') else order):
                        c128 = slice(ch * 128, (ch + 1) * 128)
                        pa, po = P[4 * d + (si % 2)], P[4 * d + 2 + (si % 2)]
                        v2_ = v2b[si % 2]
                        r_, v_ = rb[si % 2], vn[si % 2]
                        fw.op(fw.pe, lambda e: e.matmul(pa[:, 0:128], lhsT=kT[:, c128], rhs=Sb[:], start=True, stop=True), reads=[kT, Sb], writes=[pa])
                        yield
                        fw.op(fw.dve, lambda e: e.scalar_tensor_tensor(out=r_[:], in0=pa[:, 0:128], scalar=sm[:, 5, ch:ch + 1], in1=vtok[:, ch, :], op0=ALU.mult, op1=ALU.add), reads=[vtok, pa, sm], writes=[r_])
                        yield
                        fw.op(fw.pe, lambda e: e.matmul(pa[:, 128:256], lhsT=TT[:, ch, :], rhs=r_[:], start=True, stop=True), reads=[TT, r_], writes=[pa])
                        yield
                        fw.op(fw.act, lambda e: e.activation(out=v_[:], in_=pa[:, 128:256], func=AF.Copy), reads=[pa], writes=[v_])
                        yield
                        fw.op(fw.act, lambda e: e.activation(out=v2_[:], in_=pa[:, 128:256], func=AF.Copy, scale=sm[:, 3, ch:ch + 1]), reads=[pa, sm], writes=[v2_])
                        yield
                        fw.op(fw.pe, lambda e: e.matmul(po[:, 0:128], lhsT=Sb[:], rhs=qdT[:, c128], start=True, stop=False), reads=[Sb, qdT], writes=[po], inc=False)
                        yield
                        fw.op(fw.pe, lambda e: e.matmul(po[:, 0:128], lhsT=v_[:], rhs=inT[:, c128], start=False, stop=True), reads=[v_, inT], writes=[po])
                        yield
                        fw.op(fw.dve, lambda e: e.tensor_tensor(out=oT[:, c128], in0=oT[:, c128], in1=po[:, 0:128], op=ALU.add), reads=[po, oT], writes=[oT])
                        yield
                        fw.op(fw.pe, lambda e: e.matmul(pa[:, 256:384], lhsT=ktok[:, ch, :], rhs=v2_[:], start=True, stop=True), reads=[ktok, v2_], writes=[pa])
                        yield
                        fw.op(fw.dve, lambda e: e.scalar_tensor_tensor(out=Sb[:], in0=Sf[:], scalar=sm[:, 4, ch:ch + 1], in1=pa[:, 256:384], op0=ALU.mult, op1=ALU.add), reads=[Sf, sm, pa], writes=[Sb])
                        yield
                        fw.op(fw.dve, lambda e: e.scalar_tensor_tensor(out=Sf[:], in0=Sf[:], scalar=sm[:, 4, ch:ch + 1], in1=pa[:, 256:384], op0=ALU.mult, op1=ALU.add), reads=[Sf, sm, pa], writes=[Sf])
                        yield
            dirs_ = [int(v) for v in os.environ.get('DN_DIRS', '0,1').split(',')]
            fw.op(fw.dve, lambda e: e.memset(oT[:], 0.0), writes=[oT])
            BB = [dir_tables(d) for d in dirs_]
            self.interleave([scan_gen(d, B) for d, B in zip(dirs_, BB)])
            self.prot = list(range(8))
            wz = fw.buf([128, 8, 128], BF16, "dn_wz")
            fw.dma(fw.pool, wz[:], wsrc[:, :, 1536 + h * 128:1536 + (h + 1) * 128], writes=[wz])
            sq = [fw.buf([128, 512], BF16, "dn_fsq%d" % i) for i in range(2)]
            rs = [fw.buf([128, 512], F32, "dn_frs%d" % i) for i in range(2)]
            sg = [fw.buf([128, 512], BF16, "dn_fsg%d" % i) for i in range(4)]
            for nt in range(NT):
                pz = self.proj(wz, 0, 128, nt)
                fw.op(fw.act, lambda e: e.activation(out=sg[nt][:], in_=pz[:], func=AF.Silu), reads=[pz], writes=[sg[nt]])
            for nt in range(NT):
                sl = slice(nt * 512, (nt + 1) * 512)
                q_, r_, g_ = sq[nt % 2], rs[nt % 2], sg[nt]
                fw.op(fw.act, lambda e: e.activation(out=q_[:], in_=oT[:, sl], func=AF.Square), reads=[oT], writes=[q_])
                pn = self.nextp()
                fw.op(fw.pe, lambda e: e.matmul(pn[:], lhsT=self.onesb[:], rhs=q_[:], start=True, stop=True), reads=[q_, self.onesb], writes=[pn])
                fw.op(fw.act, lambda e: e.activation(out=r_[:], in_=pn[:], func=AF.Ln, bias=self.eps[:], scale=1.0 / 128), reads=[pn, self.eps], writes=[r_])
                fw.op(fw.act, lambda e: e.activation(out=r_[:], in_=r_[:], func=AF.Exp, scale=-0.5), reads=[r_], writes=[r_])
                fw.op(fw.dve, lambda e: e.scalar_tensor_tensor(out=r_[:], in0=oT[:, sl], scalar=self.dnp[:, 24:25], in1=r_[:], op0=ALU.mult, op1=ALU.mult), reads=[oT, self.dnp, r_], writes=[r_])
                fw.op(fw.dve, lambda e: e.tensor_tensor(out=self.yT_t[:, h, sl], in0=r_[:], in1=g_[:], op=ALU.mult), reads=[r_, g_], writes=[self.yT[h]])
            fw.release(mh, hard=True)
        fw.release(m)

    def interleave(self, gens):
        gens = list(gens)
        while gens:
            for g in list(gens):
                try:
                    next(g)
                except StopIteration:
                    gens.remove(g)

    def load_w(self, wap, c0, ncols, tag):
        fw = self.fw
        w = fw.buf([128, 8, ncols], BF16, tag)
        src = wap.rearrange("(kc p) n -> p kc n", p=128)
        half = ncols // 2 if ncols >= 256 else ncols
        for a in range(0, ncols, half):
            fw.dma(fw.pool, w[:, :, a:a + half], src[:, :, c0 + a:c0 + a + half], writes=[w])
        return w

    def proj(self, w, col, ncols, nt, p=None):
        fw = self.fw
        if p is None:
            p = self.nextp()
        sl = slice(nt * 512, (nt + 1) * 512)
        for kc in range(8):
            fw.op(fw.pe, lambda e: e.matmul(p[0:ncols, :], lhsT=w[:, kc, col:col + ncols], rhs=self.hT_t[:, kc, sl], start=(kc == 0), stop=(kc == 7)),
                  reads=[w, self.hT[kc][nt]], writes=[p], inc=(kc == 7))
        return p

    def out_proj(self, wap):
        fw = self.fw
        m = fw.mark()
        self.prot = list(range(8))
        wo = self.load_w(wap, 0, D, "wo")
        for nt in range(NT):
            for c in range(8):
                sl = slice(nt * 512, (nt + 1) * 512)
                po = self.nextp()
                for kc in range(8):
                    fw.op(fw.pe, lambda e: e.matmul(po[:], lhsT=wo[:, kc, c * 128:(c + 1) * 128], rhs=self.yT_t[:, kc, sl], start=(kc == 0), stop=(kc == 7)),
                          reads=[wo, self.yT[kc]], writes=[po], inc=(kc == 7))
                fw.op(fw.dve, lambda e: e.tensor_tensor(out=self.xT_t[:, c, sl], in0=self.xT_t[:, c, sl], in1=po[:], op=ALU.add),
                      reads=[po, self.xT[c][nt]], writes=[self.xT[c][nt]])
        fw.release(m)

    def mixer_cd(self):
        fw = self.fw
        m0 = fw.mark()
        self.dlt = fw.buf([128, 512], F32, "dlt")
        fw.dma(fw.sp, self.dlt[:], self.dlt_d[:, :], writes=[self.dlt])
        self.cnyrow = fw.buf([1, S], BF16, "cnyrow")
        fw.dma(fw.sp, self.cnyrow[:], self.cnyrow_d[:, :], writes=[self.cnyrow])
        if "ret" in self.stages:
            self.retention()
        else:
            for c in range(4):
                fw.op(fw.dve, lambda e: e.memset(self.yT_t[:, c, :], 0.0), writes=[self.yT[c]])
        if "hy" in self.stages:
            self.hyena()
        else:
            for c in range(4, 8):
                fw.op(fw.dve, lambda e: e.memset(self.yT_t[:, c, :], 0.0), writes=[self.yT[c]])
        fw.release(m0)
        self.out_proj(self.cd_w_out)

    def retention(self):
        fw = self.fw
        W = self.cd_w_in
        m = fw.mark()
        self.prot = list(range(8))
        vtok = fw.buf([128, 16, 512], BF16, "vtok")
        qr = [fw.buf([128, S], BF16, "qr%d" % i) for i in range(2)]
        kr = [fw.buf([128, S], BF16, "kr%d" % i) for i in range(2)]
        m2 = fw.mark()
        wv = self.load_w(W, 512, 512, "wv")
        for tt in range(16):
            p = self.nextp()
            for kc in range(8):
                fw.op(fw.pe, lambda e: e.matmul(p[:], lhsT=self.hT_t[:, kc, tt * 128:(tt + 1) * 128], rhs=wv[:, kc, :], start=(kc == 0), stop=(kc == 7)),
                      reads=[wv, self.hT[kc][tt // 4]], writes=[p], inc=(kc == 7))
            fw.op(fw.act, lambda e: e.activation(out=vtok[:, tt, :], in_=p[:], func=AF.Copy), reads=[p], writes=[vtok])
        fw.release(m2)
        m2 = fw.mark()
        wqk = self.load_w(W, 0, 512, "wqk")
        qsb = [fw.buf([128, 512], BF16, "rqs%d" % i) for i in range(2)]
        t1 = [fw.buf([128, 512], F32, "rt1_%d" % i) for i in range(2)]
        t2 = [fw.buf([128, 512], F32, "rt2_%d" % i) for i in range(2)]
        k = 0
        for which, dst in ((0, qr), (1, kr)):
            for qc in range(2):
                col = which * 256 + qc * 128
                for nt in range(NT):
                    sl = slice(nt * 512, (nt + 1) * 512)
                    p1 = self.proj(wqk, col, 128, nt)
                    qs_ = qsb[k % 2]
                    fw.op(fw.dve, lambda e: e.tensor_copy(out=qs_[:], in_=p1[:]), reads=[p1], writes=[qs_])
                    p2 = self.nextp()
                    fw.op(fw.pe, lambda e: e.matmul(p2[:], lhsT=self.rperm[:], rhs=qs_[:], start=True, stop=True), reads=[self.rperm, qs_], writes=[p2])
                    a, b = t1[k % 2], t2[k % 2]
                    k += 1
                    fw.op(fw.dve, lambda e: e.tensor_tensor(out=a[:], in0=p1[:], in1=self.ropec[:, sl], op=ALU.mult), reads=[p1, self.ropec], writes=[a])
                    fw.op(fw.dve, lambda e: e.tensor_tensor(out=b[:], in0=p2[:], in1=self.ropes[:, sl], op=ALU.mult), reads=[p2, self.ropes], writes=[b])
                    fw.op(fw.pool, lambda e: e.tensor_tensor(out=dst[qc][:, sl], in0=a[:], in1=b[:], op=ALU.add), reads=[a, b], writes=[dst[qc]])
        fw.release(m2)
        ld = self.ld
        Lf = fw.buf([128, 512], BF16, "Lf")
        Lb = fw.buf([128, 512], BF16, "Lb")
        DC = [fw.buf([128, 512], BF16, "DC%d" % r) for r in range(4)]
        fac = fw.buf([128, 32], F32, "fac")
        scb = [fw.buf([128, 512], BF16, "scb%d" % i) for i in range(6)]
        oT = fw.buf([128, S], F32, "oT")
        sqb = [fw.buf([128, 512], BF16, "rsq%d" % i) for i in range(2)]
        rsb = [fw.buf([128, 512], F32, "rrs%d" % i) for i in range(2)]
        ea, eb = rsb[0], rsb[1]
        sgt = [fw.buf([128, 512], BF16, "rsg%d" % i) for i in range(2)]
        wg = fw.buf([128, 8, 128], BF16, "wg")
        wsrc = W.rearrange("(kc p) n -> p kc n", p=128)
        ks = 0
        for h in range(4):
            qc, po = h // 2, (h % 2) * 64
            lgf, lgb = ld[:, h:h + 1], ld[:, 4 + h:5 + h]
            fw.op(fw.act, lambda e: e.activation(out=Lf[:], in_=self.dlt[:], func=AF.Exp, bias=self.ld128[:, h:h + 1], scale=lgf), reads=[self.dlt, self.ld128, ld], writes=[Lf])
            fw.op(fw.act, lambda e: e.activation(out=Lb[:], in_=self.dlt[:], func=AF.Exp, bias=self.ld512[:, 4 + h:5 + h], scale=self.ldn[:, 4 + h:5 + h]), reads=[self.dlt, self.ld512, self.ldn], writes=[Lb])
            fw.op(fw.act, lambda e: e.activation(out=fac[:, 0:16], in_=self.iota16[:], func=AF.Exp, bias=self.ln8[:], scale=lgf), reads=[self.iota16, self.ln8, ld], writes=[fac])
            fw.op(fw.act, lambda e: e.activation(out=fac[:, 16:32], in_=self.iota16[:], func=AF.Exp, bias=self.ln8[:], scale=lgb), reads=[self.iota16, self.ln8, ld], writes=[fac])
            for r in range(4):
                fw.op(fw.dve, lambda e: e.tensor_scalar(out=ea[:], in0=self.dlt[:], scalar1=float(-128 * r), scalar2=0.0, op0=ALU.add, op1=ALU.max), reads=[self.dlt], writes=[ea])
                fw.op(fw.dve, lambda e: e.tensor_scalar(out=ea[:], in0=ea[:], scalar1=lgf, scalar2=None, op0=ALU.mult), reads=[ea, ld], writes=[ea])
                fw.op(fw.dve, lambda e: e.tensor_scalar(out=eb[:], in0=self.dlt[:], scalar1=float(-128 * r), scalar2=0.0, op0=ALU.add, op1=ALU.min), reads=[self.dlt], writes=[eb])
                fw.op(fw.dve, lambda e: e.scalar_tensor_tensor(out=eb[:], in0=eb[:], scalar=self.ldn[:, 4 + h:5 + h], in1=ea[:], op0=ALU.mult, op1=ALU.add), reads=[eb, ea, self.ldn], writes=[eb])
                fw.op(fw.act, lambda e: e.activation(out=DC[r][:], in_=eb[:], func=AF.Exp, bias=self.ln8[:], scale=1.0), reads=[eb, self.ln8], writes=[DC[r]])
            for nj in range(NT):
                pacc = self.P[nj % 2]
                self.prot = list(range(2, 8))
                pend = []

                def pv(sc, mi):
                    fw.op(fw.pe, lambda e: e.matmul(pacc[:], lhsT=vtok[:, mi, h * 128:(h + 1) * 128], rhs=sc[:], start=(mi == 0), stop=(mi == 15)),
                          reads=[vtok, sc], writes=[pacc], inc=True)
                for mi in range(16):
                    ps = self.nextp()
                    fw.op(fw.pe, lambda e: e.matmul(ps[:], lhsT=kr[qc][po:po + 64, mi * 128:(mi + 1) * 128], rhs=qr[qc][po:po + 64, nj * 512:(nj + 1) * 512], start=True, stop=True),
                          reads=[kr[qc], qr[qc]], writes=[ps], inc=True)
                    sc = scb[ks % 6]
                    ks += 1
                    r = mi - 4 * nj
                    if 0 <= r <= 3:
                        fw.op(fw.dve, lambda e: e.tensor_tensor(out=sc[:], in0=ps[:], in1=DC[r][:], op=ALU.mult), reads=[ps, DC[r]], writes=[sc])
                    elif r < 0:
                        kk = -r - 1
                        fw.op(fw.dve, lambda e: e.scalar_tensor_tensor(out=sc[:], in0=ps[:], scalar=fac[:, kk:kk + 1], in1=Lf[:], op0=ALU.mult, op1=ALU.mult), reads=[ps, fac, Lf], writes=[sc])
                    else:
                        kk = r - 4
                        fw.op(fw.dve, lambda e: e.scalar_tensor_tensor(out=sc[:], in0=ps[:], scalar=fac[:, 16 + kk:17 + kk], in1=Lb[:], op0=ALU.mult, op1=ALU.mult), reads=[ps, fac, Lb], writes=[sc])
                    pend.append((sc, mi))
                    if len(pend) > 4:
                        pv(*pend.pop(0))
                while pend:
                    pv(*pend.pop(0))
                fw.op(fw.act, lambda e: e.activation(out=oT[:, nj * 512:(nj + 1) * 512], in_=pacc[:], func=AF.Copy), reads=[pacc], writes=[oT])
            self.prot = list(range(2, 8))
            fw.dma(fw.pool, wg[:], wsrc[:, :, 1024 + h * 128:1024 + (h + 1) * 128], writes=[wg])
            for nt in range(NT):
                sl = slice(nt * 512, (nt + 1) * 512)
                sq, rs, sg = sqb[nt % 2], rsb[nt % 2], sgt[nt % 2]
                tb = rs
                fw.op(fw.act, lambda e: e.activation(out=sq[:], in_=oT[:, sl], func=AF.Square), reads=[oT], writes=[sq])
                pn = self.nextp()
                fw.op(fw.pe, lambda e: e.matmul(pn[:], lhsT=self.onesb[:], rhs=sq[:], start=True, stop=True), reads=[sq, self.onesb], writes=[pn])
                fw.op(fw.act, lambda e: e.activation(out=rs[:], in_=pn[:], func=AF.Ln, bias=self.eps[:], scale=1.0 / 128), reads=[pn, self.eps], writes=[rs])
                fw.op(fw.act, lambda e: e.activation(out=rs[:], in_=rs[:], func=AF.Exp, scale=-0.5), reads=[rs], writes=[rs])
                pg = self.proj(wg, 0, 128, nt)
                fw.op(fw.act, lambda e: e.activation(out=sg[:], in_=pg[:], func=AF.Silu), reads=[pg], writes=[sg])
                fw.op(fw.dve, lambda e: e.tensor_tensor(out=tb[:], in0=oT[:, sl], in1=rs[:], op=ALU.mult), reads=[oT, rs], writes=[tb])
                fw.op(fw.dve, lambda e: e.tensor_tensor(out=self.yT_t[:, h, sl], in0=tb[:], in1=sg[:], op=ALU.mult), reads=[tb, sg], writes=[self.yT[h]])
        self.prot = list(range(8))
        fw.release(m)


    def sin_rr(self, dst, src_ps, b_ap, f_ap, rows, tmpf, tmpi):
        fw = self.fw
        R = slice(0, rows)
        fw.op(fw.dve, lambda e: e.tensor_scalar(out=tmpf[0][R, :], in0=src_ps[R, :], scalar1=b_ap, scalar2=f_ap, op0=ALU.add, op1=ALU.mult),
              reads=[src_ps, self.hyp], writes=[tmpf[0]])
        fw.op(fw.dve, lambda e: e.tensor_scalar(out=tmpf[1][R, :], in0=tmpf[0][R, :], scalar1=1.0 / (2 * math.pi), scalar2=None, op0=ALU.mult),
              reads=[tmpf[0]], writes=[tmpf[1]])
        fw.op(fw.dve, lambda e: e.tensor_copy(out=tmpi[R, :], in_=tmpf[1][R, :]), reads=[tmpf[1]], writes=[tmpi])
        fw.op(fw.dve, lambda e: e.tensor_copy(out=tmpf[1][R, :], in_=tmpi[R, :]), reads=[tmpi], writes=[tmpf[1]])
        fw.op(fw.dve, lambda e: e.scalar_tensor_tensor(out=tmpf[0][R, :], in0=tmpf[1][R, :], scalar=-2 * math.pi, in1=tmpf[0][R, :], op0=ALU.mult, op1=ALU.add),
              reads=[tmpf[0], tmpf[1]], writes=[tmpf[0]])
        fw.op(fw.dve, lambda e: e.tensor_scalar(out=tmpf[0][R, :], in0=tmpf[0][R, :], scalar1=3.14159, scalar2=-3.14159, op0=ALU.min, op1=ALU.max),
              reads=[tmpf[0]], writes=[tmpf[0]])
        fw.op(fw.act, lambda e: e.activation(out=dst, in_=tmpf[0][R, :], func=AF.Sin), reads=[tmpf[0]], writes=[self.hidb])

    def fwd_dft(self, inC, inS, inCb, inSb, consume):
        fw = self.fw
        tabC = [fw.buf([128, 8, 128], BF16, "ftabC%d" % i) for i in range(2)]
        tabS = [fw.buf([128, 8, 128], BF16, "ftabS%d" % i) for i in range(2)]
        self.prot = [4, 5, 6, 7]
        for ft in range(16):
            csrc = self.dftcf[ft].rearrange("p (tt f) -> p tt f", tt=16)
            ssrc = self.dftsf[ft].rearrange("p (tt f) -> p tt f", tt=16)
            for hf in range(2):
                fw.dma(fw.sp, tabC[hf][:], csrc[:, hf * 8:hf * 8 + 8, :], writes=[tabC[hf]])
            pr = self.nextp()
            for tt in range(16):
                tb_ = tabC[tt // 8]
                fw.op(fw.pe, lambda e: e.matmul(pr[:], lhsT=tb_[:, tt % 8, :], rhs=inC[:, tt, :], start=(tt == 0), stop=(tt == 15)),
                      reads=[tb_, inCb], writes=[pr], inc=(tt % 8 == 7))
            for hf in range(2):
                fw.dma(fw.act, tabS[hf][:], ssrc[:, hf * 8:hf * 8 + 8, :], writes=[tabS[hf]])
            pi = self.nextp()
            for tt in range(16):
                tb_ = tabS[tt // 8]
                fw.op(fw.pe, lambda e: e.matmul(pi[:], lhsT=tb_[:, tt % 8, :], rhs=inS[:, tt, :], start=(tt == 0), stop=(tt == 15)),
                      reads=[tb_, inSb], writes=[pi], inc=(tt % 8 == 7))
            consume(ft, pr, pi)
        pr = self.nextp()
        for tt in range(16):
            fw.op(fw.pe, lambda e: e.matmul(pr[0:1, :], lhsT=self.cnycol[:, tt:tt + 1], rhs=inC[:, tt, :], start=(tt == 0), stop=(tt == 15)),
                  reads=[self.cnycol, inCb], writes=[pr], inc=(tt == 15))
        consume(16, pr, None)

    def hyena_setup(self):
        fw = self.fw
        m = fw.mark()
        self.prot = list(range(4))
        self.hidb = fw.buf([64, 2, S], F32, "hid")
        w3 = fw.buf([64, 1024], F32, "hw3")
        fw.dma(fw.sp, w3[:], self.hy_w3[:, :], writes=[w3])
        ma = fw.mark()
        zT = fw.buf([33, S], F32, "zT")
        fw.dma(fw.sp, zT[:], self.hyz[:, :], writes=[zT])
        w1 = fw.buf([33, 64], F32, "hw1")
        fw.dma(fw.sp, w1[:], self.hy_w1[:, :], writes=[w1])
        w2 = fw.buf([64, 64], F32, "hw2")
        fw.dma(fw.sp, w2[:], self.hy_w2[:, :], writes=[w2])
        hid = self.hidb
        tmpf = [fw.buf([64, 512], F32, "stf%d" % i) for i in range(2)]
        tmpi = fw.buf([64, 512], I32, "sti")
        hp = self.hyp
        for lyr in range(2):
            for nt in range(NT):
                sl = slice(nt * 512, (nt + 1) * 512)
                p = self.nextp()
                if lyr == 0:
                    fw.op(fw.pe, lambda e: e.matmul(p[0:64, :], lhsT=w1[0:33, :], rhs=zT[0:33, sl], start=True, stop=True), reads=[w1, zT], writes=[p])
                else:
                    fw.op(fw.pe, lambda e: e.matmul(p[0:64, :], lhsT=w2[0:64, :], rhs=hid[0:64, 0, sl], start=True, stop=True), reads=[w2, hid], writes=[p])
                self.sin_rr(hid[0:64, lyr, sl], p, hp[0:64, 2 * lyr:2 * lyr + 1], hp[0:64, 2 * lyr + 1:2 * lyr + 2], 64, tmpf, tmpi)
        fw.release(ma)
        hs = self.hT_t[:, 0:4, :].rearrange("p c (a b) -> p (c a) b", b=512)
        hd = self.hT_t[:, 4:8, :].rearrange("p c (a b) -> p (c a) b", b=512)
        hsb = fw.view(self.hT_t, "hsb", alias=[self.hT[c][n] for c in range(8) for n in range(NT)])
        dec = [fw.buf([128, 512], F32, "dec%d" % i) for i in range(2)]
        hfb = [fw.buf([128, 512], F32, "hfb%d" % i) for i in range(2)]
        hbb = [fw.buf([128, 512], F32, "hbb%d" % i) for i in range(2)]
        for tt in range(16):
            d_, hf, hb = dec[tt % 2], hfb[tt % 2], hbb[tt % 2]
            fw.dma(fw.sp, d_[:], self.hydec[tt * 128:(tt + 1) * 128, :], writes=[d_])
            pf = self.nextp()
            fw.op(fw.pe, lambda e: e.matmul(pf[:], lhsT=hid[0:64, 1, tt * 128:(tt + 1) * 128], rhs=w3[0:64, 0:512], start=True, stop=True), reads=[hid, w3], writes=[pf])
            pb = self.nextp()
            fw.op(fw.pe, lambda e: e.matmul(pb[:], lhsT=hid[0:64, 1, tt * 128:(tt + 1) * 128], rhs=w3[0:64, 512:1024], start=True, stop=True), reads=[hid, w3], writes=[pb])
            fw.op(fw.dve, lambda e: e.tensor_tensor(out=hf[:], in0=pf[:], in1=d_[:], op=ALU.mult), reads=[pf, d_], writes=[hf])
            fw.op(fw.dve, lambda e: e.tensor_tensor(out=hb[:], in0=pb[:], in1=d_[:], op=ALU.mult), reads=[pb, d_], writes=[hb])
            fw.op(fw.dve, lambda e: e.tensor_tensor(out=hs[:, tt, :], in0=hf[:], in1=hb[:], op=ALU.add), reads=[hf, hb], writes=[hsb])
            fw.op(fw.dve, lambda e: e.tensor_tensor(out=hd[:, tt, :], in0=hf[:], in1=hb[:], op=ALU.subtract), reads=[hf, hb], writes=[hsb])
        fw.release(m)
        m = fw.mark()
        so = [fw.buf([128, 2, 512], F32, "so%d" % i) for i in range(2)]

        def consume(ft, pr, pi):
            o = so[ft % 2]
            if ft < 16:
                fw.op(fw.dve, lambda e: e.tensor_scalar(out=o[:, 0, :], in0=pr[:], scalar1=self.wf[:, ft:ft + 1], scalar2=None, op0=ALU.mult), reads=[pr, self.wf], writes=[o])
                fw.op(fw.dve, lambda e: e.tensor_scalar(out=o[:, 1, :], in0=pi[:], scalar1=self.wf[:, ft:ft + 1], scalar2=None, op0=ALU.mult), reads=[pi, self.wf], writes=[o])
                fw.dma(fw.sp, self.spec_d[:, ft * 128:(ft + 1) * 128, :].rearrange("a p c -> p a c"), o[:], reads=[o])
            else:
                fw.op(fw.dve, lambda e: e.tensor_scalar(out=o[0:1, 0, :], in0=pr[0:1, :], scalar1=self.wf[0:1, 16:17], scalar2=None, op0=ALU.mult), reads=[pr, self.wf], writes=[o])
                fw.dma(fw.sp, self.spec_d[0:1, 2048:2049, :].rearrange("a p c -> p a c"), o[0:1, 0:1, :], reads=[o])
        self.fwd_dft(hs, hd, hsb, hsb, consume)
        self.prot = list(range(8))
        fw.release(m, hard=True)

    def hyena(self):
        fw = self.fw
        W = self.cd_w_in
        wsrc = W.rearrange("(kc p) n -> p kc n", p=128)
        m = fw.mark()
        x0T = fw.buf([128, 4, S], BF16, "x0T")
        utok = fw.buf([128, 16, 512], BF16, "utok")
        m2 = fw.mark()
        self.prot = list(range(8))
        gb = [fw.buf([128, S + 2], F32, "hgb%d" % i) for i in range(1)]
        for g in gb:
            fw.op(fw.dve, lambda e: e.memset(g[:, 0:1], 0.0), writes=[g])
            fw.op(fw.dve, lambda e: e.memset(g[:, S + 1:S + 2], 0.0), writes=[g])
        a1 = fw.buf([128, S], F32, "ha1")
        a2 = fw.buf([128, S], F32, "ha2")
        wp = [fw.buf([128, 8, 128], BF16, "hwp%d" % i) for i in range(2)]
        hcw = self.hcw
        k = 0
        for cc in range(4):
            for part in (1, 2, 0):
                ch = part * 4 + cc
                w = wp[k % 2]
                g = gb[0]
                k += 1
                fw.dma(fw.pool, w[:], wsrc[:, :, 1536 + ch * 128:1536 + (ch + 1) * 128], writes=[w])
                for nt in range(NT):
                    p = self.proj(w, 0, 128, nt)
                    fw.op(fw.act, lambda e: e.activation(out=g[:, 1 + nt * 512:1 + (nt + 1) * 512], in_=p[:], func=AF.Copy), reads=[p], writes=[g])
                dst = a1 if part == 1 else a2
                fw.op(fw.dve, lambda e: e.tensor_scalar(out=dst[:], in0=g[:, 0:S], scalar1=hcw[:, ch, 0:1], scalar2=hcw[:, ch, 3:4], op0=ALU.mult, op1=ALU.add), reads=[g, hcw], writes=[dst])
                fw.op(fw.dve, lambda e: e.scalar_tensor_tensor(out=dst[:], in0=g[:, 1:S + 1], scalar=hcw[:, ch, 1:2], in1=dst[:], op0=ALU.mult, op1=ALU.add), reads=[g, hcw, dst], writes=[dst])
                if part == 1:
                    fw.op(fw.dve, lambda e: e.scalar_tensor_tensor(out=dst[:], in0=g[:, 2:S + 2], scalar=hcw[:, ch, 2:3], in1=dst[:], op0=ALU.mult, op1=ALU.add), reads=[g, hcw, dst], writes=[dst])
                elif part == 2:
                    fw.op(fw.dve, lambda e: e.scalar_tensor_tensor(out=dst[:], in0=g[:, 2:S + 2], scalar=hcw[:, ch, 2:3], in1=dst[:], op0=ALU.mult, op1=ALU.add), reads=[g, hcw, dst], writes=[dst])
                    fw.op(fw.dve, lambda e: e.tensor_tensor(out=self.yT_t[:, 4 + cc, :], in0=a1[:], in1=a2[:], op=ALU.mult), reads=[a1, a2], writes=[self.yT[4 + cc]])
                    for tq in range(4):
                        p = self.nextp()
                        pb = p[:].bitcast(BF16)
                        for t4 in range(4):
                            tt = tq * 4 + t4
                            fw.op(fw.pe, lambda e: e.transpose(pb[:, t4 * 128:(t4 + 1) * 128], self.yT_t[:, 4 + cc, tt * 128:(tt + 1) * 128], self.identb[:]),
                                  reads=[self.yT[4 + cc], self.identb], writes=[p], inc=(t4 == 3))
                        fw.op(fw.act, lambda e: e.activation(out=utok[:, tq * 4:tq * 4 + 4, cc * 128:(cc + 1) * 128], in_=pb[:, 0:512].rearrange("p (a b) -> p a b", a=4), func=AF.Copy),
                              reads=[p], writes=[utok])
                else:
                    fw.op(fw.dve, lambda e: e.scalar_tensor_tensor(out=x0T[:, cc, :], in0=g[:, 2:S + 2], scalar=hcw[:, ch, 2:3], in1=dst[:], op0=ALU.mult, op1=ALU.add), reads=[g, hcw, dst], writes=[x0T])
        fw.release(m2)
        Yr_t = self.hT_t[:, 0:4, :].rearrange("p c (a b) -> p (c a) b", b=512)
        Yi_t = self.hT_t[:, 4:8, :].rearrange("p c (a b) -> p (c a) b", b=512)
        Y = fw.view(self.hT_t, "Yspec", alias=[self.hT[c][n] for c in range(8) for n in range(NT)])
        Yny = fw.buf([1, 512], BF16, "Yny")
        m3 = fw.mark()
        spt = [fw.buf([128, 2, 512], F32, "spt%d" % i) for i in range(2)]
        tA = [fw.buf([128, 512], F32, "tA%d" % i) for i in range(2)]
        tB = [fw.buf([128, 512], F32, "tB%d" % i) for i in range(2)]

        def consume(ft, pr, pi):
            sp_ = spt[ft % 2]
            A, B = tA[ft % 2], tB[ft % 2]
            if ft < 16:
                fw.dma(fw.sp, sp_[:], self.spec_d[:, ft * 128:(ft + 1) * 128, :].rearrange("a p c -> p a c"), writes=[sp_])
                fw.op(fw.dve, lambda e: e.tensor_tensor(out=A[:], in0=pr[:], in1=sp_[:, 0, :], op=ALU.mult), reads=[pr, sp_], writes=[A])
                fw.op(fw.dve, lambda e: e.tensor_tensor(out=B[:], in0=pi[:], in1=sp_[:, 1, :], op=ALU.mult), reads=[pi, sp_], writes=[B])
                fw.op(fw.pool, lambda e: e.tensor_tensor(out=Yr_t[:, ft, :], in0=A[:], in1=B[:], op=ALU.subtract), reads=[A, B], writes=[Y])
                A2, B2 = tA[(ft + 1) % 2], tB[(ft + 1) % 2]
                fw.op(fw.dve, lambda e: e.tensor_tensor(out=A2[:], in0=pr[:], in1=sp_[:, 1, :], op=ALU.mult), reads=[pr, sp_], writes=[A2])
                fw.op(fw.dve, lambda e: e.tensor_tensor(out=B2[:], in0=pi[:], in1=sp_[:, 0, :], op=ALU.mult), reads=[pi, sp_], writes=[B2])
                fw.op(fw.pool, lambda e: e.tensor_tensor(out=Yi_t[:, ft, :], in0=A2[:], in1=B2[:], op=ALU.add), reads=[A2, B2], writes=[Y])
            else:
                fw.dma(fw.sp, sp_[0:1, 0:1, :], self.spec_d[0:1, 2048:2049, :].rearrange("a p c -> p a c"), writes=[sp_])
                fw.op(fw.dve, lambda e: e.tensor_tensor(out=Yny[0:1, :], in0=pr[0:1, :], in1=sp_[0:1, 0, :], op=ALU.mult), reads=[pr, sp_], writes=[Yny])
        self.fwd_dft(utok[:, :, :], utok[:, :, :], utok, utok, consume)
        fw.release(m3)
        itC = [fw.buf([128, 4, 512], BF16, "itC%d" % i) for i in range(2)]
        itS = [fw.buf([128, 4, 512], BF16, "itS%d" % i) for i in range(2)]
        csrc = self.dftc.rearrange("(ft p) t -> p ft t", p=128)
        ssrc = self.dfts.rearrange("(ft p) t -> p ft t", p=128)
        ep = [fw.buf([128, 512], F32, "hep%d" % i) for i in range(2)]
        kk = 0
        ke = 0
        for nt in range(NT):
            sl = slice(nt * 512, (nt + 1) * 512)
            banks = [self.P[(nt % 2) * 4 + cc] for cc in range(4)]
            for fg in range(4):
                tc_, ts_ = itC[kk % 2], itS[kk % 2]
                kk += 1
                fw.dma(fw.sp, tc_[:], csrc[:, fg * 4:fg * 4 + 4, sl], writes=[tc_])
                fw.dma(fw.act, ts_[:], ssrc[:, fg * 4:fg * 4 + 4, sl], writes=[ts_])
                for f4 in range(4):
                    ft = fg * 4 + f4
                    for cc in range(4):
                        fw.op(fw.pe, lambda e: e.matmul(banks[cc][:], lhsT=Yr_t[:, ft, cc * 128:(cc + 1) * 128], rhs=tc_[:, f4, :], start=(ft == 0), stop=False),
                              reads=[Y, tc_], writes=[banks[cc]], inc=False)
                        fw.op(fw.pe, lambda e: e.matmul(banks[cc][:], lhsT=Yi_t[:, ft, cc * 128:(cc + 1) * 128], rhs=ts_[:, f4, :], start=False, stop=False),
                              reads=[Y, ts_], writes=[banks[cc]], inc=(cc == 3 and f4 == 3))
            for cc in range(4):
                fw.op(fw.pe, lambda e: e.matmul(banks[cc][:], lhsT=Yny[0:1, cc * 128:(cc + 1) * 128], rhs=self.cnyrow[0:1, sl], start=False, stop=True),
                      reads=[Yny, self.cnyrow], writes=[banks[cc]], inc=True)
                t_ = ep[ke % 2]
                ke += 1
                fw.op(fw.dve, lambda e: e.scalar_tensor_tensor(out=t_[:], in0=self.yT_t[:, 4 + cc, sl], scalar=self.hybias[:, cc:cc + 1], in1=banks[cc][:], op0=ALU.mult, op1=ALU.add),
                      reads=[self.yT[4 + cc], self.hybias, banks[cc]], writes=[t_])
                fw.op(fw.dve, lambda e: e.tensor_tensor(out=self.yT_t[:, 4 + cc, sl], in0=t_[:], in1=x0T[:, cc, sl], op=ALU.mult), reads=[t_, x0T], writes=[self.yT[4 + cc]])
        self.prot = list(range(8))
        fw.release(m, hard=True)


def host_consts():
    c = {"ident": np.eye(128, dtype=np.float32)}
    inv = 10000.0 ** (-np.arange(0, 64, 2, dtype=np.float64) / 64)
    ang = np.arange(S, dtype=np.float64)[None, :] * inv[:, None]
    cos64 = np.concatenate([np.cos(ang), np.cos(ang)], 0)
    sin64 = np.concatenate([-np.sin(ang), np.sin(ang)], 0)
    c["ropec"] = np.concatenate([cos64, cos64], 0).astype(ml_dtypes.bfloat16)
    c["ropes"] = np.concatenate([sin64, sin64], 0).astype(ml_dtypes.bfloat16)
    pm = np.zeros((128, 128), np.float32)
    pm[rot_perm(128, 64), np.arange(128)] = 1.0
    c["rperm"] = pm.astype(ml_dtypes.bfloat16)
    c["dlt"] = (np.arange(512, dtype=np.float32)[None, :] - np.arange(128, dtype=np.float32)[:, None]).astype(np.float32)
    c["iota16"] = np.tile(128.0 * np.arange(16, dtype=np.float32)[None, :], (128, 1)).astype(np.float32)
    dl = np.arange(2944)[None, :] - np.arange(128)[:, None] - 1408
    ad = np.abs(dl)
    mult = (ad <= 64).astype(np.float32) + ((dl % 4 == 0) & (ad <= 256)) + ((dl % 16 == 0) & (ad <= 1024))
    c["dilmask"] = mult.astype(ml_dtypes.bfloat16)
    ii = np.arange(128)
    triF = (ii[:, None] <= ii[None, :]).astype(np.float32)
    triB = (ii[:, None] >= ii[None, :]).astype(np.float32)
    c["dn_tri"] = np.ascontiguousarray(np.stack([triF, triB], 1))
    mL = np.where(ii[None, :] <= ii[:, None], 0.0, -30000.0).astype(np.float32)
    mU = np.where(ii[None, :] >= ii[:, None], 0.0, -30000.0).astype(np.float32)
    noti = (1.0 - np.eye(128)).astype(np.float32)
    c["dn_msk"] = np.ascontiguousarray(np.stack([np.tile(mL, (1, 4)), np.tile(mU, (1, 4)), np.tile(noti, (1, 4))], 1)).astype(ml_dtypes.bfloat16)
    L = S
    t = np.linspace(0.0, 1.0, L, dtype=np.float32)[:, None]
    w = (2.0 * math.pi * np.arange(L, dtype=np.float32) / L).astype(np.float32)
    fb = np.linspace(1e-4, 15, 16, dtype=np.float32)
    angz = w[:, None] * fb[None, :]
    z = np.concatenate([t, np.cos(angz), -np.sin(angz)], -1).astype(np.float32)
    c["hyz"] = np.ascontiguousarray(z.T)
    deltas = np.abs(np.linspace(math.log(1e-2) / 1.5, math.log(1e-2) / 0.3, 512, dtype=np.float32))
    c["hydec"] = np.exp(-t * deltas[None, :]).astype(np.float32)
    ab = (np.arange(S, dtype=np.int64)[:, None] * np.arange(S, dtype=np.int64)[None, :]) % (2 * S)
    th = ab.astype(np.float64) * (2.0 * math.pi / (2 * S))
    c["dftc"] = np.cos(th).astype(ml_dtypes.bfloat16)
    c["dfts"] = np.sin(th).astype(ml_dtypes.bfloat16)
    c["dftcf"] = np.ascontiguousarray(c["dftc"].reshape(16, 128, 16, 128).transpose(2, 1, 0, 3).reshape(16, 128, 2048))
    c["dftsf"] = np.ascontiguousarray(c["dfts"].reshape(16, 128, 16, 128).transpose(2, 1, 0, 3).reshape(16, 128, 2048))
    sign = (1.0 - 2.0 * (np.arange(S) % 2)).astype(np.float32)
    c["cnycol"] = np.ascontiguousarray(sign.reshape(16, 128).T).astype(ml_dtypes.bfloat16)
    c["cnyrow"] = sign.reshape(1, S).astype(ml_dtypes.bfloat16)
    wf = np.full((128, 17), 2.0 / (2 * S), dtype=np.float32)
    wf[0, 0] = 1.0 / (2 * S)
    wf[:, 16] = 1.0 / (2 * S)
    c["wf"] = wf
    return c


def rot_perm(ncols, hd):
    idx = np.arange(ncols)
    h, i = idx // hd, idx % hd
    return h * hd + (i + hd // 2) % hd


def host_layout(inputs):
    f = lambda a: np.ascontiguousarray(a, dtype=np.float32)
    o = {}
    for k in ("norm_mix", "norm_ffn", "final_norm", "ffn_conv_w", "ffn_conv_b", "ffn_w_out"):
        o[k] = f(inputs[k])
    wi = np.asarray(inputs["ffn_w_in"], dtype=np.float32).reshape(2, 8, 128, 2, 22, 128)
    o["ffn_w_in_t"] = f(wi.transpose(0, 4, 2, 1, 3, 5).reshape(2, 22, 128, 2048))
    ab = f(inputs["ab_w_in"][0])
    o["ab_w_in"] = ab
    o["ab_w_rot"] = f(ab[:, 2064:2064 + 1024][:, rot_perm(1024, 64)])
    o["ab_w_out"] = f(inputs["ab_w_out"][0])
    dnp = np.zeros((128, 25), np.float32)
    dnp[:, 0:8] = inputs["dn_a_log"][0].reshape(1, 8)
    dnp[:, 8:16] = inputs["dn_dt_bias"][0].reshape(1, 8)
    dnp[:, 24] = inputs["dn_norm_w"][0]
    o["dnp"] = dnp
    o["dcw"] = f(inputs["dn_conv_w"][0].reshape(3, 12, 128).transpose(2, 1, 0))
    cd = f(inputs["cd_w_in"][0])
    o["cd_w_in"] = cd
    o["cd_w_rot"] = f(cd[:, 0:512][:, rot_perm(512, 64)])
    o["cd_w_out"] = f(inputs["cd_w_out"][0])
    o["ret_log_decay"] = f(inputs["ret_log_decay"][0].reshape(8))
    o["hy_w1"] = f(inputs["hy_w1"][0])
    o["hy_w2"] = f(inputs["hy_w2"][0])
    o["hy_w3"] = f(inputs["hy_w3"][0])
    o["hyp"] = f(np.stack([inputs["hy_b1"][0], inputs["hy_f1"][0], inputs["hy_b2"][0], inputs["hy_f2"][0]], 1))
    hc = np.concatenate([inputs["hy_conv_w"][0], inputs["hy_conv_b"][0][None, :]], 0)
    o["hcw"] = f(hc.reshape(4, 12, 128).transpose(2, 1, 0))
    o["hybias"] = f(inputs["hy_bias"][0].reshape(4, 128).T)
    return o


_CACHE = {}


def run(inputs, nseq_per_core=4, ncores=NCORES, stages=("ab", "dn", "dil", "ffn0", "cd", "ret", "hy", "ffn1")):
    key = (nseq_per_core, tuple(stages))
    prog = Prog(nseq_per_core, stages)
    nc = prog.build()
    consts = host_consts()
    lay = host_layout(inputs)
    x = np.ascontiguousarray(inputs["x"], dtype=np.float32)
    in_maps = []
    for c in range(ncores):
        m = {}
        for name in prog.dram:
            if name == "x":
                m[name] = x[c * nseq_per_core:(c + 1) * nseq_per_core]
            elif name in consts:
                m[name] = consts[name]
            else:
                m[name] = lay[name]
        in_maps.append(m)
    res = run_bass_kernel_spmd(nc, in_maps, core_ids=list(range(ncores)))
    return np.concatenate([res.results[c]["out"] for c in range(ncores)], axis=0)


def kernel(**inputs):
    return run(inputs).astype(np.float32)
```

```python
import math
import numpy as np
import ml_dtypes
import concourse.bass as bass
import concourse.mybir as mybir
from concourse.bass_utils import run_bass_kernel_spmd

F32 = mybir.dt.float32
BF16 = mybir.dt.bfloat16
I32 = mybir.dt.int32
AF = mybir.ActivationFunctionType
ALU = mybir.AluOpType
AX = mybir.AxisListType

S = 2048
D = 1024
NT = 4
DFF = 2816
NCORES = 8


class Buf:
    __slots__ = ("t", "writer", "readers", "dsem", "dval", "name", "depth", "pver", "bdepth")

    def __init__(self, t, name=""):
        self.t = t
        self.writer = None
        self.readers = {}
        self.dsem = None
        self.dval = 0
        self.name = name
        self.pver = 0
        self.bdepth = 0

    def __getitem__(self, idx):
        return self.t[idx]


class Eng:
    def __init__(self, h, sem, name):
        self.applied = 0
        self.h = h
        self.sem = sem
        self.cnt = 0
        self.seen = {}
        self.name = name


class FW:
    def __init__(self, nc):
        self.nc = nc
        self._ctx = []
        self._semctx = []
        self.dsems = []
        self.pe = Eng(nc.tensor, self.sem("pe"), "pe")
        self.act = Eng(nc.scalar, self.sem("act"), "act")
        self.dve = Eng(nc.vector, self.sem("dve"), "dve")
        self.pool = Eng(nc.gpsimd, self.sem("pool"), "pool")
        self.sp = Eng(nc.sync, self.sem("sp"), "sp")
        self.engs = (self.pe, self.act, self.dve, self.pool, self.sp)
        self.out_stamps = []
        self.dma_bufs = []
        self.free_dsems = []
        self.live = []
        self.gp = {}
        self.gpv = 0
        self.gp_snap = {0: []}
        self.nds = 0
        self.nps = 0

    def sem(self, name):
        cm = self.nc.semaphore(name)
        s = cm.__enter__()
        self._semctx.append(cm)
        return s

    def sb(self, shape, dt, name):
        self.nps += 1
        name = "%s_u%d" % (name, self.nps)
        cm = self.nc.sbuf_tensor(name, list(shape), dt)
        t = cm.__enter__()
        self._ctx.append(cm)
        return t

    def ps(self, shape, dt, name):
        cm = self.nc.psum_tensor(name, list(shape), dt)
        t = cm.__enter__()
        self._ctx.append(cm)
        return t

    def buf(self, shape, dt, name):
        b = Buf(self.sb(shape, dt, name), name)
        b.pver = self.gpv
        b.bdepth = len(self._ctx)
        self.live.append(b)
        return b

    def view(self, t, name="", alias=()):
        b = Buf(t, name)
        if alias:
            for a in alias:
                self._merge(a)
            self._snap()
        b.pver = self.gpv
        b.bdepth = len(self._ctx) + 1
        self.live.append(b)
        return b

    def _merge(self, b):
        st = list(b.readers.values())
        if b.writer is not None:
            st.append(b.writer)
        for (sm, v) in st:
            k = id(sm)
            if k not in self.gp or self.gp[k][1] < v:
                self.gp[k] = (sm, v)

    def _snap(self):
        self.gpv += 1
        self.gp_snap[self.gpv] = list(self.gp.values())

    def mark(self):
        return len(self._ctx)

    def release(self, mark, hard=False):
        if hard:
            self.barrier()
        kl = []
        for b in self.live:
            if b.bdepth > mark:
                self._merge(b)
            else:
                kl.append(b)
        self.live = kl
        self._snap()
        keep = []
        for tb in self.dsems:
            if tb.depth > mark:
                self.free_dsems.append((tb.dsem, tb.dval))
            else:
                keep.append(tb)
        self.dsems = keep
        while len(self._ctx) > mark:
            self._ctx.pop().__exit__(None, None, None)

    def close(self):
        while self._ctx:
            self._ctx.pop().__exit__(None, None, None)
        while self._semctx:
            self._semctx.pop().__exit__(None, None, None)

    def _wait(self, E, sem, val):
        k = id(sem)
        if E.seen.get(k, 0) >= val:
            return
        E.h.wait_ge(sem, val)
        E.seen[k] = val

    def _deps(self, E, reads, writes, own_dsem=None):
        pv = 0
        for b in reads:
            if b.pver > pv:
                pv = b.pver
        for b in writes:
            if b.pver > pv:
                pv = b.pver
        if pv > E.applied:
            for (sm, v) in self.gp_snap[pv]:
                self._wait(E, sm, v)
            E.applied = pv
        for b in reads:
            if b.writer is not None:
                s, v = b.writer
                if s is E.sem and E is self.pe:
                    continue
                self._wait(E, s, v)
        for b in writes:
            if b.writer is not None:
                s, v = b.writer
                if not (s is E.sem and E is self.pe) and s is not own_dsem:
                    self._wait(E, s, v)
            for k, (s, v) in b.readers.items():
                if s is E.sem:
                    continue
                self._wait(E, s, v)

    def op(self, E, issue, reads=(), writes=(), inc=True):
        self._deps(E, reads, writes)
        ins = issue(E.h)
        if inc:
            E.cnt += 1
            ins.then_inc(E.sem, 1)
            stamp = (E.sem, E.cnt)
        else:
            stamp = (E.sem, E.cnt + 1)
        for b in reads:
            b.readers[id(E.sem)] = stamp
        for b in writes:
            b.writer = stamp
            b.readers = {}
        return ins

    def dma(self, Q, out_ap, in_ap, reads=(), writes=(), is_output=False, **kw):
        tb = writes[0] if writes else reads[0]
        if tb.dsem is None:
            if self.free_dsems:
                tb.dsem, tb.dval = self.free_dsems.pop()
            else:
                self.nds += 1
                tb.dsem = self.sem("d%d" % self.nds)
            tb.depth = len(self._ctx)
            self.dsems.append(tb)
        self._deps(Q, reads, writes, own_dsem=tb.dsem)
        tb.dval += 16
        Q.h.dma_start(out=out_ap, in_=in_ap, **kw).then_inc(tb.dsem, 16)
        stamp = (tb.dsem, tb.dval)
        for b in reads:
            b.readers[id(tb.dsem)] = stamp
        for b in writes:
            b.writer = stamp
            b.readers = {}
        if is_output:
            self.out_stamps.append(stamp)

    def barrier(self):
        for E in self.engs:
            for X in self.engs:
                if X is not E and X.cnt:
                    self._wait(E, X.sem, X.cnt)
            for tb in self.dsems:
                if tb.dval:
                    self._wait(E, tb.dsem, tb.dval)

    def finish(self):
        for s, v in self.out_stamps:
            self._wait(self.sp, s, v)
        for E in (self.pe, self.act, self.dve, self.pool):
            if E.cnt:
                self._wait(self.sp, E.sem, E.cnt)


class Prog:
    def __init__(self, nseq, stages):
        self.nseq = nseq
        self.stages = stages
        self.nc = bass.Bass("TRN2", target_bir_lowering=False)
        self.fw = FW(self.nc)
        self.dram = {}

    def din(self, name, shape, dt=F32):
        t = self.nc.dram_tensor(name, list(shape), dt, kind="ExternalInput").ap()
        self.dram[name] = t
        return t

    def nextp(self):
        p = self.P[self.prot[self.pi % len(self.prot)]]
        self.pi += 1
        return p

    def build(self):
        nc, fw = self.nc, self.fw
        nseq = self.nseq
        x_d = self.din("x", [nseq, S, D])
        out_d = nc.dram_tensor("out", [nseq, S, D], F32, kind="ExternalOutput").ap()
        norm_mix = self.din("norm_mix", [2, D])
        norm_ffn = self.din("norm_ffn", [2, D])
        final_norm = self.din("final_norm", [D])
        self.ffn_w_in = self.din("ffn_w_in_t", [2, 22, 128, 2048])
        self.ffn_conv_w = self.din("ffn_conv_w", [2, 3, DFF])
        self.ffn_conv_b = self.din("ffn_conv_b", [2, DFF])
        self.ffn_w_out = self.din("ffn_w_out", [2, DFF, D])
        ident_d = self.din("ident", [128, 128])
        rperm_d = self.din("rperm", [128, 128], BF16)
        L1 = "cd" in self.stages
        L0 = "ab" in self.stages
        if L0:
            self.ab_w_in = self.din("ab_w_in", [D, 3600])
            self.ab_w_rot = self.din("ab_w_rot", [D, 1024])
            self.ab_w_out = self.din("ab_w_out", [D, D])
            self.dilmask = self.din("dilmask", [128, 2944], BF16)
            self.dnp_d = self.din("dnp", [128, 25])
            self.dcw_d = self.din("dcw", [128, 12, 3])
            self.tri_d = self.din("dn_tri", [128, 2, 128])
            self.msk_d = self.din("dn_msk", [128, 3, 512], BF16)
        if L0 and not L1:
            ropec_d = self.din("ropec", [128, S], BF16)
            ropes_d = self.din("ropes", [128, S], BF16)
        if L1:
            self.cd_w_in = self.din("cd_w_in", [D, 3072])
            self.cd_w_rot = self.din("cd_w_rot", [D, 512])
            self.cd_w_out = self.din("cd_w_out", [D, D])
            ret_ld = self.din("ret_log_decay", [8])
            ropec_d = self.din("ropec", [128, S], BF16)
            ropes_d = self.din("ropes", [128, S], BF16)
            dlt_d = self.din("dlt", [128, 512])
            iota16_d = self.din("iota16", [128, 16])
            self.hyz = self.din("hyz", [33, S])
            self.hy_w1 = self.din("hy_w1", [33, 64])
            self.hy_w2 = self.din("hy_w2", [64, 64])
            self.hy_w3 = self.din("hy_w3", [64, 1024])
            hyp_d = self.din("hyp", [64, 4])
            self.hydec = self.din("hydec", [S, 512])
            self.dftc = self.din("dftc", [S, S], BF16)
            self.dfts = self.din("dfts", [S, S], BF16)
            self.dftcf = self.din("dftcf", [16, 128, 2048], BF16)
            self.dftsf = self.din("dftsf", [16, 128, 2048], BF16)
            cnycol_d = self.din("cnycol", [128, 16], BF16)
            cnyrow_d = self.din("cnyrow", [1, S], BF16)
            wf_d = self.din("wf", [128, 17])
            hcw_d = self.din("hcw", [128, 12, 4])
            hybias_d = self.din("hybias", [128, 4])
            self.spec_d = nc.dram_tensor("spec_scratch", [2, 2176, 512], F32, kind="Internal").ap()

        self.xT_t = fw.sb([128, 8, S], F32, "xT")
        self.xT = [[fw.view(self.xT_t, "xT%d_%d" % (c, n)) for n in range(NT)] for c in range(8)]
        self.hT_t = fw.sb([128, 8, S], BF16, "hT")
        self.hT = [[fw.view(self.hT_t, "hT%d_%d" % (c, n)) for n in range(NT)] for c in range(8)]
        self.yT_t = fw.sb([128, 8, S], BF16, "yT")
        self.yT = [fw.view(self.yT_t, "yT%d" % c) for c in range(8)]
        self.P = [fw.view(fw.ps([128, 512], F32, "ps%d" % i), "ps%d" % i) for i in range(8)]
        self.pi = 0
        self.prot = list(range(8))
        self.ident = fw.buf([128, 128], F32, "ident")
        fw.dma(fw.sp, self.ident[:], ident_d[:, :], writes=[self.ident])
        self.identb = fw.buf([128, 128], BF16, "identb")
        fw.op(fw.dve, lambda e: e.tensor_copy(out=self.identb[:], in_=self.ident[:]), reads=[self.ident], writes=[self.identb])
        self.onesb = fw.buf([128, 128], BF16, "onesb")
        fw.op(fw.dve, lambda e: e.memset(self.onesb[:], 1.0), writes=[self.onesb])
        self.eps = fw.buf([128, 1], F32, "eps")
        fw.op(fw.dve, lambda e: e.memset(self.eps[:], 1e-6), writes=[self.eps])
        if L1 or L0:
            self.rperm = fw.buf([128, 128], BF16, "rperm")
            fw.dma(fw.sp, self.rperm[:], rperm_d[:, :], writes=[self.rperm])
            self.ropec = fw.buf([128, S], BF16, "ropec")
            self.ropes = fw.buf([128, S], BF16, "ropes")
            fw.dma(fw.sp, self.ropec[:], ropec_d[:, :], writes=[self.ropec])
            fw.dma(fw.sp, self.ropes[:], ropes_d[:, :], writes=[self.ropes])
        if L1:
            self.dlt_d = dlt_d
            self.iota16 = fw.buf([128, 16], F32, "iota16")
            fw.dma(fw.sp, self.iota16[:], iota16_d[:, :], writes=[self.iota16])
            self.ld = fw.buf([128, 8], F32, "ld")
            fw.dma(fw.sp, self.ld[:], ret_ld.partition_broadcast(128), writes=[self.ld])
            self.ld128 = fw.buf([128, 8], F32, "ld128")
            self.ld512 = fw.buf([128, 8], F32, "ld512")
            self.ldn = fw.buf([128, 8], F32, "ldn")
            fw.op(fw.dve, lambda e: e.tensor_scalar(out=self.ld128[:], in0=self.ld[:], scalar1=128.0, scalar2=None, op0=ALU.mult), reads=[self.ld], writes=[self.ld128])
            fw.op(fw.dve, lambda e: e.tensor_scalar(out=self.ld512[:], in0=self.ld[:], scalar1=512.0, scalar2=None, op0=ALU.mult), reads=[self.ld], writes=[self.ld512])
            fw.op(fw.dve, lambda e: e.tensor_scalar(out=self.ldn[:], in0=self.ld[:], scalar1=-1.0, scalar2=None, op0=ALU.mult), reads=[self.ld], writes=[self.ldn])
            self.hyp = fw.buf([64, 4], F32, "hyp")
            fw.dma(fw.sp, self.hyp[:], hyp_d[:, :], writes=[self.hyp])
            self.cnycol = fw.buf([128, 16], BF16, "cnycol")
            fw.dma(fw.sp, self.cnycol[:], cnycol_d[:, :], writes=[self.cnycol])
            self.cnyrow_d = cnyrow_d
            self.wf = fw.buf([128, 17], F32, "wf")
            fw.dma(fw.sp, self.wf[:], wf_d[:, :], writes=[self.wf])
            self.hcw = fw.buf([128, 12, 4], F32, "hcw")
            fw.dma(fw.sp, self.hcw[:], hcw_d[:, :, :], writes=[self.hcw])
            self.hybias = fw.buf([128, 4], F32, "hybias")
            fw.dma(fw.sp, self.hybias[:], hybias_d[:, :], writes=[self.hybias])
            self.ln8 = fw.buf([128, 1], F32, "ln8")
            fw.op(fw.dve, lambda e: e.memset(self.ln8[:], math.log(0.125)), writes=[self.ln8])
        self.nw = fw.buf([128, 5, 8], F32, "nw")
        srcs = [norm_mix[0], norm_ffn[0], norm_mix[1], norm_ffn[1], final_norm]
        for i, sap in enumerate(srcs):
            fw.dma(fw.sp, self.nw[:, i, :], sap.rearrange("(c p) -> p c", p=128), writes=[self.nw],
                   allow_slow_non_contiguous=True)
        self.fcw = fw.buf([128, 2, 22, 4], F32, "fcw")
        for l in range(2):
            for k in range(3):
                fw.dma(fw.sp, self.fcw[:, l, :, k], self.ffn_conv_w[l, k].rearrange("(j p) -> p j", p=128),
                       writes=[self.fcw], allow_slow_non_contiguous=True)
            fw.dma(fw.sp, self.fcw[:, l, :, 3], self.ffn_conv_b[l].rearrange("(j p) -> p j", p=128),
                   writes=[self.fcw], allow_slow_non_contiguous=True)

        if L1 and "hy" in self.stages:
            self.hyena_setup()
        for s in range(nseq):
            if s == 0:
                self.load_x(x_d[s])
            for layer in range(2):
                if ("ab" if layer == 0 else "cd") in self.stages:
                    self.rmsnorm(2 * layer)
                    (self.mixer_ab if layer == 0 else self.mixer_cd)()
                if ("ffn%d" % layer) in self.stages:
                    self.rmsnorm(2 * layer + 1)
                    self.ffn(layer)
            self.store_out(out_d[s], x_d[s + 1] if s + 1 < nseq else None)
        fw.finish()
        fw.close()
        return nc

    def l0c(self):
        fw = self.fw
        self.dnp = fw.buf([128, 25], F32, "dnp")
        fw.dma(fw.sp, self.dnp[:], self.dnp_d[:, :], writes=[self.dnp])
        fw.op(fw.act, lambda e: e.activation(out=self.dnp[:, 16:24], in_=self.dnp[:, 0:8], func=AF.Exp), reads=[self.dnp], writes=[self.dnp])
        fw.op(fw.dve, lambda e: e.tensor_scalar(out=self.dnp[:, 16:24], in0=self.dnp[:, 16:24], scalar1=-1.0, scalar2=None, op0=ALU.mult), reads=[self.dnp], writes=[self.dnp])
        self.dcw = fw.buf([128, 12, 3], F32, "dcw")
        fw.dma(fw.sp, self.dcw[:], self.dcw_d[:, :, :], writes=[self.dcw])
        tri = fw.buf([128, 2, 128], F32, "dn_tri")
        fw.dma(fw.sp, tri[:], self.tri_d[:, :, :], writes=[tri])
        self.triF = fw.view(tri.t[:, 0, :], "triF")
        self.triB = fw.view(tri.t[:, 1, :], "triB")
        msk = fw.buf([128, 3, 512], BF16, "dn_msk")
        fw.dma(fw.sp, msk[:], self.msk_d[:, :, :], writes=[msk])
        self.mL4 = fw.view(msk.t[:, 0, :], "mL4")
        self.mU4 = fw.view(msk.t[:, 1, :], "mU4")
        self.noti4 = fw.view(msk.t[:, 2, :], "noti4")
        for v_ in (self.triF, self.triB, self.mL4, self.mU4, self.noti4):
            v_.writer = (tri.writer if v_ in (self.triF, self.triB) else msk.writer)
        self.identf4 = fw.buf([128, 512], BF16, "identf4")
        for i4 in range(4):
            fw.op(fw.dve, lambda e: e.tensor_copy(out=self.identf4[:, i4 * 128:(i4 + 1) * 128], in_=self.ident[:]), reads=[self.ident], writes=[self.identf4])
        self.onesf = fw.buf([128, 128], F32, "onesf")
        fw.op(fw.dve, lambda e: e.memset(self.onesf[:], 1.0), writes=[self.onesf])
        self.one1 = fw.buf([128, 1], F32, "one1")
        fw.op(fw.dve, lambda e: e.memset(self.one1[:], 1.0), writes=[self.one1])

    def load_tile(self, xs, tt, xb):
        fw = self.fw
        fw.dma(fw.sp, xb[:], xs[tt * 128:(tt + 1) * 128, :], writes=[xb])
        nt = tt // 4
        for half in range(2):
            p = self.nextp()
            for cc in range(4):
                c = half * 4 + cc
                fw.op(fw.pe, lambda e: e.transpose(p[:, cc * 128:(cc + 1) * 128], xb[:, c * 128:(c + 1) * 128], self.ident[:]),
                      reads=[xb, self.ident], writes=[p], inc=(cc == 3))
            dst = self.xT_t[:, half * 4:half * 4 + 4, tt * 128:(tt + 1) * 128]
            wr = [self.xT[half * 4 + cc][nt] for cc in range(4)]
            if half == 0:
                fw.op(fw.act, lambda e: e.activation(out=dst, in_=p[:].rearrange("p (c t) -> p c t", c=4), func=AF.Copy), reads=[p], writes=wr)
            else:
                fw.op(fw.dve, lambda e: e.tensor_copy(out=dst, in_=p[:].rearrange("p (c t) -> p c t", c=4)), reads=[p], writes=wr)

    def load_x(self, xs):
        fw = self.fw
        m = fw.mark()
        xin = [fw.buf([128, D], F32, "xin%d" % i) for i in range(2)]
        for tt in range(16):
            self.load_tile(xs, tt, xin[tt % 2])
        fw.release(m)

    def store_out(self, outs, next_xs=None):
        fw = self.fw
        self.rmsnorm(4, inplace=True)
        m = fw.mark()
        ob = [fw.buf([128, D], F32, "ob%d" % i) for i in range(2)]
        xin = [fw.buf([128, D], F32, "xinb%d" % i) for i in range(2)] if next_xs is not None else None
        for tt in range(16):
            o = ob[tt % 2]
            nt = tt // 4
            for half in range(2):
                p = self.nextp()
                for cc in range(4):
                    c = half * 4 + cc
                    fw.op(fw.pe, lambda e, c=c, cc=cc, p=p: e.transpose(p[:, cc * 128:(cc + 1) * 128], self.xT_t[:, c, tt * 128:(tt + 1) * 128], self.ident[:]),
                          reads=[self.xT[c][nt], self.ident], writes=[p], inc=(cc == 3))
                if half == 0:
                    fw.op(fw.act, lambda e, p=p, o=o: e.activation(out=o[:, 0:512], in_=p[:], func=AF.Copy), reads=[p], writes=[o])
                else:
                    fw.op(fw.dve, lambda e, p=p, o=o: e.tensor_copy(out=o[:, 512:1024], in_=p[:]), reads=[p], writes=[o])
            fw.dma(fw.sp, outs[tt * 128:(tt + 1) * 128, :], o[:], reads=[o], is_output=True)
            if next_xs is not None and tt % 4 == 3:
                for t2 in range(tt - 3, tt + 1):
                    self.load_tile(next_xs, t2, xin[t2 % 2])
        fw.release(m)

    def rmsnorm(self, widx, inplace=False):
        fw = self.fw
        m = fw.mark()
        sq = [fw.buf([128, 512], BF16, "sq%d" % i) for i in range(3)]
        rstd = [fw.buf([128, 512], F32, "rstd%d" % i) for i in range(2)]
        k = 0
        for nt in range(NT):
            sl = slice(nt * 512, (nt + 1) * 512)
            p = self.nextp()
            for c in range(8):
                q = sq[k % 3]
                k += 1
                fw.op(fw.act, lambda e, q=q, c=c: e.activation(out=q[:], in_=self.xT_t[:, c, sl], func=AF.Square),
                      reads=[self.xT[c][nt]], writes=[q])
                fw.op(fw.pe, lambda e, q=q, c=c, p=p: e.matmul(p[:], lhsT=self.onesb[:], rhs=q[:], start=(c == 0), stop=(c == 7)),
                      reads=[q, self.onesb], writes=[p], inc=True)
            r = rstd[nt % 2]
            fw.op(fw.act, lambda e, r=r, p=p: e.activation(out=r[:], in_=p[:], func=AF.Ln, bias=self.eps[:], scale=1.0 / D),
                  reads=[p, self.eps], writes=[r])
            fw.op(fw.act, lambda e, r=r: e.activation(out=r[:], in_=r[:], func=AF.Exp, scale=-0.5), reads=[r], writes=[r])
            for c in range(8):
                if inplace:
                    dst, wr = self.xT_t[:, c, sl], [self.xT[c][nt]]
                else:
                    dst, wr = self.hT_t[:, c, sl], [self.hT[c][nt]]
                fw.op(fw.dve, lambda e, c=c, r=r, dst=dst: e.scalar_tensor_tensor(out=dst, in0=self.xT_t[:, c, sl], scalar=self.nw[:, widx, c:c + 1],
                                                                                 in1=r[:], op0=ALU.mult, op1=ALU.mult),
                      reads=[self.xT[c][nt], r, self.nw], writes=wr)
        fw.release(m)

    def ffn(self, layer):
        fw = self.fw
        m = fw.mark()
        win_d = self.ffn_w_in[layer]
        wout_d = self.ffn_w_out[layer].rearrange("(j p) n -> p j n", p=128)
        win = [fw.buf([128, 8, 256], BF16, "win%d" % i) for i in range(2)]
        wout = fw.buf([128, 8, D], BF16, "wout")
        gbuf = [fw.buf([128, S + 2], F32, "gbuf%d" % i) for i in range(2)]
        ubuf = [fw.buf([128, S], BF16, "ubuf%d" % i) for i in range(2)]
        acc = [fw.buf([128, 512], F32, "acc%d" % i) for i in range(2)]
        sg = [fw.buf([128, 512], BF16, "sg%d" % i) for i in range(2)]
        for g in gbuf:
            fw.op(fw.dve, lambda e, g=g: e.memset(g[:, 0:1], 0.0), writes=[g])
            fw.op(fw.dve, lambda e, g=g: e.memset(g[:, S + 1:S + 2], 0.0), writes=[g])
        groups = [list(range(0, 8)), list(range(8, 16)), list(range(16, 22))]
        k = 0
        def load_win(j):
            w = win[j % 2]
            wj = win_d[j].rearrange("p (kc n) -> p kc n", kc=8)
            fw.dma(fw.pool, w[:, 0:4, :], wj[:, 0:4, :], writes=[w])
            fw.dma(fw.pool, w[:, 4:8, :], wj[:, 4:8, :], writes=[w])
        for grp in groups:
            load_win(grp[0])
            for si, j in enumerate(grp):
                fw.dma(fw.pool, wout[:, si, :], wout_d[:, j, :], writes=[wout])
            for si, j in enumerate(grp):
                w = win[j % 2]
                gb = gbuf[j % 2]
                ub = ubuf[j % 2]
                if si > 0:
                    load_win(j)
                for nt in range(NT):
                    sl = slice(nt * 512, (nt + 1) * 512)
                    pg = self.nextp()
                    for kc in range(8):
                        fw.op(fw.pe, lambda e, kc=kc, pg=pg, w=w: e.matmul(pg[:], lhsT=w[:, kc, 0:128], rhs=self.hT_t[:, kc, sl], start=(kc == 0), stop=(kc == 7)),
                              reads=[w, self.hT[kc][nt]], writes=[pg], inc=(kc == 7))
                    fw.op(fw.act, lambda e, pg=pg, gb=gb, nt=nt: e.activation(out=gb[:, 1 + nt * 512:1 + (nt + 1) * 512], in_=pg[:], func=AF.Copy),
                          reads=[pg], writes=[gb])
                    pu = self.nextp()
                    for kc in range(8):
                        fw.op(fw.pe, lambda e, kc=kc, pu=pu, w=w: e.matmul(pu[:], lhsT=w[:, kc, 128:256], rhs=self.hT_t[:, kc, sl], start=(kc == 0), stop=(kc == 7)),
                              reads=[w, self.hT[kc][nt]], writes=[pu], inc=(kc == 7))
                    fw.op(fw.act, lambda e, pu=pu, ub=ub, sl=sl: e.activation(out=ub[:, sl], in_=pu[:], func=AF.Copy),
                          reads=[pu], writes=[ub])
                cw = self.fcw
                for nt in range(NT):
                    a = acc[k % 2]
                    sgb = sg[k % 2]
                    k += 1
                    o = nt * 512
                    fw.op(fw.dve, lambda e, a=a, gb=gb, o=o, j=j: e.tensor_scalar(out=a[:], in0=gb[:, o:o + 512], scalar1=cw[:, layer, j, 0:1], scalar2=None, op0=ALU.mult),
                          reads=[gb, cw], writes=[a])
                    fw.op(fw.dve, lambda e, a=a, gb=gb, o=o, j=j: e.scalar_tensor_tensor(out=a[:], in0=gb[:, o + 1:o + 513], scalar=cw[:, layer, j, 1:2], in1=a[:], op0=ALU.mult, op1=ALU.add),
                          reads=[gb, cw, a], writes=[a])
                    fw.op(fw.dve, lambda e, a=a, gb=gb, o=o, j=j: e.scalar_tensor_tensor(out=a[:], in0=gb[:, o + 2:o + 514], scalar=cw[:, layer, j, 2:3], in1=a[:], op0=ALU.mult, op1=ALU.add),
                          reads=[gb, cw, a], writes=[a])
                    fw.op(fw.act, lambda e, a=a, sgb=sgb, j=j: e.activation(out=sgb[:], in_=a[:], func=AF.Silu, bias=cw[:, layer, j, 3:4], scale=1.0),
                          reads=[a, cw], writes=[sgb])
                    fw.op(fw.dve, lambda e, sgb=sgb, ub=ub, si=si, o=o: e.tensor_tensor(out=self.yT_t[:, si, o:o + 512], in0=sgb[:], in1=ub[:, o:o + 512], op=ALU.mult),
                          reads=[sgb, ub], writes=[self.yT[si]])
            for nt in range(NT):
                for c in range(8):
                    sl = slice(nt * 512, (nt + 1) * 512)
                    po = self.nextp()
                    for si in range(len(grp)):
                        fw.op(fw.pe, lambda e, si=si, po=po, c=c: e.matmul(po[:], lhsT=wout[:, si, c * 128:(c + 1) * 128], rhs=self.yT_t[:, si, sl], start=(si == 0), stop=(si == len(grp) - 1)),
                              reads=[wout, self.yT[si]], writes=[po], inc=(si == len(grp) - 1))
                    fw.op(fw.dve, lambda e, po=po, c=c: e.tensor_tensor(out=self.xT_t[:, c, sl], in0=self.xT_t[:, c, sl], in1=po[:], op=ALU.add),
                          reads=[po, self.xT[c][nt]], writes=[self.xT[c][nt]])
        fw.release(m)


    def mixer_ab(self):
        fw = self.fw
        m0 = fw.mark()
        self.l0c()
        if "dn" in self.stages:
            self.deltanet()
        else:
            for c in range(4):
                fw.op(fw.dve, lambda e: e.memset(self.yT_t[:, c, :], 0.0), writes=[self.yT[c]])
        if "dil" in self.stages:
            self.dilated()
        else:
            for c in range(4, 8):
                fw.op(fw.dve, lambda e: e.memset(self.yT_t[:, c, :], 0.0), writes=[self.yT[c]])
        fw.release(m0)
        self.out_proj(self.ab_w_out)

    def dilated(self):
        fw = self.fw
        W = self.ab_w_in
        wsrc = W.rearrange("(kc p) n -> p kc n", p=128)
        rsrc = self.ab_w_rot.rearrange("(kc p) n -> p kc n", p=128)
        m = fw.mark()
        Q0, K0, V0 = 2064, 2064 + 512, 2064 + 1024
        mask = fw.buf([128, 2944], BF16, "dmask")
        fw.dma(fw.sp, mask[:], self.dilmask[:, :], writes=[mask])
        qe = fw.buf([65, S], BF16, "qe")
        ke = fw.buf([65, S], BF16, "ke")
        fw.op(fw.dve, lambda e: e.memset(ke[64:65, :], 1.0), writes=[ke])
        vx = fw.buf([128, 16, 128], BF16, "vx")
        wq = [fw.buf([128, 8, 64], BF16, "dwq%d" % i) for i in range(4)]
        wv = fw.buf([128, 8, 64], BF16, "dwv")
        t1 = [fw.buf([64, 512], F32, "dt1_%d" % i) for i in range(2)]
        t2 = [fw.buf([64, 512], F32, "dt2_%d" % i) for i in range(2)]
        sqq = fw.buf([64, S], BF16, "dsqq")
        sqk = fw.buf([64, S], BF16, "dsqk")
        kmx = fw.buf([128, 8], F32, "dkmx")
        qn = [fw.buf([128, 512], F32, "dqn%d" % i) for i in range(2)]
        ex = [fw.buf([128, 512], BF16, "dex%d" % i) for i in range(6)]
        pt = [fw.buf([128, 512], BF16, "dpt%d" % i) for i in range(7)]
        rd = [fw.buf([128, 512], F32, "drd%d" % i) for i in range(2)]
        kx = 0
        kp = 0
        for h in range(8):
            hc, odd = h // 2, h % 2
            self.prot = list(range(2, 8))
            fw.dma(fw.pool, wq[0][:], wsrc[:, :, Q0 + h * 64:Q0 + (h + 1) * 64], writes=[wq[0]])
            fw.dma(fw.pool, wq[2][:], wsrc[:, :, K0 + h * 64:K0 + (h + 1) * 64], writes=[wq[2]])
            fw.dma(fw.pool, wv[:], wsrc[:, :, V0 + h * 64:V0 + (h + 1) * 64], writes=[wv])
            vcol, ocol = (64, 0) if odd else (0, 64)
            fw.op(fw.dve, lambda e: e.memset(vx[:, :, ocol:ocol + 64], 1.0), writes=[vx])
            for tq in range(4):
                p = self.nextp()
                for t4 in range(4):
                    tt = tq * 4 + t4
                    for kc in range(8):
                        fw.op(fw.pe, lambda e: e.matmul(p[:, t4 * 64:(t4 + 1) * 64], lhsT=self.hT_t[:, kc, tt * 128:(tt + 1) * 128], rhs=wv[:, kc, :], start=(kc == 0), stop=(kc == 7)),
                              reads=[wv, self.hT[kc][tq]], writes=[p], inc=(kc == 7 and t4 == 3))
                fw.op(fw.act, lambda e: e.activation(out=vx[:, tq * 4:tq * 4 + 4, vcol:vcol + 64], in_=p[:, 0:256].rearrange("p (a b) -> p a b", a=4), func=AF.Copy), reads=[p], writes=[vx])
            for which, dst, sq in ((0, qe, sqq), (1, ke, sqk)):
                for nt in range(NT):
                    sl = slice(nt * 512, (nt + 1) * 512)
                    p1 = self.proj(wq[2 * which], 0, 64, nt)
                    qs_ = wq[1] if kx % 2 == 0 else wq[3]
                    qsv = qs_[:].rearrange("p a b -> p (a b)")
                    fw.op(fw.dve, lambda e: e.tensor_copy(out=qsv[0:64, :], in_=p1[0:64, :]), reads=[p1], writes=[qs_])
                    p2 = self.nextp()
                    fw.op(fw.pe, lambda e: e.matmul(p2[0:64, :], lhsT=self.rperm[0:64, 0:64], rhs=qsv[0:64, :], start=True, stop=True), reads=[self.rperm, qs_], writes=[p2])
                    a, b = t1[kx % 2], t2[kx % 2]
                    kx += 1
                    sc_ = 0.125 if which == 0 else 1.0
                    fw.op(fw.dve, lambda e: e.scalar_tensor_tensor(out=a[:], in0=p1[0:64, :], scalar=sc_, in1=self.ropec[0:64, sl], op0=ALU.mult, op1=ALU.mult), reads=[p1, self.ropec], writes=[a])
                    fw.op(fw.dve, lambda e: e.scalar_tensor_tensor(out=b[:], in0=p2[0:64, :], scalar=sc_, in1=self.ropes[0:64, sl], op0=ALU.mult, op1=ALU.mult), reads=[p2, self.ropes], writes=[b])
                    fw.op(fw.pool, lambda e: e.tensor_tensor(out=dst[0:64, sl], in0=a[:], in1=b[:], op=ALU.add), reads=[a, b], writes=[dst])
                    fw.op(fw.act, lambda e: e.activation(out=sq[:, sl], in_=dst[0:64, sl], func=AF.Square), reads=[dst], writes=[sq])
            for nt in range(NT):
                sl = slice(nt * 512, (nt + 1) * 512)
                pk = self.nextp()
                fw.op(fw.pe, lambda e: e.matmul(pk[:], lhsT=self.onesb[0:64, :], rhs=sqk[:, sl], start=True, stop=True), reads=[self.onesb, sqk], writes=[pk])
                fw.op(fw.dve, lambda e: e.tensor_reduce(out=kmx[:, nt:nt + 1], in_=pk[:], axis=AX.X, op=ALU.max), reads=[pk], writes=[kmx])
            fw.op(fw.dve, lambda e: e.tensor_reduce(out=kmx[:, 4:5], in_=kmx[:, 0:4], axis=AX.X, op=ALU.max), reads=[kmx], writes=[kmx])
            fw.op(fw.act, lambda e: e.activation(out=kmx[:, 5:6], in_=kmx[:, 4:5], func=AF.Sqrt), reads=[kmx], writes=[kmx])
            fw.op(fw.dve, lambda e: e.tensor_scalar(out=kmx[:, 6:7], in0=kmx[:, 5:6], scalar1=-1.0, scalar2=None, op0=ALU.mult), reads=[kmx], writes=[kmx])
            for nt in range(NT):
                sl = slice(nt * 512, (nt + 1) * 512)
                pq = self.nextp()
                fw.op(fw.pe, lambda e: e.matmul(pq[:], lhsT=self.onesb[0:64, :], rhs=sqq[:, sl], start=True, stop=True), reads=[self.onesb, sqq], writes=[pq])
                q_ = qn[nt % 2]
                fw.op(fw.act, lambda e: e.activation(out=q_[:], in_=pq[:], func=AF.Sqrt), reads=[pq], writes=[q_])
                fw.op(fw.dve, lambda e: e.tensor_scalar(out=qe[64:65, sl], in0=q_[64:65, :], scalar1=kmx[64:65, 6:7], scalar2=None, op0=ALU.mult), reads=[q_, kmx], writes=[qe])
            for nj in range(NT):
                pacc = self.P[nj % 2]
                pend = []
                mis = [mi for mi in range(16) if -8 <= mi - 4 * nj <= 11]

                def pv(sc, mi):
                    fw.op(fw.pe, lambda e: e.matmul(pacc[:], lhsT=vx[:, mi, :], rhs=sc[:], start=(mi == mis[0]), stop=(mi == mis[-1])),
                          reads=[vx, sc], writes=[pacc], inc=True)
                for mi in mis:
                    r = mi - 4 * nj
                    ps = self.nextp()
                    fw.op(fw.pe, lambda e: e.matmul(ps[:], lhsT=ke[0:65, mi * 128:(mi + 1) * 128], rhs=qe[0:65, nj * 512:(nj + 1) * 512], start=True, stop=True),
                          reads=[ke, qe], writes=[ps], inc=True)
                    e_ = ex[kp % 6]
                    sc = pt[kp % 7]
                    kp += 1
                    fw.op(fw.act, lambda e: e.activation(out=e_[:], in_=ps[:], func=AF.Exp), reads=[ps], writes=[e_])
                    fw.op(fw.dve, lambda e: e.tensor_tensor(out=sc[:], in0=e_[:], in1=mask[:, (11 - r) * 128:(11 - r) * 128 + 512], op=ALU.mult), reads=[e_, mask], writes=[sc])
                    pend.append((sc, mi))
                    if len(pend) > 4:
                        pv(*pend.pop(0))
                while pend:
                    pv(*pend.pop(0))
                r_ = rd[nj % 2]
                nlo, dlo = (64, 0) if odd else (0, 64)
                fw.op(fw.act, lambda e: e.activation(out=r_[nlo:nlo + 64, :], in_=pacc[dlo:dlo + 64, :], func=AF.Ln), reads=[pacc], writes=[r_])
                fw.op(fw.act, lambda e: e.activation(out=r_[nlo:nlo + 64, :], in_=r_[nlo:nlo + 64, :], func=AF.Exp, scale=-1.0), reads=[r_], writes=[r_])
                fw.op(fw.dve, lambda e: e.tensor_tensor(out=self.yT_t[nlo:nlo + 64, 4 + hc, nj * 512:(nj + 1) * 512], in0=pacc[nlo:nlo + 64, :], in1=r_[nlo:nlo + 64, :], op=ALU.mult),
                      reads=[pacc, r_], writes=[self.yT[4 + hc]])
        self.prot = list(range(8))
        fw.release(m)


    def deltanet(self):
        fw = self.fw
        W = self.ab_w_in
        wsrc = W.rearrange("(kc p) n -> p kc n", p=128)
        P = self.P
        m = fw.mark()
        beta = fw.buf([128, 8, 16], F32, "dn_beta")
        nbeta = fw.buf([128, 8, 16], F32, "dn_nbeta")
        gl = fw.buf([128, 8, 16], F32, "dn_g")
        mba = fw.mark()
        ba = fw.buf([128, 16, 16], F32, "dn_ba")
        wba = fw.buf([128, 8, 16], BF16, "dn_wba")
        fw.dma(fw.pool, wba[:], wsrc[:, :, 2048:2064], writes=[wba])
        p = P[0]
        for tt in range(16):
            for kc in range(8):
                fw.op(fw.pe, lambda e: e.matmul(p[:, tt * 16:(tt + 1) * 16], lhsT=self.hT_t[:, kc, tt * 128:(tt + 1) * 128], rhs=wba[:, kc, :], start=(kc == 0), stop=(kc == 7)),
                      reads=[wba, self.hT[kc][tt // 4]], writes=[p], inc=(kc == 7 and tt == 15))
        fw.op(fw.dve, lambda e: e.tensor_copy(out=ba[:], in_=p[:, 0:256].rearrange("p (t c) -> p c t", c=16)), reads=[p], writes=[ba])
        fw.op(fw.act, lambda e: e.activation(out=beta[:], in_=ba[:, 0:8, :], func=AF.Sigmoid), reads=[ba], writes=[beta])
        fw.op(fw.dve, lambda e: e.tensor_scalar(out=nbeta[:], in0=beta[:], scalar1=-1.0, scalar2=None, op0=ALU.mult), reads=[beta], writes=[nbeta])
        tx = fw.buf([128, 16], F32, "dn_tx")
        ta = fw.buf([128, 16], F32, "dn_ta")
        for c in range(8):
            fw.op(fw.dve, lambda e: e.tensor_scalar(out=tx[:], in0=ba[:, 8 + c, :], scalar1=self.dnp[:, 8 + c:9 + c], scalar2=None, op0=ALU.add), reads=[ba, self.dnp], writes=[tx])
            fw.op(fw.dve, lambda e: e.tensor_scalar(out=ta[:], in0=tx[:], scalar1=-1.0, scalar2=None, op0=ALU.mult), reads=[tx], writes=[ta])
            fw.op(fw.dve, lambda e: e.tensor_tensor(out=ta[:], in0=ta[:], in1=tx[:], op=ALU.max), reads=[tx, ta], writes=[ta])
            fw.op(fw.act, lambda e: e.activation(out=ta[:], in_=ta[:], func=AF.Exp, scale=-1.0), reads=[ta], writes=[ta])
            fw.op(fw.act, lambda e: e.activation(out=ta[:], in_=ta[:], func=AF.Ln, bias=self.one1[:], scale=1.0), reads=[ta, self.one1], writes=[ta])
            fw.op(fw.dve, lambda e: e.scalar_tensor_tensor(out=tx[:], in0=tx[:], scalar=0.0, in1=ta[:], op0=ALU.max, op1=ALU.add), reads=[tx, ta], writes=[tx])
            fw.op(fw.dve, lambda e: e.tensor_scalar(out=gl[:, c, :], in0=tx[:], scalar1=self.dnp[:, 16 + c:17 + c], scalar2=None, op0=ALU.mult), reads=[tx, self.dnp], writes=[gl])
        fw.release(mba)
        for h in range(4):
            mh = fw.mark()
            al = [self.yT[4], self.yT[5], self.yT[6], self.yT[7]]
            qT = fw.view(self.yT_t[:, 6, :], "dn_qT", alias=al)
            kT = fw.view(self.yT_t[:, 7, :], "dn_kT", alias=al)
            oT = fw.view(self.yT_t[:, 4:6, :].rearrange("p a s -> p (a s)").bitcast(F32), "dn_oT", alias=al)
            ktok = fw.buf([128, 16, 128], BF16, "dn_ktok")
            vtok = fw.buf([128, 16, 128], BF16, "dn_vtok")
            mp = fw.mark()
            self.prot = list(range(8))
            gb = fw.buf([128, S + 2], F32, "dn_gb")
            fw.op(fw.dve, lambda e: e.memset(gb[:, 0:1], 0.0), writes=[gb])
            fw.op(fw.dve, lambda e: e.memset(gb[:, S + 1:S + 2], 0.0), writes=[gb])
            a1 = fw.buf([128, S], F32, "dn_a1")
            vT = fw.buf([128, S], BF16, "dn_vT")
            wps = [fw.buf([128, 8, 128], BF16, "dn_wp%d" % i) for i in range(3)]
            for part in range(3):
                ch_ = part * 4 + h
                fw.dma(fw.pool, wps[part][:], wsrc[:, :, ch_ * 128:(ch_ + 1) * 128], writes=[wps[part]])
            sq = [fw.buf([128, 512], BF16, "dn_sq%d" % i) for i in range(2)]
            rs = [fw.buf([128, 512], F32, "dn_rs%d" % i) for i in range(2)]
            dcw = self.dcw
            for part in range(3):
                ch = part * 4 + h
                wp = wps[part]
                for nt in range(NT):
                    pp = self.proj(wp, 0, 128, nt)
                    fw.op(fw.act, lambda e: e.activation(out=gb[:, 1 + nt * 512:1 + (nt + 1) * 512], in_=pp[:], func=AF.Copy), reads=[pp], writes=[gb])
                fw.op(fw.dve, lambda e: e.tensor_scalar(out=a1[:], in0=gb[:, 0:S], scalar1=dcw[:, ch, 0:1], scalar2=None, op0=ALU.mult), reads=[gb, dcw], writes=[a1])
                fw.op(fw.dve, lambda e: e.scalar_tensor_tensor(out=a1[:], in0=gb[:, 1:S + 1], scalar=dcw[:, ch, 1:2], in1=a1[:], op0=ALU.mult, op1=ALU.add), reads=[gb, dcw, a1], writes=[a1])
                fw.op(fw.dve, lambda e: e.scalar_tensor_tensor(out=a1[:], in0=gb[:, 2:S + 2], scalar=dcw[:, ch, 2:3], in1=a1[:], op0=ALU.mult, op1=ALU.add), reads=[gb, dcw, a1], writes=[a1])
                if part == 2:
                    fw.op(fw.act, lambda e: e.activation(out=vT[:], in_=a1[:], func=AF.Silu), reads=[a1], writes=[vT])
                    srcT, dtok = vT, vtok
                else:
                    fw.op(fw.act, lambda e: e.activation(out=a1[:], in_=a1[:], func=AF.Silu), reads=[a1], writes=[a1])
                    dstT = qT if part == 0 else kT
                    for nt in range(NT):
                        sl = slice(nt * 512, (nt + 1) * 512)
                        q_, r_ = sq[nt % 2], rs[nt % 2]
                        fw.op(fw.act, lambda e: e.activation(out=q_[:], in_=a1[:, sl], func=AF.Square), reads=[a1], writes=[q_])
                        pn = self.nextp()
                        fw.op(fw.pe, lambda e: e.matmul(pn[:], lhsT=self.onesb[:], rhs=q_[:], start=True, stop=True), reads=[q_, self.onesb], writes=[pn])
                        fw.op(fw.act, lambda e: e.activation(out=r_[:], in_=pn[:], func=AF.Ln, bias=self.eps[:], scale=1.0), reads=[pn, self.eps], writes=[r_])
                        fw.op(fw.act, lambda e: e.activation(out=r_[:], in_=r_[:], func=AF.Exp, scale=-0.5), reads=[r_], writes=[r_])
                        scl = 128.0 ** -0.5 if part == 0 else 1.0
                        fw.op(fw.dve, lambda e: e.scalar_tensor_tensor(out=dstT[:, sl], in0=a1[:, sl], scalar=scl, in1=r_[:], op0=ALU.mult, op1=ALU.mult), reads=[a1, r_], writes=[dstT])
                    srcT, dtok = kT, ktok
                if part >= 1:
                    for tq in range(4):
                        pp = self.nextp()
                        pb = pp[:].bitcast(BF16)
                        for t4 in range(4):
                            tt = tq * 4 + t4
                            fw.op(fw.pe, lambda e: e.transpose(pb[:, t4 * 128:(t4 + 1) * 128], srcT[:, tt * 128:(tt + 1) * 128], self.identb[:]),
                                  reads=[srcT, self.identb], writes=[pp], inc=(t4 == 3))
                        fw.op(fw.act, lambda e: e.activation(out=dtok[:, tq * 4:tq * 4 + 4, :], in_=pb[:, 0:512].rearrange("p (a b) -> p a b", a=4), func=AF.Copy), reads=[pp], writes=[dtok])
            fw.release(mp)
            import os
            def dir_tables(d):
                    col = d * 4 + h
                    qdT = fw.buf([128, S], BF16, "dn_qdT")
                    TT = fw.buf([128, 16, 128], BF16, "dn_TT")
                    inT = fw.buf([128, S], BF16, "dn_inT")
                    sm = fw.buf([128, 6, 16], F32, "dn_sm")
                    TRI = self.triF if d == 0 else self.triB
                    MI = self.mL4 if d == 0 else self.mU4
                    MT = self.mU4 if d == 0 else self.mL4
                    gcol = gl[:, col, :]
                    pc = P[0]
                    fw.op(fw.pe, lambda e: e.matmul(pc[:, 0:16], lhsT=TRI[:], rhs=gcol, start=True, stop=True), reads=[TRI, gl], writes=[pc], inc=False)
                    fw.op(fw.pe, lambda e: e.matmul(pc[:, 16:32], lhsT=self.onesf[:], rhs=gcol, start=True, stop=True), reads=[self.onesf, gl], writes=[pc])
                    fw.op(fw.dve, lambda e: e.tensor_copy(out=sm[:, 0, :], in_=pc[:, 0:16]), reads=[pc], writes=[sm])
                    fw.op(fw.dve, lambda e: e.tensor_scalar(out=sm[:, 1, :], in0=pc[:, 0:16], scalar1=-1.0, scalar2=None, op0=ALU.mult), reads=[pc], writes=[sm])
                    fw.op(fw.dve, lambda e: e.tensor_copy(out=sm[:, 2, :], in_=pc[:, 16:32]), reads=[pc], writes=[sm])
                    fw.op(fw.dve, lambda e: e.tensor_tensor(out=sm[:, 3, :], in0=sm[:, 2, :], in1=sm[:, 0, :], op=ALU.subtract), reads=[sm], writes=[sm])
                    fw.op(fw.act, lambda e: e.activation(out=sm[:, 3, :], in_=sm[:, 3, :], func=AF.Exp), reads=[sm], writes=[sm])
                    fw.op(fw.act, lambda e: e.activation(out=sm[:, 4, :], in_=sm[:, 2, :], func=AF.Exp), reads=[sm], writes=[sm])
                    fw.op(fw.act, lambda e: e.activation(out=sm[:, 5, :], in_=sm[:, 0, :], func=AF.Exp), reads=[sm], writes=[sm])
                    fw.op(fw.dve, lambda e: e.tensor_scalar(out=sm[:, 5, :], in0=sm[:, 5, :], scalar1=-1.0, scalar2=None, op0=ALU.mult), reads=[sm], writes=[sm])
                    mt = fw.mark()
                    def mk_tmp(tag):
                        return (fw.buf([128, 512], F32, 'dn_tI' + tag), fw.buf([128, 512], F32, 'dn_tT' + tag),
                                [fw.buf([128, 512], F32, 'dn_Pb%d%s' % (i, tag)) for i in range(2)],
                                [fw.buf([128, 512], F32, 'dn_Qb%d%s' % (i, tag)) for i in range(2)],
                                fw.buf([128, 512], F32, 'dn_Rb' + tag))
                    def tab_gen(gq, tmp, banks):
                        tI, tT, Pb, Qb, Rb1 = tmp
                        dg, eg = tT, tI
                        Rb = [Rb1, Rb1]
                        cs = slice(gq * 512, (gq + 1) * 512)
                        chs = [gq * 4 + i for i in range(4)]
                        C4 = lambda i: slice(i * 128, (i + 1) * 128)
                        pcb, pG, pKQ = banks; pP, pQ, pR = banks
                        for i, ch in enumerate(chs):
                            fw.op(fw.dve, lambda e: e.tensor_scalar(out=dg[:, C4(i)], in0=self.ident[:], scalar1=sm[:, 0, ch:ch + 1], scalar2=None, op0=ALU.mult), reads=[self.ident, sm], writes=[dg])
                            yield
                        for i, ch in enumerate(chs):
                            fw.op(fw.pe, lambda e: e.matmul(pcb[:, C4(i)], lhsT=self.onesf[:], rhs=dg[:, C4(i)], start=True, stop=True), reads=[self.onesf, dg], writes=[pcb], inc=(i == 3))
                            yield
                        fw.op(fw.act, lambda e: e.activation(out=eg[:], in_=pcb[:], func=AF.Exp), reads=[pcb], writes=[eg])
                        yield
                        fw.op(fw.dve, lambda e: e.tensor_tensor(out=qdT[:, cs], in0=qT[:, cs], in1=eg[:], op=ALU.mult), reads=[qT, eg], writes=[qdT])
                        yield
                        fw.op(fw.dve, lambda e: e.scalar_tensor_tensor(out=tI[:], in0=pcb[:], scalar=-1.0, in1=MI[:], op0=ALU.mult, op1=ALU.add), reads=[pcb, MI], writes=[tI])
                        yield
                        for i, ch in enumerate(chs):
                            fw.op(fw.act, lambda e: e.activation(out=tI[:, C4(i)], in_=tI[:, C4(i)], func=AF.Exp, bias=sm[:, 0, ch:ch + 1], scale=1.0), reads=[tI, sm], writes=[tI])
                            yield
                            fw.op(fw.dve, lambda e: e.scalar_tensor_tensor(out=tT[:, C4(i)], in0=pcb[:, C4(i)], scalar=sm[:, 1, ch:ch + 1], in1=MT[:, C4(i)], op0=ALU.add, op1=ALU.add), reads=[pcb, sm, MT], writes=[tT])
                            yield
                        fw.op(fw.act, lambda e: e.activation(out=tT[:], in_=tT[:], func=AF.Exp), reads=[tT], writes=[tT])
                        yield
                        for i, ch in enumerate(chs):
                            c128 = slice(ch * 128, (ch + 1) * 128)
                            fw.op(fw.pe, lambda e: e.matmul(pG[:, C4(i)], lhsT=kT[:, c128], rhs=kT[:, c128], start=True, stop=True), reads=[kT], writes=[pG], inc=(i == 3))
                            yield
                        for i, ch in enumerate(chs):
                            c128 = slice(ch * 128, (ch + 1) * 128)
                            fw.op(fw.pe, lambda e: e.matmul(pKQ[:, C4(i)], lhsT=kT[:, c128], rhs=qT[:, c128], start=True, stop=True), reads=[kT, qT], writes=[pKQ], inc=(i == 3))
                            yield
                        fw.op(fw.dve, lambda e: e.tensor_tensor(out=inT[:, cs], in0=pKQ[:], in1=tT[:], op=ALU.mult), reads=[pKQ, tT], writes=[inT])
                        yield
                        fw.op(fw.pool, lambda e: e.tensor_tensor(out=tI[:], in0=tI[:], in1=self.noti4[:], op=ALU.mult), reads=[tI, self.noti4], writes=[tI])
                        yield
                        Pc, Qc, Rc = Pb[0], Qb[0], Rb[0]
                        for i, ch in enumerate(chs):
                            fw.op(fw.dve, lambda e: e.scalar_tensor_tensor(out=Pc[:, C4(i)], in0=pG[:, C4(i)], scalar=nbeta[:, col, ch:ch + 1], in1=tI[:, C4(i)], op0=ALU.mult, op1=ALU.mult), reads=[pG, nbeta, tI], writes=[Pc])
                            yield
                        for i in range(4):
                            fw.op(fw.pe, lambda e: e.transpose(pQ[:, C4(i)], Pc[:, C4(i)], self.ident[:]), reads=[Pc, self.ident], writes=[pQ], inc=(i == 3))
                            yield
                        fw.op(fw.act, lambda e: e.activation(out=Qc[:], in_=pQ[:], func=AF.Copy), reads=[pQ], writes=[Qc])
                        yield
                        fw.op(fw.dve, lambda e: e.tensor_tensor(out=Rc[:], in0=Qc[:], in1=self.identf4[:], op=ALU.add), reads=[Qc, self.identf4], writes=[Rc])
                        yield
                        for k in (range(6, 7) if os.environ.get('DN_SKIPT') else range(1, 7)):
                            Pn, Qn, Rn = Pb[k % 2], Qb[k % 2], Rb[k % 2]
                            for i in range(4):
                                fw.op(fw.pe, lambda e: e.matmul(pP[:, C4(i)], lhsT=Qc[:, C4(i)], rhs=Pc[:, C4(i)], start=True, stop=True), reads=[Qc, Pc], writes=[pP], inc=(i == 3))
                                yield
                            fw.op(fw.dve, lambda e: e.tensor_copy(out=Pn[:], in_=pP[:]), reads=[pP], writes=[Pn])
                            yield
                            if k < 6:
                                for i in range(4):
                                    fw.op(fw.pe, lambda e: e.matmul(pQ[:, C4(i)], lhsT=Pc[:, C4(i)], rhs=Qc[:, C4(i)], start=True, stop=True), reads=[Qc, Pc], writes=[pQ], inc=(i == 3))
                                    yield
                                fw.op(fw.act, lambda e: e.activation(out=Qn[:], in_=pQ[:], func=AF.Copy), reads=[pQ], writes=[Qn])
                                yield
                            for i in range(4):
                                fw.op(fw.pe, lambda e: e.matmul(pR[:, C4(i)], lhsT=Pn[:, C4(i)], rhs=Rc[:, C4(i)], start=True, stop=True), reads=[Pn, Rc], writes=[pR], inc=(i == 3))
                                yield
                            fw.op(fw.dve, lambda e: e.tensor_tensor(out=Rn[:], in0=Rc[:], in1=pR[:], op=ALU.add), reads=[pR, Rc], writes=[Rn])
                            yield
                            if k == 6:
                                for i, ch in enumerate(chs):
                                    if os.environ.get("DN_NOT"):
                                        fw.op(fw.dve, lambda e: e.tensor_scalar(out=TT[:, ch, :], in0=self.ident[:], scalar1=beta[:, col, ch:ch + 1], scalar2=None, op0=ALU.mult), reads=[pR, beta], writes=[TT])
                                        yield
                                    else:
                                        fw.op(fw.pool, lambda e: e.tensor_scalar(out=TT[:, ch, :], in0=Rn[:, C4(i)], scalar1=beta[:, col, ch:ch + 1], scalar2=None, op0=ALU.mult), reads=[Rn, beta], writes=[TT])
                                        yield
                            Pc, Qc, Rc = Pn, Qn, Rn
                    tmpA, tmpB = mk_tmp('a'), mk_tmp('b')
                    if not os.environ.get('DN_SKIPTAB'):
                        for ga, gb_ in ((0, 1), (2, 3)):
                            self.interleave([tab_gen(ga, tmpA, [P[1], P[2], P[3]]), tab_gen(gb_, tmpB, [P[4], P[5], P[6]])])
                    fw.release(mt)
                    return dict(col=col, qdT=qdT, TT=TT, inT=inT, sm=sm)
            def scan_gen(d, B):
                    col, qdT, TT, inT, sm = B["col"], B["qdT"], B["TT"], B["inT"], B["sm"]
                    Sf = fw.buf([128, 128], F32, "dn_Sf")
                    v2b = [fw.buf([128, 128], BF16, "dn_v2%d" % i) for i in range(2)]
                    Sb = fw.buf([128, 128], BF16, "dn_Sb")
                    rb = [fw.buf([128, 128], BF16, "dn_r%d" % i) for i in range(2)]
                    vn = [fw.buf([128, 128], BF16, "dn_vn%d" % i) for i in range(2)]
                    fw.op(fw.dve, lambda e: e.memset(Sf[:], 0.0), writes=[Sf])
                    yield
                    fw.op(fw.dve, lambda e: e.memset(Sb[:], 0.0), writes=[Sb])
                    yield
                    order = list(range(16)) if d == 0 else list(range(15, -1, -1))
                    for si, ch in enumerate(order[:1] if os.environ.get('DN_SKIPS') else order):
                        c128 = slice(ch * 128, (ch + 1) * 128)
                        pa, po = P[4 * d + (si % 2)], P[4 * d + 2 + (si % 2)]
                        v2_ = v2b[si % 2]
                        r_, v_ = rb[si % 2], vn[si % 2]
                        fw.op(fw.pe, lambda e: e.matmul(pa[:, 0:128], lhsT=kT[:, c128], rhs=Sb[:], start=True, stop=True), reads=[kT, Sb], writes=[pa])
                        yield
                        fw.op(fw.dve, lambda e: e.scalar_tensor_tensor(out=r_[:], in0=pa[:, 0:128], scalar=sm[:, 5, ch:ch + 1], in1=vtok[:, ch, :], op0=ALU.mult, op1=ALU.add), reads=[vtok, pa, sm], writes=[r_])
                        yield
                        fw.op(fw.pe, lambda e: e.matmul(pa[:, 128:256], lhsT=TT[:, ch, :], rhs=r_[:], start=True, stop=True), reads=[TT, r_], writes=[pa])
                        yield
                        fw.op(fw.act, lambda e: e.activation(out=v_[:], in_=pa[:, 128:256], func=AF.Copy), reads=[pa], writes=[v_])
                        yield
                        fw.op(fw.act, lambda e: e.activation(out=v2_[:], in_=pa[:, 128:256], func=AF.Copy, scale=sm[:, 3, ch:ch + 1]), reads=[pa, sm], writes=[v2_])
                        yield
                        fw.op(fw.pe, lambda e: e.matmul(po[:, 0:128], lhsT=Sb[:], rhs=qdT[:, c128], start=True, stop=False), reads=[Sb, qdT], writes=[po], inc=False)
                        yield
                        fw.op(fw.pe, lambda e: e.matmul(po[:, 0:128], lhsT=v_[:], rhs=inT[:, c128], start=False, stop=True), reads=[v_, inT], writes=[po])
                        yield
                        fw.op(fw.dve, lambda e: e.tensor_tensor(out=oT[:, c128], in0=oT[:, c128], in1=po[:, 0:128], op=ALU.add), reads=[po, oT], writes=[oT])
                        yield
                        fw.op(fw.pe, lambda e: e.matmul(pa[:, 256:384], lhsT=ktok[:, ch, :], rhs=v2_[:], start=True, stop=True), reads=[ktok, v2_], writes=[pa])
                        yield
                        fw.op(fw.dve, lambda e: e.scalar_tensor_tensor(out=Sb[:], in0=Sf[:], scalar=sm[:, 4, ch:ch + 1], in1=pa[:, 256:384], op0=ALU.mult, op1=ALU.add), reads=[Sf, sm, pa], writes=[Sb])
                        yield
                        fw.op(fw.dve, lambda e: e.scalar_tensor_tensor(out=Sf[:], in0=Sf[:], scalar=sm[:, 4, ch:ch + 1], in1=pa[:, 256:384], op0=ALU.mult, op1=ALU.add), reads=[Sf, sm, pa], writes=[Sf])
                        yield
            dirs_ = [int(v) for v in os.environ.get('DN_DIRS', '0,1').split(',')]
            fw.op(fw.dve, lambda e: e.memset(oT[:], 0.0), writes=[oT])
            BB = [dir_tables(d) for d in dirs_]
            self.interleave([scan_gen(d, B) for d, B in zip(dirs_, BB)])
            self.prot = list(range(8))
            wz = fw.buf([128, 8, 128], BF16, "dn_wz")
            fw.dma(fw.pool, wz[:], wsrc[:, :, 1536 + h * 128:1536 + (h + 1) * 128], writes=[wz])
            sq = [fw.buf([128, 512], BF16, "dn_fsq%d" % i) for i in range(2)]
            rs = [fw.buf([128, 512], F32, "dn_frs%d" % i) for i in range(2)]
            sg = [fw.buf([128, 512], BF16, "dn_fsg%d" % i) for i in range(4)]
            for nt in range(NT):
                pz = self.proj(wz, 0, 128, nt)
                fw.op(fw.act, lambda e: e.activation(out=sg[nt][:], in_=pz[:], func=AF.Silu), reads=[pz], writes=[sg[nt]])
            for nt in range(NT):
                sl = slice(nt * 512, (nt + 1) * 512)
                q_, r_, g_ = sq[nt % 2], rs[nt % 2], sg[nt]
                fw.op(fw.act, lambda e: e.activation(out=q_[:], in_=oT[:, sl], func=AF.Square), reads=[oT], writes=[q_])
                pn = self.nextp()
                fw.op(fw.pe, lambda e: e.matmul(pn[:], lhsT=self.onesb[:], rhs=q_[:], start=True, stop=True), reads=[q_, self.onesb], writes=[pn])
                fw.op(fw.act, lambda e: e.activation(out=r_[:], in_=pn[:], func=AF.Ln, bias=self.eps[:], scale=1.0 / 128), reads=[pn, self.eps], writes=[r_])
                fw.op(fw.act, lambda e: e.activation(out=r_[:], in_=r_[:], func=AF.Exp, scale=-0.5), reads=[r_], writes=[r_])
                fw.op(fw.dve, lambda e: e.scalar_tensor_tensor(out=r_[:], in0=oT[:, sl], scalar=self.dnp[:, 24:25], in1=r_[:], op0=ALU.mult, op1=ALU.mult), reads=[oT, self.dnp, r_], writes=[r_])
                fw.op(fw.dve, lambda e: e.tensor_tensor(out=self.yT_t[:, h, sl], in0=r_[:], in1=g_[:], op=ALU.mult), reads=[r_, g_], writes=[self.yT[h]])
            fw.release(mh, hard=True)
        fw.release(m)

    def interleave(self, gens):
        gens = list(gens)
        while gens:
            for g in list(gens):
                try:
                    next(g)
                except StopIteration:
                    gens.remove(g)

    def load_w(self, wap, c0, ncols, tag):
        fw = self.fw
        w = fw.buf([128, 8, ncols], BF16, tag)
        src = wap.rearrange("(kc p) n -> p kc n", p=128)
        half = ncols // 2 if ncols >= 256 else ncols
        for a in range(0, ncols, half):
            fw.dma(fw.pool, w[:, :, a:a + half], src[:, :, c0 + a:c0 + a + half], writes=[w])
        return w

    def proj(self, w, col, ncols, nt, p=None):
        fw = self.fw
        if p is None:
            p = self.nextp()
        sl = slice(nt * 512, (nt + 1) * 512)
        for kc in range(8):
            fw.op(fw.pe, lambda e: e.matmul(p[0:ncols, :], lhsT=w[:, kc, col:col + ncols], rhs=self.hT_t[:, kc, sl], start=(kc == 0), stop=(kc == 7)),
                  reads=[w, self.hT[kc][nt]], writes=[p], inc=(kc == 7))
        return p

    def out_proj(self, wap):
        fw = self.fw
        m = fw.mark()
        self.prot = list(range(8))
        wo = fw.sb([128, 8, D], BF16, "wo")
        wsrc_ = wap.rearrange("(kc p) n -> p kc n", p=128)
        wob = [fw.view(wo, "wo%d" % c) for c in range(8)]
        for c in range(8):
            fw.dma(fw.pool, wo[:, :, c * 128:(c + 1) * 128], wsrc_[:, :, c * 128:(c + 1) * 128], writes=[wob[c]])
        for nt in range(NT):
            for c in range(8):
                sl = slice(nt * 512, (nt + 1) * 512)
                po = self.nextp()
                for kc in range(8):
                    fw.op(fw.pe, lambda e: e.matmul(po[:], lhsT=wo[:, kc, c * 128:(c + 1) * 128], rhs=self.yT_t[:, kc, sl], start=(kc == 0), stop=(kc == 7)),
                          reads=[wob[c], self.yT[kc]], writes=[po], inc=(kc == 7))
                fw.op(fw.dve, lambda e: e.tensor_tensor(out=self.xT_t[:, c, sl], in0=self.xT_t[:, c, sl], in1=po[:], op=ALU.add),
                      reads=[po, self.xT[c][nt]], writes=[self.xT[c][nt]])
        fw.release(m)

    def mixer_cd(self):
        fw = self.fw
        m0 = fw.mark()
        self.dlt = fw.buf([128, 512], F32, "dlt")
        fw.dma(fw.sp, self.dlt[:], self.dlt_d[:, :], writes=[self.dlt])
        self.cnyrow = fw.buf([1, S], BF16, "cnyrow")
        fw.dma(fw.sp, self.cnyrow[:], self.cnyrow_d[:, :], writes=[self.cnyrow])
        if "ret" in self.stages:
            self.retention()
        else:
            for c in range(4):
                fw.op(fw.dve, lambda e: e.memset(self.yT_t[:, c, :], 0.0), writes=[self.yT[c]])
        if "hy" in self.stages:
            self.hyena()
        else:
            for c in range(4, 8):
                fw.op(fw.dve, lambda e: e.memset(self.yT_t[:, c, :], 0.0), writes=[self.yT[c]])
        fw.release(m0)
        self.out_proj(self.cd_w_out)

    def retention(self):
        fw = self.fw
        W = self.cd_w_in
        m = fw.mark()
        self.prot = list(range(8))
        vtok = fw.buf([128, 16, 512], BF16, "vtok")
        qr = [fw.buf([128, S], BF16, "qr%d" % i) for i in range(2)]
        kr = [fw.buf([128, S], BF16, "kr%d" % i) for i in range(2)]
        m2 = fw.mark()
        wv = self.load_w(W, 512, 512, "wv")
        for tt in range(16):
            p = self.nextp()
            for kc in range(8):
                fw.op(fw.pe, lambda e: e.matmul(p[:], lhsT=self.hT_t[:, kc, tt * 128:(tt + 1) * 128], rhs=wv[:, kc, :], start=(kc == 0), stop=(kc == 7)),
                      reads=[wv, self.hT[kc][tt // 4]], writes=[p], inc=(kc == 7))
            fw.op(fw.act, lambda e: e.activation(out=vtok[:, tt, :], in_=p[:], func=AF.Copy), reads=[p], writes=[vtok])
        fw.release(m2)
        m2 = fw.mark()
        wqk = self.load_w(W, 0, 512, "wqk")
        qsb = [fw.buf([128, 512], BF16, "rqs%d" % i) for i in range(2)]
        t1 = [fw.buf([128, 512], F32, "rt1_%d" % i) for i in range(2)]
        t2 = [fw.buf([128, 512], F32, "rt2_%d" % i) for i in range(2)]
        k = 0
        for which, dst in ((0, qr), (1, kr)):
            for qc in range(2):
                col = which * 256 + qc * 128
                for nt in range(NT):
                    sl = slice(nt * 512, (nt + 1) * 512)
                    p1 = self.proj(wqk, col, 128, nt)
                    qs_ = qsb[k % 2]
                    fw.op(fw.dve, lambda e: e.tensor_copy(out=qs_[:], in_=p1[:]), reads=[p1], writes=[qs_])
                    p2 = self.nextp()
                    fw.op(fw.pe, lambda e: e.matmul(p2[:], lhsT=self.rperm[:], rhs=qs_[:], start=True, stop=True), reads=[self.rperm, qs_], writes=[p2])
                    a, b = t1[k % 2], t2[k % 2]
                    k += 1
                    fw.op(fw.dve, lambda e: e.tensor_tensor(out=a[:], in0=p1[:], in1=self.ropec[:, sl], op=ALU.mult), reads=[p1, self.ropec], writes=[a])
                    fw.op(fw.dve, lambda e: e.tensor_tensor(out=b[:], in0=p2[:], in1=self.ropes[:, sl], op=ALU.mult), reads=[p2, self.ropes], writes=[b])
                    fw.op(fw.pool, lambda e: e.tensor_tensor(out=dst[qc][:, sl], in0=a[:], in1=b[:], op=ALU.add), reads=[a, b], writes=[dst[qc]])
        fw.release(m2)
        ld = self.ld
        Lf = fw.buf([128, 512], BF16, "Lf")
        Lb = fw.buf([128, 512], BF16, "Lb")
        DC = [fw.buf([128, 512], BF16, "DC%d" % r) for r in range(4)]
        fac = fw.buf([128, 32], F32, "fac")
        scb = [fw.buf([128, 512], BF16, "scb%d" % i) for i in range(6)]
        oT = fw.buf([128, S], F32, "oT")
        sqb = [fw.buf([128, 512], BF16, "rsq%d" % i) for i in range(2)]
        rsb = [fw.buf([128, 512], F32, "rrs%d" % i) for i in range(2)]
        ea, eb = rsb[0], rsb[1]
        sgt = [fw.buf([128, 512], BF16, "rsg%d" % i) for i in range(2)]
        wg = fw.buf([128, 8, 128], BF16, "wg")
        wsrc = W.rearrange("(kc p) n -> p kc n", p=128)
        ks = 0
        for h in range(4):
            qc, po = h // 2, (h % 2) * 64
            lgf, lgb = ld[:, h:h + 1], ld[:, 4 + h:5 + h]
            fw.op(fw.act, lambda e: e.activation(out=Lf[:], in_=self.dlt[:], func=AF.Exp, bias=self.ld128[:, h:h + 1], scale=lgf), reads=[self.dlt, self.ld128, ld], writes=[Lf])
            fw.op(fw.act, lambda e: e.activation(out=Lb[:], in_=self.dlt[:], func=AF.Exp, bias=self.ld512[:, 4 + h:5 + h], scale=self.ldn[:, 4 + h:5 + h]), reads=[self.dlt, self.ld512, self.ldn], writes=[Lb])
            fw.op(fw.act, lambda e: e.activation(out=fac[:, 0:16], in_=self.iota16[:], func=AF.Exp, bias=self.ln8[:], scale=lgf), reads=[self.iota16, self.ln8, ld], writes=[fac])
            fw.op(fw.act, lambda e: e.activation(out=fac[:, 16:32], in_=self.iota16[:], func=AF.Exp, bias=self.ln8[:], scale=lgb), reads=[self.iota16, self.ln8, ld], writes=[fac])
            for r in range(4):
                fw.op(fw.dve, lambda e: e.tensor_scalar(out=ea[:], in0=self.dlt[:], scalar1=float(-128 * r), scalar2=0.0, op0=ALU.add, op1=ALU.max), reads=[self.dlt], writes=[ea])
                fw.op(fw.dve, lambda e: e.tensor_scalar(out=ea[:], in0=ea[:], scalar1=lgf, scalar2=None, op0=ALU.mult), reads=[ea, ld], writes=[ea])
                fw.op(fw.dve, lambda e: e.tensor_scalar(out=eb[:], in0=self.dlt[:], scalar1=float(-128 * r), scalar2=0.0, op0=ALU.add, op1=ALU.min), reads=[self.dlt], writes=[eb])
                fw.op(fw.dve, lambda e: e.scalar_tensor_tensor(out=eb[:], in0=eb[:], scalar=self.ldn[:, 4 + h:5 + h], in1=ea[:], op0=ALU.mult, op1=ALU.add), reads=[eb, ea, self.ldn], writes=[eb])
                fw.op(fw.act, lambda e: e.activation(out=DC[r][:], in_=eb[:], func=AF.Exp, bias=self.ln8[:], scale=1.0), reads=[eb, self.ln8], writes=[DC[r]])
            for nj in range(NT):
                pacc = self.P[nj % 2]
                self.prot = list(range(2, 8))
                pend = []

                def pv(sc, mi):
                    fw.op(fw.pe, lambda e: e.matmul(pacc[:], lhsT=vtok[:, mi, h * 128:(h + 1) * 128], rhs=sc[:], start=(mi == 0), stop=(mi == 15)),
                          reads=[vtok, sc], writes=[pacc], inc=True)
                for mi in range(16):
                    ps = self.nextp()
                    fw.op(fw.pe, lambda e: e.matmul(ps[:], lhsT=kr[qc][po:po + 64, mi * 128:(mi + 1) * 128], rhs=qr[qc][po:po + 64, nj * 512:(nj + 1) * 512], start=True, stop=True),
                          reads=[kr[qc], qr[qc]], writes=[ps], inc=True)
                    sc = scb[ks % 6]
                    ks += 1
                    r = mi - 4 * nj
                    if 0 <= r <= 3:
                        fw.op(fw.dve, lambda e: e.tensor_tensor(out=sc[:], in0=ps[:], in1=DC[r][:], op=ALU.mult), reads=[ps, DC[r]], writes=[sc])
                    elif r < 0:
                        kk = -r - 1
                        fw.op(fw.dve, lambda e: e.scalar_tensor_tensor(out=sc[:], in0=ps[:], scalar=fac[:, kk:kk + 1], in1=Lf[:], op0=ALU.mult, op1=ALU.mult), reads=[ps, fac, Lf], writes=[sc])
                    else:
                        kk = r - 4
                        fw.op(fw.dve, lambda e: e.scalar_tensor_tensor(out=sc[:], in0=ps[:], scalar=fac[:, 16 + kk:17 + kk], in1=Lb[:], op0=ALU.mult, op1=ALU.mult), reads=[ps, fac, Lb], writes=[sc])
                    pend.append((sc, mi))
                    if len(pend) > 4:
                        pv(*pend.pop(0))
                while pend:
                    pv(*pend.pop(0))
                fw.op(fw.act, lambda e: e.activation(out=oT[:, nj * 512:(nj + 1) * 512], in_=pacc[:], func=AF.Copy), reads=[pacc], writes=[oT])
            self.prot = list(range(2, 8))
            fw.dma(fw.pool, wg[:], wsrc[:, :, 1024 + h * 128:1024 + (h + 1) * 128], writes=[wg])
            for nt in range(NT):
                sl = slice(nt * 512, (nt + 1) * 512)
                sq, rs, sg = sqb[nt % 2], rsb[nt % 2], sgt[nt % 2]
                tb = rs
                fw.op(fw.act, lambda e: e.activation(out=sq[:], in_=oT[:, sl], func=AF.Square), reads=[oT], writes=[sq])
                pn = self.nextp()
                fw.op(fw.pe, lambda e: e.matmul(pn[:], lhsT=self.onesb[:], rhs=sq[:], start=True, stop=True), reads=[sq, self.onesb], writes=[pn])
                fw.op(fw.act, lambda e: e.activation(out=rs[:], in_=pn[:], func=AF.Ln, bias=self.eps[:], scale=1.0 / 128), reads=[pn, self.eps], writes=[rs])
                fw.op(fw.act, lambda e: e.activation(out=rs[:], in_=rs[:], func=AF.Exp, scale=-0.5), reads=[rs], writes=[rs])
                pg = self.proj(wg, 0, 128, nt)
                fw.op(fw.act, lambda e: e.activation(out=sg[:], in_=pg[:], func=AF.Silu), reads=[pg], writes=[sg])
                fw.op(fw.dve, lambda e: e.tensor_tensor(out=tb[:], in0=oT[:, sl], in1=rs[:], op=ALU.mult), reads=[oT, rs], writes=[tb])
                fw.op(fw.dve, lambda e: e.tensor_tensor(out=self.yT_t[:, h, sl], in0=tb[:], in1=sg[:], op=ALU.mult), reads=[tb, sg], writes=[self.yT[h]])
        self.prot = list(range(8))
        fw.release(m)


    def sin_rr(self, dst, src_ps, b_ap, f_ap, rows, tmpf, tmpi):
        fw = self.fw
        R = slice(0, rows)
        fw.op(fw.dve, lambda e: e.tensor_scalar(out=tmpf[0][R, :], in0=src_ps[R, :], scalar1=b_ap, scalar2=f_ap, op0=ALU.add, op1=ALU.mult),
              reads=[src_ps, self.hyp], writes=[tmpf[0]])
        fw.op(fw.dve, lambda e: e.tensor_scalar(out=tmpf[1][R, :], in0=tmpf[0][R, :], scalar1=1.0 / (2 * math.pi), scalar2=None, op0=ALU.mult),
              reads=[tmpf[0]], writes=[tmpf[1]])
        fw.op(fw.dve, lambda e: e.tensor_copy(out=tmpi[R, :], in_=tmpf[1][R, :]), reads=[tmpf[1]], writes=[tmpi])
        fw.op(fw.dve, lambda e: e.tensor_copy(out=tmpf[1][R, :], in_=tmpi[R, :]), reads=[tmpi], writes=[tmpf[1]])
        fw.op(fw.dve, lambda e: e.scalar_tensor_tensor(out=tmpf[0][R, :], in0=tmpf[1][R, :], scalar=-2 * math.pi, in1=tmpf[0][R, :], op0=ALU.mult, op1=ALU.add),
              reads=[tmpf[0], tmpf[1]], writes=[tmpf[0]])
        fw.op(fw.dve, lambda e: e.tensor_scalar(out=tmpf[0][R, :], in0=tmpf[0][R, :], scalar1=3.14159, scalar2=-3.14159, op0=ALU.min, op1=ALU.max),
              reads=[tmpf[0]], writes=[tmpf[0]])
        fw.op(fw.act, lambda e: e.activation(out=dst, in_=tmpf[0][R, :], func=AF.Sin), reads=[tmpf[0]], writes=[self.hidb])

    def fwd_dft(self, inC, inS, inCb, inSb, consume):
        fw = self.fw
        tabC = [fw.buf([128, 8, 128], BF16, "ftabC%d" % i) for i in range(2)]
        tabS = [fw.buf([128, 8, 128], BF16, "ftabS%d" % i) for i in range(2)]
        self.prot = [4, 5, 6, 7]
        for ft in range(16):
            csrc = self.dftcf[ft].rearrange("p (tt f) -> p tt f", tt=16)
            ssrc = self.dftsf[ft].rearrange("p (tt f) -> p tt f", tt=16)
            for hf in range(2):
                fw.dma(fw.sp, tabC[hf][:], csrc[:, hf * 8:hf * 8 + 8, :], writes=[tabC[hf]])
            pr = self.nextp()
            for tt in range(16):
                tb_ = tabC[tt // 8]
                fw.op(fw.pe, lambda e: e.matmul(pr[:], lhsT=tb_[:, tt % 8, :], rhs=inC[:, tt, :], start=(tt == 0), stop=(tt == 15)),
                      reads=[tb_, inCb], writes=[pr], inc=(tt % 8 == 7))
            for hf in range(2):
                fw.dma(fw.act, tabS[hf][:], ssrc[:, hf * 8:hf * 8 + 8, :], writes=[tabS[hf]])
            pi = self.nextp()
            for tt in range(16):
                tb_ = tabS[tt // 8]
                fw.op(fw.pe, lambda e: e.matmul(pi[:], lhsT=tb_[:, tt % 8, :], rhs=inS[:, tt, :], start=(tt == 0), stop=(tt == 15)),
                      reads=[tb_, inSb], writes=[pi], inc=(tt % 8 == 7))
            consume(ft, pr, pi)
        pr = self.nextp()
        for tt in range(16):
            fw.op(fw.pe, lambda e: e.matmul(pr[0:1, :], lhsT=self.cnycol[:, tt:tt + 1], rhs=inC[:, tt, :], start=(tt == 0), stop=(tt == 15)),
                  reads=[self.cnycol, inCb], writes=[pr], inc=(tt == 15))
        consume(16, pr, None)

    def hyena_setup(self):
        fw = self.fw
        m = fw.mark()
        self.prot = list(range(4))
        self.hidb = fw.buf([64, 2, S], F32, "hid")
        w3 = fw.buf([64, 1024], F32, "hw3")
        fw.dma(fw.sp, w3[:], self.hy_w3[:, :], writes=[w3])
        ma = fw.mark()
        zT = fw.buf([33, S], F32, "zT")
        fw.dma(fw.sp, zT[:], self.hyz[:, :], writes=[zT])
        w1 = fw.buf([33, 64], F32, "hw1")
        fw.dma(fw.sp, w1[:], self.hy_w1[:, :], writes=[w1])
        w2 = fw.buf([64, 64], F32, "hw2")
        fw.dma(fw.sp, w2[:], self.hy_w2[:, :], writes=[w2])
        hid = self.hidb
        tmpf = [fw.buf([64, 512], F32, "stf%d" % i) for i in range(2)]
        tmpi = fw.buf([64, 512], I32, "sti")
        hp = self.hyp
        for lyr in range(2):
            for nt in range(NT):
                sl = slice(nt * 512, (nt + 1) * 512)
                p = self.nextp()
                if lyr == 0:
                    fw.op(fw.pe, lambda e: e.matmul(p[0:64, :], lhsT=w1[0:33, :], rhs=zT[0:33, sl], start=True, stop=True), reads=[w1, zT], writes=[p])
                else:
                    fw.op(fw.pe, lambda e: e.matmul(p[0:64, :], lhsT=w2[0:64, :], rhs=hid[0:64, 0, sl], start=True, stop=True), reads=[w2, hid], writes=[p])
                self.sin_rr(hid[0:64, lyr, sl], p, hp[0:64, 2 * lyr:2 * lyr + 1], hp[0:64, 2 * lyr + 1:2 * lyr + 2], 64, tmpf, tmpi)
        fw.release(ma)
        hs = self.hT_t[:, 0:4, :].rearrange("p c (a b) -> p (c a) b", b=512)
        hd = self.hT_t[:, 4:8, :].rearrange("p c (a b) -> p (c a) b", b=512)
        hsb = fw.view(self.hT_t, "hsb", alias=[self.hT[c][n] for c in range(8) for n in range(NT)])
        dec = [fw.buf([128, 512], F32, "dec%d" % i) for i in range(2)]
        hfb = [fw.buf([128, 512], F32, "hfb%d" % i) for i in range(2)]
        hbb = [fw.buf([128, 512], F32, "hbb%d" % i) for i in range(2)]
        for tt in range(16):
            d_, hf, hb = dec[tt % 2], hfb[tt % 2], hbb[tt % 2]
            fw.dma(fw.sp, d_[:], self.hydec[tt * 128:(tt + 1) * 128, :], writes=[d_])
            pf = self.nextp()
            fw.op(fw.pe, lambda e: e.matmul(pf[:], lhsT=hid[0:64, 1, tt * 128:(tt + 1) * 128], rhs=w3[0:64, 0:512], start=True, stop=True), reads=[hid, w3], writes=[pf])
            pb = self.nextp()
            fw.op(fw.pe, lambda e: e.matmul(pb[:], lhsT=hid[0:64, 1, tt * 128:(tt + 1) * 128], rhs=w3[0:64, 512:1024], start=True, stop=True), reads=[hid, w3], writes=[pb])
            fw.op(fw.dve, lambda e: e.tensor_tensor(out=hf[:], in0=pf[:], in1=d_[:], op=ALU.mult), reads=[pf, d_], writes=[hf])
            fw.op(fw.dve, lambda e: e.tensor_tensor(out=hb[:], in0=pb[:], in1=d_[:], op=ALU.mult), reads=[pb, d_], writes=[hb])
            fw.op(fw.dve, lambda e: e.tensor_tensor(out=hs[:, tt, :], in0=hf[:], in1=hb[:], op=ALU.add), reads=[hf, hb], writes=[hsb])
            fw.op(fw.dve, lambda e: e.tensor_tensor(out=hd[:, tt, :], in0=hf[:], in1=hb[:], op=ALU.subtract), reads=[hf, hb], writes=[hsb])
        fw.release(m)
        m = fw.mark()
        so = [fw.buf([128, 2, 512], F32, "so%d" % i) for i in range(2)]

        def consume(ft, pr, pi):
            o = so[ft % 2]
            if ft < 16:
                fw.op(fw.dve, lambda e: e.tensor_scalar(out=o[:, 0, :], in0=pr[:], scalar1=self.wf[:, ft:ft + 1], scalar2=None, op0=ALU.mult), reads=[pr, self.wf], writes=[o])
                fw.op(fw.dve, lambda e: e.tensor_scalar(out=o[:, 1, :], in0=pi[:], scalar1=self.wf[:, ft:ft + 1], scalar2=None, op0=ALU.mult), reads=[pi, self.wf], writes=[o])
                fw.dma(fw.sp, self.spec_d[:, ft * 128:(ft + 1) * 128, :].rearrange("a p c -> p a c"), o[:], reads=[o])
            else:
                fw.op(fw.dve, lambda e: e.tensor_scalar(out=o[0:1, 0, :], in0=pr[0:1, :], scalar1=self.wf[0:1, 16:17], scalar2=None, op0=ALU.mult), reads=[pr, self.wf], writes=[o])
                fw.dma(fw.sp, self.spec_d[0:1, 2048:2049, :].rearrange("a p c -> p a c"), o[0:1, 0:1, :], reads=[o])
        self.fwd_dft(hs, hd, hsb, hsb, consume)
        self.prot = list(range(8))
        fw.release(m, hard=True)

    def hyena(self):
        fw = self.fw
        W = self.cd_w_in
        wsrc = W.rearrange("(kc p) n -> p kc n", p=128)
        m = fw.mark()
        x0T = fw.buf([128, 4, S], BF16, "x0T")
        utok = fw.buf([128, 16, 512], BF16, "utok")
        m2 = fw.mark()
        self.prot = list(range(8))
        gb = [fw.buf([128, S + 2], F32, "hgb%d" % i) for i in range(1)]
        for g in gb:
            fw.op(fw.dve, lambda e: e.memset(g[:, 0:1], 0.0), writes=[g])
            fw.op(fw.dve, lambda e: e.memset(g[:, S + 1:S + 2], 0.0), writes=[g])
        a1 = fw.buf([128, S], F32, "ha1")
        a2 = fw.buf([128, S], F32, "ha2")
        wp = [fw.buf([128, 8, 128], BF16, "hwp%d" % i) for i in range(2)]
        hcw = self.hcw
        k = 0
        for cc in range(4):
            for part in (1, 2, 0):
                ch = part * 4 + cc
                w = wp[k % 2]
                g = gb[0]
                k += 1
                fw.dma(fw.pool, w[:], wsrc[:, :, 1536 + ch * 128:1536 + (ch + 1) * 128], writes=[w])
                for nt in range(NT):
                    p = self.proj(w, 0, 128, nt)
                    fw.op(fw.act, lambda e: e.activation(out=g[:, 1 + nt * 512:1 + (nt + 1) * 512], in_=p[:], func=AF.Copy), reads=[p], writes=[g])
                dst = a1 if part == 1 else a2
                fw.op(fw.dve, lambda e: e.tensor_scalar(out=dst[:], in0=g[:, 0:S], scalar1=hcw[:, ch, 0:1], scalar2=hcw[:, ch, 3:4], op0=ALU.mult, op1=ALU.add), reads=[g, hcw], writes=[dst])
                fw.op(fw.dve, lambda e: e.scalar_tensor_tensor(out=dst[:], in0=g[:, 1:S + 1], scalar=hcw[:, ch, 1:2], in1=dst[:], op0=ALU.mult, op1=ALU.add), reads=[g, hcw, dst], writes=[dst])
                if part == 1:
                    fw.op(fw.dve, lambda e: e.scalar_tensor_tensor(out=dst[:], in0=g[:, 2:S + 2], scalar=hcw[:, ch, 2:3], in1=dst[:], op0=ALU.mult, op1=ALU.add), reads=[g, hcw, dst], writes=[dst])
                elif part == 2:
                    fw.op(fw.dve, lambda e: e.scalar_tensor_tensor(out=dst[:], in0=g[:, 2:S + 2], scalar=hcw[:, ch, 2:3], in1=dst[:], op0=ALU.mult, op1=ALU.add), reads=[g, hcw, dst], writes=[dst])
                    fw.op(fw.dve, lambda e: e.tensor_tensor(out=self.yT_t[:, 4 + cc, :], in0=a1[:], in1=a2[:], op=ALU.mult), reads=[a1, a2], writes=[self.yT[4 + cc]])
                    for tq in range(4):
                        p = self.nextp()
                        pb = p[:].bitcast(BF16)
                        for t4 in range(4):
                            tt = tq * 4 + t4
                            fw.op(fw.pe, lambda e: e.transpose(pb[:, t4 * 128:(t4 + 1) * 128], self.yT_t[:, 4 + cc, tt * 128:(tt + 1) * 128], self.identb[:]),
                                  reads=[self.yT[4 + cc], self.identb], writes=[p], inc=(t4 == 3))
                        fw.op(fw.act, lambda e: e.activation(out=utok[:, tq * 4:tq * 4 + 4, cc * 128:(cc + 1) * 128], in_=pb[:, 0:512].rearrange("p (a b) -> p a b", a=4), func=AF.Copy),
                              reads=[p], writes=[utok])
                else:
                    fw.op(fw.dve, lambda e: e.scalar_tensor_tensor(out=x0T[:, cc, :], in0=g[:, 2:S + 2], scalar=hcw[:, ch, 2:3], in1=dst[:], op0=ALU.mult, op1=ALU.add), reads=[g, hcw, dst], writes=[x0T])
        fw.release(m2)
        Yr_t = self.hT_t[:, 0:4, :].rearrange("p c (a b) -> p (c a) b", b=512)
        Yi_t = self.hT_t[:, 4:8, :].rearrange("p c (a b) -> p (c a) b", b=512)
        Y = fw.view(self.hT_t, "Yspec", alias=[self.hT[c][n] for c in range(8) for n in range(NT)])
        Yny = fw.buf([1, 512], BF16, "Yny")
        m3 = fw.mark()
        spt = [fw.buf([128, 2, 512], F32, "spt%d" % i) for i in range(2)]
        tA = [fw.buf([128, 512], F32, "tA%d" % i) for i in range(2)]
        tB = [fw.buf([128, 512], F32, "tB%d" % i) for i in range(2)]

        def consume(ft, pr, pi):
            sp_ = spt[ft % 2]
            A, B = tA[ft % 2], tB[ft % 2]
            if ft < 16:
                fw.dma(fw.sp, sp_[:], self.spec_d[:, ft * 128:(ft + 1) * 128, :].rearrange("a p c -> p a c"), writes=[sp_])
                fw.op(fw.dve, lambda e: e.tensor_tensor(out=A[:], in0=pr[:], in1=sp_[:, 0, :], op=ALU.mult), reads=[pr, sp_], writes=[A])
                fw.op(fw.dve, lambda e: e.tensor_tensor(out=B[:], in0=pi[:], in1=sp_[:, 1, :], op=ALU.mult), reads=[pi, sp_], writes=[B])
                fw.op(fw.pool, lambda e: e.tensor_tensor(out=Yr_t[:, ft, :], in0=A[:], in1=B[:], op=ALU.subtract), reads=[A, B], writes=[Y])
                A2, B2 = tA[(ft + 1) % 2], tB[(ft + 1) % 2]
                fw.op(fw.dve, lambda e: e.tensor_tensor(out=A2[:], in0=pr[:], in1=sp_[:, 1, :], op=ALU.mult), reads=[pr, sp_], writes=[A2])
                fw.op(fw.dve, lambda e: e.tensor_tensor(out=B2[:], in0=pi[:], in1=sp_[:, 0, :], op=ALU.mult), reads=[pi, sp_], writes=[B2])
                fw.op(fw.pool, lambda e: e.tensor_tensor(out=Yi_t[:, ft, :], in0=A2[:], in1=B2[:], op=ALU.add), reads=[A2, B2], writes=[Y])
            else:
                fw.dma(fw.sp, sp_[0:1, 0:1, :], self.spec_d[0:1, 2048:2049, :].rearrange("a p c -> p a c"), writes=[sp_])
                fw.op(fw.dve, lambda e: e.tensor_tensor(out=Yny[0:1, :], in0=pr[0:1, :], in1=sp_[0:1, 0, :], op=ALU.mult), reads=[pr, sp_], writes=[Yny])
        self.fwd_dft(utok[:, :, :], utok[:, :, :], utok, utok, consume)
        fw.release(m3)
        itC = [fw.buf([128, 4, 512], BF16, "itC%d" % i) for i in range(2)]
        itS = [fw.buf([128, 4, 512], BF16, "itS%d" % i) for i in range(2)]
        csrc = self.dftc.rearrange("(ft p) t -> p ft t", p=128)
        ssrc = self.dfts.rearrange("(ft p) t -> p ft t", p=128)
        ep = [fw.buf([128, 512], F32, "hep%d" % i) for i in range(2)]
        kk = 0
        ke = 0
        for nt in range(NT):
            sl = slice(nt * 512, (nt + 1) * 512)
            banks = [self.P[(nt % 2) * 4 + cc] for cc in range(4)]
            for fg in range(4):
                tc_, ts_ = itC[kk % 2], itS[kk % 2]
                kk += 1
                fw.dma(fw.sp, tc_[:], csrc[:, fg * 4:fg * 4 + 4, sl], writes=[tc_])
                fw.dma(fw.act, ts_[:], ssrc[:, fg * 4:fg * 4 + 4, sl], writes=[ts_])
                for f4 in range(4):
                    ft = fg * 4 + f4
                    for cc in range(4):
                        fw.op(fw.pe, lambda e: e.matmul(banks[cc][:], lhsT=Yr_t[:, ft, cc * 128:(cc + 1) * 128], rhs=tc_[:, f4, :], start=(ft == 0), stop=False),
                              reads=[Y, tc_], writes=[banks[cc]], inc=False)
                        fw.op(fw.pe, lambda e: e.matmul(banks[cc][:], lhsT=Yi_t[:, ft, cc * 128:(cc + 1) * 128], rhs=ts_[:, f4, :], start=False, stop=False),
                              reads=[Y, ts_], writes=[banks[cc]], inc=(cc == 3 and f4 == 3))
            for cc in range(4):
                fw.op(fw.pe, lambda e: e.matmul(banks[cc][:], lhsT=Yny[0:1, cc * 128:(cc + 1) * 128], rhs=self.cnyrow[0:1, sl], start=False, stop=True),
                      reads=[Yny, self.cnyrow], writes=[banks[cc]], inc=True)
                t_ = ep[ke % 2]
                ke += 1
                fw.op(fw.dve, lambda e: e.scalar_tensor_tensor(out=t_[:], in0=self.yT_t[:, 4 + cc, sl], scalar=self.hybias[:, cc:cc + 1], in1=banks[cc][:], op0=ALU.mult, op1=ALU.add),
                      reads=[self.yT[4 + cc], self.hybias, banks[cc]], writes=[t_])
                fw.op(fw.dve, lambda e: e.tensor_tensor(out=self.yT_t[:, 4 + cc, sl], in0=t_[:], in1=x0T[:, cc, sl], op=ALU.mult), reads=[t_, x0T], writes=[self.yT[4 + cc]])
        self.prot = list(range(8))
        fw.release(m, hard=True)


def host_consts():
    c = {"ident": np.eye(128, dtype=np.float32)}
    inv = 10000.0 ** (-np.arange(0, 64, 2, dtype=np.float64) / 64)
    ang = np.arange(S, dtype=np.float64)[None, :] * inv[:, None]
    cos64 = np.concatenate([np.cos(ang), np.cos(ang)], 0)
    sin64 = np.concatenate([-np.sin(ang), np.sin(ang)], 0)
    c["ropec"] = np.concatenate([cos64, cos64], 0).astype(ml_dtypes.bfloat16)
    c["ropes"] = np.concatenate([sin64, sin64], 0).astype(ml_dtypes.bfloat16)
    pm = np.zeros((128, 128), np.float32)
    pm[rot_perm(128, 64), np.arange(128)] = 1.0
    c["rperm"] = pm.astype(ml_dtypes.bfloat16)
    c["dlt"] = (np.arange(512, dtype=np.float32)[None, :] - np.arange(128, dtype=np.float32)[:, None]).astype(np.float32)
    c["iota16"] = np.tile(128.0 * np.arange(16, dtype=np.float32)[None, :], (128, 1)).astype(np.float32)
    dl = np.arange(2944)[None, :] - np.arange(128)[:, None] - 1408
    ad = np.abs(dl)
    mult = (ad <= 64).astype(np.float32) + ((dl % 4 == 0) & (ad <= 256)) + ((dl % 16 == 0) & (ad <= 1024))
    c["dilmask"] = mult.astype(ml_dtypes.bfloat16)
    ii = np.arange(128)
    triF = (ii[:, None] <= ii[None, :]).astype(np.float32)
    triB = (ii[:, None] >= ii[None, :]).astype(np.float32)
    c["dn_tri"] = np.ascontiguousarray(np.stack([triF, triB], 1))
    mL = np.where(ii[None, :] <= ii[:, None], 0.0, -30000.0).astype(np.float32)
    mU = np.where(ii[None, :] >= ii[:, None], 0.0, -30000.0).astype(np.float32)
    noti = (1.0 - np.eye(128)).astype(np.float32)
    c["dn_msk"] = np.ascontiguousarray(np.stack([np.tile(mL, (1, 4)), np.tile(mU, (1, 4)), np.tile(noti, (1, 4))], 1)).astype(ml_dtypes.bfloat16)
    L = S
    t = np.linspace(0.0, 1.0, L, dtype=np.float32)[:, None]
    w = (2.0 * math.pi * np.arange(L, dtype=np.float32) / L).astype(np.float32)
    fb = np.linspace(1e-4, 15, 16, dtype=np.float32)
    angz = w[:, None] * fb[None, :]
    z = np.concatenate([t, np.cos(angz), -np.sin(angz)], -1).astype(np.float32)
    c["hyz"] = np.ascontiguousarray(z.T)
    deltas = np.abs(np.linspace(math.log(1e-2) / 1.5, math.log(1e-2) / 0.3, 512, dtype=np.float32))
    c["hydec"] = np.exp(-t * deltas[None, :]).astype(np.float32)
    ab = (np.arange(S, dtype=np.int64)[:, None] * np.arange(S, dtype=np.int64)[None, :]) % (2 * S)
    th = ab.astype(np.float64) * (2.0 * math.pi / (2 * S))
    c["dftc"] = np.cos(th).astype(ml_dtypes.bfloat16)
    c["dfts"] = np.sin(th).astype(ml_dtypes.bfloat16)
    c["dftcf"] = np.ascontiguousarray(c["dftc"].reshape(16, 128, 16, 128).transpose(2, 1, 0, 3).reshape(16, 128, 2048))
    c["dftsf"] = np.ascontiguousarray(c["dfts"].reshape(16, 128, 16, 128).transpose(2, 1, 0, 3).reshape(16, 128, 2048))
    sign = (1.0 - 2.0 * (np.arange(S) % 2)).astype(np.float32)
    c["cnycol"] = np.ascontiguousarray(sign.reshape(16, 128).T).astype(ml_dtypes.bfloat16)
    c["cnyrow"] = sign.reshape(1, S).astype(ml_dtypes.bfloat16)
    wf = np.full((128, 17), 2.0 / (2 * S), dtype=np.float32)
    wf[0, 0] = 1.0 / (2 * S)
    wf[:, 16] = 1.0 / (2 * S)
    c["wf"] = wf
    return c


def rot_perm(ncols, hd):
    idx = np.arange(ncols)
    h, i = idx // hd, idx % hd
    return h * hd + (i + hd // 2) % hd


def host_layout(inputs):
    f = lambda a: np.ascontiguousarray(a, dtype=np.float32)
    o = {}
    for k in ("norm_mix", "norm_ffn", "final_norm", "ffn_conv_w", "ffn_conv_b", "ffn_w_out"):
        o[k] = f(inputs[k])
    wi = np.asarray(inputs["ffn_w_in"], dtype=np.float32).reshape(2, 8, 128, 2, 22, 128)
    o["ffn_w_in_t"] = f(wi.transpose(0, 4, 2, 1, 3, 5).reshape(2, 22, 128, 2048))
    ab = f(inputs["ab_w_in"][0])
    o["ab_w_in"] = ab
    o["ab_w_rot"] = f(ab[:, 2064:2064 + 1024][:, rot_perm(1024, 64)])
    o["ab_w_out"] = f(inputs["ab_w_out"][0])
    dnp = np.zeros((128, 25), np.float32)
    dnp[:, 0:8] = inputs["dn_a_log"][0].reshape(1, 8)
    dnp[:, 8:16] = inputs["dn_dt_bias"][0].reshape(1, 8)
    dnp[:, 24] = inputs["dn_norm_w"][0]
    o["dnp"] = dnp
    o["dcw"] = f(inputs["dn_conv_w"][0].reshape(3, 12, 128).transpose(2, 1, 0))
    cd = f(inputs["cd_w_in"][0])
    o["cd_w_in"] = cd
    o["cd_w_rot"] = f(cd[:, 0:512][:, rot_perm(512, 64)])
    o["cd_w_out"] = f(inputs["cd_w_out"][0])
    o["ret_log_decay"] = f(inputs["ret_log_decay"][0].reshape(8))
    o["hy_w1"] = f(inputs["hy_w1"][0])
    o["hy_w2"] = f(inputs["hy_w2"][0])
    o["hy_w3"] = f(inputs["hy_w3"][0])
    o["hyp"] = f(np.stack([inputs["hy_b1"][0], inputs["hy_f1"][0], inputs["hy_b2"][0], inputs["hy_f2"][0]], 1))
    hc = np.concatenate([inputs["hy_conv_w"][0], inputs["hy_conv_b"][0][None, :]], 0)
    o["hcw"] = f(hc.reshape(4, 12, 128).transpose(2, 1, 0))
    o["hybias"] = f(inputs["hy_bias"][0].reshape(4, 128).T)
    return o


_CACHE = {}


def run(inputs, nseq_per_core=4, ncores=NCORES, stages=("ab", "dn", "dil", "ffn0", "cd", "ret", "hy", "ffn1")):
    key = (nseq_per_core, tuple(stages))
    prog = Prog(nseq_per_core, stages)
    nc = prog.build()
    consts = host_consts()
    lay = host_layout(inputs)
    x = np.ascontiguousarray(inputs["x"], dtype=np.float32)
    in_maps = []
    for c in range(ncores):
        m = {}
        for name in prog.dram:
            if name == "x":
                m[name] = x[c * nseq_per_core:(c + 1) * nseq_per_core]
            elif name in consts:
                m[name] = consts[name]
            else:
                m[name] = lay[name]
        in_maps.append(m)
    res = run_bass_kernel_spmd(nc, in_maps, core_ids=list(range(ncores)))
    return np.concatenate([res.results[c]["out"] for c in range(ncores)], axis=0)


def kernel(**inputs):
    return run(inputs).astype(np.float32)
```
